# Optimizing a Trainium2 kernel written in Bass

```python
import jax, jax.numpy as jnp
from jax import lax
import numpy as np

D_MODEL = 1024
BATCH = 32
SEQ = 256
DEPTH = 2
DEC_BATCH = 4
DEC_SEQ = 1024
PAST_LEN = 512

GRID_W = 64
HEAD_DIM = 64
FOURIER_W = D_MODEL // 4
FOURIER_GROUPS = FOURIER_W // HEAD_DIM
HGRN_W = D_MODEL // 4
HGRN_HEADS = HGRN_W // HEAD_DIM
ATTN_W = D_MODEL // 2
ATTN_HEADS = ATTN_W // HEAD_DIM
ATTN_KV_HEADS = 2
ATTN_GROUP = ATTN_HEADS // ATTN_KV_HEADS
KV_W = ATTN_KV_HEADS * HEAD_DIM
MIX_W = FOURIER_W + HGRN_W + ATTN_W
SPLIT_SIZES = (FOURIER_W, HGRN_W, HGRN_W, HGRN_W, HGRN_W, HGRN_W, ATTN_W, KV_W, KV_W)
IN_W = sum(SPLIT_SIZES)
WINDOW = 128
BLOCK = 128
CHUNK = 16
D_FF = 2816
ROPE_BASE = 10000.0
EPS = 1e-6
N_MOD = 9
NEG_BIG = -1e30
LB_FLOOR = 1e-30

kernel_name = 'hybrid_fourier_hgrn2_swa_diffusion_step'


def rmsnorm(x, w):
    xf = x.astype(jnp.float32)
    y = xf * lax.rsqrt(jnp.mean(xf * xf, axis=-1, keepdims=True) + EPS)
    return (y * w.astype(jnp.float32)).astype(x.dtype)


def modulate(x, g, shift, scale):
    return rmsnorm(x, g) * (1 + scale) + shift


def adaln(cond, w_ada, b_ada):
    m = jax.nn.silu(cond) @ w_ada + b_ada
    return m.reshape(cond.shape[0], 1, N_MOD, D_MODEL)


def swiglu(h, w_gu, w_down):
    g, u = jnp.split(h @ w_gu, 2, axis=-1)
    return (jax.nn.silu(g) * u) @ w_down


def split_in(z):
    idx = np.cumsum(SPLIT_SIZES)[:-1].tolist()
    return jnp.split(z, idx, axis=-1)


def fourier_mix(u):
    B, L, _ = u.shape
    ug = u.astype(jnp.float32).reshape(B, L, FOURIER_GROUPS, HEAD_DIM)
    f = jnp.fft.fft2(ug, axes=(1, 3), norm='ortho')
    return jnp.real(f).reshape(B, L, FOURIER_W).astype(u.dtype)


def gla_chunk_scan(q, k, v, logf, s0):
    B, L, H, DK = q.shape
    n = L // CHUNK
    rs = lambda t: t.reshape(B, n, CHUNK, H, t.shape[-1])
    qc, kc, vc, gc = rs(q), rs(k), rs(v), rs(logf)
    b = jnp.cumsum(gc, axis=2)
    bT = b[:, :, -1]
    causal = jnp.tril(jnp.ones((CHUNK, CHUNK), dtype=bool))[:, :, None, None]
    diff = b[:, :, :, None] - b[:, :, None, :]
    dec = jnp.where(causal, jnp.exp(jnp.where(causal, diff, 0.0)), 0.0)
    A = jnp.einsum('bnihk,bnijhk,bnjhk->bnhij', qc, dec, kc)
    o_intra = jnp.einsum('bnhij,bnjhv->bnihv', A, vc)
    q_in = qc * jnp.exp(b)
    k_out = kc * jnp.exp(bT[:, :, None] - b)

    def step(S, xs):
        qi, ki, vi, dT = xs
        o = jnp.einsum('bchk,bhkv->bchv', qi, S)
        S = S * jnp.exp(dT)[..., None] + jnp.einsum('bchk,bchv->bhkv', ki, vi)
        return S, o

    xs = (jnp.moveaxis(q_in, 1, 0), jnp.moveaxis(k_out, 1, 0),
          jnp.moveaxis(vc, 1, 0), jnp.moveaxis(bT, 1, 0))
    s_fin, o_inter = lax.scan(step, s0, xs)
    o = o_intra + jnp.moveaxis(o_inter, 0, 1)
    return o.reshape(B, L, H, v.shape[-1]), s_fin


def hgrn_mixer(hq, hi, hf_fwd, hf_bwd, hg, lb, norm_w, s0_fwd, s0_bwd):
    B, L, _ = hq.shape
    f32 = jnp.float32
    heads = lambda t: t.astype(f32).reshape(B, L, HGRN_HEADS, HEAD_DIM)
    q = heads(hq) * (HEAD_DIM ** -0.5)
    v = heads(hi)

    def gates(hf, lbd):
        z = hf.astype(f32)
        logf = jnp.logaddexp(jnp.log(jnp.maximum(lbd, LB_FLOOR)), jnp.log1p(-lbd) + jax.nn.log_sigmoid(z))
        kk = (1.0 - lbd) * jax.nn.sigmoid(-z)
        return heads(kk), heads(logf)

    k_f, g_f = gates(hf_fwd, lb[0])
    k_b, g_b = gates(hf_bwd, lb[1])
    o_f, s_f = gla_chunk_scan(q, k_f, v, g_f, s0_fwd.astype(f32))
    flip = lambda t: jnp.flip(t, axis=1)
    o_b, s_b = gla_chunk_scan(flip(q), flip(k_b), flip(v), flip(g_b), s0_bwd.astype(f32))
    o = o_f + flip(o_b)
    o = rmsnorm(o, norm_w.reshape(HGRN_HEADS, HEAD_DIM)) * jax.nn.silu(heads(hg))
    return o.reshape(B, L, HGRN_W).astype(hq.dtype), jnp.stack([s_f, s_b], axis=1)


def qkv_heads(aq, ak, av, q_norm, k_norm):
    B, L, _ = aq.shape
    q = rmsnorm(aq.reshape(B, L, ATTN_HEADS, HEAD_DIM), q_norm)
    q = q.reshape(B, L, ATTN_KV_HEADS, ATTN_GROUP, HEAD_DIM)
    k = rmsnorm(ak.reshape(B, L, ATTN_KV_HEADS, HEAD_DIM), k_norm)
    v = av.reshape(B, L, ATTN_KV_HEADS, HEAD_DIM)
    return q, k, v


def axial_rope(x):
    T = x.shape[1]
    rows = T // GRID_W
    row = jnp.repeat(jnp.arange(rows), GRID_W)
    col = jnp.tile(jnp.arange(GRID_W), rows)
    half = HEAD_DIM // 2
    inv = ROPE_BASE ** (-jnp.arange(0, half, 2, dtype=jnp.float32) / half)
    bshape = (T,) + (1,) * (x.ndim - 3) + (half // 2,)

    def rot(xa, pos):
        ang = pos.astype(jnp.float32)[:, None] * inv[None]
        cos = jnp.cos(ang).reshape(bshape)
        sin = jnp.sin(ang).reshape(bshape)
        x1, x2 = jnp.split(xa.astype(jnp.float32), 2, axis=-1)
        return jnp.concatenate([x1 * cos - x2 * sin, x2 * cos + x1 * sin], axis=-1)

    xr, xc = jnp.split(x, 2, axis=-1)
    return jnp.concatenate([rot(xr, row), rot(xc, col)], axis=-1).astype(x.dtype)


def context_attention(q, k, v, sink):
    scale = HEAD_DIM ** -0.5
    s = jnp.einsum('bqkgd,bskd->bkgqs', q, k).astype(jnp.float32) * scale
    sk = jnp.broadcast_to(sink.astype(jnp.float32).reshape(ATTN_KV_HEADS, ATTN_GROUP, 1, 1), s.shape[:-1] + (1,))
    p = jax.nn.softmax(jnp.concatenate([s, sk], axis=-1), axis=-1)[..., :-1]
    o = jnp.einsum('bkgqs,bskd->bqkgd', p.astype(v.dtype), v)
    B, L = q.shape[:2]
    return o.reshape(B, L, ATTN_W)


def latent_attention(q, k, v, kc, vc, sink):
    scale = HEAD_DIM ** -0.5
    B, T = q.shape[:2]
    nb = T // BLOCK
    qb = q.reshape(B, nb, BLOCK, ATTN_KV_HEADS, ATTN_GROUP, HEAD_DIM)
    pad = lambda t: jnp.pad(t, ((0, 0), (BLOCK, BLOCK), (0, 0), (0, 0))).reshape(B, nb + 2, BLOCK, ATTN_KV_HEADS, HEAD_DIM)
    win = lambda t: jnp.concatenate([t[:, :-2], t[:, 1:-1], t[:, 2:]], axis=2)
    kw, vw = win(pad(k)), win(pad(v))
    qpos = jnp.arange(nb)[:, None] * BLOCK + jnp.arange(BLOCK)[None]
    kpos = (jnp.arange(nb)[:, None] - 1) * BLOCK + jnp.arange(3 * BLOCK)[None]
    mask = (jnp.abs(qpos[:, :, None] - kpos[:, None, :]) <= WINDOW) & (kpos[:, None, :] >= 0) & (kpos[:, None, :] < T)
    s_loc = jnp.einsum('bnqkgd,bnskd->bnkgqs', qb, kw).astype(jnp.float32) * scale
    s_loc = jnp.where(mask[None, :, None, None], s_loc, NEG_BIG)
    s_ctx = jnp.einsum('bnqkgd,bskd->bnkgqs', qb, kc).astype(jnp.float32) * scale
    sk = jnp.broadcast_to(sink.astype(jnp.float32).reshape(ATTN_KV_HEADS, ATTN_GROUP, 1, 1), s_loc.shape[:-1] + (1,))
    p = jax.nn.softmax(jnp.concatenate([s_loc, s_ctx, sk], axis=-1), axis=-1)
    n_loc = 3 * BLOCK
    n_ctx = kc.shape[1]
    p_loc = p[..., :n_loc].astype(v.dtype)
    p_ctx = p[..., n_loc:n_loc + n_ctx].astype(v.dtype)
    o = jnp.einsum('bnkgqs,bnskd->bnqkgd', p_loc, vw) + jnp.einsum('bnkgqs,bskd->bnqkgd', p_ctx, vc)
    return o.reshape(B, T, ATTN_W)


def context_layer(x, mod, p):
    B = x.shape[0]
    x = x + 0.5 * mod[:, :, 2] * swiglu(modulate(x, p['norm_ffn1'], mod[:, :, 0], mod[:, :, 1]), p['ffn1_w_gate_up'], p['ffn1_w_down'])
    h = modulate(x, p['norm_mix'], mod[:, :, 3], mod[:, :, 4])
    u, hq, hi, hff, hfb, hg, aq, ak, av = split_in(h @ p['w_in'])
    fo = fourier_mix(u)
    zero = jnp.zeros((B, HGRN_HEADS, HEAD_DIM, HEAD_DIM), jnp.float32)
    ho, s_hgrn = hgrn_mixer(hq, hi, hff, hfb, hg, p['lb'], p['hgrn_norm'], zero, zero)
    q, k, v = qkv_heads(aq, ak, av, p['q_norm'], p['k_norm'])
    ao = context_attention(q, k, v, p['attn_sink'])
    x = x + mod[:, :, 5] * (jnp.concatenate([fo, ho, ao], axis=-1) @ p['w_out'])
    x = x + 0.5 * mod[:, :, 8] * swiglu(modulate(x, p['norm_ffn2'], mod[:, :, 6], mod[:, :, 7]), p['ffn2_w_gate_up'], p['ffn2_w_down'])
    return x, k, v, s_hgrn


def latent_layer(x, mod, kc, vc, s_ctx, p):
    x = x + 0.5 * mod[:, :, 2] * swiglu(modulate(x, p['norm_ffn1'], mod[:, :, 0], mod[:, :, 1]), p['ffn1_w_gate_up'], p['ffn1_w_down'])
    h = modulate(x, p['norm_mix'], mod[:, :, 3], mod[:, :, 4])
    u, hq, hi, hff, hfb, hg, aq, ak, av = split_in(h @ p['w_in'])
    fo = fourier_mix(u)
    ho, _ = hgrn_mixer(hq, hi, hff, hfb, hg, p['lb'], p['hgrn_norm'], s_ctx[:, 0], s_ctx[:, 1])
    q, k, v = qkv_heads(aq, ak, av, p['q_norm'], p['k_norm'])
    q, k = axial_rope(q), axial_rope(k)
    ao = latent_attention(q, k, v, kc, vc, p['attn_sink'])
    x = x + mod[:, :, 5] * (jnp.concatenate([fo, ho, ao], axis=-1) @ p['w_out'])
    x = x + 0.5 * mod[:, :, 8] * swiglu(modulate(x, p['norm_ffn2'], mod[:, :, 6], mod[:, :, 7]), p['ffn2_w_gate_up'], p['ffn2_w_down'])
    return x


def setup_inputs(seed: int = 0) -> dict:
    key = jax.random.key(seed)
    ks = jax.random.split(key, 23)
    nrm = lambda k, shape, s: jax.random.normal(k, shape, jnp.float32) * s
    gain = lambda k, shape: 1.0 + 0.05 * jax.random.normal(k, shape, jnp.float32)
    return {
        'x_prompt': nrm(ks[0], (BATCH, SEQ, D_MODEL), 1.0),
        'x_sample': nrm(ks[1], (DEC_BATCH, DEC_SEQ, D_MODEL), 1.0),
        'c': nrm(ks[2], (DEC_BATCH, D_MODEL), 1.0),
        'cache_attn_k': nrm(ks[3], (DEC_BATCH, DEPTH, PAST_LEN, ATTN_KV_HEADS, HEAD_DIM), 1.0),
        'cache_attn_v': nrm(ks[4], (DEC_BATCH, DEPTH, PAST_LEN, ATTN_KV_HEADS, HEAD_DIM), 1.0),
        'state_hgrn': nrm(ks[5], (DEC_BATCH, DEPTH, 2, HGRN_HEADS, HEAD_DIM, HEAD_DIM), 0.5),
        'c_ctx': nrm(ks[6], (D_MODEL,), 1.0),
        'w_ada': nrm(ks[7], (DEPTH, D_MODEL, N_MOD * D_MODEL), 0.5 * D_MODEL ** -0.5),
        'b_ada': nrm(ks[8], (DEPTH, N_MOD * D_MODEL), 0.01),
        'norm_ffn1': gain(ks[9], (DEPTH, D_MODEL)),
        'norm_mix': gain(ks[10], (DEPTH, D_MODEL)),
        'norm_ffn2': gain(ks[11], (DEPTH, D_MODEL)),
        'ffn1_w_gate_up': nrm(ks[12], (DEPTH, D_MODEL, 2 * D_FF), D_MODEL ** -0.5),
        'ffn1_w_down': nrm(ks[13], (DEPTH, D_FF, D_MODEL), D_FF ** -0.5),
        'ffn2_w_gate_up': nrm(ks[14], (DEPTH, D_MODEL, 2 * D_FF), D_MODEL ** -0.5),
        'ffn2_w_down': nrm(ks[15], (DEPTH, D_FF, D_MODEL), D_FF ** -0.5),
        'w_in': nrm(ks[16], (DEPTH, D_MODEL, IN_W), D_MODEL ** -0.5),
        'w_out': nrm(ks[17], (DEPTH, MIX_W, D_MODEL), MIX_W ** -0.5),
        'hgrn_lower_bounds': nrm(ks[18], (DEPTH, 2, HGRN_W), 1.0),
        'hgrn_norm': gain(ks[19], (DEPTH, HGRN_W)),
        'q_norm': gain(ks[20], (DEPTH, HEAD_DIM)),
        'k_norm': gain(ks[21], (DEPTH, HEAD_DIM)),
        'attn_sink': nrm(ks[22], (DEPTH, ATTN_HEADS), 0.5),
    }


def reference(x_prompt, x_sample, c, cache_attn_k, cache_attn_v, state_hgrn, c_ctx, w_ada, b_ada,
              norm_ffn1, norm_mix, norm_ffn2, ffn1_w_gate_up, ffn1_w_down, ffn2_w_gate_up, ffn2_w_down,
              w_in, w_out, hgrn_lower_bounds, hgrn_norm, q_norm, k_norm, attn_sink):
    lb_soft = jax.nn.softmax(hgrn_lower_bounds.astype(jnp.float32), axis=0)
    lower_bounds = jnp.cumsum(lb_soft, axis=0) - lb_soft[0]
    y_p = x_prompt
    y_s = x_sample
    k_list, v_list, s_list = [], [], []
    for l in range(DEPTH):
        p = {
            'norm_ffn1': norm_ffn1[l], 'norm_mix': norm_mix[l], 'norm_ffn2': norm_ffn2[l],
            'ffn1_w_gate_up': ffn1_w_gate_up[l], 'ffn1_w_down': ffn1_w_down[l],
            'ffn2_w_gate_up': ffn2_w_gate_up[l], 'ffn2_w_down': ffn2_w_down[l],
            'w_in': w_in[l], 'w_out': w_out[l], 'lb': lower_bounds[l], 'hgrn_norm': hgrn_norm[l],
            'q_norm': q_norm[l], 'k_norm': k_norm[l], 'attn_sink': attn_sink[l],
        }
        mod_ctx = adaln(c_ctx[None], w_ada[l], b_ada[l])
        mod_lat = adaln(c, w_ada[l], b_ada[l])
        y_p, k_l, v_l, s_l = context_layer(y_p, mod_ctx, p)
        y_s = latent_layer(y_s, mod_lat, cache_attn_k[:, l], cache_attn_v[:, l], state_hgrn[:, l], p)
        k_list.append(k_l)
        v_list.append(v_l)
        s_list.append(s_l)
    new_cache_attn_k = jnp.stack(k_list, axis=1)
    new_cache_attn_v = jnp.stack(v_list, axis=1)
    new_state_hgrn = jnp.stack(s_list, axis=1)
    return (y_p, y_s, new_cache_attn_k, new_cache_attn_v, new_state_hgrn)
```

```python
import numpy as np
from contextlib import ExitStack
import concourse.bass as bass
import concourse.mybir as mybir
from concourse.bass_utils import run_bass_kernel_spmd

F32 = mybir.dt.float32
BF16 = mybir.dt.bfloat16
AF = mybir.ActivationFunctionType
ALU = mybir.AluOpType

EP = 8192
ENGS = ("pe", "act", "dve", "pool", "sp")
SAME_ENGINE_SYNC = ("act", "dve", "pool")


class Buf:
    def __init__(self, name, fence=None):
        self.name = name
        self.w = None
        self.r = list(fence) if fence else []
        self.dsem = None
        self.dcnt = 0
        self.excl = False


class Op:
    pass


class Tracker:
    def __init__(self):
        self.ops = []
        self.per_eng = {e: [] for e in ENGS}
        self.dma_bufs = []
        self.dma_ids = []

    def fence(self):
        ids = [l[-1].id for l in self.per_eng.values() if l]
        return ids + list(self.dma_ids)

    def op(self, eng, fn, reads=(), writes=(), dma_buf=None):
        o = Op()
        o.eng = eng
        o.fn = fn
        o.dma_buf = dma_buf
        o.marked = False
        o.id = len(self.ops)
        o.pos = len(self.per_eng[eng])
        deps = set()
        for b in reads:
            if b.w is not None:
                deps.add(b.w)
            if b.excl:
                deps.update(b.r)
        for b in writes:
            if b.w is not None:
                deps.add(b.w)
            deps.update(b.r)
        o.deps = deps
        for b in reads:
            b.r.append(o.id)
        for b in writes:
            b.w = o.id
            b.r = []
        if dma_buf is not None:
            if dma_buf.dsem is None:
                dma_buf.dsem = True
                self.dma_bufs.append(dma_buf)
            dma_buf.dcnt += 16
            o.dval = dma_buf.dcnt
            self.dma_ids.append(o.id)
        self.ops.append(o)
        self.per_eng[eng].append(o)
        return o

    def plan(self):
        ops = self.ops
        seen = {e: {f: -1 for f in ENGS} for e in ENGS}
        seen_dma = {e: {} for e in ENGS}
        for o in ops:
            w_eng = {}
            w_dma = {}
            for d in o.deps:
                p = ops[d]
                if p.dma_buf is not None:
                    b = p.dma_buf
                    if seen_dma[o.eng].get(id(b), 0) >= p.dval:
                        continue
                    w_dma[id(b)] = (b, max(p.dval, w_dma.get(id(b), (None, 0))[1]))
                else:
                    if p.eng == o.eng:
                        if p.eng not in SAME_ENGINE_SYNC:
                            continue
                        if p.pos < o.pos - 2:
                            continue
                    if seen[o.eng][p.eng] >= d:
                        continue
                    w_eng[p.eng] = max(w_eng.get(p.eng, -1), d)
            for f, d in w_eng.items():
                seen[o.eng][f] = d
                ops[d].marked = True
            for k, (b, v) in w_dma.items():
                seen_dma[o.eng][k] = v
            o.w_eng = w_eng
            o.w_dma = list(w_dma.values())
        cnt = {e: 0 for e in ENGS}
        for o in ops:
            if o.dma_buf is None and o.marked:
                cnt[o.eng] += 1
            o.cnt = cnt[o.eng]
        self.final_cnt = cnt

    def emit(self, nc, final_eng="sp"):
        self.plan()
        ops = self.ops
        with ExitStack() as st:
            esems = {}
            for e in ENGS:
                n = max(1, (self.final_cnt[e] + EP - 1) // EP)
                esems[e] = [st.enter_context(nc.semaphore(f"s_{e}_{i}")) for i in range(n)]
            for i, b in enumerate(self.dma_bufs):
                b.dsem = st.enter_context(nc.semaphore(f"d_{i}"))
            block = st.enter_context(nc.Block())

            def run(eng_name, eng):
                for o in self.per_eng[eng_name]:
                    for f, d in o.w_eng.items():
                        c = ops[d].cnt
                        eng.wait_ge(esems[f][(c - 1) // EP], (c - 1) % EP + 1)
                    for (b, v) in o.w_dma:
                        eng.wait_ge(b.dsem, v)
                    ins = o.fn(eng)
                    if o.dma_buf is not None:
                        ins.then_inc(o.dma_buf.dsem, 16)
                    elif o.marked:
                        c = o.cnt
                        ins.then_inc(esems[eng_name][(c - 1) // EP], 1)
                if eng_name == final_eng:
                    for b in self.dma_bufs:
                        eng.wait_ge(b.dsem, b.dcnt)
                    for f in ENGS:
                        c = self.final_cnt[f]
                        if c > 0 and f != eng_name:
                            eng.wait_ge(esems[f][(c - 1) // EP], (c - 1) % EP + 1)

            @block.tensor
            def _(e):
                run("pe", e)

            @block.scalar
            def _(e):
                run("act", e)

            @block.vector
            def _(e):
                run("dve", e)

            @block.gpsimd
            def _(e):
                run("pool", e)

            @block.sync
            def _(e):
                run("sp", e)


TOK = 1536
NT = 12
NG = 3
D = 1024
DFF = 2816
EPS = 1e-6
P_COND, P_BADA, P_NRM, P_LBS, P_HNRM, P_QN, P_KN, P_SINK, P_KEEP, P_CBIAS, NPAR = 0, 16, 160, 208, 216, 220, 222, 224, 240, 241, 242
C_ID, C_PERM, C_COS, C_SIN, C_MF, C_MB, NCF = 0, 128, 256, 1280, 2304, 2368, 2432
B_ONES, B_BLK, B_ID, B_DFT, B_MASK, NCB = 0, 128, 256, 384, 640, 640 + 24 * 128
SLOT = 4096
ENABLE_HGRN = True
DEBUG_PAIR = None
ATTN_CORE = True
SKIP = set()
ENABLE_ATTN = True
NSL = 3
ARENA = 23400


def build(stop_after=None):
    nc = bass.Bass("TRN2", target_bir_lowering=False)
    T = Tracker()
    di = lambda n, s: nc.dram_tensor(n, s, F32, kind="ExternalInput").ap()
    do = lambda n, s: nc.dram_tensor(n, s, F32, kind="ExternalOutput").ap()
    xin = di("xin", [TOK, D]); par_d = di("par", [128, NPAR]); cf_d = di("cf", [128, NCF]); cb_d = di("cb", [128, NCB])
    dftl_d = di("dftl", [2, 1024, 1024]); dftp_d = di("dftp", [2, 256, 256])
    ck_d = di("ck", [2, 512, 128]); cv_d = di("cv", [2, 512, 128]); s0_d = di("s0", [2, 2, 4, 64, 64])
    wada_d = di("w_ada", [2, D, 9 * D])
    wgu_d = [di("ffn1_gu", [2, D, 2 * DFF]), di("ffn2_gu", [2, D, 2 * DFF])]
    wdn_d = [di("ffn1_d", [2, DFF, D]), di("ffn2_d", [2, DFF, D])]
    win_d = di("w_in", [2, D, 2304]); wout_d = di("w_out", [2, D, D])
    yout = do("yout", [TOK, D]); kout = do("kout", [2, TOK, 128]); vout = do("vout", [2, TOK, 128])
    sout = do("sout", [2, 6, 2, 4, 64, 64])

    st = ExitStack()
    sbt = lambda n, s, dt: st.enter_context(nc.sbuf_tensor("sb_" + n, s, dt))
    xT = sbt("xT", [128, 8, TOK], F32)
    hT = sbt("hT", [128, 8, TOK], BF16)
    par = sbt("par", [128, NPAR], F32)
    cf = sbt("cf", [128, NCF], F32)
    cb = sbt("cb", [128, NCB], BF16)
    modv = sbt("modv", [128, 2, 72, 2], F32)
    mA = sbt("mA", [128, 2, 3, 8, 2], F32)
    mG = sbt("mG", [128, 2, 3, 8, 2], F32)
    sml = sbt("sml", [128, 64], F32)
    slots = [sbt(f"slot{i}", [128, SLOT], BF16) for i in range(NSL)]
    arena = sbt("arena", [128, ARENA], F32)
    psum = [st.enter_context(nc.psum_tensor(f"ps{i}", [128, 512], F32)) for i in range(8)]
    bps = [Buf(f"ps{i}") for i in range(8)]
    for b_ in bps:
        b_.excl = True
    bslot = [Buf(f"slot{i}") for i in range(NSL)]
    bx = [[Buf(f"x{c}_{g}") for g in range(NG)] for c in range(8)]
    bh = [[Buf(f"h{c}_{g}") for g in range(NG)] for c in range(8)]
    bpar, bcf, bcb, bmod = Buf("par"), Buf("cf"), Buf("cb"), Buf("mod")
    bsml = Buf("sml")

    state = {"ps": 0, "aoff": 0}

    def nps():
        i = state["ps"] % 8
        state["ps"] += 1
        return psum[i], bps[i]

    def aalloc(n32):
        o = state["aoff"]
        state["aoff"] += n32
        assert state["aoff"] <= ARENA, state["aoff"]
        return arena[:, o:o + n32]

    def areset():
        state["aoff"] = 0

    def nb(name):
        return Buf(name, T.fence())

    PE = lambda fn, r, w: T.op("pe", fn, r, w)
    ACT = lambda fn, r, w: T.op("act", fn, r, w)
    DVE = lambda fn, r, w: T.op("dve", fn, r, w)
    POOL = lambda fn, r, w: T.op("pool", fn, r, w)

    def DMA(eng, out, in_, buf, reads=(), writes=()):
        T.op(eng, lambda q: q.dma_start(out=out, in_=in_), reads, writes, dma_buf=buf)

    wq = []
    wstate = {"loaded": 0, "used": 0}

    def witem(parts):
        wq.append(parts)

    def wnext(keep_prev=False):
        i = wstate["used"]
        oldest = i - 1 if keep_prev else i
        while wstate["loaded"] < min(len(wq), oldest + NSL):
            k = wstate["loaded"]
            s = k % NSL
            for (fn, dap) in wq[k]:
                DMA("pool", fn(slots[s]), dap, bslot[s], writes=[bslot[s]])
            wstate["loaded"] += 1
        wstate["used"] += 1
        return slots[i % NSL], bslot[i % NSL]

    def kview(ap2d):
        return ap2d.rearrange("(c p) n -> p c n", p=128)

    def sched_weights():
        for l in range(2):
            for i in range(18):
                witem([(lambda s: s[:, 0:4096].rearrange("p (c n) -> p c n", c=8), kview(wada_d[l][:, i * 512:(i + 1) * 512]))])
        for l in range(2):
            for f in range(2):
                sched_ffn(l, f)
                if f == 0:
                    sched_mixer(l)

    def sched_ffn(l, f):
        for h in range(2):
            for j0 in range(0, 11, 2):
                nf = min(2, 11 - j0)
                c0 = (h * 11 + j0) * 128
                witem([
                    (lambda s, nf=nf: s[:, 0:8 * nf * 128].rearrange("p (c n) -> p c n", c=8), kview(wgu_d[f][l][:, c0:c0 + nf * 128])),
                    (lambda s, nf=nf: s[:, 2048:2048 + 8 * nf * 128].rearrange("p (c n) -> p c n", c=8), kview(wgu_d[f][l][:, DFF + c0:DFF + c0 + nf * 128])),
                ])

    def sched_mixer(l):
        w = win_d[l]
        witem([(lambda s: s[:, 0:2048].rearrange("p (c n) -> p c n", c=8), kview(w[:, 0:256]))])
        for hf in range(2):
            for cs in range(2):
                witem([(lambda s: s[:, 0:4096].rearrange("p (c n) -> p c n", c=8), kview(dftl_d[cs][:, hf * 512:(hf + 1) * 512]))])
        witem([(lambda s, cs=cs: s[:, cs * 512:(cs + 1) * 512].rearrange("p (c n) -> p c n", c=2), kview(dftp_d[cs])) for cs in range(2)])
        for pr in range(2):
            cols = [256 + pr * 128, 512 + pr * 128, 768 + pr * 128, 1024 + pr * 128, 1280 + pr * 128]
            witem([(lambda s, k=k: s[:, k * 1024:(k + 1) * 1024].rearrange("p (c n) -> p c n", c=8), kview(w[:, cols[k]:cols[k] + 128])) for k in range(3)])
            witem([(lambda s, k=k: s[:, k * 1024:(k + 1) * 1024].rearrange("p (c n) -> p c n", c=8), kview(w[:, cols[3 + k]:cols[3 + k] + 128])) for k in range(2)])
        witem([(lambda s: s[:, 0:4096].rearrange("p (c n) -> p c n", c=8), kview(w[:, 1536:2048]))])
        parts = []
        for kv in range(2):
            for dup in range(2):
                parts.append((lambda s, kv=kv, dup=dup: s[:, kv * 1024:(kv + 1) * 1024].rearrange("p (c n) -> p c n", c=8)[:, :, dup * 64:(dup + 1) * 64],
                              kview(w[:, 2048 + kv * 64:2048 + (kv + 1) * 64])))
        parts.append((lambda s: s[:, 2048:3072].rearrange("p (c n) -> p c n", c=8), kview(w[:, 2176:2304])))
        witem(parts)
        for i in range(2):
            witem([(lambda s: s[:, 0:4096].rearrange("p (c n) -> p c n", c=8), kview(wout_d[l][:, i * 512:(i + 1) * 512]))])

    sched_weights()

    DMA("sp", par[:], par_d, bpar, writes=[bpar])
    DMA("sp", cf[:], cf_d, bcf, writes=[bcf])
    DMA("pool", cb[:], cb_d, bcb, writes=[bcb])
    ident = cf[:, C_ID:C_ID + 128]
    ones_b = cb[:, B_ONES:B_ONES + 128]
    blk_b = cb[:, B_BLK:B_BLK + 128]
    ident_b = cb[:, B_ID:B_ID + 128]

    areset()
    xst = [aalloc(1024), aalloc(1024)]
    bxst = [nb("xst0"), nb("xst1")]
    for t in range(NT):
        s = t % 2
        DMA("sp", xst[s], xin[t * 128:(t + 1) * 128, :], bxst[s], writes=[bxst[s]])
        for half in range(2):
            ps, bp = nps()
            for k in range(4):
                c = half * 4 + k
                PE(lambda q, ps=ps, k=k, c=c, s=s: q.transpose(ps[:, k * 128:(k + 1) * 128], xst[s][:, c * 128:(c + 1) * 128], ident),
                   [bxst[s], bcf], [bp])
            g = t // 4
            eng = ACT if half == 0 else DVE
            fn = (lambda q, ps=ps, half=half, t=t: q.activation(out=xT[:, half * 4:half * 4 + 4, t * 128:(t + 1) * 128], in_=ps[:].rearrange("p (k n) -> p k n", k=4), func=AF.Copy)) if half == 0 else \
                 (lambda q, ps=ps, half=half, t=t: q.tensor_copy(out=xT[:, half * 4:half * 4 + 4, t * 128:(t + 1) * 128], in_=ps[:].rearrange("p (k n) -> p k n", k=4)))
            eng(fn, [bp], [bx[c][g] for c in range(half * 4, half * 4 + 4)])

    areset()
    sc = aalloc(16)
    scb = aalloc(8).bitcast(BF16)
    bsc = nb("sc")
    ACT(lambda q: q.activation(out=sc, in_=par[:, P_COND:P_COND + 16], func=AF.Exp, scale=-1.0), [bpar], [bsc])
    DVE(lambda q: q.tensor_scalar_add(out=sc, in0=sc, scalar1=1.0), [bsc], [bsc])
    DVE(lambda q: q.reciprocal(out=sc, in_=sc), [bsc], [bsc])
    DVE(lambda q: q.tensor_tensor(out=scb, in0=sc, in1=par[:, P_COND:P_COND + 16], op=ALU.mult), [bsc, bpar], [bsc])
    scb3 = scb.rearrange("p (c r) -> p c r", c=8)
    for l in range(2):
        ps, bp = nps()
        for i in range(18):
            sl, bs = wnext()
            wv = sl[:, 0:4096].rearrange("p (c n) -> p c n", c=8)
            for k in range(4):
                nn = i * 4 + k
                for c in range(8):
                    PE(lambda q, ps=ps, wv=wv, k=k, c=c, nn=nn: q.matmul(ps[:, nn * 2:nn * 2 + 2], lhsT=wv[:, c, k * 128:(k + 1) * 128], rhs=scb3[:, c, :], start=(c == 0), stop=(c == 7)),
                       [bs, bsc], [bp])
        DVE(lambda q, ps=ps, l=l: q.tensor_tensor(out=modv[:, l, :, :], in0=ps[:, 0:144].rearrange("p (n r) -> p n r", r=2),
                                                 in1=par[:, P_BADA + l * 72:P_BADA + (l + 1) * 72].unsqueeze(2).to_broadcast([128, 72, 2]), op=ALU.add),
            [bp, bpar], [bmod])
        for w3 in range(3):
            nrm = par[:, P_NRM + l * 24 + w3 * 8:P_NRM + l * 24 + w3 * 8 + 8].unsqueeze(2).to_broadcast([128, 8, 2])
            DVE(lambda q, l=l, w3=w3, nrm=nrm: q.scalar_tensor_tensor(out=mA[:, l, w3, :, :], in0=modv[:, l, (3 * w3 + 1) * 8:(3 * w3 + 2) * 8, :], scalar=1.0, in1=nrm, op0=ALU.add, op1=ALU.mult),
                [bmod, bpar], [bmod])
            DVE(lambda q, l=l, w3=w3: q.tensor_scalar_mul(out=mG[:, l, w3, :, :], in0=modv[:, l, (3 * w3 + 2) * 8:(3 * w3 + 3) * 8, :], scalar1=(1.0 if w3 == 1 else 0.5)),
                [bmod], [bmod])

    def mod_A(l, w3, c, g):
        r = 0 if g < 2 else 1
        return mA[:, l, w3, c, r:r + 1]

    def mod_B(l, w3, c, g):
        r = 0 if g < 2 else 1
        return modv[:, l, (3 * w3) * 8 + c, r:r + 1]

    def mod_G(l, w3, c, g):
        r = 0 if g < 2 else 1
        return mG[:, l, w3, c, r:r + 1]

    def norm_mod(l, w3):
        sq = [aalloc(256).bitcast(BF16), aalloc(256).bitcast(BF16)]
        bsq = [nb("sq0"), nb("sq1")]
        tmp = [aalloc(512), aalloc(512)]
        btmp = [nb("tmp0"), nb("tmp1")]
        rs = aalloc(512)
        brs = nb("rs")
        for g in range(NG):
            ps, bp = nps()
            for c in range(8):
                s = c % 2
                ACT(lambda q, s=s, c=c, g=g: q.activation(out=sq[s], in_=xT[:, c, g * 512:(g + 1) * 512], func=AF.Square), [bx[c][g]], [bsq[s]])
                PE(lambda q, ps=ps, s=s, c=c: q.matmul(ps[:], lhsT=ones_b, rhs=sq[s], start=(c == 0), stop=(c == 7)), [bsq[s], bcb], [bp])
            DVE(lambda q, ps=ps: q.tensor_scalar(out=rs, in0=ps[:], scalar1=1.0 / D, scalar2=EPS, op0=ALU.mult, op1=ALU.add), [bp], [brs])
            ACT(lambda q: q.activation(out=rs, in_=rs, func=AF.Ln), [brs], [brs])
            ACT(lambda q: q.activation(out=rs, in_=rs, func=AF.Exp, scale=-0.5), [brs], [brs])
            for c in range(8):
                s = c % 2
                DVE(lambda q, s=s, c=c, g=g: q.tensor_tensor(out=tmp[s], in0=xT[:, c, g * 512:(g + 1) * 512], in1=rs, op=ALU.mult), [bx[c][g], brs], [btmp[s]])
                ACT(lambda q, s=s, c=c, g=g: q.activation(out=hT[:, c, g * 512:(g + 1) * 512], in_=tmp[s], func=AF.Identity,
                                                          scale=mod_A(l, w3, c, g), bias=mod_B(l, w3, c, g)), [btmp[s], bmod], [bh[c][g]])

    def ffn(l, f):
        w3 = 0 if f == 0 else 2
        areset()
        norm_mod(l, w3)
        actT = aalloc(11 * TOK // 2).bitcast(BF16).rearrange("p (j n) -> p j n", j=11)
        bact = [nb(f"act{g}") for g in range(NG)]
        wd = aalloc(11 * 1024 // 2).bitcast(BF16).rearrange("p (j n) -> p j n", j=11)
        bwd = nb("wd")
        ebuf = [aalloc(512), aalloc(512)]
        bebuf = [nb("e0"), nb("e1")]
        tbuf = [aalloc(512), aalloc(512)]
        btbuf = [nb("t0"), nb("t1")]
        k = 0
        for h in range(2):
            DMA("pool", wd, wdn_d[f][l][h * 1408:(h + 1) * 1408, :].rearrange("(j p) n -> p j n", p=128), bwd, writes=[bwd])
            for j0 in range(0, 11, 2):
                nf = min(2, 11 - j0)
                sl, bs = wnext()
                wg = sl[:, 0:8 * nf * 128].rearrange("p (c n) -> p c n", c=8)
                wu = sl[:, 2048:2048 + 8 * nf * 128].rearrange("p (c n) -> p c n", c=8)
                for jj in range(nf):
                    j = j0 + jj
                    for g in range(NG):
                        pg, bpg = nps()
                        pu, bpu = nps()
                        for c in range(8):
                            PE(lambda q, pg=pg, wg=wg, c=c, jj=jj, g=g: q.matmul(pg[:], lhsT=wg[:, c, jj * 128:(jj + 1) * 128], rhs=hT[:, c, g * 512:(g + 1) * 512], start=(c == 0), stop=(c == 7)),
                               [bs, bh[c][g]], [bpg])
                        for c in range(8):
                            PE(lambda q, pu=pu, wu=wu, c=c, jj=jj, g=g: q.matmul(pu[:], lhsT=wu[:, c, jj * 128:(jj + 1) * 128], rhs=hT[:, c, g * 512:(g + 1) * 512], start=(c == 0), stop=(c == 7)),
                               [bs, bh[c][g]], [bpu])
                        s = k % 2
                        k += 1
                        ACT(lambda q, pg=pg, s=s: q.activation(out=ebuf[s], in_=pg[:], func=AF.Exp, scale=-1.0), [bpg], [bebuf[s]])
                        DVE(lambda q, s=s: q.tensor_scalar_add(out=ebuf[s], in0=ebuf[s], scalar1=1.0), [bebuf[s]], [bebuf[s]])
                        DVE(lambda q, s=s: q.reciprocal(out=ebuf[s], in_=ebuf[s]), [bebuf[s]], [bebuf[s]])
                        DVE(lambda q, pg=pg, s=s: q.tensor_tensor(out=tbuf[s], in0=pg[:], in1=ebuf[s], op=ALU.mult), [bpg, bebuf[s]], [btbuf[s]])
                        DVE(lambda q, pu=pu, s=s, j=j, g=g: q.tensor_tensor(out=actT[:, j, g * 512:(g + 1) * 512], in0=pu[:], in1=tbuf[s], op=ALU.mult), [bpu, btbuf[s]], [bact[g]])
            for g in range(NG):
                for dc in range(8):
                    ps, bp = nps()
                    for j in range(11):
                        PE(lambda q, ps=ps, j=j, dc=dc, g=g: q.matmul(ps[:], lhsT=wd[:, j, dc * 128:(dc + 1) * 128], rhs=actT[:, j, g * 512:(g + 1) * 512], start=(j == 0), stop=(j == 10)),
                           [bwd, bact[g]], [bp])
                    DVE(lambda q, ps=ps, dc=dc, g=g: q.scalar_tensor_tensor(out=xT[:, dc, g * 512:(g + 1) * 512], in0=ps[:], scalar=mod_G(l, w3, dc, g), in1=xT[:, dc, g * 512:(g + 1) * 512], op0=ALU.mult, op1=ALU.add),
                        [bp, bmod, bx[dc][g]], [bx[dc][g]])

    def mixer(l):
        areset()
        norm_mod(l, 1)
        areset()
        catT = aalloc(8 * TOK // 2).bitcast(BF16).rearrange("p (c n) -> p c n", c=8)
        bcat = [[nb(f"cat{c}_{g}") for g in range(NG)] for c in range(8)]
        base_off = state["aoff"]

        def inproj(wv, k, g, ps):
            for c in range(8):
                PE(lambda q, c=c: q.matmul(ps[0][:], lhsT=wv[:, c, k * 128:(k + 1) * 128], rhs=hT[:, c, g * 512:(g + 1) * 512], start=(c == 0), stop=(c == 7)),
                   [ps[2], bh[c][g]], [ps[1]])

        uT = aalloc(2 * TOK // 2).bitcast(BF16).rearrange("p (c n) -> p c n", c=2)
        buT = nb("uT")
        ucs = aalloc(NT * 2 * 256 // 2).bitcast(BF16).rearrange("p (t c n) -> p t c n", t=NT, c=2)
        bucs = nb("ucs")
        sl, bs = wnext()
        wv = sl[:, 0:2048].rearrange("p (c n) -> p c n", c=8)
        for k in range(2):
            for g in range(NG):
                ps, bp = nps()
                inproj(wv, k, g, (ps, bp, bs))
                ACT(lambda q, ps=ps, k=k, g=g: q.activation(out=uT[:, k, g * 512:(g + 1) * 512], in_=ps[:], func=AF.Copy), [bp], [buT])
        dft64 = cb[:, B_DFT:B_DFT + 256]
        for t in range(NT):
            ps, bp = nps()
            for k in range(2):
                PE(lambda q, ps=ps, k=k, t=t: q.matmul(ps[:, k * 256:(k + 1) * 256], lhsT=uT[:, k, t * 128:(t + 1) * 128], rhs=dft64, start=True, stop=True), [buT, bcb], [bp])
            DVE(lambda q, ps=ps, t=t: q.tensor_copy(out=ucs[:, t, :, :], in_=ps[:].rearrange("p (c n) -> p c n", c=2)), [bp], [bucs])
        for hf in range(2):
            slc, bsc_ = wnext()
            sls, bss = wnext(keep_prev=True)
            cl = slc[:, 0:4096].rearrange("p (c n) -> p c n", c=8)
            sn = sls[:, 0:4096].rearrange("p (c n) -> p c n", c=8)
            for k in range(2):
                ps, bp = nps()
                for lc in range(8):
                    PE(lambda q, ps=ps, k=k, lc=lc, cl=cl: q.matmul(ps[:], lhsT=ucs[:, lc, k, 0:128], rhs=cl[:, lc, :], start=(lc == 0), stop=False), [bucs, bsc_], [bp])
                    PE(lambda q, ps=ps, k=k, lc=lc, sn=sn: q.matmul(ps[:], lhsT=ucs[:, lc, k, 128:256], rhs=sn[:, lc, :], start=False, stop=(lc == 7)), [bucs, bss], [bp])
                ACT(lambda q, ps=ps, k=k, hf=hf: q.activation(out=catT[:, k, hf * 512:(hf + 1) * 512], in_=ps[:], func=AF.Copy), [bp], [bcat[k][hf]])
        slp, bsp = wnext()
        pc = slp[:, 0:512].rearrange("p (c n) -> p c n", c=2)
        pn = slp[:, 512:1024].rearrange("p (c n) -> p c n", c=2)
        for k in range(2):
            ps, bp = nps()
            for sq_ in range(2):
                for lc in range(2):
                    tt = 8 + sq_ * 2 + lc
                    PE(lambda q, ps=ps, k=k, lc=lc, tt=tt, sq_=sq_: q.matmul(ps[:, sq_ * 256:(sq_ + 1) * 256], lhsT=ucs[:, tt, k, 0:128], rhs=pc[:, lc, :], start=(lc == 0), stop=False), [bucs, bsp], [bp])
                    PE(lambda q, ps=ps, k=k, lc=lc, tt=tt, sq_=sq_: q.matmul(ps[:, sq_ * 256:(sq_ + 1) * 256], lhsT=ucs[:, tt, k, 128:256], rhs=pn[:, lc, :], start=False, stop=(lc == 1)), [bucs, bsp], [bp])
            ACT(lambda q, ps=ps, k=k: q.activation(out=catT[:, k, 1024:1536], in_=ps[:], func=AF.Copy), [bp], [bcat[k][2]])

        if ENABLE_HGRN:
            for pr in range(2):
                if DEBUG_PAIR is not None and pr != DEBUG_PAIR:
                    wnext(); wnext()
                    for g in range(NG):
                        POOL(lambda q, pr=pr, g=g: q.memset(catT[:, 2 + pr, g * 512:(g + 1) * 512], 0.0), [], [bcat[2 + pr][g]])
                    continue
                state["aoff"] = base_off
                qf, sF, sB, tmpb, kb, hoT = [aalloc(TOK) for _ in range(6)]
                bqf, bsF, bsB, btmpb, bkb, bho = [nb(n) for n in ("qf", "sF", "sB", "tmpb", "kb", "hoT")]
                qt = [aalloc(TOK // 2).bitcast(BF16) for _ in range(2)]
                kt_ = [aalloc(TOK // 2).bitcast(BF16) for _ in range(2)]
                vT = aalloc(TOK // 2).bitcast(BF16)
                sgT = aalloc(TOK // 2).bitcast(BF16)
                bqt, bkt = [nb("qt0"), nb("qt1")], [nb("kt0"), nb("kt1")]
                bvT, bsg = nb("vT"), nb("sgT")
                ktp = [aalloc(128).bitcast(BF16).rearrange("p (a n) -> p a n", a=2) for _ in range(2)]
                vpd = [aalloc(128).bitcast(BF16).rearrange("p (a n) -> p a n", a=2) for _ in range(2)]
                bktp, bvpd = [nb("ktp0"), nb("ktp1")], [nb("vpd0"), nb("vpd1")]
                atb = [aalloc(64).bitcast(BF16) for _ in range(2)]
                batb = [nb("at0"), nb("at1")]
                S = aalloc(128); tS = aalloc(128); S0m = aalloc(64).bitcast(BF16)
                bS, btS, bS0m = nb("S"), nb("tS"), nb("S0m")
                sst = [aalloc(128), aalloc(128)]
                bsst = [nb("sst0"), nb("sst1")]
                scl = [[aalloc(48) for _ in range(4)] for _ in range(2)]
                bscl = [nb("scl0"), nb("scl1")]
                lbp = aalloc(8)
                blbp = nb("lbp")
                for i in range(2):
                    POOL(lambda q, i=i: q.memset(ktp[i], 0.0), [], [bktp[i]])
                    POOL(lambda q, i=i: q.memset(vpd[i], 0.0), [], [bvpd[i]])
                for dr in range(2):
                    c0 = dr * 3
                    if l == 0:
                        POOL(lambda q, c0=c0: q.memset(lbp[:, c0:c0 + 1], 1.0), [], [blbp])
                        POOL(lambda q, c0=c0: q.memset(lbp[:, c0 + 1:c0 + 2], 1e-30), [], [blbp])
                        POOL(lambda q, c0=c0: q.memset(lbp[:, c0 + 2:c0 + 3], -1.0), [], [blbp])
                    else:
                        a0 = P_LBS + 0 * 4 + dr * 2 + pr
                        a1 = P_LBS + 1 * 4 + dr * 2 + pr
                        DVE(lambda q, c0=c0, a0=a0, a1=a1: q.tensor_tensor(out=lbp[:, c0 + 1:c0 + 2], in0=par[:, a0:a0 + 1], in1=par[:, a1:a1 + 1], op=ALU.subtract), [bpar], [blbp])
                        ACT(lambda q, c0=c0: q.activation(out=lbp[:, c0 + 1:c0 + 2], in_=lbp[:, c0 + 1:c0 + 2], func=AF.Exp), [blbp], [blbp])
                        DVE(lambda q, c0=c0: q.tensor_scalar_add(out=lbp[:, c0 + 1:c0 + 2], in0=lbp[:, c0 + 1:c0 + 2], scalar1=1.0), [blbp], [blbp])
                        DVE(lambda q, c0=c0: q.reciprocal(out=lbp[:, c0 + 1:c0 + 2], in_=lbp[:, c0 + 1:c0 + 2]), [blbp], [blbp])
                        DVE(lambda q, c0=c0: q.tensor_scalar(out=lbp[:, c0:c0 + 1], in0=lbp[:, c0 + 1:c0 + 2], scalar1=-1.0, scalar2=1.0, op0=ALU.mult, op1=ALU.add), [blbp], [blbp])
                        DVE(lambda q, c0=c0: q.tensor_scalar_mul(out=lbp[:, c0 + 2:c0 + 3], in0=lbp[:, c0:c0 + 1], scalar1=-1.0), [blbp], [blbp])
                        DVE(lambda q, c0=c0: q.tensor_scalar_max(out=lbp[:, c0 + 1:c0 + 2], in0=lbp[:, c0 + 1:c0 + 2], scalar1=1e-30), [blbp], [blbp])
                sl, bs = wnext()
                for k in range(3):
                    wv = sl[:, k * 1024:(k + 1) * 1024].rearrange("p (c n) -> p c n", c=8)
                    for g in range(NG):
                        ps, bp = nps()
                        inproj(wv, 0, g, (ps, bp, bs))
                        gs = slice(g * 512, (g + 1) * 512)
                        if k == 0:
                            ACT(lambda q, ps=ps, gs=gs: q.activation(out=qf[:, gs], in_=ps[:], func=AF.Copy, scale=0.125), [bp], [bqf])
                        elif k == 1:
                            ACT(lambda q, ps=ps, gs=gs: q.activation(out=vT[:, gs], in_=ps[:], func=AF.Copy), [bp], [bvT])
                        else:
                            ACT(lambda q, ps=ps, gs=gs: q.activation(out=sF[:, gs], in_=ps[:], func=AF.Exp, scale=-1.0), [bp], [bsF])
                sl, bs = wnext()
                for k in range(2):
                    wv = sl[:, k * 1024:(k + 1) * 1024].rearrange("p (c n) -> p c n", c=8)
                    for g in range(NG):
                        ps, bp = nps()
                        inproj(wv, 0, g, (ps, bp, bs))
                        gs = slice(g * 512, (g + 1) * 512)
                        if k == 0:
                            ACT(lambda q, ps=ps, gs=gs: q.activation(out=sB[:, gs], in_=ps[:], func=AF.Exp, scale=-1.0), [bp], [bsB])
                        else:
                            ACT(lambda q, ps=ps, gs=gs: q.activation(out=tmpb[:, gs], in_=ps[:], func=AF.Exp, scale=-1.0), [bp], [btmpb])
                            DVE(lambda q, gs=gs: q.tensor_scalar_add(out=tmpb[:, gs], in0=tmpb[:, gs], scalar1=1.0), [btmpb], [btmpb])
                            DVE(lambda q, gs=gs: q.reciprocal(out=tmpb[:, gs], in_=tmpb[:, gs]), [btmpb], [btmpb])
                            DVE(lambda q, ps=ps, gs=gs: q.tensor_tensor(out=sgT[:, gs], in0=ps[:], in1=tmpb[:, gs], op=ALU.mult), [bp, btmpb], [bsg])
                for (sx, bsx) in ((sF, bsF), (sB, bsB)):
                    DVE(lambda q, sx=sx: q.tensor_scalar_add(out=sx, in0=sx, scalar1=1.0), [bsx], [bsx])
                    DVE(lambda q, sx=sx: q.reciprocal(out=sx, in_=sx), [bsx], [bsx])
                CH = 32
                NCH = TOK // CH
                SPC = 256 // CH
                for dr, (sx, bsx) in enumerate(((sF, bsF), (sB, bsB))):
                    c0 = dr * 3
                    DVE(lambda q, sx=sx, c0=c0: q.tensor_scalar(out=kb, in0=sx, scalar1=lbp[:, c0 + 2:c0 + 3], scalar2=lbp[:, c0:c0 + 1], op0=ALU.mult, op1=ALU.add), [bsx, blbp], [bkb])
                    DVE(lambda q, sx=sx, c0=c0: q.tensor_scalar(out=sx, in0=sx, scalar1=lbp[:, c0:c0 + 1], scalar2=lbp[:, c0 + 1:c0 + 2], op0=ALU.mult, op1=ALU.add), [bsx, blbp], [bsx])
                    ACT(lambda q, sx=sx: q.activation(out=sx, in_=sx, func=AF.Ln), [bsx], [bsx])
                    A_, bA_, B_, bB_ = sx, bsx, tmpb, btmpb
                    sh = 1
                    while sh < CH:
                        A3 = A_.rearrange("p (c n) -> p c n", n=CH)
                        B3 = B_.rearrange("p (c n) -> p c n", n=CH)
                        if dr == 0:
                            ACT(lambda q, A3=A3, B3=B3, sh=sh: q.activation(out=B3[:, :, 0:sh], in_=A3[:, :, 0:sh], func=AF.Copy), [bA_], [bB_])
                            DVE(lambda q, A3=A3, B3=B3, sh=sh: q.tensor_tensor(out=B3[:, :, sh:CH], in0=A3[:, :, sh:CH], in1=A3[:, :, 0:CH - sh], op=ALU.add), [bA_], [bB_])
                        else:
                            ACT(lambda q, A3=A3, B3=B3, sh=sh: q.activation(out=B3[:, :, CH - sh:CH], in_=A3[:, :, CH - sh:CH], func=AF.Copy), [bA_], [bB_])
                            DVE(lambda q, A3=A3, B3=B3, sh=sh: q.tensor_tensor(out=B3[:, :, 0:CH - sh], in0=A3[:, :, 0:CH - sh], in1=A3[:, :, sh:CH], op=ALU.add), [bA_], [bB_])
                        A_, bA_, B_, bB_ = B_, bB_, A_, bA_
                        sh *= 2
                    bq_, bbq_, tq_, btq_ = A_, bA_, B_, bB_
                    b3 = bq_.rearrange("p (c n) -> p c n", n=CH)
                    mid, Ti = (CH // 2 - 1, CH - 1) if dr == 0 else (CH // 2, 0)
                    bm, em, eT, eTm = scl[dr]
                    DVE(lambda q, b3=b3, mid=mid, bm=bm: q.tensor_copy(out=bm, in_=b3[:, :, mid]), [bbq_], [bscl[dr]])
                    ACT(lambda q, bm=bm, em=em: q.activation(out=em, in_=bm, func=AF.Exp), [bscl[dr]], [bscl[dr]])
                    ACT(lambda q, b3=b3, Ti=Ti, eT=eT: q.activation(out=eT, in_=b3[:, :, Ti], func=AF.Exp), [bbq_], [bscl[dr]])
                    DVE(lambda q, b3=b3, Ti=Ti, bm=bm, eTm=eTm: q.tensor_tensor(out=eTm, in0=b3[:, :, Ti], in1=bm, op=ALU.subtract), [bbq_, bscl[dr]], [bscl[dr]])
                    ACT(lambda q, eTm=eTm: q.activation(out=eTm, in_=eTm, func=AF.Exp), [bscl[dr]], [bscl[dr]])
                    DVE(lambda q, b3=b3, bm=bm: q.tensor_tensor(out=b3, in0=b3, in1=bm.unsqueeze(2).to_broadcast([128, NCH, CH]), op=ALU.subtract), [bbq_, bscl[dr]], [bbq_])
                    ACT(lambda q, bq_=bq_, tq_=tq_: q.activation(out=tq_, in_=bq_, func=AF.Exp), [bbq_], [btq_])
                    DVE(lambda q, dr=dr, tq_=tq_: q.tensor_tensor(out=qt[dr], in0=qf, in1=tq_, op=ALU.mult), [bqf, btq_], [bqt[dr]])
                    ACT(lambda q, bq_=bq_, tq_=tq_: q.activation(out=tq_, in_=bq_, func=AF.Exp, scale=-1.0), [bbq_], [btq_])
                    DVE(lambda q, dr=dr, tq_=tq_: q.tensor_tensor(out=kt_[dr], in0=kb, in1=tq_, op=ALU.mult), [bkb, btq_], [bkt[dr]])
                ev = [0]

                def emit_state(slot, dr):
                    i = ev[0] % 2
                    ev[0] += 1
                    ACT(lambda q, i=i: q.activation(out=sst[i], in_=S, func=AF.Copy), [bS], [bsst[i]])
                    for hd in range(2):
                        DMA("sp", sout[l, slot, dr, 2 * pr + hd], sst[i][hd * 64:(hd + 1) * 64, hd * 64:(hd + 1) * 64], bsst[i], reads=[bsst[i]])

                def load_state(dr):
                    POOL(lambda q: q.memset(S, 0.0), [bS], [bS])
                    for hd in range(2):
                        DMA("sp", S[hd * 64:(hd + 1) * 64, hd * 64:(hd + 1) * 64], s0_d[l, dr, 2 * pr + hd], bS, reads=[bS], writes=[bS])

                for dr in range(2):
                    bm, em, eT, eTm = scl[dr]
                    order = list(range(NCH)) if dr == 0 else list(range(NCH - 1, -1, -1))
                    mcol = C_MF if dr == 0 else C_MB
                    for n_, ch in enumerate(order):
                        first_of_seq = (ch % SPC == 0) if dr == 0 else (ch % SPC == SPC - 1)
                        if first_of_seq:
                            seq = ch // SPC
                            prev = seq - 1 if dr == 0 else seq + 1
                            if n_ > 0:
                                emit_state(prev, dr)
                            if (dr == 0 and seq == 0) or (dr == 1 and seq == 3):
                                load_state(dr)
                            elif seq >= 4:
                                POOL(lambda q: q.memset(S, 0.0), [bS], [bS])
                            else:
                                DVE(lambda q: q.tensor_scalar_mul(out=S, in0=S, scalar1=par[:, P_KEEP:P_KEEP + 1]), [bS, bpar], [bS])
                        i = n_ % 2
                        cs = slice(ch * CH, (ch + 1) * CH)
                        DVE(lambda q, ch=ch, em=em: q.tensor_scalar_mul(out=S0m, in0=S, scalar1=em[:, ch:ch + 1]), [bS, bscl[dr]], [bS0m])
                        ptk, bptk = nps()
                        ptkb = ptk[:].bitcast(BF16)
                        PE(lambda q, ptkb=ptkb, cs=cs, dr=dr: q.transpose(ptkb[0:CH, 0:128], kt_[dr][:, cs], ident_b), [bkt[dr], bcb], [bptk])
                        PE(lambda q, ptkb=ptkb, cs=cs: q.transpose(ptkb[0:CH, 128:256], vT[:, cs], ident_b), [bvT, bcb], [bptk])
                        ACT(lambda q, ptkb=ptkb, i=i: q.activation(out=ktp[i][0:CH, 0, 0:64], in_=ptkb[0:CH, 0:64], func=AF.Copy), [bptk], [bktp[i]])
                        ACT(lambda q, ptkb=ptkb, i=i: q.activation(out=ktp[i][0:CH, 1, 64:128], in_=ptkb[0:CH, 64:128], func=AF.Copy), [bptk], [bktp[i]])
                        DVE(lambda q, ptkb=ptkb, i=i: q.tensor_copy(out=vpd[i][0:CH, 0, 0:64], in_=ptkb[0:CH, 128:192]), [bptk], [bvpd[i]])
                        DVE(lambda q, ptkb=ptkb, i=i: q.tensor_copy(out=vpd[i][0:CH, 1, 64:128], in_=ptkb[0:CH, 192:256]), [bptk], [bvpd[i]])
                        for hd in range(2):
                            hs = slice(hd * 64, (hd + 1) * 64)
                            pA, bpA = nps()
                            PE(lambda q, pA=pA, hd=hd, hs=hs, cs=cs, dr=dr: q.matmul(pA[0:CH, 0:CH], lhsT=kt_[dr][hs, cs], rhs=qt[dr][hs, cs], start=True, stop=True), [bkt[dr], bqt[dr]], [bpA])
                            DVE(lambda q, pA=pA, i=i, mcol=mcol, hd=hd: q.tensor_tensor(out=atb[i][0:CH, hd * CH:(hd + 1) * CH], in0=pA[0:CH, 0:CH], in1=cf[0:CH, mcol:mcol + CH], op=ALU.mult), [bpA, bcf], [batb[i]])
                        po, bpo = nps()
                        PE(lambda q, po=po, i=i: q.matmul(po[:, 0:CH], lhsT=vpd[i][0:CH, 0, :], rhs=atb[i][0:CH, 0:CH], start=True, stop=False), [bvpd[i], batb[i]], [bpo])
                        PE(lambda q, po=po, i=i: q.matmul(po[:, 0:CH], lhsT=vpd[i][0:CH, 1, :], rhs=atb[i][0:CH, CH:2 * CH], start=False, stop=False), [bvpd[i], batb[i]], [bpo])
                        PE(lambda q, po=po, cs=cs, dr=dr: q.matmul(po[:, 0:CH], lhsT=S0m, rhs=qt[dr][:, cs], start=False, stop=True), [bS0m, bqt[dr]], [bpo])
                        if dr == 0:
                            ACT(lambda q, po=po, cs=cs: q.activation(out=hoT[:, cs], in_=po[:, 0:CH], func=AF.Copy), [bpo], [bho])
                        else:
                            DVE(lambda q, po=po, cs=cs: q.tensor_tensor(out=hoT[:, cs], in0=po[:, 0:CH], in1=hoT[:, cs], op=ALU.add), [bpo, bho], [bho])
                        pS2, bpS2 = nps()
                        PE(lambda q, pS2=pS2, i=i: q.matmul(pS2[:, 0:128], lhsT=ktp[i][0:CH, 0, :], rhs=vpd[i][0:CH, 0, :], start=True, stop=False), [bktp[i], bvpd[i]], [bpS2])
                        PE(lambda q, pS2=pS2, i=i: q.matmul(pS2[:, 0:128], lhsT=ktp[i][0:CH, 1, :], rhs=vpd[i][0:CH, 1, :], start=False, stop=True), [bktp[i], bvpd[i]], [bpS2])
                        DVE(lambda q, ch=ch, eT=eT: q.tensor_scalar_mul(out=tS, in0=S, scalar1=eT[:, ch:ch + 1]), [bS, bscl[dr]], [btS])
                        DVE(lambda q, pS2=pS2, ch=ch, eTm=eTm: q.scalar_tensor_tensor(out=S, in0=pS2[:, 0:128], scalar=eTm[:, ch:ch + 1], in1=tS, op0=ALU.mult, op1=ALU.add), [bpS2, btS, bscl[dr]], [bS])
                    emit_state(0 if dr == 1 else 5, dr)
                for g in range(NG):
                    gs = slice(g * 512, (g + 1) * 512)
                    sqh = tmpb[:, 0:256].bitcast(BF16)
                    ACT(lambda q, gs=gs, sqh=sqh: q.activation(out=sqh, in_=hoT[:, gs], func=AF.Square), [bho], [btmpb])
                    p2, bp2 = nps()
                    PE(lambda q, p2=p2, sqh=sqh: q.matmul(p2[:], lhsT=blk_b, rhs=sqh, start=True, stop=True), [btmpb, bcb], [bp2])
                    rv = kb[:, 0:512]
                    DVE(lambda q, p2=p2, rv=rv: q.tensor_scalar(out=rv, in0=p2[:], scalar1=1.0 / 64, scalar2=EPS, op0=ALU.mult, op1=ALU.add), [bp2], [bkb])
                    ACT(lambda q, rv=rv: q.activation(out=rv, in_=rv, func=AF.Ln), [bkb], [bkb])
                    ACT(lambda q, rv=rv: q.activation(out=rv, in_=rv, func=AF.Exp, scale=-0.5), [bkb], [bkb])
                    DVE(lambda q, rv=rv, gs=gs, pr=pr: q.scalar_tensor_tensor(out=rv, in0=hoT[:, gs], scalar=par[:, P_HNRM + l * 2 + pr:P_HNRM + l * 2 + pr + 1], in1=rv, op0=ALU.mult, op1=ALU.mult), [bho, bkb, bpar], [bkb])
                    DVE(lambda q, rv=rv, gs=gs, g=g, pr=pr: q.tensor_tensor(out=catT[:, 2 + pr, gs], in0=rv, in1=sgT[:, gs], op=ALU.mult), [bkb, bsg], [bcat[2 + pr][g]])
        else:
            for pr in range(2):
                wnext()
                wnext()
                for g in range(NG):
                    POOL(lambda q, pr=pr, g=g: q.memset(catT[:, 2 + pr, g * 512:(g + 1) * 512], 0.0), [], [bcat[2 + pr][g]])
        if ENABLE_ATTN:
            state["aoff"] = base_off
            qT = aalloc(4 * TOK // 2).bitcast(BF16).rearrange("p (c n) -> p c n", c=4)
            kdT = aalloc(4 * TOK // 2).bitcast(BF16).rearrange("p (c v n) -> p c v n", c=2, v=2)
            kcT = aalloc(4 * 512 // 2).bitcast(BF16).rearrange("p (c v n) -> p c v n", c=2, v=2)
            vaug = aalloc(16 * 2 * 96 // 2).bitcast(BF16).rearrange("p (t k n) -> p t k n", t=16, k=2)
            kst = aalloc(NT * 128).rearrange("p (t n) -> p t n", t=NT)
            vst = aalloc(NT * 128).rearrange("p (t n) -> p t n", t=NT)
            kcd = aalloc(4 * 256).rearrange("p (t k n) -> p t k n", t=4, k=2)
            zq = [aalloc(512), aalloc(512)]
            rr = [aalloc(512), aalloc(512)]
            t1 = [aalloc(512)] * 2
            sqa = [aalloc(256).bitcast(BF16), aalloc(256).bitcast(BF16)]
            pT = [aalloc(256).bitcast(BF16), aalloc(256).bitcast(BF16)]
            otok = [aalloc(256).bitcast(BF16).rearrange("p (h n) -> p h n", h=8), aalloc(256).bitcast(BF16).rearrange("p (h n) -> p h n", h=8)]
            den = [aalloc(4), aalloc(4)]
            esink = aalloc(8)
            bq = [[nb("q") for g in range(NG)] for c in range(4)]
            bkd = [[nb("kd") for g in range(NG)] for c in range(2)]
            bkc, bva, bkst, bvst, bkcd, besk = nb("kc"), [nb("va") for t in range(16)], nb("kst"), nb("vst"), nb("kcd"), nb("esk")
            bzq, brr, bt1, bsqa, bpT, botok, bden = [[nb(n + str(i)) for i in range(2)] for n in ("zq", "rr", "t1", "sqa", "pT", "otok", "den")]
            bt1[1] = bt1[0]
            ACT(lambda q: q.activation(out=esink, in_=par[:, P_SINK + l * 8:P_SINK + l * 8 + 8], func=AF.Exp), [bpar], [besk])
            POOL(lambda q: q.memset(vaug[:, :, :, 64:66], 1.0), [], list(bva))
            allkd = [bkd[c_][g_] for c_ in range(2) for g_ in range(NG)]
            POOL(lambda q: q.memset(kdT[64:128, :, 0, :], 0.0), [], allkd)
            POOL(lambda q: q.memset(kdT[0:64, :, 1, :], 0.0), [], allkd)
            POOL(lambda q: q.memset(kcT[64:128, :, 0, :], 0.0), [], [bkc])
            POOL(lambda q: q.memset(kcT[0:64, :, 1, :], 0.0), [], [bkc])
            for kv in (range(2) if "cachek" not in SKIP else []):
                for dup in range(2):
                    DMA("sp", kcd[:, :, kv, dup * 64:(dup + 1) * 64], ck_d[l][:, kv * 64:(kv + 1) * 64].rearrange("(t p) d -> p t d", p=128), bkcd, writes=[bkcd])
            for t in (range(4) if "cachev" not in SKIP else []):
                DMA("pool", vaug[:, 12 + t, :, 0:64], cv_d[l][t * 128:(t + 1) * 128, :].rearrange("p (k d) -> p k d", k=2), bva[12 + t], writes=[bva[12 + t]])
            for kv in (range(2) if "cachek" not in SKIP else []):
                ps, bp = nps()
                for t in range(4):
                    PE(lambda q, ps=ps, t=t, kv=kv: q.transpose(ps[:, t * 128:(t + 1) * 128], kcd[:, t, kv, :], ident), [bkcd, bcf], [bp])
                ACT(lambda q, ps=ps, kv=kv: q.activation(out=kcT[0:64, kv, 0, :], in_=ps[0:64, :], func=AF.Copy), [bp], [bkc])
                ACT(lambda q, ps=ps, kv=kv: q.activation(out=kcT[64:128, kv, 1, :], in_=ps[64:128, :], func=AF.Copy), [bp], [bkc])
            cnt = [0]

            def qk_post(ps, bp, g, nw_col, dst, bdst, scale_extra, kfin=None):
                s = cnt[0] % 2
                cnt[0] += 1
                ACT(lambda q: q.activation(out=zq[s], in_=ps[:], func=AF.Copy), [bp], [bzq[s]])
                ACT(lambda q: q.activation(out=sqa[s], in_=ps[:], func=AF.Square), [bp], [bsqa[s]])
                p2, bp2 = nps()
                PE(lambda q: q.matmul(p2[:], lhsT=blk_b, rhs=sqa[s], start=True, stop=True), [bsqa[s], bcb], [bp2])
                DVE(lambda q: q.tensor_scalar(out=rr[s], in0=p2[:], scalar1=1.0 / 64, scalar2=EPS, op0=ALU.mult, op1=ALU.add), [bp2], [brr[s]])
                ACT(lambda q: q.activation(out=rr[s], in_=rr[s], func=AF.Ln), [brr[s]], [brr[s]])
                ACT(lambda q: q.activation(out=rr[s], in_=rr[s], func=AF.Exp, scale=-0.5), [brr[s]], [brr[s]])
                DVE(lambda q: q.scalar_tensor_tensor(out=zq[s], in0=zq[s], scalar=par[:, nw_col:nw_col + 1], in1=rr[s], op0=ALU.mult, op1=ALU.mult), [bzq[s], brr[s], bpar], [bzq[s]])
                if g < 2 and "rope" not in SKIP:
                    p3, bp3 = nps()
                    PE(lambda q: q.matmul(p3[:], lhsT=cf[:, C_PERM:C_PERM + 128], rhs=zq[s], start=True, stop=True), [bzq[s], bcf], [bp3])
                    DVE(lambda q: q.tensor_tensor(out=t1[s], in0=p3[:], in1=cf[:, C_SIN + g * 512:C_SIN + (g + 1) * 512], op=ALU.mult), [bp3, bcf], [bt1[s]])
                    DVE(lambda q: q.tensor_tensor(out=zq[s], in0=zq[s], in1=cf[:, C_COS + g * 512:C_COS + (g + 1) * 512], op=ALU.mult), [bzq[s], bcf], [bzq[s]])
                    DVE(lambda q: q.tensor_tensor(out=zq[s], in0=zq[s], in1=t1[s], op=ALU.add), [bzq[s], bt1[s]], [bzq[s]])
                if kfin is None:
                    ACT(lambda q: q.activation(out=dst, in_=zq[s], func=AF.Copy, scale=scale_extra), [bzq[s]], [bdst])
                else:
                    ACT(lambda q: q.activation(out=dst[0][0:64, :], in_=zq[s][0:64, :], func=AF.Copy), [bzq[s]], [bdst])
                    ACT(lambda q: q.activation(out=dst[1][64:128, :], in_=zq[s][64:128, :], func=AF.Copy), [bzq[s]], [bdst])
                if kfin is not None and "ktr" not in SKIP:
                    kv = kfin
                    p4, bp4 = nps()
                    for tt in range(4):
                        PE(lambda q, tt=tt: q.transpose(p4[:, tt * 64:(tt + 1) * 64], zq[s][0:64, tt * 128:(tt + 1) * 128], ident[0:64, 0:64]), [bzq[s], bcf], [bp4])
                    DVE(lambda q: q.tensor_copy(out=kst[:, g * 4:(g + 1) * 4, kv * 64:(kv + 1) * 64], in_=p4[:, 0:256].rearrange("p (t n) -> p t n", t=4)), [bp4], [bkst])

            sl, bs = wnext()
            wv = sl[:, 0:4096].rearrange("p (c n) -> p c n", c=8)
            for c in range(4):
                for g in range(NG):
                    ps, bp = nps()
                    inproj(wv, c, g, (ps, bp, bs))
                    qk_post(ps, bp, g, P_QN + l, qT[:, c, g * 512:(g + 1) * 512], bq[c][g], 0.125)
            sl, bs = wnext()
            for kv in range(2):
                wv = sl[:, kv * 1024:(kv + 1) * 1024].rearrange("p (c n) -> p c n", c=8)
                for g in range(NG):
                    ps, bp = nps()
                    inproj(wv, 0, g, (ps, bp, bs))
                    qk_post(ps, bp, g, P_KN + l, (kdT[:, kv, 0, g * 512:(g + 1) * 512], kdT[:, kv, 1, g * 512:(g + 1) * 512]), bkd[kv][g], 1.0, kfin=kv)
            wv = sl[:, 2048:3072].rearrange("p (c n) -> p c n", c=8)
            for g in range(NG):
                ps, bp = nps()
                inproj(wv, 0, g, (ps, bp, bs))
                s = cnt[0] % 2
                cnt[0] += 1
                ACT(lambda q, ps=ps, s=s: q.activation(out=zq[s], in_=ps[:], func=AF.Copy), [bp], [bzq[s]])
                if "vtr" in SKIP:
                    continue
                p4, bp4 = nps()
                for tt in range(4):
                    PE(lambda q, tt=tt, s=s, p4=p4: q.transpose(p4[:, tt * 128:(tt + 1) * 128], zq[s][:, tt * 128:(tt + 1) * 128], ident), [bzq[s], bcf], [bp4])
                if "vtr_dve" not in SKIP:
                    DVE(lambda q, p4=p4, g=g: q.tensor_copy(out=vst[:, g * 4:(g + 1) * 4, :], in_=p4[:].rearrange("p (t n) -> p t n", t=4)), [bp4], [bvst])
                for tt in (range(4) if "vtr_act" not in SKIP else []):
                    ACT(lambda q, p4=p4, g=g, tt=tt: q.activation(out=vaug[:, g * 4 + tt, :, 0:64], in_=p4[:, tt * 128:(tt + 1) * 128].rearrange("p (k d) -> p k d", k=2), func=AF.Copy), [bp4], [bva[g * 4 + tt]])
            if "kvout" not in SKIP:
                DMA("sp", kout[l].rearrange("(t p) n -> p t n", p=128), kst, bkst, reads=[bkst])
            if "kvout" not in SKIP:
                DMA("sp", vout[l].rearrange("(t p) n -> p t n", p=128), vst, bvst, reads=[bvst])
            it = 0
            if not ATTN_CORE:
                for c in range(4):
                    for g in range(NG):
                        POOL(lambda q, c=c, g=g: q.memset(catT[:, 4 + c, g * 512:(g + 1) * 512], 0.0), [], [bcat[4 + c][g]])
            for j in (range(NT) if ATTN_CORE else []):
                g = j // 4
                if j < 8:
                    kts = [("n", j + d_, (j * 3 + 1 + d_) if d_ != 0 else None) for d_ in (-1, 0, 1) if 0 <= j + d_ < 8] + [("c", t, None) for t in range(4)]
                else:
                    b0 = 8 + 2 * ((j - 8) // 2)
                    kts = [("n", b0, None), ("n", b0 + 1, None)]
                so = j % 2
                for kv in range(2):
                    po, bpo = nps()
                    pov = po[:, 0:264].rearrange("p (h n) -> p h n", h=4)
                    for ki, (kind, kt, mi) in enumerate(kts):
                        pS, bpS = nps()
                        for hh in range(4):
                            cq = kv * 2 + hh // 2
                            hf = hh % 2
                            if kind == "n":
                                lhs = kdT[:, kv, hf, kt * 128:(kt + 1) * 128]
                                rb = bkd[kv][kt // 4]
                            else:
                                lhs = kcT[:, kv, hf, kt * 128:(kt + 1) * 128]
                                rb = bkc
                            PE(lambda q, pS=pS, hh=hh, lhs=lhs, cq=cq, hf=hf, j=j: q.matmul(pS[:, hh * 128:(hh + 1) * 128], lhsT=lhs, rhs=qT[:, cq, j * 128:(j + 1) * 128], start=True, stop=True),
                               [rb, bq[cq][g]], [bpS])
                        s = it % 2
                        it += 1
                        if kind == "c":
                            ACT(lambda q, pS=pS, s=s: q.activation(out=pT[s], in_=pS[:], func=AF.Exp, bias=par[:, P_CBIAS:P_CBIAS + 1]), [bpS, bpar], [bpT[s]])
                        else:
                            ACT(lambda q, pS=pS, s=s: q.activation(out=pT[s], in_=pS[:], func=AF.Exp), [bpS], [bpT[s]])
                        if mi is not None and "mask" not in SKIP:
                            DVE(lambda q, s=s, mi=mi: q.tensor_tensor(out=pT[s].rearrange("p (h n) -> p h n", h=4), in0=pT[s].rearrange("p (h n) -> p h n", h=4),
                                                                      in1=cb[:, B_MASK + mi * 128:B_MASK + (mi + 1) * 128].unsqueeze(1).to_broadcast([128, 4, 128]), op=ALU.mult), [bpT[s], bcb], [bpT[s]])
                        vt = kt if kind == "n" else 12 + kt
                        for hh in (range(4) if "pv" not in SKIP else []):
                            PE(lambda q, pov=pov, hh=hh, s=s, vt=vt, kv=kv, ki=ki, n=len(kts): q.matmul(pov[:, hh, 0:66], lhsT=pT[s][:, hh * 128:(hh + 1) * 128], rhs=vaug[:, vt, kv, 0:66], start=(ki == 0 and hh == 0), stop=(ki == n - 1 and hh == 3)),
                               [bpT[s], bva[vt]], [bpo])
                    DVE(lambda q, pov=pov, so=so, kv=kv: q.tensor_tensor(out=den[so], in0=pov[:, :, 64], in1=esink[:, kv * 4:(kv + 1) * 4], op=ALU.add), [bpo, besk], [bden[so]])
                    DVE(lambda q, so=so: q.reciprocal(out=den[so], in_=den[so]), [bden[so]], [bden[so]])
                    DVE(lambda q, pov=pov, so=so, kv=kv: q.tensor_tensor(out=otok[so][:, kv * 4:(kv + 1) * 4, :], in0=pov[:, :, 0:64], in1=den[so].unsqueeze(2).to_broadcast([128, 4, 64]), op=ALU.mult), [bpo, bden[so]], [botok[so]])
                pt_, bpt = nps()
                ptb = pt_[:].bitcast(BF16)
                for c in range(4):
                    PE(lambda q, ptb=ptb, c=c, so=so: q.transpose(ptb[:, c * 128:(c + 1) * 128], otok[so][:, 2 * c:2 * c + 2, :].rearrange("p h n -> p (h n)"), ident_b), [botok[so], bcb], [bpt])
                ACT(lambda q, ptb=ptb, j=j: q.activation(out=catT[:, 4:8, j * 128:(j + 1) * 128], in_=ptb[:, 0:512].rearrange("p (c n) -> p c n", c=4), func=AF.Copy), [bpt], [bcat[4 + c][g] for c in range(4)])

        else:
            wnext()
            wnext()
            for c in range(4):
                for g in range(NG):
                    POOL(lambda q, c=c, g=g: q.memset(catT[:, 4 + c, g * 512:(g + 1) * 512], 0.0), [], [bcat[4 + c][g]])
        for i in range(2):
            sl, bs = wnext()
            wv = sl[:, 0:4096].rearrange("p (c n) -> p c n", c=8)
            for k in range(4):
                dc = i * 4 + k
                for g in range(NG):
                    ps, bp = nps()
                    for c in range(8):
                        PE(lambda q, ps=ps, wv=wv, c=c, k=k, g=g: q.matmul(ps[:], lhsT=wv[:, c, k * 128:(k + 1) * 128], rhs=catT[:, c, g * 512:(g + 1) * 512], start=(c == 0), stop=(c == 7)),
                           [bs, bcat[c][g]], [bp])
                    DVE(lambda q, ps=ps, dc=dc, g=g: q.scalar_tensor_tensor(out=xT[:, dc, g * 512:(g + 1) * 512], in0=ps[:], scalar=mod_G(l, 1, dc, g), in1=xT[:, dc, g * 512:(g + 1) * 512], op0=ALU.mult, op1=ALU.add),
                        [bp, bmod, bx[dc][g]], [bx[dc][g]])

    for l in range(2):
        ffn(l, 0)
        mixer(l)
        ffn(l, 1)

    areset()
    yst = [aalloc(1024), aalloc(1024)]
    byst = [nb("yst0"), nb("yst1")]
    for t in range(NT):
        s = t % 2
        g = t // 4
        for half in range(2):
            ps, bp = nps()
            for k in range(4):
                c = half * 4 + k
                PE(lambda q, ps=ps, k=k, c=c, t=t: q.transpose(ps[:, k * 128:(k + 1) * 128], xT[:, c, t * 128:(t + 1) * 128], ident), [bx[c][g], bcf], [bp])
            if half == 0:
                ACT(lambda q, ps=ps, s=s: q.activation(out=yst[s][:, 0:512], in_=ps[:], func=AF.Copy), [bp], [byst[s]])
            else:
                DVE(lambda q, ps=ps, s=s: q.tensor_copy(out=yst[s][:, 512:1024], in_=ps[:]), [bp], [byst[s]])
        DMA("sp", yout[t * 128:(t + 1) * 128, :], yst[s], byst[s], reads=[byst[s]])

    T.emit(nc)
    st.close()
    return nc


def _consts(sample_mode):
    cf = np.zeros((128, NCF), np.float32)
    cf[:, C_ID:C_ID + 128] = np.eye(128, dtype=np.float32)
    Pm = np.zeros((64, 64), np.float32)
    for d in range(64):
        blk = d // 16
        if blk % 2 == 0:
            Pm[d, d + 16] = -1.0
        else:
            Pm[d, d - 16] = 1.0
    P2 = np.zeros((128, 128), np.float32)
    P2[:64, :64] = Pm
    P2[64:, 64:] = Pm
    cf[:, C_PERM:C_PERM + 128] = P2.T
    t = np.arange(1024)
    inv = (10000.0 ** (-np.arange(0, 32, 2, dtype=np.float32) / 32)).astype(np.float32)
    d = np.arange(128) % 64
    pos = np.where((d < 32)[:, None], (t // 64)[None, :], (t % 64)[None, :]).astype(np.float32)
    ang = pos * inv[d % 16][:, None]
    if sample_mode:
        cf[:, C_COS:C_COS + 1024] = np.cos(ang)
        cf[:, C_SIN:C_SIN + 1024] = np.sin(ang)
    else:
        cf[:, C_COS:C_COS + 1024] = 1.0
    j = np.arange(64)
    cf[:64, C_MF:C_MF + 64] = (j[:, None] <= j[None, :]).astype(np.float32)
    cf[:64, C_MB:C_MB + 64] = (j[:, None] >= j[None, :]).astype(np.float32)
    cb = np.zeros((128, NCB), np.float32)
    cb[:, B_ONES:B_ONES + 128] = 1.0
    cb[:64, B_BLK:B_BLK + 64] = 1.0
    cb[64:, B_BLK + 64:B_BLK + 128] = 1.0
    cb[:, B_ID:B_ID + 128] = np.eye(128, dtype=np.float32)
    c = np.arange(64)
    a = 2 * np.pi * np.outer(c, c) / 64.0
    for gq in range(2):
        cb[gq * 64:(gq + 1) * 64, B_DFT + gq * 64:B_DFT + (gq + 1) * 64] = np.cos(a)
        cb[gq * 64:(gq + 1) * 64, B_DFT + 128 + gq * 64:B_DFT + 128 + (gq + 1) * 64] = np.sin(a)
    ki = np.arange(128)[:, None]
    qi = np.arange(128)[None, :]
    for jt in range(8):
        for s in range(3):
            if sample_mode:
                m = (ki >= qi) if s == 0 else (np.ones((128, 128), bool) if s == 1 else (ki <= qi))
            else:
                if s == 1:
                    m = np.ones((128, 128), bool)
                elif s == 0:
                    m = np.full((128, 128), jt % 2 == 1)
                else:
                    m = np.full((128, 128), jt % 2 == 0)
            cb[:, B_MASK + (jt * 3 + s) * 128:B_MASK + (jt * 3 + s + 1) * 128] = m.astype(np.float32)

    def dft(L):
        ll = np.arange(L)
        aa = 2 * np.pi * np.outer(ll, ll) / L
        sc_ = 1.0 / np.sqrt(L * 64.0)
        return (np.cos(aa) * sc_).astype(np.float32), (-np.sin(aa) * sc_).astype(np.float32)

    dftp = np.stack(dft(256))
    if sample_mode:
        dftl = np.stack(dft(1024))
    else:
        dftl = np.zeros((2, 1024, 1024), np.float32)
        for i in range(4):
            dftl[:, i * 256:(i + 1) * 256, i * 256:(i + 1) * 256] = dftp
    return cf, cb, dftl, dftp


_NC_CACHE = {}


def _prep(inputs):
    I = {k: np.asarray(v) for k, v in inputs.items()}
    xp, xs = I["x_prompt"], I["x_sample"]
    consts = {True: _consts(True), False: _consts(False)}
    fm = lambda v: np.ascontiguousarray(v.reshape(-1, 128).T)
    in_maps = []
    assign = []
    for core in range(8):
        sm = core < 4
        if sm:
            gtok = xs[core]
            pseq = [2 * core, 2 * core + 1]
            gseq = []
            cond = np.stack([I["c"][core], I["c_ctx"]])
        else:
            base = 8 + 6 * (core - 4)
            gseq = [base, base + 1, base + 2, base + 3]
            gtok = xp[gseq].reshape(1024, D)
            pseq = [base + 4, base + 5]
            cond = np.stack([I["c_ctx"], I["c_ctx"]])
        assign.append((sm, gseq, pseq))
        xin = np.concatenate([gtok, xp[pseq].reshape(512, D)], axis=0)
        par = np.zeros((128, NPAR), np.float32)
        par[:, P_COND:P_COND + 16] = cond.reshape(2, 8, 128).transpose(2, 1, 0).reshape(128, 16)
        for l in range(2):
            par[:, P_BADA + l * 72:P_BADA + (l + 1) * 72] = fm(I["b_ada"][l])
            for w3, nm in enumerate(["norm_ffn1", "norm_mix", "norm_ffn2"]):
                par[:, P_NRM + l * 24 + w3 * 8:P_NRM + l * 24 + w3 * 8 + 8] = fm(I[nm][l])
            for dr in range(2):
                par[:, P_LBS + l * 4 + dr * 2:P_LBS + l * 4 + dr * 2 + 2] = fm(I["hgrn_lower_bounds"][l, dr])
            par[:, P_HNRM + l * 2:P_HNRM + l * 2 + 2] = fm(I["hgrn_norm"][l])
            par[:, P_QN + l] = np.tile(I["q_norm"][l], 2)
            par[:, P_KN + l] = np.tile(I["k_norm"][l], 2)
            par[:, P_SINK + l * 8:P_SINK + (l + 1) * 8] = I["attn_sink"][l][None, :]
        par[:, P_KEEP] = 1.0 if sm else 0.0
        par[:, P_CBIAS] = 0.0 if sm else -30000.0
        cf, cb, dftl, dftp = consts[sm]
        if sm:
            ck = I["cache_attn_k"][core].reshape(2, 512, 128)
            cv = I["cache_attn_v"][core].reshape(2, 512, 128)
            s0 = I["state_hgrn"][core]
        else:
            ck = np.zeros((2, 512, 128), np.float32)
            cv = np.zeros((2, 512, 128), np.float32)
            s0 = np.zeros((2, 2, 4, 64, 64), np.float32)
        in_maps.append(dict(
            xin=np.ascontiguousarray(xin), par=par, cf=cf, cb=cb, dftl=dftl, dftp=dftp,
            ck=np.ascontiguousarray(ck), cv=np.ascontiguousarray(cv), s0=np.ascontiguousarray(s0),
            w_ada=I["w_ada"], ffn1_gu=I["ffn1_w_gate_up"], ffn2_gu=I["ffn2_w_gate_up"],
            ffn1_d=I["ffn1_w_down"], ffn2_d=I["ffn2_w_down"], w_in=I["w_in"], w_out=I["w_out"]))
    return in_maps, assign


def _post(results, assign):
    y_p = np.zeros((32, 256, D), np.float32)
    y_s = np.zeros((4, 1024, D), np.float32)
    nk = np.zeros((32, 2, 256, 2, 64), np.float32)
    nv = np.zeros((32, 2, 256, 2, 64), np.float32)
    ns = np.zeros((32, 2, 2, 4, 64, 64), np.float32)
    for core in range(8):
        r = results[core]
        sm, gseq, pseq = assign[core]
        y = r["yout"]
        ko = r["kout"].reshape(2, TOK, 2, 64)
        vo = r["vout"].reshape(2, TOK, 2, 64)
        so = r["sout"]
        if sm:
            y_s[core] = y[:1024]
        seqs = [(s, i) for i, s in enumerate(gseq)] + [(s, 4 + i) for i, s in enumerate(pseq)]
        for s, slot in seqs:
            y_p[s] = y[slot * 256:(slot + 1) * 256]
            nk[s] = ko[:, slot * 256:(slot + 1) * 256]
            nv[s] = vo[:, slot * 256:(slot + 1) * 256]
            ns[s] = so[:, slot]
    return (y_p, y_s, nk, nv, ns)


def kernel(**inputs):
    in_maps, assign = _prep(inputs)
    if "nc" not in _NC_CACHE:
        _NC_CACHE["nc"] = build()
    res = run_bass_kernel_spmd(_NC_CACHE["nc"], in_maps, core_ids=list(range(8)))
    return _post(res.results, assign)
```

```python
import numpy as np
from contextlib import ExitStack
import concourse.bass as bass
import concourse.mybir as mybir
from concourse.bass_utils import run_bass_kernel_spmd

F32 = mybir.dt.float32
BF16 = mybir.dt.bfloat16
AF = mybir.ActivationFunctionType
ALU = mybir.AluOpType

EP = 8192
ENGS = ("pe", "act", "dve", "pool", "sp")
SAME_ENGINE_SYNC = ("act", "dve", "pool")


class Buf:
    def __init__(self, name, fence=None):
        self.name = name
        self.w = None
        self.r = list(fence) if fence else []
        self.dsem = None
        self.dcnt = 0
        self.excl = False


class Op:
    pass


class Tracker:
    def __init__(self):
        self.ops = []
        self.per_eng = {e: [] for e in ENGS}
        self.dma_bufs = []
        self.dma_ids = []

    def fence(self):
        ids = [l[-1].id for l in self.per_eng.values() if l]
        return ids + list(self.dma_ids)

    def op(self, eng, fn, reads=(), writes=(), dma_buf=None):
        o = Op()
        o.eng = eng
        o.fn = fn
        o.dma_buf = dma_buf
        o.marked = False
        o.id = len(self.ops)
        o.pos = len(self.per_eng[eng])
        deps = set()
        for b in reads:
            if b.w is not None:
                deps.add(b.w)
            if b.excl:
                deps.update(b.r)
        for b in writes:
            if b.w is not None:
                deps.add(b.w)
            deps.update(b.r)
        o.deps = deps
        for b in reads:
            b.r.append(o.id)
        for b in writes:
            b.w = o.id
            b.r = []
        if dma_buf is not None:
            if dma_buf.dsem is None:
                dma_buf.dsem = True
                self.dma_bufs.append(dma_buf)
            dma_buf.dcnt += 16
            o.dval = dma_buf.dcnt
            self.dma_ids.append(o.id)
        self.ops.append(o)
        self.per_eng[eng].append(o)
        return o

    def plan(self):
        ops = self.ops
        seen = {e: {f: -1 for f in ENGS} for e in ENGS}
        seen_dma = {e: {} for e in ENGS}
        for o in ops:
            w_eng = {}
            w_dma = {}
            for d in o.deps:
                p = ops[d]
                if p.dma_buf is not None:
                    b = p.dma_buf
                    if seen_dma[o.eng].get(id(b), 0) >= p.dval:
                        continue
                    w_dma[id(b)] = (b, max(p.dval, w_dma.get(id(b), (None, 0))[1]))
                else:
                    if p.eng == o.eng:
                        if p.eng not in SAME_ENGINE_SYNC:
                            continue
                        if p.pos < o.pos - 2:
                            continue
                    if seen[o.eng][p.eng] >= d:
                        continue
                    w_eng[p.eng] = max(w_eng.get(p.eng, -1), d)
            for f, d in w_eng.items():
                seen[o.eng][f] = d
                ops[d].marked = True
            for k, (b, v) in w_dma.items():
                seen_dma[o.eng][k] = v
            o.w_eng = w_eng
            o.w_dma = list(w_dma.values())
        cnt = {e: 0 for e in ENGS}
        for o in ops:
            if o.dma_buf is None and o.marked:
                cnt[o.eng] += 1
            o.cnt = cnt[o.eng]
        self.final_cnt = cnt

    def emit(self, nc, final_eng="sp"):
        self.plan()
        ops = self.ops
        with ExitStack() as st:
            esems = {}
            for e in ENGS:
                n = max(1, (self.final_cnt[e] + EP - 1) // EP)
                esems[e] = [st.enter_context(nc.semaphore(f"s_{e}_{i}")) for i in range(n)]
            for i, b in enumerate(self.dma_bufs):
                b.dsem = st.enter_context(nc.semaphore(f"d_{i}"))
            block = st.enter_context(nc.Block())

            def run(eng_name, eng):
                for o in self.per_eng[eng_name]:
                    for f, d in o.w_eng.items():
                        c = ops[d].cnt
                        eng.wait_ge(esems[f][(c - 1) // EP], (c - 1) % EP + 1)
                    for (b, v) in o.w_dma:
                        eng.wait_ge(b.dsem, v)
                    ins = o.fn(eng)
                    if o.dma_buf is not None:
                        ins.then_inc(o.dma_buf.dsem, 16)
                    elif o.marked:
                        c = o.cnt
                        ins.then_inc(esems[eng_name][(c - 1) // EP], 1)
                if eng_name == final_eng:
                    for b in self.dma_bufs:
                        eng.wait_ge(b.dsem, b.dcnt)
                    for f in ENGS:
                        c = self.final_cnt[f]
                        if c > 0 and f != eng_name:
                            eng.wait_ge(esems[f][(c - 1) // EP], (c - 1) % EP + 1)

            @block.tensor
            def _(e):
                run("pe", e)

            @block.scalar
            def _(e):
                run("act", e)

            @block.vector
            def _(e):
                run("dve", e)

            @block.gpsimd
            def _(e):
                run("pool", e)

            @block.sync
            def _(e):
                run("sp", e)


TOK = 1536
NT = 12
NG = 3
D = 1024
DFF = 2816
EPS = 1e-6
P_COND, P_BADA, P_NRM, P_LBS, P_HNRM, P_QN, P_KN, P_SINK, P_KEEP, P_CBIAS, NPAR = 0, 16, 160, 208, 216, 220, 222, 224, 240, 241, 242
C_ID, C_PERM, C_COS, C_SIN, C_MF, C_MB, NCF = 0, 128, 256, 1280, 2304, 2368, 2432
B_ONES, B_BLK, B_ID, B_DFT, B_MASK, NCB = 0, 128, 256, 384, 640, 640 + 24 * 128
SLOT = 4096
ENABLE_HGRN = True
DEBUG_PAIR = None
ATTN_CORE = True
SKIP = set()
ENABLE_ATTN = True
NSL = 3
ARENA = 23400


def build(stop_after=None):
    nc = bass.Bass("TRN2", target_bir_lowering=False)
    T = Tracker()
    di = lambda n, s: nc.dram_tensor(n, s, F32, kind="ExternalInput").ap()
    do = lambda n, s: nc.dram_tensor(n, s, F32, kind="ExternalOutput").ap()
    xin = di("xin", [TOK, D]); par_d = di("par", [128, NPAR]); cf_d = di("cf", [128, NCF]); cb_d = di("cb", [128, NCB])
    dftl_d = di("dftl", [2, 1024, 1024]); dftp_d = di("dftp", [2, 256, 256])
    ck_d = di("ck", [2, 512, 128]); cv_d = di("cv", [2, 512, 128]); s0_d = di("s0", [2, 2, 4, 64, 64])
    wada_d = di("w_ada", [2, D, 9 * D])
    wgu_d = [di("ffn1_gu", [2, D, 2 * DFF]), di("ffn2_gu", [2, D, 2 * DFF])]
    wdn_d = [di("ffn1_d", [2, DFF, D]), di("ffn2_d", [2, DFF, D])]
    win_d = di("w_in", [2, D, 2304]); wout_d = di("w_out", [2, D, D])
    yout = do("yout", [TOK, D]); kout = do("kout", [2, TOK, 128]); vout = do("vout", [2, TOK, 128])
    sout = do("sout", [2, 6, 2, 4, 64, 64])

    st = ExitStack()
    sbt = lambda n, s, dt: st.enter_context(nc.sbuf_tensor("sb_" + n, s, dt))
    xT = sbt("xT", [128, 8, TOK], F32)
    hT = sbt("hT", [128, 8, TOK], BF16)
    par = sbt("par", [128, NPAR], F32)
    cf = sbt("cf", [128, NCF], F32)
    cb = sbt("cb", [128, NCB], BF16)
    modv = sbt("modv", [128, 2, 72, 2], F32)
    mA = sbt("mA", [128, 2, 3, 8, 2], F32)
    mG = sbt("mG", [128, 2, 3, 8, 2], F32)
    sml = sbt("sml", [128, 64], F32)
    slots = [sbt(f"slot{i}", [128, SLOT], BF16) for i in range(NSL)]
    arena = sbt("arena", [128, ARENA], F32)
    psum = [st.enter_context(nc.psum_tensor(f"ps{i}", [128, 512], F32)) for i in range(8)]
    bps = [Buf(f"ps{i}") for i in range(8)]
    for b_ in bps:
        b_.excl = True
    bslot = [Buf(f"slot{i}") for i in range(NSL)]
    bx = [[Buf(f"x{c}_{g}") for g in range(NG)] for c in range(8)]
    bh = [[Buf(f"h{c}_{g}") for g in range(NG)] for c in range(8)]
    bpar, bcf, bcb, bmod = Buf("par"), Buf("cf"), Buf("cb"), Buf("mod")
    bsml = Buf("sml")

    state = {"ps": 0, "aoff": 0}

    def nps():
        i = state["ps"] % 8
        state["ps"] += 1
        return psum[i], bps[i]

    def aalloc(n32):
        o = state["aoff"]
        state["aoff"] += n32
        assert state["aoff"] <= ARENA, state["aoff"]
        return arena[:, o:o + n32]

    def areset():
        state["aoff"] = 0

    def nb(name):
        return Buf(name, T.fence())

    PE = lambda fn, r, w: T.op("pe", fn, r, w)
    ACT = lambda fn, r, w: T.op("act", fn, r, w)
    DVE = lambda fn, r, w: T.op("dve", fn, r, w)
    POOL = lambda fn, r, w: T.op("pool", fn, r, w)

    def DMA(eng, out, in_, buf, reads=(), writes=()):
        T.op(eng, lambda q: q.dma_start(out=out, in_=in_), reads, writes, dma_buf=buf)

    wq = []
    wstate = {"loaded": 0, "used": 0}

    def witem(parts):
        wq.append(parts)

    def wnext(keep_prev=False):
        i = wstate["used"]
        oldest = i - 1 if keep_prev else i
        while wstate["loaded"] < min(len(wq), oldest + NSL):
            k = wstate["loaded"]
            s = k % NSL
            for (fn, dap) in wq[k]:
                DMA("pool", fn(slots[s]), dap, bslot[s], writes=[bslot[s]])
            wstate["loaded"] += 1
        wstate["used"] += 1
        return slots[i % NSL], bslot[i % NSL]

    def kview(ap2d):
        return ap2d.rearrange("(c p) n -> p c n", p=128)

    def sched_weights():
        for l in range(2):
            for i in range(18):
                witem([(lambda s: s[:, 0:4096].rearrange("p (c n) -> p c n", c=8), kview(wada_d[l][:, i * 512:(i + 1) * 512]))])
        for l in range(2):
            for f in range(2):
                sched_ffn(l, f)
                if f == 0:
                    sched_mixer(l)

    def sched_ffn(l, f):
        for h in range(2):
            for j0 in range(0, 11, 2):
                nf = min(2, 11 - j0)
                c0 = (h * 11 + j0) * 128
                witem([
                    (lambda s, nf=nf: s[:, 0:8 * nf * 128].rearrange("p (c n) -> p c n", c=8), kview(wgu_d[f][l][:, c0:c0 + nf * 128])),
                    (lambda s, nf=nf: s[:, 2048:2048 + 8 * nf * 128].rearrange("p (c n) -> p c n", c=8), kview(wgu_d[f][l][:, DFF + c0:DFF + c0 + nf * 128])),
                ])

    def sched_mixer(l):
        w = win_d[l]
        witem([(lambda s: s[:, 0:2048].rearrange("p (c n) -> p c n", c=8), kview(w[:, 0:256]))])
        for hf in range(2):
            for cs in range(2):
                witem([(lambda s: s[:, 0:4096].rearrange("p (c n) -> p c n", c=8), kview(dftl_d[cs][:, hf * 512:(hf + 1) * 512]))])
        witem([(lambda s, cs=cs: s[:, cs * 512:(cs + 1) * 512].rearrange("p (c n) -> p c n", c=2), kview(dftp_d[cs])) for cs in range(2)])
        for pr in range(2):
            cols = [256 + pr * 128, 512 + pr * 128, 768 + pr * 128, 1024 + pr * 128, 1280 + pr * 128]
            witem([(lambda s, k=k: s[:, k * 1024:(k + 1) * 1024].rearrange("p (c n) -> p c n", c=8), kview(w[:, cols[k]:cols[k] + 128])) for k in range(3)])
            witem([(lambda s, k=k: s[:, k * 1024:(k + 1) * 1024].rearrange("p (c n) -> p c n", c=8), kview(w[:, cols[3 + k]:cols[3 + k] + 128])) for k in range(2)])
        witem([(lambda s: s[:, 0:4096].rearrange("p (c n) -> p c n", c=8), kview(w[:, 1536:2048]))])
        parts = []
        for kv in range(2):
            for dup in range(2):
                parts.append((lambda s, kv=kv, dup=dup: s[:, kv * 1024:(kv + 1) * 1024].rearrange("p (c n) -> p c n", c=8)[:, :, dup * 64:(dup + 1) * 64],
                              kview(w[:, 2048 + kv * 64:2048 + (kv + 1) * 64])))
        parts.append((lambda s: s[:, 2048:3072].rearrange("p (c n) -> p c n", c=8), kview(w[:, 2176:2304])))
        witem(parts)
        for i in range(2):
            witem([(lambda s: s[:, 0:4096].rearrange("p (c n) -> p c n", c=8), kview(wout_d[l][:, i * 512:(i + 1) * 512]))])

    sched_weights()

    DMA("sp", par[:], par_d, bpar, writes=[bpar])
    DMA("sp", cf[:], cf_d, bcf, writes=[bcf])
    DMA("pool", cb[:], cb_d, bcb, writes=[bcb])
    ident = cf[:, C_ID:C_ID + 128]
    ones_b = cb[:, B_ONES:B_ONES + 128]
    blk_b = cb[:, B_BLK:B_BLK + 128]
    ident_b = cb[:, B_ID:B_ID + 128]

    areset()
    xst = [aalloc(1024), aalloc(1024)]
    bxst = [nb("xst0"), nb("xst1")]
    for t in range(NT):
        s = t % 2
        DMA("sp", xst[s], xin[t * 128:(t + 1) * 128, :], bxst[s], writes=[bxst[s]])
        for half in range(2):
            ps, bp = nps()
            for k in range(4):
                c = half * 4 + k
                PE(lambda q, ps=ps, k=k, c=c, s=s: q.transpose(ps[:, k * 128:(k + 1) * 128], xst[s][:, c * 128:(c + 1) * 128], ident),
                   [bxst[s], bcf], [bp])
            g = t // 4
            eng = ACT if half == 0 else DVE
            fn = (lambda q, ps=ps, half=half, t=t: q.activation(out=xT[:, half * 4:half * 4 + 4, t * 128:(t + 1) * 128], in_=ps[:].rearrange("p (k n) -> p k n", k=4), func=AF.Copy)) if half == 0 else \
                 (lambda q, ps=ps, half=half, t=t: q.tensor_copy(out=xT[:, half * 4:half * 4 + 4, t * 128:(t + 1) * 128], in_=ps[:].rearrange("p (k n) -> p k n", k=4)))
            eng(fn, [bp], [bx[c][g] for c in range(half * 4, half * 4 + 4)])

    areset()
    sc = aalloc(16)
    scb = aalloc(8).bitcast(BF16)
    bsc = nb("sc")
    ACT(lambda q: q.activation(out=sc, in_=par[:, P_COND:P_COND + 16], func=AF.Exp, scale=-1.0), [bpar], [bsc])
    DVE(lambda q: q.tensor_scalar_add(out=sc, in0=sc, scalar1=1.0), [bsc], [bsc])
    DVE(lambda q: q.reciprocal(out=sc, in_=sc), [bsc], [bsc])
    DVE(lambda q: q.tensor_tensor(out=scb, in0=sc, in1=par[:, P_COND:P_COND + 16], op=ALU.mult), [bsc, bpar], [bsc])
    scb3 = scb.rearrange("p (c r) -> p c r", c=8)
    for l in range(2):
        ps, bp = nps()
        for i in range(18):
            sl, bs = wnext()
            wv = sl[:, 0:4096].rearrange("p (c n) -> p c n", c=8)
            for k in range(4):
                nn = i * 4 + k
                for c in range(8):
                    PE(lambda q, ps=ps, wv=wv, k=k, c=c, nn=nn: q.matmul(ps[:, nn * 2:nn * 2 + 2], lhsT=wv[:, c, k * 128:(k + 1) * 128], rhs=scb3[:, c, :], start=(c == 0), stop=(c == 7)),
                       [bs, bsc], [bp])
        DVE(lambda q, ps=ps, l=l: q.tensor_tensor(out=modv[:, l, :, :], in0=ps[:, 0:144].rearrange("p (n r) -> p n r", r=2),
                                                 in1=par[:, P_BADA + l * 72:P_BADA + (l + 1) * 72].unsqueeze(2).to_broadcast([128, 72, 2]), op=ALU.add),
            [bp, bpar], [bmod])
        for w3 in range(3):
            nrm = par[:, P_NRM + l * 24 + w3 * 8:P_NRM + l * 24 + w3 * 8 + 8].unsqueeze(2).to_broadcast([128, 8, 2])
            DVE(lambda q, l=l, w3=w3, nrm=nrm: q.scalar_tensor_tensor(out=mA[:, l, w3, :, :], in0=modv[:, l, (3 * w3 + 1) * 8:(3 * w3 + 2) * 8, :], scalar=1.0, in1=nrm, op0=ALU.add, op1=ALU.mult),
                [bmod, bpar], [bmod])
            DVE(lambda q, l=l, w3=w3: q.tensor_scalar_mul(out=mG[:, l, w3, :, :], in0=modv[:, l, (3 * w3 + 2) * 8:(3 * w3 + 3) * 8, :], scalar1=(1.0 if w3 == 1 else 0.5)),
                [bmod], [bmod])

    def mod_A(l, w3, c, g):
        r = 0 if g < 2 else 1
        return mA[:, l, w3, c, r:r + 1]

    def mod_B(l, w3, c, g):
        r = 0 if g < 2 else 1
        return modv[:, l, (3 * w3) * 8 + c, r:r + 1]

    def mod_G(l, w3, c, g):
        r = 0 if g < 2 else 1
        return mG[:, l, w3, c, r:r + 1]

    def norm_mod(l, w3):
        sq = [aalloc(256).bitcast(BF16), aalloc(256).bitcast(BF16)]
        bsq = [nb("sq0"), nb("sq1")]
        tmp = [aalloc(512), aalloc(512)]
        btmp = [nb("tmp0"), nb("tmp1")]
        rs = aalloc(512)
        brs = nb("rs")
        for g in range(NG):
            ps, bp = nps()
            for c in range(8):
                s = c % 2
                ACT(lambda q, s=s, c=c, g=g: q.activation(out=sq[s], in_=xT[:, c, g * 512:(g + 1) * 512], func=AF.Square), [bx[c][g]], [bsq[s]])
                PE(lambda q, ps=ps, s=s, c=c: q.matmul(ps[:], lhsT=ones_b, rhs=sq[s], start=(c == 0), stop=(c == 7)), [bsq[s], bcb], [bp])
            DVE(lambda q, ps=ps: q.tensor_scalar(out=rs, in0=ps[:], scalar1=1.0 / D, scalar2=EPS, op0=ALU.mult, op1=ALU.add), [bp], [brs])
            ACT(lambda q: q.activation(out=rs, in_=rs, func=AF.Ln), [brs], [brs])
            ACT(lambda q: q.activation(out=rs, in_=rs, func=AF.Exp, scale=-0.5), [brs], [brs])
            for c in range(8):
                s = c % 2
                DVE(lambda q, s=s, c=c, g=g: q.tensor_tensor(out=tmp[s], in0=xT[:, c, g * 512:(g + 1) * 512], in1=rs, op=ALU.mult), [bx[c][g], brs], [btmp[s]])
                ACT(lambda q, s=s, c=c, g=g: q.activation(out=hT[:, c, g * 512:(g + 1) * 512], in_=tmp[s], func=AF.Identity,
                                                          scale=mod_A(l, w3, c, g), bias=mod_B(l, w3, c, g)), [btmp[s], bmod], [bh[c][g]])

    def ffn(l, f):
        w3 = 0 if f == 0 else 2
        areset()
        norm_mod(l, w3)
        actT = aalloc(11 * TOK // 2).bitcast(BF16).rearrange("p (j n) -> p j n", j=11)
        bact = [nb(f"act{g}") for g in range(NG)]
        wd = aalloc(11 * 1024 // 2).bitcast(BF16).rearrange("p (j n) -> p j n", j=11)
        bwd = nb("wd")
        ebuf = [aalloc(512), aalloc(512)]
        bebuf = [nb("e0"), nb("e1")]
        tbuf = [aalloc(512), aalloc(512)]
        btbuf = [nb("t0"), nb("t1")]
        k = 0
        for h in range(2):
            DMA("pool", wd, wdn_d[f][l][h * 1408:(h + 1) * 1408, :].rearrange("(j p) n -> p j n", p=128), bwd, writes=[bwd])
            for j0 in range(0, 11, 2):
                nf = min(2, 11 - j0)
                sl, bs = wnext()
                wg = sl[:, 0:8 * nf * 128].rearrange("p (c n) -> p c n", c=8)
                wu = sl[:, 2048:2048 + 8 * nf * 128].rearrange("p (c n) -> p c n", c=8)
                for jj in range(nf):
                    j = j0 + jj
                    for g in range(NG):
                        pg, bpg = nps()
                        pu, bpu = nps()
                        for c in range(8):
                            PE(lambda q, pg=pg, wg=wg, c=c, jj=jj, g=g: q.matmul(pg[:], lhsT=wg[:, c, jj * 128:(jj + 1) * 128], rhs=hT[:, c, g * 512:(g + 1) * 512], start=(c == 0), stop=(c == 7)),
                               [bs, bh[c][g]], [bpg])
                        for c in range(8):
                            PE(lambda q, pu=pu, wu=wu, c=c, jj=jj, g=g: q.matmul(pu[:], lhsT=wu[:, c, jj * 128:(jj + 1) * 128], rhs=hT[:, c, g * 512:(g + 1) * 512], start=(c == 0), stop=(c == 7)),
                               [bs, bh[c][g]], [bpu])
                        s = k % 2
                        k += 1
                        ACT(lambda q, pg=pg, s=s: q.activation(out=ebuf[s], in_=pg[:], func=AF.Exp, scale=-1.0), [bpg], [bebuf[s]])
                        DVE(lambda q, s=s: q.tensor_scalar_add(out=ebuf[s], in0=ebuf[s], scalar1=1.0), [bebuf[s]], [bebuf[s]])
                        DVE(lambda q, s=s: q.reciprocal(out=ebuf[s], in_=ebuf[s]), [bebuf[s]], [bebuf[s]])
                        DVE(lambda q, pg=pg, s=s: q.tensor_tensor(out=tbuf[s], in0=pg[:], in1=ebuf[s], op=ALU.mult), [bpg, bebuf[s]], [btbuf[s]])
                        DVE(lambda q, pu=pu, s=s, j=j, g=g: q.tensor_tensor(out=actT[:, j, g * 512:(g + 1) * 512], in0=pu[:], in1=tbuf[s], op=ALU.mult), [bpu, btbuf[s]], [bact[g]])
            for g in range(NG):
                for dc in range(8):
                    ps, bp = nps()
                    for j in range(11):
                        PE(lambda q, ps=ps, j=j, dc=dc, g=g: q.matmul(ps[:], lhsT=wd[:, j, dc * 128:(dc + 1) * 128], rhs=actT[:, j, g * 512:(g + 1) * 512], start=(j == 0), stop=(j == 10)),
                           [bwd, bact[g]], [bp])
                    DVE(lambda q, ps=ps, dc=dc, g=g: q.scalar_tensor_tensor(out=xT[:, dc, g * 512:(g + 1) * 512], in0=ps[:], scalar=mod_G(l, w3, dc, g), in1=xT[:, dc, g * 512:(g + 1) * 512], op0=ALU.mult, op1=ALU.add),
                        [bp, bmod, bx[dc][g]], [bx[dc][g]])

    def mixer(l):
        areset()
        norm_mod(l, 1)
        areset()
        catT = aalloc(8 * TOK // 2).bitcast(BF16).rearrange("p (c n) -> p c n", c=8)
        bcat = [[nb(f"cat{c}_{g}") for g in range(NG)] for c in range(8)]
        base_off = state["aoff"]

        def inproj(wv, k, g, ps):
            for c in range(8):
                PE(lambda q, c=c: q.matmul(ps[0][:], lhsT=wv[:, c, k * 128:(k + 1) * 128], rhs=hT[:, c, g * 512:(g + 1) * 512], start=(c == 0), stop=(c == 7)),
                   [ps[2], bh[c][g]], [ps[1]])

        uT = aalloc(2 * TOK // 2).bitcast(BF16).rearrange("p (c n) -> p c n", c=2)
        buT = nb("uT")
        ucs = aalloc(NT * 2 * 256 // 2).bitcast(BF16).rearrange("p (t c n) -> p t c n", t=NT, c=2)
        bucs = nb("ucs")
        sl, bs = wnext()
        wv = sl[:, 0:2048].rearrange("p (c n) -> p c n", c=8)
        for k in range(2):
            for g in range(NG):
                ps, bp = nps()
                inproj(wv, k, g, (ps, bp, bs))
                ACT(lambda q, ps=ps, k=k, g=g: q.activation(out=uT[:, k, g * 512:(g + 1) * 512], in_=ps[:], func=AF.Copy), [bp], [buT])
        dft64 = cb[:, B_DFT:B_DFT + 256]
        for t in range(NT):
            ps, bp = nps()
            for k in range(2):
                PE(lambda q, ps=ps, k=k, t=t: q.matmul(ps[:, k * 256:(k + 1) * 256], lhsT=uT[:, k, t * 128:(t + 1) * 128], rhs=dft64, start=True, stop=True), [buT, bcb], [bp])
            DVE(lambda q, ps=ps, t=t: q.tensor_copy(out=ucs[:, t, :, :], in_=ps[:].rearrange("p (c n) -> p c n", c=2)), [bp], [bucs])
        for hf in range(2):
            slc, bsc_ = wnext()
            sls, bss = wnext(keep_prev=True)
            cl = slc[:, 0:4096].rearrange("p (c n) -> p c n", c=8)
            sn = sls[:, 0:4096].rearrange("p (c n) -> p c n", c=8)
            for k in range(2):
                ps, bp = nps()
                for lc in range(8):
                    PE(lambda q, ps=ps, k=k, lc=lc, cl=cl: q.matmul(ps[:], lhsT=ucs[:, lc, k, 0:128], rhs=cl[:, lc, :], start=(lc == 0), stop=False), [bucs, bsc_], [bp])
                    PE(lambda q, ps=ps, k=k, lc=lc, sn=sn: q.matmul(ps[:], lhsT=ucs[:, lc, k, 128:256], rhs=sn[:, lc, :], start=False, stop=(lc == 7)), [bucs, bss], [bp])
                ACT(lambda q, ps=ps, k=k, hf=hf: q.activation(out=catT[:, k, hf * 512:(hf + 1) * 512], in_=ps[:], func=AF.Copy), [bp], [bcat[k][hf]])
        slp, bsp = wnext()
        pc = slp[:, 0:512].rearrange("p (c n) -> p c n", c=2)
        pn = slp[:, 512:1024].rearrange("p (c n) -> p c n", c=2)
        for k in range(2):
            ps, bp = nps()
            for sq_ in range(2):
                for lc in range(2):
                    tt = 8 + sq_ * 2 + lc
                    PE(lambda q, ps=ps, k=k, lc=lc, tt=tt, sq_=sq_: q.matmul(ps[:, sq_ * 256:(sq_ + 1) * 256], lhsT=ucs[:, tt, k, 0:128], rhs=pc[:, lc, :], start=(lc == 0), stop=False), [bucs, bsp], [bp])
                    PE(lambda q, ps=ps, k=k, lc=lc, tt=tt, sq_=sq_: q.matmul(ps[:, sq_ * 256:(sq_ + 1) * 256], lhsT=ucs[:, tt, k, 128:256], rhs=pn[:, lc, :], start=False, stop=(lc == 1)), [bucs, bsp], [bp])
            ACT(lambda q, ps=ps, k=k: q.activation(out=catT[:, k, 1024:1536], in_=ps[:], func=AF.Copy), [bp], [bcat[k][2]])

        if ENABLE_HGRN:
            for pr in range(2):
                if DEBUG_PAIR is not None and pr != DEBUG_PAIR:
                    wnext(); wnext()
                    for g in range(NG):
                        POOL(lambda q, pr=pr, g=g: q.memset(catT[:, 2 + pr, g * 512:(g + 1) * 512], 0.0), [], [bcat[2 + pr][g]])
                    continue
                state["aoff"] = base_off
                qf, sF, sB, tmpb, kb, hoT = [aalloc(TOK) for _ in range(6)]
                bqf, bsF, bsB, btmpb, bkb, bho = [nb(n) for n in ("qf", "sF", "sB", "tmpb", "kb", "hoT")]
                qt = [aalloc(TOK // 2).bitcast(BF16) for _ in range(2)]
                kt_ = [aalloc(TOK // 2).bitcast(BF16) for _ in range(2)]
                vT = aalloc(TOK // 2).bitcast(BF16)
                sgT = aalloc(TOK // 2).bitcast(BF16)
                bqt, bkt = [nb("qt0"), nb("qt1")], [nb("kt0"), nb("kt1")]
                bvT, bsg = nb("vT"), nb("sgT")
                ktp = [aalloc(128).bitcast(BF16).rearrange("p (a n) -> p a n", a=2) for _ in range(2)]
                vpd = [aalloc(128).bitcast(BF16).rearrange("p (a n) -> p a n", a=2) for _ in range(2)]
                bktp, bvpd = [nb("ktp0"), nb("ktp1")], [nb("vpd0"), nb("vpd1")]
                atb = [aalloc(64).bitcast(BF16) for _ in range(2)]
                batb = [nb("at0"), nb("at1")]
                S = aalloc(128); tS = aalloc(128); S0m = aalloc(64).bitcast(BF16)
                bS, btS, bS0m = nb("S"), nb("tS"), nb("S0m")
                sst = [aalloc(128), aalloc(128)]
                bsst = [nb("sst0"), nb("sst1")]
                scl = [[aalloc(48) for _ in range(4)] for _ in range(2)]
                bscl = [nb("scl0"), nb("scl1")]
                lbp = aalloc(8)
                blbp = nb("lbp")
                for i in range(2):
                    POOL(lambda q, i=i: q.memset(ktp[i], 0.0), [], [bktp[i]])
                    POOL(lambda q, i=i: q.memset(vpd[i], 0.0), [], [bvpd[i]])
                for dr in range(2):
                    c0 = dr * 3
                    if l == 0:
                        POOL(lambda q, c0=c0: q.memset(lbp[:, c0:c0 + 1], 1.0), [], [blbp])
                        POOL(lambda q, c0=c0: q.memset(lbp[:, c0 + 1:c0 + 2], 1e-30), [], [blbp])
                        POOL(lambda q, c0=c0: q.memset(lbp[:, c0 + 2:c0 + 3], -1.0), [], [blbp])
                    else:
                        a0 = P_LBS + 0 * 4 + dr * 2 + pr
                        a1 = P_LBS + 1 * 4 + dr * 2 + pr
                        DVE(lambda q, c0=c0, a0=a0, a1=a1: q.tensor_tensor(out=lbp[:, c0 + 1:c0 + 2], in0=par[:, a0:a0 + 1], in1=par[:, a1:a1 + 1], op=ALU.subtract), [bpar], [blbp])
                        ACT(lambda q, c0=c0: q.activation(out=lbp[:, c0 + 1:c0 + 2], in_=lbp[:, c0 + 1:c0 + 2], func=AF.Exp), [blbp], [blbp])
                        DVE(lambda q, c0=c0: q.tensor_scalar_add(out=lbp[:, c0 + 1:c0 + 2], in0=lbp[:, c0 + 1:c0 + 2], scalar1=1.0), [blbp], [blbp])
                        DVE(lambda q, c0=c0: q.reciprocal(out=lbp[:, c0 + 1:c0 + 2], in_=lbp[:, c0 + 1:c0 + 2]), [blbp], [blbp])
                        DVE(lambda q, c0=c0: q.tensor_scalar(out=lbp[:, c0:c0 + 1], in0=lbp[:, c0 + 1:c0 + 2], scalar1=-1.0, scalar2=1.0, op0=ALU.mult, op1=ALU.add), [blbp], [blbp])
                        DVE(lambda q, c0=c0: q.tensor_scalar_mul(out=lbp[:, c0 + 2:c0 + 3], in0=lbp[:, c0:c0 + 1], scalar1=-1.0), [blbp], [blbp])
                        DVE(lambda q, c0=c0: q.tensor_scalar_max(out=lbp[:, c0 + 1:c0 + 2], in0=lbp[:, c0 + 1:c0 + 2], scalar1=1e-30), [blbp], [blbp])
                sl, bs = wnext()
                for k in range(3):
                    wv = sl[:, k * 1024:(k + 1) * 1024].rearrange("p (c n) -> p c n", c=8)
                    for g in range(NG):
                        ps, bp = nps()
                        inproj(wv, 0, g, (ps, bp, bs))
                        gs = slice(g * 512, (g + 1) * 512)
                        if k == 0:
                            ACT(lambda q, ps=ps, gs=gs: q.activation(out=qf[:, gs], in_=ps[:], func=AF.Copy, scale=0.125), [bp], [bqf])
                        elif k == 1:
                            ACT(lambda q, ps=ps, gs=gs: q.activation(out=vT[:, gs], in_=ps[:], func=AF.Copy), [bp], [bvT])
                        else:
                            ACT(lambda q, ps=ps, gs=gs: q.activation(out=sF[:, gs], in_=ps[:], func=AF.Exp, scale=-1.0), [bp], [bsF])
                sl, bs = wnext()
                for k in range(2):
                    wv = sl[:, k * 1024:(k + 1) * 1024].rearrange("p (c n) -> p c n", c=8)
                    for g in range(NG):
                        ps, bp = nps()
                        inproj(wv, 0, g, (ps, bp, bs))
                        gs = slice(g * 512, (g + 1) * 512)
                        if k == 0:
                            ACT(lambda q, ps=ps, gs=gs: q.activation(out=sB[:, gs], in_=ps[:], func=AF.Exp, scale=-1.0), [bp], [bsB])
                        else:
                            ACT(lambda q, ps=ps, gs=gs: q.activation(out=tmpb[:, gs], in_=ps[:], func=AF.Exp, scale=-1.0), [bp], [btmpb])
                            DVE(lambda q, gs=gs: q.tensor_scalar_add(out=tmpb[:, gs], in0=tmpb[:, gs], scalar1=1.0), [btmpb], [btmpb])
                            DVE(lambda q, gs=gs: q.reciprocal(out=tmpb[:, gs], in_=tmpb[:, gs]), [btmpb], [btmpb])
                            DVE(lambda q, ps=ps, gs=gs: q.tensor_tensor(out=sgT[:, gs], in0=ps[:], in1=tmpb[:, gs], op=ALU.mult), [bp, btmpb], [bsg])
                for (sx, bsx) in ((sF, bsF), (sB, bsB)):
                    DVE(lambda q, sx=sx: q.tensor_scalar_add(out=sx, in0=sx, scalar1=1.0), [bsx], [bsx])
                    DVE(lambda q, sx=sx: q.reciprocal(out=sx, in_=sx), [bsx], [bsx])
                CH = 32
                NCH = TOK // CH
                SPC = 256 // CH
                for dr, (sx, bsx) in enumerate(((sF, bsF), (sB, bsB))):
                    c0 = dr * 3
                    DVE(lambda q, sx=sx, c0=c0: q.tensor_scalar(out=kb, in0=sx, scalar1=lbp[:, c0 + 2:c0 + 3], scalar2=lbp[:, c0:c0 + 1], op0=ALU.mult, op1=ALU.add), [bsx, blbp], [bkb])
                    DVE(lambda q, sx=sx, c0=c0: q.tensor_scalar(out=sx, in0=sx, scalar1=lbp[:, c0:c0 + 1], scalar2=lbp[:, c0 + 1:c0 + 2], op0=ALU.mult, op1=ALU.add), [bsx, blbp], [bsx])
                    ACT(lambda q, sx=sx: q.activation(out=sx, in_=sx, func=AF.Ln), [bsx], [bsx])
                    A_, bA_, B_, bB_ = sx, bsx, tmpb, btmpb
                    sh = 1
                    while sh < CH:
                        A3 = A_.rearrange("p (c n) -> p c n", n=CH)
                        B3 = B_.rearrange("p (c n) -> p c n", n=CH)
                        if dr == 0:
                            ACT(lambda q, A3=A3, B3=B3, sh=sh: q.activation(out=B3[:, :, 0:sh], in_=A3[:, :, 0:sh], func=AF.Copy), [bA_], [bB_])
                            DVE(lambda q, A3=A3, B3=B3, sh=sh: q.tensor_tensor(out=B3[:, :, sh:CH], in0=A3[:, :, sh:CH], in1=A3[:, :, 0:CH - sh], op=ALU.add), [bA_], [bB_])
                        else:
                            ACT(lambda q, A3=A3, B3=B3, sh=sh: q.activation(out=B3[:, :, CH - sh:CH], in_=A3[:, :, CH - sh:CH], func=AF.Copy), [bA_], [bB_])
                            DVE(lambda q, A3=A3, B3=B3, sh=sh: q.tensor_tensor(out=B3[:, :, 0:CH - sh], in0=A3[:, :, 0:CH - sh], in1=A3[:, :, sh:CH], op=ALU.add), [bA_], [bB_])
                        A_, bA_, B_, bB_ = B_, bB_, A_, bA_
                        sh *= 2
                    bq_, bbq_, tq_, btq_ = A_, bA_, B_, bB_
                    b3 = bq_.rearrange("p (c n) -> p c n", n=CH)
                    mid, Ti = (CH // 2 - 1, CH - 1) if dr == 0 else (CH // 2, 0)
                    bm, em, eT, eTm = scl[dr]
                    DVE(lambda q, b3=b3, mid=mid, bm=bm: q.tensor_copy(out=bm, in_=b3[:, :, mid]), [bbq_], [bscl[dr]])
                    ACT(lambda q, bm=bm, em=em: q.activation(out=em, in_=bm, func=AF.Exp), [bscl[dr]], [bscl[dr]])
                    ACT(lambda q, b3=b3, Ti=Ti, eT=eT: q.activation(out=eT, in_=b3[:, :, Ti], func=AF.Exp), [bbq_], [bscl[dr]])
                    DVE(lambda q, b3=b3, Ti=Ti, bm=bm, eTm=eTm: q.tensor_tensor(out=eTm, in0=b3[:, :, Ti], in1=bm, op=ALU.subtract), [bbq_, bscl[dr]], [bscl[dr]])
                    ACT(lambda q, eTm=eTm: q.activation(out=eTm, in_=eTm, func=AF.Exp), [bscl[dr]], [bscl[dr]])
                    DVE(lambda q, b3=b3, bm=bm: q.tensor_tensor(out=b3, in0=b3, in1=bm.unsqueeze(2).to_broadcast([128, NCH, CH]), op=ALU.subtract), [bbq_, bscl[dr]], [bbq_])
                    ACT(lambda q, bq_=bq_, tq_=tq_: q.activation(out=tq_, in_=bq_, func=AF.Exp), [bbq_], [btq_])
                    DVE(lambda q, dr=dr, tq_=tq_: q.tensor_tensor(out=qt[dr], in0=qf, in1=tq_, op=ALU.mult), [bqf, btq_], [bqt[dr]])
                    ACT(lambda q, bq_=bq_, tq_=tq_: q.activation(out=tq_, in_=bq_, func=AF.Exp, scale=-1.0), [bbq_], [btq_])
                    DVE(lambda q, dr=dr, tq_=tq_: q.tensor_tensor(out=kt_[dr], in0=kb, in1=tq_, op=ALU.mult), [bkb, btq_], [bkt[dr]])
                ev = [0]

                def emit_state(slot, dr):
                    i = ev[0] % 2
                    ev[0] += 1
                    ACT(lambda q, i=i: q.activation(out=sst[i], in_=S, func=AF.Copy), [bS], [bsst[i]])
                    for hd in range(2):
                        DMA("sp", sout[l, slot, dr, 2 * pr + hd], sst[i][hd * 64:(hd + 1) * 64, hd * 64:(hd + 1) * 64], bsst[i], reads=[bsst[i]])

                def load_state(dr):
                    POOL(lambda q: q.memset(S, 0.0), [bS], [bS])
                    for hd in range(2):
                        DMA("sp", S[hd * 64:(hd + 1) * 64, hd * 64:(hd + 1) * 64], s0_d[l, dr, 2 * pr + hd], bS, reads=[bS], writes=[bS])

                Xb = [aalloc(128), aalloc(128)]
                bXb = [nb("X0"), nb("X1")]
                for dr in range(2):
                    bm, em, eT, eTm = scl[dr]
                    order = list(range(NCH)) if dr == 0 else list(range(NCH - 1, -1, -1))
                    mcol = C_MF if dr == 0 else C_MB
                    pst = {}

                    def prep(n_, dr=dr, eTm=eTm, mcol=mcol, order=order, pst=pst):
                        ch = order[n_]
                        i = n_ % 2
                        cs = slice(ch * CH, (ch + 1) * CH)
                        ptk, bptk = nps()
                        ptkb = ptk[:].bitcast(BF16)
                        PE(lambda q: q.transpose(ptkb[0:CH, 0:128], kt_[dr][:, cs], ident_b), [bkt[dr], bcb], [bptk])
                        PE(lambda q: q.transpose(ptkb[0:CH, 128:256], vT[:, cs], ident_b), [bvT, bcb], [bptk])
                        ACT(lambda q: q.activation(out=ktp[i][0:CH, 0, 0:64], in_=ptkb[0:CH, 0:64], func=AF.Copy), [bptk], [bktp[i]])
                        ACT(lambda q: q.activation(out=ktp[i][0:CH, 1, 64:128], in_=ptkb[0:CH, 64:128], func=AF.Copy), [bptk], [bktp[i]])
                        ACT(lambda q: q.activation(out=vpd[i][0:CH, 0, 0:64], in_=ptkb[0:CH, 128:192], func=AF.Copy), [bptk], [bvpd[i]])
                        ACT(lambda q: q.activation(out=vpd[i][0:CH, 1, 64:128], in_=ptkb[0:CH, 192:256], func=AF.Copy), [bptk], [bvpd[i]])
                        for hd in range(2):
                            hs = slice(hd * 64, (hd + 1) * 64)
                            pA, bpA = nps()
                            PE(lambda q, pA=pA, hs=hs: q.matmul(pA[0:CH, 0:CH], lhsT=kt_[dr][hs, cs], rhs=qt[dr][hs, cs], start=True, stop=True), [bkt[dr], bqt[dr]], [bpA])
                            DVE(lambda q, pA=pA, hd=hd: q.tensor_tensor(out=atb[i][0:CH, hd * CH:(hd + 1) * CH], in0=pA[0:CH, 0:CH], in1=cf[0:CH, mcol:mcol + CH], op=ALU.mult), [bpA, bcf], [batb[i]])
                        pS2, bpS2 = nps()
                        PE(lambda q: q.matmul(pS2[:, 0:128], lhsT=ktp[i][0:CH, 0, :], rhs=vpd[i][0:CH, 0, :], start=True, stop=False), [bktp[i], bvpd[i]], [bpS2])
                        PE(lambda q: q.matmul(pS2[:, 0:128], lhsT=ktp[i][0:CH, 1, :], rhs=vpd[i][0:CH, 1, :], start=False, stop=True), [bktp[i], bvpd[i]], [bpS2])
                        ACT(lambda q: q.activation(out=Xb[i], in_=pS2[:, 0:128], func=AF.Copy, scale=eTm[:, ch:ch + 1]), [bpS2, bscl[dr]], [bXb[i]])

                    def chain(n_, dr=dr, em=em, eT=eT, order=order):
                        ch = order[n_]
                        i = n_ % 2
                        cs = slice(ch * CH, (ch + 1) * CH)
                        first_of_seq = (ch % SPC == 0) if dr == 0 else (ch % SPC == SPC - 1)
                        if first_of_seq:
                            seq = ch // SPC
                            prev = seq - 1 if dr == 0 else seq + 1
                            if n_ > 0:
                                emit_state(prev, dr)
                            if (dr == 0 and seq == 0) or (dr == 1 and seq == 3):
                                load_state(dr)
                            elif seq >= 4:
                                POOL(lambda q: q.memset(S, 0.0), [bS], [bS])
                            else:
                                DVE(lambda q: q.tensor_scalar_mul(out=S, in0=S, scalar1=par[:, P_KEEP:P_KEEP + 1]), [bS, bpar], [bS])
                        DVE(lambda q: q.tensor_scalar_mul(out=S0m, in0=S, scalar1=em[:, ch:ch + 1]), [bS, bscl[dr]], [bS0m])
                        DVE(lambda q: q.scalar_tensor_tensor(out=S, in0=S, scalar=eT[:, ch:ch + 1], in1=Xb[i], op0=ALU.mult, op1=ALU.add), [bS, bscl[dr], bXb[i]], [bS])
                        po, bpo = nps()
                        PE(lambda q: q.matmul(po[:, 0:CH], lhsT=vpd[i][0:CH, 0, :], rhs=atb[i][0:CH, 0:CH], start=True, stop=False), [bvpd[i], batb[i]], [bpo])
                        PE(lambda q: q.matmul(po[:, 0:CH], lhsT=vpd[i][0:CH, 1, :], rhs=atb[i][0:CH, CH:2 * CH], start=False, stop=False), [bvpd[i], batb[i]], [bpo])
                        PE(lambda q: q.matmul(po[:, 0:CH], lhsT=S0m, rhs=qt[dr][:, cs], start=False, stop=True), [bS0m, bqt[dr]], [bpo])
                        if dr == 0:
                            ACT(lambda q: q.activation(out=hoT[:, cs], in_=po[:, 0:CH], func=AF.Copy), [bpo], [bho])
                        else:
                            DVE(lambda q: q.tensor_tensor(out=hoT[:, cs], in0=po[:, 0:CH], in1=hoT[:, cs], op=ALU.add), [bpo, bho], [bho])

                    prep(0)
                    for n_ in range(NCH):
                        if n_ + 1 < NCH:
                            prep(n_ + 1)
                        chain(n_)
                    emit_state(0 if dr == 1 else 5, dr)
                for g in range(NG):
                    gs = slice(g * 512, (g + 1) * 512)
                    sqh = tmpb[:, 0:256].bitcast(BF16)
                    ACT(lambda q, gs=gs, sqh=sqh: q.activation(out=sqh, in_=hoT[:, gs], func=AF.Square), [bho], [btmpb])
                    p2, bp2 = nps()
                    PE(lambda q, p2=p2, sqh=sqh: q.matmul(p2[:], lhsT=blk_b, rhs=sqh, start=True, stop=True), [btmpb, bcb], [bp2])
                    rv = kb[:, 0:512]
                    DVE(lambda q, p2=p2, rv=rv: q.tensor_scalar(out=rv, in0=p2[:], scalar1=1.0 / 64, scalar2=EPS, op0=ALU.mult, op1=ALU.add), [bp2], [bkb])
                    ACT(lambda q, rv=rv: q.activation(out=rv, in_=rv, func=AF.Ln), [bkb], [bkb])
                    ACT(lambda q, rv=rv: q.activation(out=rv, in_=rv, func=AF.Exp, scale=-0.5), [bkb], [bkb])
                    DVE(lambda q, rv=rv, gs=gs, pr=pr: q.scalar_tensor_tensor(out=rv, in0=hoT[:, gs], scalar=par[:, P_HNRM + l * 2 + pr:P_HNRM + l * 2 + pr + 1], in1=rv, op0=ALU.mult, op1=ALU.mult), [bho, bkb, bpar], [bkb])
                    DVE(lambda q, rv=rv, gs=gs, g=g, pr=pr: q.tensor_tensor(out=catT[:, 2 + pr, gs], in0=rv, in1=sgT[:, gs], op=ALU.mult), [bkb, bsg], [bcat[2 + pr][g]])
        else:
            for pr in range(2):
                wnext()
                wnext()
                for g in range(NG):
                    POOL(lambda q, pr=pr, g=g: q.memset(catT[:, 2 + pr, g * 512:(g + 1) * 512], 0.0), [], [bcat[2 + pr][g]])
        if ENABLE_ATTN:
            state["aoff"] = base_off
            qT = aalloc(4 * TOK // 2).bitcast(BF16).rearrange("p (c n) -> p c n", c=4)
            kdT = aalloc(4 * TOK // 2).bitcast(BF16).rearrange("p (c v n) -> p c v n", c=2, v=2)
            kcT = aalloc(4 * 512 // 2).bitcast(BF16).rearrange("p (c v n) -> p c v n", c=2, v=2)
            vaug = aalloc(16 * 2 * 96 // 2).bitcast(BF16).rearrange("p (t k n) -> p t k n", t=16, k=2)
            kst = aalloc(NT * 128).rearrange("p (t n) -> p t n", t=NT)
            vst = aalloc(NT * 128).rearrange("p (t n) -> p t n", t=NT)
            kcd = aalloc(4 * 256).rearrange("p (t k n) -> p t k n", t=4, k=2)
            zq = [aalloc(512), aalloc(512)]
            rr = [aalloc(512), aalloc(512)]
            t1 = [aalloc(512)] * 2
            sqa = [aalloc(256).bitcast(BF16), aalloc(256).bitcast(BF16)]
            pT = [aalloc(256).bitcast(BF16) for _ in range(3)]
            otok = [aalloc(256).bitcast(BF16).rearrange("p (h n) -> p h n", h=8), aalloc(256).bitcast(BF16).rearrange("p (h n) -> p h n", h=8)]
            den = [aalloc(4), aalloc(4)]
            esink = aalloc(8)
            bq = [[nb("q") for g in range(NG)] for c in range(4)]
            bkd = [[nb("kd") for g in range(NG)] for c in range(2)]
            bkc, bva, bkst, bvst, bkcd, besk = nb("kc"), [nb("va") for t in range(16)], nb("kst"), nb("vst"), nb("kcd"), nb("esk")
            bzq, brr, bt1, bsqa, bpT, botok, bden = [[nb(n + str(i)) for i in range(3)] for n in ("zq", "rr", "t1", "sqa", "pT", "otok", "den")]
            bt1[1] = bt1[0]
            ACT(lambda q: q.activation(out=esink, in_=par[:, P_SINK + l * 8:P_SINK + l * 8 + 8], func=AF.Exp), [bpar], [besk])
            POOL(lambda q: q.memset(vaug[:, :, :, 64:66], 1.0), [], list(bva))
            allkd = [bkd[c_][g_] for c_ in range(2) for g_ in range(NG)]
            POOL(lambda q: q.memset(kdT[64:128, :, 0, :], 0.0), [], allkd)
            POOL(lambda q: q.memset(kdT[0:64, :, 1, :], 0.0), [], allkd)
            POOL(lambda q: q.memset(kcT[64:128, :, 0, :], 0.0), [], [bkc])
            POOL(lambda q: q.memset(kcT[0:64, :, 1, :], 0.0), [], [bkc])
            for kv in (range(2) if "cachek" not in SKIP else []):
                for dup in range(2):
                    DMA("sp", kcd[:, :, kv, dup * 64:(dup + 1) * 64], ck_d[l][:, kv * 64:(kv + 1) * 64].rearrange("(t p) d -> p t d", p=128), bkcd, writes=[bkcd])
            for t in (range(4) if "cachev" not in SKIP else []):
                DMA("pool", vaug[:, 12 + t, :, 0:64], cv_d[l][t * 128:(t + 1) * 128, :].rearrange("p (k d) -> p k d", k=2), bva[12 + t], writes=[bva[12 + t]])
            for kv in (range(2) if "cachek" not in SKIP else []):
                ps, bp = nps()
                for t in range(4):
                    PE(lambda q, ps=ps, t=t, kv=kv: q.transpose(ps[:, t * 128:(t + 1) * 128], kcd[:, t, kv, :], ident), [bkcd, bcf], [bp])
                ACT(lambda q, ps=ps, kv=kv: q.activation(out=kcT[0:64, kv, 0, :], in_=ps[0:64, :], func=AF.Copy), [bp], [bkc])
                ACT(lambda q, ps=ps, kv=kv: q.activation(out=kcT[64:128, kv, 1, :], in_=ps[64:128, :], func=AF.Copy), [bp], [bkc])
            cnt = [0]

            def qk_post(ps, bp, g, nw_col, dst, bdst, scale_extra, kfin=None):
                s = cnt[0] % 2
                cnt[0] += 1
                ACT(lambda q: q.activation(out=zq[s], in_=ps[:], func=AF.Copy), [bp], [bzq[s]])
                ACT(lambda q: q.activation(out=sqa[s], in_=ps[:], func=AF.Square), [bp], [bsqa[s]])
                p2, bp2 = nps()
                PE(lambda q: q.matmul(p2[:], lhsT=blk_b, rhs=sqa[s], start=True, stop=True), [bsqa[s], bcb], [bp2])
                DVE(lambda q: q.tensor_scalar(out=rr[s], in0=p2[:], scalar1=1.0 / 64, scalar2=EPS, op0=ALU.mult, op1=ALU.add), [bp2], [brr[s]])
                ACT(lambda q: q.activation(out=rr[s], in_=rr[s], func=AF.Ln), [brr[s]], [brr[s]])
                ACT(lambda q: q.activation(out=rr[s], in_=rr[s], func=AF.Exp, scale=-0.5), [brr[s]], [brr[s]])
                DVE(lambda q: q.scalar_tensor_tensor(out=zq[s], in0=zq[s], scalar=par[:, nw_col:nw_col + 1], in1=rr[s], op0=ALU.mult, op1=ALU.mult), [bzq[s], brr[s], bpar], [bzq[s]])
                if g < 2 and "rope" not in SKIP:
                    p3, bp3 = nps()
                    PE(lambda q: q.matmul(p3[:], lhsT=cf[:, C_PERM:C_PERM + 128], rhs=zq[s], start=True, stop=True), [bzq[s], bcf], [bp3])
                    DVE(lambda q: q.tensor_tensor(out=t1[s], in0=p3[:], in1=cf[:, C_SIN + g * 512:C_SIN + (g + 1) * 512], op=ALU.mult), [bp3, bcf], [bt1[s]])
                    DVE(lambda q: q.tensor_tensor(out=zq[s], in0=zq[s], in1=cf[:, C_COS + g * 512:C_COS + (g + 1) * 512], op=ALU.mult), [bzq[s], bcf], [bzq[s]])
                    DVE(lambda q: q.tensor_tensor(out=zq[s], in0=zq[s], in1=t1[s], op=ALU.add), [bzq[s], bt1[s]], [bzq[s]])
                if kfin is None:
                    ACT(lambda q: q.activation(out=dst, in_=zq[s], func=AF.Copy, scale=scale_extra), [bzq[s]], [bdst])
                else:
                    ACT(lambda q: q.activation(out=dst[0][0:64, :], in_=zq[s][0:64, :], func=AF.Copy), [bzq[s]], [bdst])
                    ACT(lambda q: q.activation(out=dst[1][64:128, :], in_=zq[s][64:128, :], func=AF.Copy), [bzq[s]], [bdst])
                if kfin is not None and "ktr" not in SKIP:
                    kv = kfin
                    p4, bp4 = nps()
                    for tt in range(4):
                        PE(lambda q, tt=tt: q.transpose(p4[:, tt * 64:(tt + 1) * 64], zq[s][0:64, tt * 128:(tt + 1) * 128], ident[0:64, 0:64]), [bzq[s], bcf], [bp4])
                    DVE(lambda q: q.tensor_copy(out=kst[:, g * 4:(g + 1) * 4, kv * 64:(kv + 1) * 64], in_=p4[:, 0:256].rearrange("p (t n) -> p t n", t=4)), [bp4], [bkst])

            sl, bs = wnext()
            wv = sl[:, 0:4096].rearrange("p (c n) -> p c n", c=8)
            for c in range(4):
                for g in range(NG):
                    ps, bp = nps()
                    inproj(wv, c, g, (ps, bp, bs))
                    qk_post(ps, bp, g, P_QN + l, qT[:, c, g * 512:(g + 1) * 512], bq[c][g], 0.125)
            sl, bs = wnext()
            for kv in range(2):
                wv = sl[:, kv * 1024:(kv + 1) * 1024].rearrange("p (c n) -> p c n", c=8)
                for g in range(NG):
                    ps, bp = nps()
                    inproj(wv, 0, g, (ps, bp, bs))
                    qk_post(ps, bp, g, P_KN + l, (kdT[:, kv, 0, g * 512:(g + 1) * 512], kdT[:, kv, 1, g * 512:(g + 1) * 512]), bkd[kv][g], 1.0, kfin=kv)
            wv = sl[:, 2048:3072].rearrange("p (c n) -> p c n", c=8)
            for g in range(NG):
                ps, bp = nps()
                inproj(wv, 0, g, (ps, bp, bs))
                s = cnt[0] % 2
                cnt[0] += 1
                ACT(lambda q, ps=ps, s=s: q.activation(out=zq[s], in_=ps[:], func=AF.Copy), [bp], [bzq[s]])
                if "vtr" in SKIP:
                    continue
                p4, bp4 = nps()
                for tt in range(4):
                    PE(lambda q, tt=tt, s=s, p4=p4: q.transpose(p4[:, tt * 128:(tt + 1) * 128], zq[s][:, tt * 128:(tt + 1) * 128], ident), [bzq[s], bcf], [bp4])
                if "vtr_dve" not in SKIP:
                    DVE(lambda q, p4=p4, g=g: q.tensor_copy(out=vst[:, g * 4:(g + 1) * 4, :], in_=p4[:].rearrange("p (t n) -> p t n", t=4)), [bp4], [bvst])
                for tt in (range(4) if "vtr_act" not in SKIP else []):
                    ACT(lambda q, p4=p4, g=g, tt=tt: q.activation(out=vaug[:, g * 4 + tt, :, 0:64], in_=p4[:, tt * 128:(tt + 1) * 128].rearrange("p (k d) -> p k d", k=2), func=AF.Copy), [bp4], [bva[g * 4 + tt]])
            if "kvout" not in SKIP:
                DMA("sp", kout[l].rearrange("(t p) n -> p t n", p=128), kst, bkst, reads=[bkst])
            if "kvout" not in SKIP:
                DMA("sp", vout[l].rearrange("(t p) n -> p t n", p=128), vst, bvst, reads=[bvst])
            it = 0
            if not ATTN_CORE:
                for c in range(4):
                    for g in range(NG):
                        POOL(lambda q, c=c, g=g: q.memset(catT[:, 4 + c, g * 512:(g + 1) * 512], 0.0), [], [bcat[4 + c][g]])
            for j in (range(NT) if ATTN_CORE else []):
                g = j // 4
                if j < 8:
                    kts = [("n", j + d_, (j * 3 + 1 + d_) if d_ != 0 else None) for d_ in (-1, 0, 1) if 0 <= j + d_ < 8] + [("c", t, None) for t in range(4)]
                else:
                    b0 = 8 + 2 * ((j - 8) // 2)
                    kts = [("n", b0, None), ("n", b0 + 1, None)]
                so = j % 2
                for kv in range(2):
                    po, bpo = nps()
                    pov = po[:, 0:264].rearrange("p (h n) -> p h n", h=4)
                    nk = len(kts)

                    def scores(ki, j=j, kv=kv, g=g):
                        kind, kt, mi = kts[ki]
                        pS, bpS = nps()
                        for hh in range(4):
                            cq = kv * 2 + hh // 2
                            hf = hh % 2
                            if kind == "n":
                                lhs = kdT[:, kv, hf, kt * 128:(kt + 1) * 128]
                                rb = bkd[kv][kt // 4]
                            else:
                                lhs = kcT[:, kv, hf, kt * 128:(kt + 1) * 128]
                                rb = bkc
                            PE(lambda q, pS=pS, hh=hh, lhs=lhs, cq=cq, j=j: q.matmul(pS[:, hh * 128:(hh + 1) * 128], lhsT=lhs, rhs=qT[:, cq, j * 128:(j + 1) * 128], start=True, stop=True),
                               [rb, bq[cq][g]], [bpS])
                        return pS, bpS

                    LOOK = 2
                    sc = {}
                    for ki in range(min(LOOK, nk)):
                        sc[ki] = scores(ki)
                    for ki, (kind, kt, mi) in enumerate(kts):
                        pS, bpS = sc.pop(ki)
                        s = it % 3
                        it += 1
                        if kind == "c":
                            ACT(lambda q, pS=pS, s=s: q.activation(out=pT[s], in_=pS[:], func=AF.Exp, bias=par[:, P_CBIAS:P_CBIAS + 1]), [bpS, bpar], [bpT[s]])
                        else:
                            ACT(lambda q, pS=pS, s=s: q.activation(out=pT[s], in_=pS[:], func=AF.Exp), [bpS], [bpT[s]])
                        if mi is not None:
                            DVE(lambda q, s=s, mi=mi: q.tensor_tensor(out=pT[s].rearrange("p (h n) -> p h n", h=4), in0=pT[s].rearrange("p (h n) -> p h n", h=4),
                                                                      in1=cb[:, B_MASK + mi * 128:B_MASK + (mi + 1) * 128].unsqueeze(1).to_broadcast([128, 4, 128]), op=ALU.mult), [bpT[s], bcb], [bpT[s]])
                        if ki + LOOK < nk:
                            sc[ki + LOOK] = scores(ki + LOOK)
                        vt = kt if kind == "n" else 12 + kt
                        for hh in range(4):
                            PE(lambda q, pov=pov, hh=hh, s=s, vt=vt, kv=kv, ki=ki, n=nk: q.matmul(pov[:, hh, 0:66], lhsT=pT[s][:, hh * 128:(hh + 1) * 128], rhs=vaug[:, vt, kv, 0:66], start=(ki == 0 and hh == 0), stop=(ki == n - 1 and hh == 3)),
                               [bpT[s], bva[vt]], [bpo])
                    DVE(lambda q, pov=pov, so=so, kv=kv: q.tensor_tensor(out=den[so], in0=pov[:, :, 64], in1=esink[:, kv * 4:(kv + 1) * 4], op=ALU.add), [bpo, besk], [bden[so]])
                    DVE(lambda q, so=so: q.reciprocal(out=den[so], in_=den[so]), [bden[so]], [bden[so]])
                    DVE(lambda q, pov=pov, so=so, kv=kv: q.tensor_tensor(out=otok[so][:, kv * 4:(kv + 1) * 4, :], in0=pov[:, :, 0:64], in1=den[so].unsqueeze(2).to_broadcast([128, 4, 64]), op=ALU.mult), [bpo, bden[so]], [botok[so]])
                pt_, bpt = nps()
                ptb = pt_[:].bitcast(BF16)
                for c in range(4):
                    PE(lambda q, ptb=ptb, c=c, so=so: q.transpose(ptb[:, c * 128:(c + 1) * 128], otok[so][:, 2 * c:2 * c + 2, :].rearrange("p h n -> p (h n)"), ident_b), [botok[so], bcb], [bpt])
                ACT(lambda q, ptb=ptb, j=j: q.activation(out=catT[:, 4:8, j * 128:(j + 1) * 128], in_=ptb[:, 0:512].rearrange("p (c n) -> p c n", c=4), func=AF.Copy), [bpt], [bcat[4 + c][g] for c in range(4)])

        else:
            wnext()
            wnext()
            for c in range(4):
                for g in range(NG):
                    POOL(lambda q, c=c, g=g: q.memset(catT[:, 4 + c, g * 512:(g + 1) * 512], 0.0), [], [bcat[4 + c][g]])
        for i in range(2):
            sl, bs = wnext()
            wv = sl[:, 0:4096].rearrange("p (c n) -> p c n", c=8)
            for k in range(4):
                dc = i * 4 + k
                for g in range(NG):
                    ps, bp = nps()
                    for c in range(8):
                        PE(lambda q, ps=ps, wv=wv, c=c, k=k, g=g: q.matmul(ps[:], lhsT=wv[:, c, k * 128:(k + 1) * 128], rhs=catT[:, c, g * 512:(g + 1) * 512], start=(c == 0), stop=(c == 7)),
                           [bs, bcat[c][g]], [bp])
                    DVE(lambda q, ps=ps, dc=dc, g=g: q.scalar_tensor_tensor(out=xT[:, dc, g * 512:(g + 1) * 512], in0=ps[:], scalar=mod_G(l, 1, dc, g), in1=xT[:, dc, g * 512:(g + 1) * 512], op0=ALU.mult, op1=ALU.add),
                        [bp, bmod, bx[dc][g]], [bx[dc][g]])

    for l in range(2):
        ffn(l, 0)
        mixer(l)
        ffn(l, 1)

    areset()
    yst = [aalloc(1024), aalloc(1024)]
    byst = [nb("yst0"), nb("yst1")]
    for t in range(NT):
        s = t % 2
        g = t // 4
        for half in range(2):
            ps, bp = nps()
            for k in range(4):
                c = half * 4 + k
                PE(lambda q, ps=ps, k=k, c=c, t=t: q.transpose(ps[:, k * 128:(k + 1) * 128], xT[:, c, t * 128:(t + 1) * 128], ident), [bx[c][g], bcf], [bp])
            if half == 0:
                ACT(lambda q, ps=ps, s=s: q.activation(out=yst[s][:, 0:512], in_=ps[:], func=AF.Copy), [bp], [byst[s]])
            else:
                DVE(lambda q, ps=ps, s=s: q.tensor_copy(out=yst[s][:, 512:1024], in_=ps[:]), [bp], [byst[s]])
        DMA("sp", yout[t * 128:(t + 1) * 128, :], yst[s], byst[s], reads=[byst[s]])

    T.emit(nc)
    st.close()
    return nc


def _consts(sample_mode):
    cf = np.zeros((128, NCF), np.float32)
    cf[:, C_ID:C_ID + 128] = np.eye(128, dtype=np.float32)
    Pm = np.zeros((64, 64), np.float32)
    for d in range(64):
        blk = d // 16
        if blk % 2 == 0:
            Pm[d, d + 16] = -1.0
        else:
            Pm[d, d - 16] = 1.0
    P2 = np.zeros((128, 128), np.float32)
    P2[:64, :64] = Pm
    P2[64:, 64:] = Pm
    cf[:, C_PERM:C_PERM + 128] = P2.T
    t = np.arange(1024)
    inv = (10000.0 ** (-np.arange(0, 32, 2, dtype=np.float32) / 32)).astype(np.float32)
    d = np.arange(128) % 64
    pos = np.where((d < 32)[:, None], (t // 64)[None, :], (t % 64)[None, :]).astype(np.float32)
    ang = pos * inv[d % 16][:, None]
    if sample_mode:
        cf[:, C_COS:C_COS + 1024] = np.cos(ang)
        cf[:, C_SIN:C_SIN + 1024] = np.sin(ang)
    else:
        cf[:, C_COS:C_COS + 1024] = 1.0
    j = np.arange(64)
    cf[:64, C_MF:C_MF + 64] = (j[:, None] <= j[None, :]).astype(np.float32)
    cf[:64, C_MB:C_MB + 64] = (j[:, None] >= j[None, :]).astype(np.float32)
    cb = np.zeros((128, NCB), np.float32)
    cb[:, B_ONES:B_ONES + 128] = 1.0
    cb[:64, B_BLK:B_BLK + 64] = 1.0
    cb[64:, B_BLK + 64:B_BLK + 128] = 1.0
    cb[:, B_ID:B_ID + 128] = np.eye(128, dtype=np.float32)
    c = np.arange(64)
    a = 2 * np.pi * np.outer(c, c) / 64.0
    for gq in range(2):
        cb[gq * 64:(gq + 1) * 64, B_DFT + gq * 64:B_DFT + (gq + 1) * 64] = np.cos(a)
        cb[gq * 64:(gq + 1) * 64, B_DFT + 128 + gq * 64:B_DFT + 128 + (gq + 1) * 64] = np.sin(a)
    ki = np.arange(128)[:, None]
    qi = np.arange(128)[None, :]
    for jt in range(8):
        for s in range(3):
            if sample_mode:
                m = (ki >= qi) if s == 0 else (np.ones((128, 128), bool) if s == 1 else (ki <= qi))
            else:
                if s == 1:
                    m = np.ones((128, 128), bool)
                elif s == 0:
                    m = np.full((128, 128), jt % 2 == 1)
                else:
                    m = np.full((128, 128), jt % 2 == 0)
            cb[:, B_MASK + (jt * 3 + s) * 128:B_MASK + (jt * 3 + s + 1) * 128] = m.astype(np.float32)

    def dft(L):
        ll = np.arange(L)
        aa = 2 * np.pi * np.outer(ll, ll) / L
        sc_ = 1.0 / np.sqrt(L * 64.0)
        return (np.cos(aa) * sc_).astype(np.float32), (-np.sin(aa) * sc_).astype(np.float32)

    dftp = np.stack(dft(256))
    if sample_mode:
        dftl = np.stack(dft(1024))
    else:
        dftl = np.zeros((2, 1024, 1024), np.float32)
        for i in range(4):
            dftl[:, i * 256:(i + 1) * 256, i * 256:(i + 1) * 256] = dftp
    return cf, cb, dftl, dftp


_NC_CACHE = {}


def _prep(inputs):
    I = {k: np.asarray(v) for k, v in inputs.items()}
    xp, xs = I["x_prompt"], I["x_sample"]
    consts = {True: _consts(True), False: _consts(False)}
    fm = lambda v: np.ascontiguousarray(v.reshape(-1, 128).T)
    in_maps = []
    assign = []
    for core in range(8):
        sm = core < 4
        if sm:
            gtok = xs[core]
            pseq = [2 * core, 2 * core + 1]
            gseq = []
            cond = np.stack([I["c"][core], I["c_ctx"]])
        else:
            base = 8 + 6 * (core - 4)
            gseq = [base, base + 1, base + 2, base + 3]
            gtok = xp[gseq].reshape(1024, D)
            pseq = [base + 4, base + 5]
            cond = np.stack([I["c_ctx"], I["c_ctx"]])
        assign.append((sm, gseq, pseq))
        xin = np.concatenate([gtok, xp[pseq].reshape(512, D)], axis=0)
        par = np.zeros((128, NPAR), np.float32)
        par[:, P_COND:P_COND + 16] = cond.reshape(2, 8, 128).transpose(2, 1, 0).reshape(128, 16)
        for l in range(2):
            par[:, P_BADA + l * 72:P_BADA + (l + 1) * 72] = fm(I["b_ada"][l])
            for w3, nm in enumerate(["norm_ffn1", "norm_mix", "norm_ffn2"]):
                par[:, P_NRM + l * 24 + w3 * 8:P_NRM + l * 24 + w3 * 8 + 8] = fm(I[nm][l])
            for dr in range(2):
                par[:, P_LBS + l * 4 + dr * 2:P_LBS + l * 4 + dr * 2 + 2] = fm(I["hgrn_lower_bounds"][l, dr])
            par[:, P_HNRM + l * 2:P_HNRM + l * 2 + 2] = fm(I["hgrn_norm"][l])
            par[:, P_QN + l] = np.tile(I["q_norm"][l], 2)
            par[:, P_KN + l] = np.tile(I["k_norm"][l], 2)
            par[:, P_SINK + l * 8:P_SINK + (l + 1) * 8] = I["attn_sink"][l][None, :]
        par[:, P_KEEP] = 1.0 if sm else 0.0
        par[:, P_CBIAS] = 0.0 if sm else -30000.0
        cf, cb, dftl, dftp = consts[sm]
        if sm:
            ck = I["cache_attn_k"][core].reshape(2, 512, 128)
            cv = I["cache_attn_v"][core].reshape(2, 512, 128)
            s0 = I["state_hgrn"][core]
        else:
            ck = np.zeros((2, 512, 128), np.float32)
            cv = np.zeros((2, 512, 128), np.float32)
            s0 = np.zeros((2, 2, 4, 64, 64), np.float32)
        in_maps.append(dict(
            xin=np.ascontiguousarray(xin), par=par, cf=cf, cb=cb, dftl=dftl, dftp=dftp,
            ck=np.ascontiguousarray(ck), cv=np.ascontiguousarray(cv), s0=np.ascontiguousarray(s0),
            w_ada=I["w_ada"], ffn1_gu=I["ffn1_w_gate_up"], ffn2_gu=I["ffn2_w_gate_up"],
            ffn1_d=I["ffn1_w_down"], ffn2_d=I["ffn2_w_down"], w_in=I["w_in"], w_out=I["w_out"]))
    return in_maps, assign


def _post(results, assign):
    y_p = np.zeros((32, 256, D), np.float32)
    y_s = np.zeros((4, 1024, D), np.float32)
    nk = np.zeros((32, 2, 256, 2, 64), np.float32)
    nv = np.zeros((32, 2, 256, 2, 64), np.float32)
    ns = np.zeros((32, 2, 2, 4, 64, 64), np.float32)
    for core in range(8):
        r = results[core]
        sm, gseq, pseq = assign[core]
        y = r["yout"]
        ko = r["kout"].reshape(2, TOK, 2, 64)
        vo = r["vout"].reshape(2, TOK, 2, 64)
        so = r["sout"]
        if sm:
            y_s[core] = y[:1024]
        seqs = [(s, i) for i, s in enumerate(gseq)] + [(s, 4 + i) for i, s in enumerate(pseq)]
        for s, slot in seqs:
            y_p[s] = y[slot * 256:(slot + 1) * 256]
            nk[s] = ko[:, slot * 256:(slot + 1) * 256]
            nv[s] = vo[:, slot * 256:(slot + 1) * 256]
            ns[s] = so[:, slot]
    return (y_p, y_s, nk, nv, ns)


def kernel(**inputs):
    in_maps, assign = _prep(inputs)
    if "nc" not in _NC_CACHE:
        _NC_CACHE["nc"] = build()
    res = run_bass_kernel_spmd(_NC_CACHE["nc"], in_maps, core_ids=list(range(8)))
    return _post(res.results, assign)
```

```python
import numpy as np
from contextlib import ExitStack
import concourse.bass as bass
import concourse.mybir as mybir
from concourse.bass_utils import run_bass_kernel_spmd

F32 = mybir.dt.float32
BF16 = mybir.dt.bfloat16
AF = mybir.ActivationFunctionType
ALU = mybir.AluOpType

EP = 8192
ENGS = ("pe", "act", "dve", "pool", "sp")
SAME_ENGINE_SYNC = ("act", "dve", "pool")


class Buf:
    def __init__(self, name, fence=None):
        self.name = name
        self.w = None
        self.r = list(fence) if fence else []
        self.dsem = None
        self.dcnt = 0
        self.excl = False


class Op:
    pass


class Tracker:
    def __init__(self):
        self.ops = []
        self.per_eng = {e: [] for e in ENGS}
        self.dma_bufs = []
        self.dma_ids = []

    def fence(self):
        ids = [l[-1].id for l in self.per_eng.values() if l]
        return ids + list(self.dma_ids)

    def op(self, eng, fn, reads=(), writes=(), dma_buf=None):
        o = Op()
        o.eng = eng
        o.fn = fn
        o.dma_buf = dma_buf
        o.marked = False
        o.id = len(self.ops)
        o.pos = len(self.per_eng[eng])
        deps = set()
        for b in reads:
            if b.w is not None:
                deps.add(b.w)
            if b.excl:
                deps.update(b.r)
        for b in writes:
            if b.w is not None:
                deps.add(b.w)
            deps.update(b.r)
        o.deps = deps
        for b in reads:
            b.r.append(o.id)
        for b in writes:
            b.w = o.id
            b.r = []
        if dma_buf is not None:
            if dma_buf.dsem is None:
                dma_buf.dsem = True
                self.dma_bufs.append(dma_buf)
            dma_buf.dcnt += 16
            o.dval = dma_buf.dcnt
            self.dma_ids.append(o.id)
        self.ops.append(o)
        self.per_eng[eng].append(o)
        return o

    def plan(self):
        ops = self.ops
        seen = {e: {f: -1 for f in ENGS} for e in ENGS}
        seen_dma = {e: {} for e in ENGS}
        for o in ops:
            w_eng = {}
            w_dma = {}
            for d in o.deps:
                p = ops[d]
                if p.dma_buf is not None:
                    b = p.dma_buf
                    if seen_dma[o.eng].get(id(b), 0) >= p.dval:
                        continue
                    w_dma[id(b)] = (b, max(p.dval, w_dma.get(id(b), (None, 0))[1]))
                else:
                    if p.eng == o.eng:
                        if p.eng not in SAME_ENGINE_SYNC:
                            continue
                        if p.pos < o.pos - 2:
                            continue
                    if seen[o.eng][p.eng] >= d:
                        continue
                    w_eng[p.eng] = max(w_eng.get(p.eng, -1), d)
            for f, d in w_eng.items():
                seen[o.eng][f] = d
                ops[d].marked = True
            for k, (b, v) in w_dma.items():
                seen_dma[o.eng][k] = v
            o.w_eng = w_eng
            o.w_dma = list(w_dma.values())
        cnt = {e: 0 for e in ENGS}
        for o in ops:
            if o.dma_buf is None and o.marked:
                cnt[o.eng] += 1
            o.cnt = cnt[o.eng]
        self.final_cnt = cnt

    def emit(self, nc, final_eng="sp"):
        self.plan()
        ops = self.ops
        with ExitStack() as st:
            esems = {}
            for e in ENGS:
                n = max(1, (self.final_cnt[e] + EP - 1) // EP)
                esems[e] = [st.enter_context(nc.semaphore(f"s_{e}_{i}")) for i in range(n)]
            for i, b in enumerate(self.dma_bufs):
                b.dsem = st.enter_context(nc.semaphore(f"d_{i}"))
            block = st.enter_context(nc.Block())

            def run(eng_name, eng):
                for o in self.per_eng[eng_name]:
                    for f, d in o.w_eng.items():
                        c = ops[d].cnt
                        eng.wait_ge(esems[f][(c - 1) // EP], (c - 1) % EP + 1)
                    for (b, v) in o.w_dma:
                        eng.wait_ge(b.dsem, v)
                    ins = o.fn(eng)
                    if o.dma_buf is not None:
                        ins.then_inc(o.dma_buf.dsem, 16)
                    elif o.marked:
                        c = o.cnt
                        ins.then_inc(esems[eng_name][(c - 1) // EP], 1)
                if eng_name == final_eng:
                    for b in self.dma_bufs:
                        eng.wait_ge(b.dsem, b.dcnt)
                    for f in ENGS:
                        c = self.final_cnt[f]
                        if c > 0 and f != eng_name:
                            eng.wait_ge(esems[f][(c - 1) // EP], (c - 1) % EP + 1)

            @block.tensor
            def _(e):
                run("pe", e)

            @block.scalar
            def _(e):
                run("act", e)

            @block.vector
            def _(e):
                run("dve", e)

            @block.gpsimd
            def _(e):
                run("pool", e)

            @block.sync
            def _(e):
                run("sp", e)


TOK = 1536
NT = 12
NG = 3
D = 1024
DFF = 2816
EPS = 1e-6
P_COND, P_BADA, P_NRM, P_LBS, P_HNRM, P_QN, P_KN, P_SINK, P_KEEP, P_CBIAS, NPAR = 0, 16, 160, 208, 216, 220, 222, 224, 240, 241, 242
C_ID, C_PERM, C_COS, C_SIN, C_MF, C_MB, NCF = 0, 128, 256, 1280, 2304, 2368, 2432
B_ONES, B_BLK, B_ID, B_DFT, B_MASK, NCB = 0, 128, 256, 384, 640, 640 + 24 * 128
SLOT = 4096
ENABLE_HGRN = True
DEBUG_PAIR = None
ATTN_CORE = True
SKIP = set()
ENABLE_ATTN = True
NSL = 3
ARENA = 23400


def build(stop_after=None):
    nc = bass.Bass("TRN2", target_bir_lowering=False)
    T = Tracker()
    di = lambda n, s: nc.dram_tensor(n, s, F32, kind="ExternalInput").ap()
    do = lambda n, s: nc.dram_tensor(n, s, F32, kind="ExternalOutput").ap()
    xin = di("xin", [TOK, D]); par_d = di("par", [128, NPAR]); cf_d = di("cf", [128, NCF]); cb_d = di("cb", [128, NCB])
    dftl_d = di("dftl", [2, 1024, 1024]); dftp_d = di("dftp", [2, 256, 256])
    ck_d = di("ck", [2, 512, 128]); cv_d = di("cv", [2, 512, 128]); s0_d = di("s0", [2, 2, 4, 64, 64])
    wada_d = di("w_ada", [2, D, 9 * D])
    wgu_d = [di("ffn1_gu", [2, D, 2 * DFF]), di("ffn2_gu", [2, D, 2 * DFF])]
    wdn_d = [di("ffn1_d", [2, DFF, D]), di("ffn2_d", [2, DFF, D])]
    win_d = di("w_in", [2, D, 2304]); wout_d = di("w_out", [2, D, D])
    yout = do("yout", [TOK, D]); kout = do("kout", [2, TOK, 128]); vout = do("vout", [2, TOK, 128])
    sout = do("sout", [2, 6, 2, 4, 64, 64])

    st = ExitStack()
    sbt = lambda n, s, dt: st.enter_context(nc.sbuf_tensor("sb_" + n, s, dt))
    xT = sbt("xT", [128, 8, TOK], F32)
    hT = sbt("hT", [128, 8, TOK], BF16)
    par = sbt("par", [128, NPAR], F32)
    cf = sbt("cf", [128, NCF], F32)
    cb = sbt("cb", [128, NCB], BF16)
    modv = sbt("modv", [128, 2, 72, 2], F32)
    mA = sbt("mA", [128, 2, 3, 8, 2], F32)
    mG = sbt("mG", [128, 2, 3, 8, 2], F32)
    sml = sbt("sml", [128, 64], F32)
    slots = [sbt(f"slot{i}", [128, SLOT], BF16) for i in range(NSL)]
    arena = sbt("arena", [128, ARENA], F32)
    psum = [st.enter_context(nc.psum_tensor(f"ps{i}", [128, 512], F32)) for i in range(8)]
    bps = [Buf(f"ps{i}") for i in range(8)]
    for b_ in bps:
        b_.excl = True
    bslot = [Buf(f"slot{i}") for i in range(NSL)]
    bx = [[Buf(f"x{c}_{g}") for g in range(NG)] for c in range(8)]
    bh = [[Buf(f"h{c}_{g}") for g in range(NG)] for c in range(8)]
    bpar, bcf, bcb, bmod = Buf("par"), Buf("cf"), Buf("cb"), Buf("mod")
    bsml = Buf("sml")

    state = {"ps": 0, "aoff": 0}

    def nps():
        i = state["ps"] % 8
        state["ps"] += 1
        return psum[i], bps[i]

    def aalloc(n32):
        o = state["aoff"]
        state["aoff"] += n32
        assert state["aoff"] <= ARENA, state["aoff"]
        return arena[:, o:o + n32]

    def areset():
        state["aoff"] = 0

    def nb(name):
        return Buf(name, T.fence())

    PE = lambda fn, r, w: T.op("pe", fn, r, w)
    ACT = lambda fn, r, w: T.op("act", fn, r, w)
    DVE = lambda fn, r, w: T.op("dve", fn, r, w)
    POOL = lambda fn, r, w: T.op("pool", fn, r, w)

    def DMA(eng, out, in_, buf, reads=(), writes=()):
        T.op(eng, lambda q: q.dma_start(out=out, in_=in_), reads, writes, dma_buf=buf)

    wq = []
    wstate = {"loaded": 0, "used": 0}

    def witem(parts):
        wq.append(parts)

    def wnext(keep_prev=False):
        i = wstate["used"]
        oldest = i - 1 if keep_prev else i
        while wstate["loaded"] < min(len(wq), oldest + NSL):
            k = wstate["loaded"]
            s = k % NSL
            for (fn, dap) in wq[k]:
                DMA("pool", fn(slots[s]), dap, bslot[s], writes=[bslot[s]])
            wstate["loaded"] += 1
        wstate["used"] += 1
        return slots[i % NSL], bslot[i % NSL]

    def kview(ap2d):
        return ap2d.rearrange("(c p) n -> p c n", p=128)

    def sched_weights():
        for l in range(2):
            for i in range(18):
                witem([(lambda s: s[:, 0:4096].rearrange("p (c n) -> p c n", c=8), kview(wada_d[l][:, i * 512:(i + 1) * 512]))])
        for l in range(2):
            for f in range(2):
                sched_ffn(l, f)
                if f == 0:
                    sched_mixer(l)

    def sched_ffn(l, f):
        for h in range(2):
            for j0 in range(0, 11, 2):
                nf = min(2, 11 - j0)
                c0 = (h * 11 + j0) * 128
                witem([
                    (lambda s, nf=nf: s[:, 0:8 * nf * 128].rearrange("p (c n) -> p c n", c=8), kview(wgu_d[f][l][:, c0:c0 + nf * 128])),
                    (lambda s, nf=nf: s[:, 2048:2048 + 8 * nf * 128].rearrange("p (c n) -> p c n", c=8), kview(wgu_d[f][l][:, DFF + c0:DFF + c0 + nf * 128])),
                ])

    def sched_mixer(l):
        w = win_d[l]
        witem([(lambda s: s[:, 0:2048].rearrange("p (c n) -> p c n", c=8), kview(w[:, 0:256]))])
        for hf in range(2):
            for cs in range(2):
                witem([(lambda s: s[:, 0:4096].rearrange("p (c n) -> p c n", c=8), kview(dftl_d[cs][:, hf * 512:(hf + 1) * 512]))])
        witem([(lambda s, cs=cs: s[:, cs * 512:(cs + 1) * 512].rearrange("p (c n) -> p c n", c=2), kview(dftp_d[cs])) for cs in range(2)])
        for pr in range(2):
            cols = [256 + pr * 128, 512 + pr * 128, 768 + pr * 128, 1024 + pr * 128, 1280 + pr * 128]
            witem([(lambda s, k=k: s[:, k * 1024:(k + 1) * 1024].rearrange("p (c n) -> p c n", c=8), kview(w[:, cols[k]:cols[k] + 128])) for k in range(3)])
            witem([(lambda s, k=k: s[:, k * 1024:(k + 1) * 1024].rearrange("p (c n) -> p c n", c=8), kview(w[:, cols[3 + k]:cols[3 + k] + 128])) for k in range(2)])
        witem([(lambda s: s[:, 0:4096].rearrange("p (c n) -> p c n", c=8), kview(w[:, 1536:2048]))])
        parts = []
        for kv in range(2):
            for dup in range(2):
                parts.append((lambda s, kv=kv, dup=dup: s[:, kv * 1024:(kv + 1) * 1024].rearrange("p (c n) -> p c n", c=8)[:, :, dup * 64:(dup + 1) * 64],
                              kview(w[:, 2048 + kv * 64:2048 + (kv + 1) * 64])))
        parts.append((lambda s: s[:, 2048:3072].rearrange("p (c n) -> p c n", c=8), kview(w[:, 2176:2304])))
        witem(parts)
        for i in range(2):
            witem([(lambda s: s[:, 0:4096].rearrange("p (c n) -> p c n", c=8), kview(wout_d[l][:, i * 512:(i + 1) * 512]))])

    sched_weights()

    DMA("sp", par[:], par_d, bpar, writes=[bpar])
    DMA("sp", cf[:], cf_d, bcf, writes=[bcf])
    DMA("pool", cb[:], cb_d, bcb, writes=[bcb])
    ident = cf[:, C_ID:C_ID + 128]
    ones_b = cb[:, B_ONES:B_ONES + 128]
    blk_b = cb[:, B_BLK:B_BLK + 128]
    ident_b = cb[:, B_ID:B_ID + 128]

    areset()
    xst = [aalloc(1024), aalloc(1024)]
    bxst = [nb("xst0"), nb("xst1")]
    for t in range(NT):
        s = t % 2
        DMA("sp", xst[s], xin[t * 128:(t + 1) * 128, :], bxst[s], writes=[bxst[s]])
        for half in range(2):
            ps, bp = nps()
            for k in range(4):
                c = half * 4 + k
                PE(lambda q, ps=ps, k=k, c=c, s=s: q.transpose(ps[:, k * 128:(k + 1) * 128], xst[s][:, c * 128:(c + 1) * 128], ident),
                   [bxst[s], bcf], [bp])
            g = t // 4
            eng = ACT if half == 0 else DVE
            fn = (lambda q, ps=ps, half=half, t=t: q.activation(out=xT[:, half * 4:half * 4 + 4, t * 128:(t + 1) * 128], in_=ps[:].rearrange("p (k n) -> p k n", k=4), func=AF.Copy)) if half == 0 else \
                 (lambda q, ps=ps, half=half, t=t: q.tensor_copy(out=xT[:, half * 4:half * 4 + 4, t * 128:(t + 1) * 128], in_=ps[:].rearrange("p (k n) -> p k n", k=4)))
            eng(fn, [bp], [bx[c][g] for c in range(half * 4, half * 4 + 4)])

    areset()
    sc = aalloc(16)
    scb = aalloc(8).bitcast(BF16)
    bsc = nb("sc")
    ACT(lambda q: q.activation(out=sc, in_=par[:, P_COND:P_COND + 16], func=AF.Exp, scale=-1.0), [bpar], [bsc])
    DVE(lambda q: q.tensor_scalar_add(out=sc, in0=sc, scalar1=1.0), [bsc], [bsc])
    DVE(lambda q: q.reciprocal(out=sc, in_=sc), [bsc], [bsc])
    DVE(lambda q: q.tensor_tensor(out=scb, in0=sc, in1=par[:, P_COND:P_COND + 16], op=ALU.mult), [bsc, bpar], [bsc])
    scb3 = scb.rearrange("p (c r) -> p c r", c=8)
    for l in range(2):
        ps, bp = nps()
        for i in range(18):
            sl, bs = wnext()
            wv = sl[:, 0:4096].rearrange("p (c n) -> p c n", c=8)
            for k in range(4):
                nn = i * 4 + k
                for c in range(8):
                    PE(lambda q, ps=ps, wv=wv, k=k, c=c, nn=nn: q.matmul(ps[:, nn * 2:nn * 2 + 2], lhsT=wv[:, c, k * 128:(k + 1) * 128], rhs=scb3[:, c, :], start=(c == 0), stop=(c == 7)),
                       [bs, bsc], [bp])
        DVE(lambda q, ps=ps, l=l: q.tensor_tensor(out=modv[:, l, :, :], in0=ps[:, 0:144].rearrange("p (n r) -> p n r", r=2),
                                                 in1=par[:, P_BADA + l * 72:P_BADA + (l + 1) * 72].unsqueeze(2).to_broadcast([128, 72, 2]), op=ALU.add),
            [bp, bpar], [bmod])
        for w3 in range(3):
            nrm = par[:, P_NRM + l * 24 + w3 * 8:P_NRM + l * 24 + w3 * 8 + 8].unsqueeze(2).to_broadcast([128, 8, 2])
            DVE(lambda q, l=l, w3=w3, nrm=nrm: q.scalar_tensor_tensor(out=mA[:, l, w3, :, :], in0=modv[:, l, (3 * w3 + 1) * 8:(3 * w3 + 2) * 8, :], scalar=1.0, in1=nrm, op0=ALU.add, op1=ALU.mult),
                [bmod, bpar], [bmod])
            DVE(lambda q, l=l, w3=w3: q.tensor_scalar_mul(out=mG[:, l, w3, :, :], in0=modv[:, l, (3 * w3 + 2) * 8:(3 * w3 + 3) * 8, :], scalar1=(1.0 if w3 == 1 else 0.5)),
                [bmod], [bmod])

    def mod_A(l, w3, c, g):
        r = 0 if g < 2 else 1
        return mA[:, l, w3, c, r:r + 1]

    def mod_B(l, w3, c, g):
        r = 0 if g < 2 else 1
        return modv[:, l, (3 * w3) * 8 + c, r:r + 1]

    def mod_G(l, w3, c, g):
        r = 0 if g < 2 else 1
        return mG[:, l, w3, c, r:r + 1]

    def norm_mod(l, w3):
        sq = [aalloc(256).bitcast(BF16), aalloc(256).bitcast(BF16)]
        bsq = [nb("sq0"), nb("sq1")]
        tmp = [aalloc(512), aalloc(512)]
        btmp = [nb("tmp0"), nb("tmp1")]
        rs = aalloc(512)
        brs = nb("rs")
        for g in range(NG):
            ps, bp = nps()
            for c in range(8):
                s = c % 2
                ACT(lambda q, s=s, c=c, g=g: q.activation(out=sq[s], in_=xT[:, c, g * 512:(g + 1) * 512], func=AF.Square), [bx[c][g]], [bsq[s]])
                PE(lambda q, ps=ps, s=s, c=c: q.matmul(ps[:], lhsT=ones_b, rhs=sq[s], start=(c == 0), stop=(c == 7)), [bsq[s], bcb], [bp])
            DVE(lambda q, ps=ps: q.tensor_scalar(out=rs, in0=ps[:], scalar1=1.0 / D, scalar2=EPS, op0=ALU.mult, op1=ALU.add), [bp], [brs])
            ACT(lambda q: q.activation(out=rs, in_=rs, func=AF.Ln), [brs], [brs])
            ACT(lambda q: q.activation(out=rs, in_=rs, func=AF.Exp, scale=-0.5), [brs], [brs])
            for c in range(8):
                s = c % 2
                DVE(lambda q, s=s, c=c, g=g: q.tensor_tensor(out=tmp[s], in0=xT[:, c, g * 512:(g + 1) * 512], in1=rs, op=ALU.mult), [bx[c][g], brs], [btmp[s]])
                ACT(lambda q, s=s, c=c, g=g: q.activation(out=hT[:, c, g * 512:(g + 1) * 512], in_=tmp[s], func=AF.Identity,
                                                          scale=mod_A(l, w3, c, g), bias=mod_B(l, w3, c, g)), [btmp[s], bmod], [bh[c][g]])

    def ffn(l, f):
        w3 = 0 if f == 0 else 2
        areset()
        norm_mod(l, w3)
        actT = aalloc(11 * TOK // 2).bitcast(BF16).rearrange("p (j n) -> p j n", j=11)
        bact = [nb(f"act{g}") for g in range(NG)]
        wd = aalloc(11 * 1024 // 2).bitcast(BF16).rearrange("p (j n) -> p j n", j=11)
        bwd = nb("wd")
        ebuf = [aalloc(512), aalloc(512)]
        bebuf = [nb("e0"), nb("e1")]
        tbuf = [aalloc(512), aalloc(512)]
        btbuf = [nb("t0"), nb("t1")]
        k = 0
        for h in range(2):
            DMA("pool", wd, wdn_d[f][l][h * 1408:(h + 1) * 1408, :].rearrange("(j p) n -> p j n", p=128), bwd, writes=[bwd])
            for j0 in range(0, 11, 2):
                nf = min(2, 11 - j0)
                sl, bs = wnext()
                wg = sl[:, 0:8 * nf * 128].rearrange("p (c n) -> p c n", c=8)
                wu = sl[:, 2048:2048 + 8 * nf * 128].rearrange("p (c n) -> p c n", c=8)
                for jj in range(nf):
                    j = j0 + jj
                    for g in range(NG):
                        pg, bpg = nps()
                        pu, bpu = nps()
                        for c in range(8):
                            PE(lambda q, pg=pg, wg=wg, c=c, jj=jj, g=g: q.matmul(pg[:], lhsT=wg[:, c, jj * 128:(jj + 1) * 128], rhs=hT[:, c, g * 512:(g + 1) * 512], start=(c == 0), stop=(c == 7)),
                               [bs, bh[c][g]], [bpg])
                        for c in range(8):
                            PE(lambda q, pu=pu, wu=wu, c=c, jj=jj, g=g: q.matmul(pu[:], lhsT=wu[:, c, jj * 128:(jj + 1) * 128], rhs=hT[:, c, g * 512:(g + 1) * 512], start=(c == 0), stop=(c == 7)),
                               [bs, bh[c][g]], [bpu])
                        s = k % 2
                        k += 1
                        ACT(lambda q, pg=pg, s=s: q.activation(out=tbuf[s], in_=pg[:], func=AF.Silu), [bpg], [btbuf[s]])
                        DVE(lambda q, pu=pu, s=s, j=j, g=g: q.tensor_tensor(out=actT[:, j, g * 512:(g + 1) * 512], in0=pu[:], in1=tbuf[s], op=ALU.mult), [bpu, btbuf[s]], [bact[g]])
            for g in range(NG):
                for dc in range(8):
                    ps, bp = nps()
                    for j in range(11):
                        PE(lambda q, ps=ps, j=j, dc=dc, g=g: q.matmul(ps[:], lhsT=wd[:, j, dc * 128:(dc + 1) * 128], rhs=actT[:, j, g * 512:(g + 1) * 512], start=(j == 0), stop=(j == 10)),
                           [bwd, bact[g]], [bp])
                    DVE(lambda q, ps=ps, dc=dc, g=g: q.scalar_tensor_tensor(out=xT[:, dc, g * 512:(g + 1) * 512], in0=ps[:], scalar=mod_G(l, w3, dc, g), in1=xT[:, dc, g * 512:(g + 1) * 512], op0=ALU.mult, op1=ALU.add),
                        [bp, bmod, bx[dc][g]], [bx[dc][g]])

    def mixer(l):
        areset()
        norm_mod(l, 1)
        areset()
        catT = aalloc(8 * TOK // 2).bitcast(BF16).rearrange("p (c n) -> p c n", c=8)
        bcat = [[nb(f"cat{c}_{g}") for g in range(NG)] for c in range(8)]
        base_off = state["aoff"]

        def inproj(wv, k, g, ps):
            for c in range(8):
                PE(lambda q, c=c: q.matmul(ps[0][:], lhsT=wv[:, c, k * 128:(k + 1) * 128], rhs=hT[:, c, g * 512:(g + 1) * 512], start=(c == 0), stop=(c == 7)),
                   [ps[2], bh[c][g]], [ps[1]])

        uT = aalloc(2 * TOK // 2).bitcast(BF16).rearrange("p (c n) -> p c n", c=2)
        buT = nb("uT")
        ucs = aalloc(NT * 2 * 256 // 2).bitcast(BF16).rearrange("p (t c n) -> p t c n", t=NT, c=2)
        bucs = nb("ucs")
        sl, bs = wnext()
        wv = sl[:, 0:2048].rearrange("p (c n) -> p c n", c=8)
        for k in range(2):
            for g in range(NG):
                ps, bp = nps()
                inproj(wv, k, g, (ps, bp, bs))
                ACT(lambda q, ps=ps, k=k, g=g: q.activation(out=uT[:, k, g * 512:(g + 1) * 512], in_=ps[:], func=AF.Copy), [bp], [buT])
        dft64 = cb[:, B_DFT:B_DFT + 256]
        for t in range(NT):
            ps, bp = nps()
            for k in range(2):
                PE(lambda q, ps=ps, k=k, t=t: q.matmul(ps[:, k * 256:(k + 1) * 256], lhsT=uT[:, k, t * 128:(t + 1) * 128], rhs=dft64, start=True, stop=True), [buT, bcb], [bp])
            DVE(lambda q, ps=ps, t=t: q.tensor_copy(out=ucs[:, t, :, :], in_=ps[:].rearrange("p (c n) -> p c n", c=2)), [bp], [bucs])
        for hf in range(2):
            slc, bsc_ = wnext()
            sls, bss = wnext(keep_prev=True)
            cl = slc[:, 0:4096].rearrange("p (c n) -> p c n", c=8)
            sn = sls[:, 0:4096].rearrange("p (c n) -> p c n", c=8)
            for k in range(2):
                ps, bp = nps()
                for lc in range(8):
                    PE(lambda q, ps=ps, k=k, lc=lc, cl=cl: q.matmul(ps[:], lhsT=ucs[:, lc, k, 0:128], rhs=cl[:, lc, :], start=(lc == 0), stop=False), [bucs, bsc_], [bp])
                    PE(lambda q, ps=ps, k=k, lc=lc, sn=sn: q.matmul(ps[:], lhsT=ucs[:, lc, k, 128:256], rhs=sn[:, lc, :], start=False, stop=(lc == 7)), [bucs, bss], [bp])
                ACT(lambda q, ps=ps, k=k, hf=hf: q.activation(out=catT[:, k, hf * 512:(hf + 1) * 512], in_=ps[:], func=AF.Copy), [bp], [bcat[k][hf]])
        slp, bsp = wnext()
        pc = slp[:, 0:512].rearrange("p (c n) -> p c n", c=2)
        pn = slp[:, 512:1024].rearrange("p (c n) -> p c n", c=2)
        for k in range(2):
            ps, bp = nps()
            for sq_ in range(2):
                for lc in range(2):
                    tt = 8 + sq_ * 2 + lc
                    PE(lambda q, ps=ps, k=k, lc=lc, tt=tt, sq_=sq_: q.matmul(ps[:, sq_ * 256:(sq_ + 1) * 256], lhsT=ucs[:, tt, k, 0:128], rhs=pc[:, lc, :], start=(lc == 0), stop=False), [bucs, bsp], [bp])
                    PE(lambda q, ps=ps, k=k, lc=lc, tt=tt, sq_=sq_: q.matmul(ps[:, sq_ * 256:(sq_ + 1) * 256], lhsT=ucs[:, tt, k, 128:256], rhs=pn[:, lc, :], start=False, stop=(lc == 1)), [bucs, bsp], [bp])
            ACT(lambda q, ps=ps, k=k: q.activation(out=catT[:, k, 1024:1536], in_=ps[:], func=AF.Copy), [bp], [bcat[k][2]])

        if ENABLE_HGRN:
            for pr in range(2):
                if DEBUG_PAIR is not None and pr != DEBUG_PAIR:
                    wnext(); wnext()
                    for g in range(NG):
                        POOL(lambda q, pr=pr, g=g: q.memset(catT[:, 2 + pr, g * 512:(g + 1) * 512], 0.0), [], [bcat[2 + pr][g]])
                    continue
                state["aoff"] = base_off
                qf, sF, sB, tmpb, kb, hoT = [aalloc(TOK) for _ in range(6)]
                bqf, bsF, bsB, btmpb, bkb, bho = [nb(n) for n in ("qf", "sF", "sB", "tmpb", "kb", "hoT")]
                qt = [aalloc(TOK // 2).bitcast(BF16) for _ in range(2)]
                kt_ = [aalloc(TOK // 2).bitcast(BF16) for _ in range(2)]
                vT = aalloc(TOK // 2).bitcast(BF16)
                sgT = aalloc(TOK // 2).bitcast(BF16)
                bqt, bkt = [nb("qt0"), nb("qt1")], [nb("kt0"), nb("kt1")]
                bvT, bsg = nb("vT"), nb("sgT")
                ktp = [aalloc(128).bitcast(BF16).rearrange("p (a n) -> p a n", a=2) for _ in range(2)]
                vpd = [aalloc(128).bitcast(BF16).rearrange("p (a n) -> p a n", a=2) for _ in range(2)]
                bktp, bvpd = [nb("ktp0"), nb("ktp1")], [nb("vpd0"), nb("vpd1")]
                atb = [aalloc(64).bitcast(BF16) for _ in range(2)]
                batb = [nb("at0"), nb("at1")]
                S = aalloc(128); tS = aalloc(128); S0m = aalloc(64).bitcast(BF16)
                bS, btS, bS0m = nb("S"), nb("tS"), nb("S0m")
                sst = [aalloc(128), aalloc(128)]
                bsst = [nb("sst0"), nb("sst1")]
                scl = [[aalloc(48) for _ in range(4)] for _ in range(2)]
                bscl = [nb("scl0"), nb("scl1")]
                lbp = aalloc(8)
                blbp = nb("lbp")
                for i in range(2):
                    POOL(lambda q, i=i: q.memset(ktp[i], 0.0), [], [bktp[i]])
                    POOL(lambda q, i=i: q.memset(vpd[i], 0.0), [], [bvpd[i]])
                for dr in range(2):
                    c0 = dr * 3
                    if l == 0:
                        POOL(lambda q, c0=c0: q.memset(lbp[:, c0:c0 + 1], 1.0), [], [blbp])
                        POOL(lambda q, c0=c0: q.memset(lbp[:, c0 + 1:c0 + 2], 1e-30), [], [blbp])
                        POOL(lambda q, c0=c0: q.memset(lbp[:, c0 + 2:c0 + 3], -1.0), [], [blbp])
                    else:
                        a0 = P_LBS + 0 * 4 + dr * 2 + pr
                        a1 = P_LBS + 1 * 4 + dr * 2 + pr
                        DVE(lambda q, c0=c0, a0=a0, a1=a1: q.tensor_tensor(out=lbp[:, c0 + 1:c0 + 2], in0=par[:, a0:a0 + 1], in1=par[:, a1:a1 + 1], op=ALU.subtract), [bpar], [blbp])
                        ACT(lambda q, c0=c0: q.activation(out=lbp[:, c0 + 1:c0 + 2], in_=lbp[:, c0 + 1:c0 + 2], func=AF.Exp), [blbp], [blbp])
                        DVE(lambda q, c0=c0: q.tensor_scalar_add(out=lbp[:, c0 + 1:c0 + 2], in0=lbp[:, c0 + 1:c0 + 2], scalar1=1.0), [blbp], [blbp])
                        DVE(lambda q, c0=c0: q.reciprocal(out=lbp[:, c0 + 1:c0 + 2], in_=lbp[:, c0 + 1:c0 + 2]), [blbp], [blbp])
                        DVE(lambda q, c0=c0: q.tensor_scalar(out=lbp[:, c0:c0 + 1], in0=lbp[:, c0 + 1:c0 + 2], scalar1=-1.0, scalar2=1.0, op0=ALU.mult, op1=ALU.add), [blbp], [blbp])
                        DVE(lambda q, c0=c0: q.tensor_scalar_mul(out=lbp[:, c0 + 2:c0 + 3], in0=lbp[:, c0:c0 + 1], scalar1=-1.0), [blbp], [blbp])
                        DVE(lambda q, c0=c0: q.tensor_scalar_max(out=lbp[:, c0 + 1:c0 + 2], in0=lbp[:, c0 + 1:c0 + 2], scalar1=1e-30), [blbp], [blbp])
                sl, bs = wnext()
                for k in range(3):
                    wv = sl[:, k * 1024:(k + 1) * 1024].rearrange("p (c n) -> p c n", c=8)
                    for g in range(NG):
                        ps, bp = nps()
                        inproj(wv, 0, g, (ps, bp, bs))
                        gs = slice(g * 512, (g + 1) * 512)
                        if k == 0:
                            ACT(lambda q, ps=ps, gs=gs: q.activation(out=qf[:, gs], in_=ps[:], func=AF.Copy, scale=0.125), [bp], [bqf])
                        elif k == 1:
                            ACT(lambda q, ps=ps, gs=gs: q.activation(out=vT[:, gs], in_=ps[:], func=AF.Copy), [bp], [bvT])
                        else:
                            ACT(lambda q, ps=ps, gs=gs: q.activation(out=sF[:, gs], in_=ps[:], func=AF.Exp, scale=-1.0), [bp], [bsF])
                sl, bs = wnext()
                for k in range(2):
                    wv = sl[:, k * 1024:(k + 1) * 1024].rearrange("p (c n) -> p c n", c=8)
                    for g in range(NG):
                        ps, bp = nps()
                        inproj(wv, 0, g, (ps, bp, bs))
                        gs = slice(g * 512, (g + 1) * 512)
                        if k == 0:
                            ACT(lambda q, ps=ps, gs=gs: q.activation(out=sB[:, gs], in_=ps[:], func=AF.Exp, scale=-1.0), [bp], [bsB])
                        else:
                            ACT(lambda q, ps=ps, gs=gs: q.activation(out=tmpb[:, gs], in_=ps[:], func=AF.Exp, scale=-1.0), [bp], [btmpb])
                            DVE(lambda q, gs=gs: q.tensor_scalar_add(out=tmpb[:, gs], in0=tmpb[:, gs], scalar1=1.0), [btmpb], [btmpb])
                            DVE(lambda q, gs=gs: q.reciprocal(out=tmpb[:, gs], in_=tmpb[:, gs]), [btmpb], [btmpb])
                            DVE(lambda q, ps=ps, gs=gs: q.tensor_tensor(out=sgT[:, gs], in0=ps[:], in1=tmpb[:, gs], op=ALU.mult), [bp, btmpb], [bsg])
                for (sx, bsx) in ((sF, bsF), (sB, bsB)):
                    DVE(lambda q, sx=sx: q.tensor_scalar_add(out=sx, in0=sx, scalar1=1.0), [bsx], [bsx])
                    DVE(lambda q, sx=sx: q.reciprocal(out=sx, in_=sx), [bsx], [bsx])
                CH = 32
                NCH = TOK // CH
                SPC = 256 // CH
                for dr, (sx, bsx) in enumerate(((sF, bsF), (sB, bsB))):
                    c0 = dr * 3
                    DVE(lambda q, sx=sx, c0=c0: q.tensor_scalar(out=kb, in0=sx, scalar1=lbp[:, c0 + 2:c0 + 3], scalar2=lbp[:, c0:c0 + 1], op0=ALU.mult, op1=ALU.add), [bsx, blbp], [bkb])
                    DVE(lambda q, sx=sx, c0=c0: q.tensor_scalar(out=sx, in0=sx, scalar1=lbp[:, c0:c0 + 1], scalar2=lbp[:, c0 + 1:c0 + 2], op0=ALU.mult, op1=ALU.add), [bsx, blbp], [bsx])
                    ACT(lambda q, sx=sx: q.activation(out=sx, in_=sx, func=AF.Ln), [bsx], [bsx])
                    A_, bA_, B_, bB_ = sx, bsx, tmpb, btmpb
                    sh = 1
                    while sh < CH:
                        A3 = A_.rearrange("p (c n) -> p c n", n=CH)
                        B3 = B_.rearrange("p (c n) -> p c n", n=CH)
                        if dr == 0:
                            ACT(lambda q, A3=A3, B3=B3, sh=sh: q.activation(out=B3[:, :, 0:sh], in_=A3[:, :, 0:sh], func=AF.Copy), [bA_], [bB_])
                            DVE(lambda q, A3=A3, B3=B3, sh=sh: q.tensor_tensor(out=B3[:, :, sh:CH], in0=A3[:, :, sh:CH], in1=A3[:, :, 0:CH - sh], op=ALU.add), [bA_], [bB_])
                        else:
                            ACT(lambda q, A3=A3, B3=B3, sh=sh: q.activation(out=B3[:, :, CH - sh:CH], in_=A3[:, :, CH - sh:CH], func=AF.Copy), [bA_], [bB_])
                            DVE(lambda q, A3=A3, B3=B3, sh=sh: q.tensor_tensor(out=B3[:, :, 0:CH - sh], in0=A3[:, :, 0:CH - sh], in1=A3[:, :, sh:CH], op=ALU.add), [bA_], [bB_])
                        A_, bA_, B_, bB_ = B_, bB_, A_, bA_
                        sh *= 2
                    bq_, bbq_, tq_, btq_ = A_, bA_, B_, bB_
                    b3 = bq_.rearrange("p (c n) -> p c n", n=CH)
                    mid, Ti = (CH // 2 - 1, CH - 1) if dr == 0 else (CH // 2, 0)
                    bm, em, eT, eTm = scl[dr]
                    DVE(lambda q, b3=b3, mid=mid, bm=bm: q.tensor_copy(out=bm, in_=b3[:, :, mid]), [bbq_], [bscl[dr]])
                    ACT(lambda q, bm=bm, em=em: q.activation(out=em, in_=bm, func=AF.Exp), [bscl[dr]], [bscl[dr]])
                    ACT(lambda q, b3=b3, Ti=Ti, eT=eT: q.activation(out=eT, in_=b3[:, :, Ti], func=AF.Exp), [bbq_], [bscl[dr]])
                    DVE(lambda q, b3=b3, Ti=Ti, bm=bm, eTm=eTm: q.tensor_tensor(out=eTm, in0=b3[:, :, Ti], in1=bm, op=ALU.subtract), [bbq_, bscl[dr]], [bscl[dr]])
                    ACT(lambda q, eTm=eTm: q.activation(out=eTm, in_=eTm, func=AF.Exp), [bscl[dr]], [bscl[dr]])
                    DVE(lambda q, b3=b3, bm=bm: q.tensor_tensor(out=b3, in0=b3, in1=bm.unsqueeze(2).to_broadcast([128, NCH, CH]), op=ALU.subtract), [bbq_, bscl[dr]], [bbq_])
                    ACT(lambda q, bq_=bq_, tq_=tq_: q.activation(out=tq_, in_=bq_, func=AF.Exp), [bbq_], [btq_])
                    DVE(lambda q, dr=dr, tq_=tq_: q.tensor_tensor(out=qt[dr], in0=qf, in1=tq_, op=ALU.mult), [bqf, btq_], [bqt[dr]])
                    ACT(lambda q, bq_=bq_, tq_=tq_: q.activation(out=tq_, in_=bq_, func=AF.Exp, scale=-1.0), [bbq_], [btq_])
                    DVE(lambda q, dr=dr, tq_=tq_: q.tensor_tensor(out=kt_[dr], in0=kb, in1=tq_, op=ALU.mult), [bkb, btq_], [bkt[dr]])
                ev = [0]

                def emit_state(slot, dr):
                    i = ev[0] % 2
                    ev[0] += 1
                    ACT(lambda q, i=i: q.activation(out=sst[i], in_=S, func=AF.Copy), [bS], [bsst[i]])
                    for hd in range(2):
                        DMA("sp", sout[l, slot, dr, 2 * pr + hd], sst[i][hd * 64:(hd + 1) * 64, hd * 64:(hd + 1) * 64], bsst[i], reads=[bsst[i]])

                def load_state(dr):
                    POOL(lambda q: q.memset(S, 0.0), [bS], [bS])
                    for hd in range(2):
                        DMA("sp", S[hd * 64:(hd + 1) * 64, hd * 64:(hd + 1) * 64], s0_d[l, dr, 2 * pr + hd], bS, reads=[bS], writes=[bS])

                Xb = [aalloc(128), aalloc(128)]
                bXb = [nb("X0"), nb("X1")]
                for dr in range(2):
                    bm, em, eT, eTm = scl[dr]
                    order = list(range(NCH)) if dr == 0 else list(range(NCH - 1, -1, -1))
                    mcol = C_MF if dr == 0 else C_MB
                    pst = {}

                    def prep(n_, dr=dr, eTm=eTm, mcol=mcol, order=order, pst=pst):
                        ch = order[n_]
                        i = n_ % 2
                        cs = slice(ch * CH, (ch + 1) * CH)
                        ptk, bptk = nps()
                        ptkb = ptk[:].bitcast(BF16)
                        PE(lambda q: q.transpose(ptkb[0:CH, 0:128], kt_[dr][:, cs], ident_b), [bkt[dr], bcb], [bptk])
                        PE(lambda q: q.transpose(ptkb[0:CH, 128:256], vT[:, cs], ident_b), [bvT, bcb], [bptk])
                        ACT(lambda q: q.activation(out=ktp[i][0:CH, 0, 0:64], in_=ptkb[0:CH, 0:64], func=AF.Copy), [bptk], [bktp[i]])
                        ACT(lambda q: q.activation(out=ktp[i][0:CH, 1, 64:128], in_=ptkb[0:CH, 64:128], func=AF.Copy), [bptk], [bktp[i]])
                        ACT(lambda q: q.activation(out=vpd[i][0:CH, 0, 0:64], in_=ptkb[0:CH, 128:192], func=AF.Copy), [bptk], [bvpd[i]])
                        ACT(lambda q: q.activation(out=vpd[i][0:CH, 1, 64:128], in_=ptkb[0:CH, 192:256], func=AF.Copy), [bptk], [bvpd[i]])
                        for hd in range(2):
                            hs = slice(hd * 64, (hd + 1) * 64)
                            pA, bpA = nps()
                            PE(lambda q, pA=pA, hs=hs: q.matmul(pA[0:CH, 0:CH], lhsT=kt_[dr][hs, cs], rhs=qt[dr][hs, cs], start=True, stop=True), [bkt[dr], bqt[dr]], [bpA])
                            DVE(lambda q, pA=pA, hd=hd: q.tensor_tensor(out=atb[i][0:CH, hd * CH:(hd + 1) * CH], in0=pA[0:CH, 0:CH], in1=cf[0:CH, mcol:mcol + CH], op=ALU.mult), [bpA, bcf], [batb[i]])
                        pS2, bpS2 = nps()
                        PE(lambda q: q.matmul(pS2[:, 0:128], lhsT=ktp[i][0:CH, 0, :], rhs=vpd[i][0:CH, 0, :], start=True, stop=False), [bktp[i], bvpd[i]], [bpS2])
                        PE(lambda q: q.matmul(pS2[:, 0:128], lhsT=ktp[i][0:CH, 1, :], rhs=vpd[i][0:CH, 1, :], start=False, stop=True), [bktp[i], bvpd[i]], [bpS2])
                        ACT(lambda q: q.activation(out=Xb[i], in_=pS2[:, 0:128], func=AF.Copy, scale=eTm[:, ch:ch + 1]), [bpS2, bscl[dr]], [bXb[i]])

                    def chain(n_, dr=dr, em=em, eT=eT, order=order):
                        ch = order[n_]
                        i = n_ % 2
                        cs = slice(ch * CH, (ch + 1) * CH)
                        first_of_seq = (ch % SPC == 0) if dr == 0 else (ch % SPC == SPC - 1)
                        if first_of_seq:
                            seq = ch // SPC
                            prev = seq - 1 if dr == 0 else seq + 1
                            if n_ > 0:
                                emit_state(prev, dr)
                            if (dr == 0 and seq == 0) or (dr == 1 and seq == 3):
                                load_state(dr)
                            elif seq >= 4:
                                POOL(lambda q: q.memset(S, 0.0), [bS], [bS])
                            else:
                                DVE(lambda q: q.tensor_scalar_mul(out=S, in0=S, scalar1=par[:, P_KEEP:P_KEEP + 1]), [bS, bpar], [bS])
                        DVE(lambda q: q.tensor_scalar_mul(out=S0m, in0=S, scalar1=em[:, ch:ch + 1]), [bS, bscl[dr]], [bS0m])
                        DVE(lambda q: q.scalar_tensor_tensor(out=S, in0=S, scalar=eT[:, ch:ch + 1], in1=Xb[i], op0=ALU.mult, op1=ALU.add), [bS, bscl[dr], bXb[i]], [bS])
                        po, bpo = nps()
                        PE(lambda q: q.matmul(po[:, 0:CH], lhsT=vpd[i][0:CH, 0, :], rhs=atb[i][0:CH, 0:CH], start=True, stop=False), [bvpd[i], batb[i]], [bpo])
                        PE(lambda q: q.matmul(po[:, 0:CH], lhsT=vpd[i][0:CH, 1, :], rhs=atb[i][0:CH, CH:2 * CH], start=False, stop=False), [bvpd[i], batb[i]], [bpo])
                        PE(lambda q: q.matmul(po[:, 0:CH], lhsT=S0m, rhs=qt[dr][:, cs], start=False, stop=True), [bS0m, bqt[dr]], [bpo])
                        if dr == 0:
                            ACT(lambda q: q.activation(out=hoT[:, cs], in_=po[:, 0:CH], func=AF.Copy), [bpo], [bho])
                        else:
                            DVE(lambda q: q.tensor_tensor(out=hoT[:, cs], in0=po[:, 0:CH], in1=hoT[:, cs], op=ALU.add), [bpo, bho], [bho])

                    prep(0)
                    for n_ in range(NCH):
                        if n_ + 1 < NCH:
                            prep(n_ + 1)
                        chain(n_)
                    emit_state(0 if dr == 1 else 5, dr)
                for g in range(NG):
                    gs = slice(g * 512, (g + 1) * 512)
                    sqh = tmpb[:, 0:256].bitcast(BF16)
                    ACT(lambda q, gs=gs, sqh=sqh: q.activation(out=sqh, in_=hoT[:, gs], func=AF.Square), [bho], [btmpb])
                    p2, bp2 = nps()
                    PE(lambda q, p2=p2, sqh=sqh: q.matmul(p2[:], lhsT=blk_b, rhs=sqh, start=True, stop=True), [btmpb, bcb], [bp2])
                    rv = kb[:, 0:512]
                    DVE(lambda q, p2=p2, rv=rv: q.tensor_scalar(out=rv, in0=p2[:], scalar1=1.0 / 64, scalar2=EPS, op0=ALU.mult, op1=ALU.add), [bp2], [bkb])
                    ACT(lambda q, rv=rv: q.activation(out=rv, in_=rv, func=AF.Ln), [bkb], [bkb])
                    ACT(lambda q, rv=rv: q.activation(out=rv, in_=rv, func=AF.Exp, scale=-0.5), [bkb], [bkb])
                    DVE(lambda q, rv=rv, gs=gs, pr=pr: q.scalar_tensor_tensor(out=rv, in0=hoT[:, gs], scalar=par[:, P_HNRM + l * 2 + pr:P_HNRM + l * 2 + pr + 1], in1=rv, op0=ALU.mult, op1=ALU.mult), [bho, bkb, bpar], [bkb])
                    DVE(lambda q, rv=rv, gs=gs, g=g, pr=pr: q.tensor_tensor(out=catT[:, 2 + pr, gs], in0=rv, in1=sgT[:, gs], op=ALU.mult), [bkb, bsg], [bcat[2 + pr][g]])
        else:
            for pr in range(2):
                wnext()
                wnext()
                for g in range(NG):
                    POOL(lambda q, pr=pr, g=g: q.memset(catT[:, 2 + pr, g * 512:(g + 1) * 512], 0.0), [], [bcat[2 + pr][g]])
        if ENABLE_ATTN:
            state["aoff"] = base_off
            qT = aalloc(4 * TOK // 2).bitcast(BF16).rearrange("p (c n) -> p c n", c=4)
            kdT = aalloc(4 * TOK // 2).bitcast(BF16).rearrange("p (c v n) -> p c v n", c=2, v=2)
            kcT = aalloc(4 * 512 // 2).bitcast(BF16).rearrange("p (c v n) -> p c v n", c=2, v=2)
            vaug = aalloc(16 * 2 * 96 // 2).bitcast(BF16).rearrange("p (t k n) -> p t k n", t=16, k=2)
            kst = aalloc(NT * 128).rearrange("p (t n) -> p t n", t=NT)
            vst = aalloc(NT * 128).rearrange("p (t n) -> p t n", t=NT)
            kcd = aalloc(4 * 256).rearrange("p (t k n) -> p t k n", t=4, k=2)
            zq = [aalloc(512), aalloc(512)]
            rr = [aalloc(512), aalloc(512)]
            t1 = [aalloc(512)] * 2
            sqa = [aalloc(256).bitcast(BF16), aalloc(256).bitcast(BF16)]
            pT = [aalloc(256).bitcast(BF16) for _ in range(3)]
            otok = [aalloc(256).bitcast(BF16).rearrange("p (h n) -> p h n", h=8), aalloc(256).bitcast(BF16).rearrange("p (h n) -> p h n", h=8)]
            den = [aalloc(4), aalloc(4)]
            esink = aalloc(8)
            bq = [[nb("q") for g in range(NG)] for c in range(4)]
            bkd = [[nb("kd") for g in range(NG)] for c in range(2)]
            bkc, bva, bkst, bvst, bkcd, besk = nb("kc"), [nb("va") for t in range(16)], nb("kst"), nb("vst"), nb("kcd"), nb("esk")
            bzq, brr, bt1, bsqa, bpT, botok, bden = [[nb(n + str(i)) for i in range(3)] for n in ("zq", "rr", "t1", "sqa", "pT", "otok", "den")]
            bt1[1] = bt1[0]
            ACT(lambda q: q.activation(out=esink, in_=par[:, P_SINK + l * 8:P_SINK + l * 8 + 8], func=AF.Exp), [bpar], [besk])
            POOL(lambda q: q.memset(vaug[:, :, :, 64:66], 1.0), [], list(bva))
            allkd = [bkd[c_][g_] for c_ in range(2) for g_ in range(NG)]
            POOL(lambda q: q.memset(kdT[64:128, :, 0, :], 0.0), [], allkd)
            POOL(lambda q: q.memset(kdT[0:64, :, 1, :], 0.0), [], allkd)
            POOL(lambda q: q.memset(kcT[64:128, :, 0, :], 0.0), [], [bkc])
            POOL(lambda q: q.memset(kcT[0:64, :, 1, :], 0.0), [], [bkc])
            for kv in (range(2) if "cachek" not in SKIP else []):
                for dup in range(2):
                    DMA("sp", kcd[:, :, kv, dup * 64:(dup + 1) * 64], ck_d[l][:, kv * 64:(kv + 1) * 64].rearrange("(t p) d -> p t d", p=128), bkcd, writes=[bkcd])
            for t in (range(4) if "cachev" not in SKIP else []):
                DMA("pool", vaug[:, 12 + t, :, 0:64], cv_d[l][t * 128:(t + 1) * 128, :].rearrange("p (k d) -> p k d", k=2), bva[12 + t], writes=[bva[12 + t]])
            for kv in (range(2) if "cachek" not in SKIP else []):
                ps, bp = nps()
                for t in range(4):
                    PE(lambda q, ps=ps, t=t, kv=kv: q.transpose(ps[:, t * 128:(t + 1) * 128], kcd[:, t, kv, :], ident), [bkcd, bcf], [bp])
                ACT(lambda q, ps=ps, kv=kv: q.activation(out=kcT[0:64, kv, 0, :], in_=ps[0:64, :], func=AF.Copy), [bp], [bkc])
                ACT(lambda q, ps=ps, kv=kv: q.activation(out=kcT[64:128, kv, 1, :], in_=ps[64:128, :], func=AF.Copy), [bp], [bkc])
            cnt = [0]

            def qk_post(ps, bp, g, nw_col, dst, bdst, scale_extra, kfin=None):
                s = cnt[0] % 2
                cnt[0] += 1
                ACT(lambda q: q.activation(out=zq[s], in_=ps[:], func=AF.Copy), [bp], [bzq[s]])
                ACT(lambda q: q.activation(out=sqa[s], in_=ps[:], func=AF.Square), [bp], [bsqa[s]])
                p2, bp2 = nps()
                PE(lambda q: q.matmul(p2[:], lhsT=blk_b, rhs=sqa[s], start=True, stop=True), [bsqa[s], bcb], [bp2])
                DVE(lambda q: q.tensor_scalar(out=rr[s], in0=p2[:], scalar1=1.0 / 64, scalar2=EPS, op0=ALU.mult, op1=ALU.add), [bp2], [brr[s]])
                ACT(lambda q: q.activation(out=rr[s], in_=rr[s], func=AF.Ln), [brr[s]], [brr[s]])
                ACT(lambda q: q.activation(out=rr[s], in_=rr[s], func=AF.Exp, scale=-0.5), [brr[s]], [brr[s]])
                DVE(lambda q: q.scalar_tensor_tensor(out=zq[s], in0=zq[s], scalar=par[:, nw_col:nw_col + 1], in1=rr[s], op0=ALU.mult, op1=ALU.mult), [bzq[s], brr[s], bpar], [bzq[s]])
                if g < 2 and "rope" not in SKIP:
                    p3, bp3 = nps()
                    PE(lambda q: q.matmul(p3[:], lhsT=cf[:, C_PERM:C_PERM + 128], rhs=zq[s], start=True, stop=True), [bzq[s], bcf], [bp3])
                    DVE(lambda q: q.tensor_tensor(out=t1[s], in0=p3[:], in1=cf[:, C_SIN + g * 512:C_SIN + (g + 1) * 512], op=ALU.mult), [bp3, bcf], [bt1[s]])
                    DVE(lambda q: q.tensor_tensor(out=zq[s], in0=zq[s], in1=cf[:, C_COS + g * 512:C_COS + (g + 1) * 512], op=ALU.mult), [bzq[s], bcf], [bzq[s]])
                    DVE(lambda q: q.tensor_tensor(out=zq[s], in0=zq[s], in1=t1[s], op=ALU.add), [bzq[s], bt1[s]], [bzq[s]])
                if kfin is None:
                    ACT(lambda q: q.activation(out=dst, in_=zq[s], func=AF.Copy, scale=scale_extra), [bzq[s]], [bdst])
                else:
                    ACT(lambda q: q.activation(out=dst[0][0:64, :], in_=zq[s][0:64, :], func=AF.Copy), [bzq[s]], [bdst])
                    ACT(lambda q: q.activation(out=dst[1][64:128, :], in_=zq[s][64:128, :], func=AF.Copy), [bzq[s]], [bdst])
                if kfin is not None and "ktr" not in SKIP:
                    kv = kfin
                    p4, bp4 = nps()
                    for tt in range(4):
                        PE(lambda q, tt=tt: q.transpose(p4[:, tt * 64:(tt + 1) * 64], zq[s][0:64, tt * 128:(tt + 1) * 128], ident[0:64, 0:64]), [bzq[s], bcf], [bp4])
                    DVE(lambda q: q.tensor_copy(out=kst[:, g * 4:(g + 1) * 4, kv * 64:(kv + 1) * 64], in_=p4[:, 0:256].rearrange("p (t n) -> p t n", t=4)), [bp4], [bkst])

            sl, bs = wnext()
            wv = sl[:, 0:4096].rearrange("p (c n) -> p c n", c=8)
            for c in range(4):
                for g in range(NG):
                    ps, bp = nps()
                    inproj(wv, c, g, (ps, bp, bs))
                    qk_post(ps, bp, g, P_QN + l, qT[:, c, g * 512:(g + 1) * 512], bq[c][g], 0.125)
            sl, bs = wnext()
            for kv in range(2):
                wv = sl[:, kv * 1024:(kv + 1) * 1024].rearrange("p (c n) -> p c n", c=8)
                for g in range(NG):
                    ps, bp = nps()
                    inproj(wv, 0, g, (ps, bp, bs))
                    qk_post(ps, bp, g, P_KN + l, (kdT[:, kv, 0, g * 512:(g + 1) * 512], kdT[:, kv, 1, g * 512:(g + 1) * 512]), bkd[kv][g], 1.0, kfin=kv)
            wv = sl[:, 2048:3072].rearrange("p (c n) -> p c n", c=8)
            for g in range(NG):
                ps, bp = nps()
                inproj(wv, 0, g, (ps, bp, bs))
                s = cnt[0] % 2
                cnt[0] += 1
                ACT(lambda q, ps=ps, s=s: q.activation(out=zq[s], in_=ps[:], func=AF.Copy), [bp], [bzq[s]])
                if "vtr" in SKIP:
                    continue
                p4, bp4 = nps()
                for tt in range(4):
                    PE(lambda q, tt=tt, s=s, p4=p4: q.transpose(p4[:, tt * 128:(tt + 1) * 128], zq[s][:, tt * 128:(tt + 1) * 128], ident), [bzq[s], bcf], [bp4])
                if "vtr_dve" not in SKIP:
                    DVE(lambda q, p4=p4, g=g: q.tensor_copy(out=vst[:, g * 4:(g + 1) * 4, :], in_=p4[:].rearrange("p (t n) -> p t n", t=4)), [bp4], [bvst])
                for tt in (range(4) if "vtr_act" not in SKIP else []):
                    ACT(lambda q, p4=p4, g=g, tt=tt: q.activation(out=vaug[:, g * 4 + tt, :, 0:64], in_=p4[:, tt * 128:(tt + 1) * 128].rearrange("p (k d) -> p k d", k=2), func=AF.Copy), [bp4], [bva[g * 4 + tt]])
            if "kvout" not in SKIP:
                DMA("sp", kout[l].rearrange("(t p) n -> p t n", p=128), kst, bkst, reads=[bkst])
            if "kvout" not in SKIP:
                DMA("sp", vout[l].rearrange("(t p) n -> p t n", p=128), vst, bvst, reads=[bvst])
            it = 0
            if not ATTN_CORE:
                for c in range(4):
                    for g in range(NG):
                        POOL(lambda q, c=c, g=g: q.memset(catT[:, 4 + c, g * 512:(g + 1) * 512], 0.0), [], [bcat[4 + c][g]])
            for j in (range(NT) if ATTN_CORE else []):
                g = j // 4
                if j < 8:
                    kts = [("n", j + d_, (j * 3 + 1 + d_) if d_ != 0 else None) for d_ in (-1, 0, 1) if 0 <= j + d_ < 8] + [("c", t, None) for t in range(4)]
                else:
                    b0 = 8 + 2 * ((j - 8) // 2)
                    kts = [("n", b0, None), ("n", b0 + 1, None)]
                so = j % 2
                for kv in range(2):
                    po, bpo = nps()
                    pov = po[:, 0:264].rearrange("p (h n) -> p h n", h=4)
                    nk = len(kts)

                    def scores(ki, j=j, kv=kv, g=g):
                        kind, kt, mi = kts[ki]
                        pS, bpS = nps()
                        for hh in range(4):
                            cq = kv * 2 + hh // 2
                            hf = hh % 2
                            if kind == "n":
                                lhs = kdT[:, kv, hf, kt * 128:(kt + 1) * 128]
                                rb = bkd[kv][kt // 4]
                            else:
                                lhs = kcT[:, kv, hf, kt * 128:(kt + 1) * 128]
                                rb = bkc
                            PE(lambda q, pS=pS, hh=hh, lhs=lhs, cq=cq, j=j: q.matmul(pS[:, hh * 128:(hh + 1) * 128], lhsT=lhs, rhs=qT[:, cq, j * 128:(j + 1) * 128], start=True, stop=True),
                               [rb, bq[cq][g]], [bpS])
                        return pS, bpS

                    LOOK = 2
                    sc = {}
                    for ki in range(min(LOOK, nk)):
                        sc[ki] = scores(ki)
                    for ki, (kind, kt, mi) in enumerate(kts):
                        pS, bpS = sc.pop(ki)
                        s = it % 3
                        it += 1
                        if kind == "c":
                            ACT(lambda q, pS=pS, s=s: q.activation(out=pT[s], in_=pS[:], func=AF.Exp, bias=par[:, P_CBIAS:P_CBIAS + 1]), [bpS, bpar], [bpT[s]])
                        else:
                            ACT(lambda q, pS=pS, s=s: q.activation(out=pT[s], in_=pS[:], func=AF.Exp), [bpS], [bpT[s]])
                        if mi is not None:
                            DVE(lambda q, s=s, mi=mi: q.tensor_tensor(out=pT[s].rearrange("p (h n) -> p h n", h=4), in0=pT[s].rearrange("p (h n) -> p h n", h=4),
                                                                      in1=cb[:, B_MASK + mi * 128:B_MASK + (mi + 1) * 128].unsqueeze(1).to_broadcast([128, 4, 128]), op=ALU.mult), [bpT[s], bcb], [bpT[s]])
                        if ki + LOOK < nk:
                            sc[ki + LOOK] = scores(ki + LOOK)
                        vt = kt if kind == "n" else 12 + kt
                        for hh in range(4):
                            PE(lambda q, pov=pov, hh=hh, s=s, vt=vt, kv=kv, ki=ki, n=nk: q.matmul(pov[:, hh, 0:66], lhsT=pT[s][:, hh * 128:(hh + 1) * 128], rhs=vaug[:, vt, kv, 0:66], start=(ki == 0 and hh == 0), stop=(ki == n - 1 and hh == 3)),
                               [bpT[s], bva[vt]], [bpo])
                    DVE(lambda q, pov=pov, so=so, kv=kv: q.tensor_tensor(out=den[so], in0=pov[:, :, 64], in1=esink[:, kv * 4:(kv + 1) * 4], op=ALU.add), [bpo, besk], [bden[so]])
                    DVE(lambda q, so=so: q.reciprocal(out=den[so], in_=den[so]), [bden[so]], [bden[so]])
                    DVE(lambda q, pov=pov, so=so, kv=kv: q.tensor_tensor(out=otok[so][:, kv * 4:(kv + 1) * 4, :], in0=pov[:, :, 0:64], in1=den[so].unsqueeze(2).to_broadcast([128, 4, 64]), op=ALU.mult), [bpo, bden[so]], [botok[so]])
                pt_, bpt = nps()
                ptb = pt_[:].bitcast(BF16)
                for c in range(4):
                    PE(lambda q, ptb=ptb, c=c, so=so: q.transpose(ptb[:, c * 128:(c + 1) * 128], otok[so][:, 2 * c:2 * c + 2, :].rearrange("p h n -> p (h n)"), ident_b), [botok[so], bcb], [bpt])
                ACT(lambda q, ptb=ptb, j=j: q.activation(out=catT[:, 4:8, j * 128:(j + 1) * 128], in_=ptb[:, 0:512].rearrange("p (c n) -> p c n", c=4), func=AF.Copy), [bpt], [bcat[4 + c][g] for c in range(4)])

        else:
            wnext()
            wnext()
            for c in range(4):
                for g in range(NG):
                    POOL(lambda q, c=c, g=g: q.memset(catT[:, 4 + c, g * 512:(g + 1) * 512], 0.0), [], [bcat[4 + c][g]])
        for i in range(2):
            sl, bs = wnext()
            wv = sl[:, 0:4096].rearrange("p (c n) -> p c n", c=8)
            for k in range(4):
                dc = i * 4 + k
                for g in range(NG):
                    ps, bp = nps()
                    for c in range(8):
                        PE(lambda q, ps=ps, wv=wv, c=c, k=k, g=g: q.matmul(ps[:], lhsT=wv[:, c, k * 128:(k + 1) * 128], rhs=catT[:, c, g * 512:(g + 1) * 512], start=(c == 0), stop=(c == 7)),
                           [bs, bcat[c][g]], [bp])
                    DVE(lambda q, ps=ps, dc=dc, g=g: q.scalar_tensor_tensor(out=xT[:, dc, g * 512:(g + 1) * 512], in0=ps[:], scalar=mod_G(l, 1, dc, g), in1=xT[:, dc, g * 512:(g + 1) * 512], op0=ALU.mult, op1=ALU.add),
                        [bp, bmod, bx[dc][g]], [bx[dc][g]])

    for l in range(2):
        ffn(l, 0)
        mixer(l)
        ffn(l, 1)

    areset()
    yst = [aalloc(1024), aalloc(1024)]
    byst = [nb("yst0"), nb("yst1")]
    for t in range(NT):
        s = t % 2
        g = t // 4
        for half in range(2):
            ps, bp = nps()
            for k in range(4):
                c = half * 4 + k
                PE(lambda q, ps=ps, k=k, c=c, t=t: q.transpose(ps[:, k * 128:(k + 1) * 128], xT[:, c, t * 128:(t + 1) * 128], ident), [bx[c][g], bcf], [bp])
            if half == 0:
                ACT(lambda q, ps=ps, s=s: q.activation(out=yst[s][:, 0:512], in_=ps[:], func=AF.Copy), [bp], [byst[s]])
            else:
                DVE(lambda q, ps=ps, s=s: q.tensor_copy(out=yst[s][:, 512:1024], in_=ps[:]), [bp], [byst[s]])
        DMA("sp", yout[t * 128:(t + 1) * 128, :], yst[s], byst[s], reads=[byst[s]])

    T.emit(nc)
    st.close()
    return nc


def _consts(sample_mode):
    cf = np.zeros((128, NCF), np.float32)
    cf[:, C_ID:C_ID + 128] = np.eye(128, dtype=np.float32)
    Pm = np.zeros((64, 64), np.float32)
    for d in range(64):
        blk = d // 16
        if blk % 2 == 0:
            Pm[d, d + 16] = -1.0
        else:
            Pm[d, d - 16] = 1.0
    P2 = np.zeros((128, 128), np.float32)
    P2[:64, :64] = Pm
    P2[64:, 64:] = Pm
    cf[:, C_PERM:C_PERM + 128] = P2.T
    t = np.arange(1024)
    inv = (10000.0 ** (-np.arange(0, 32, 2, dtype=np.float32) / 32)).astype(np.float32)
    d = np.arange(128) % 64
    pos = np.where((d < 32)[:, None], (t // 64)[None, :], (t % 64)[None, :]).astype(np.float32)
    ang = pos * inv[d % 16][:, None]
    if sample_mode:
        cf[:, C_COS:C_COS + 1024] = np.cos(ang)
        cf[:, C_SIN:C_SIN + 1024] = np.sin(ang)
    else:
        cf[:, C_COS:C_COS + 1024] = 1.0
    j = np.arange(64)
    cf[:64, C_MF:C_MF + 64] = (j[:, None] <= j[None, :]).astype(np.float32)
    cf[:64, C_MB:C_MB + 64] = (j[:, None] >= j[None, :]).astype(np.float32)
    cb = np.zeros((128, NCB), np.float32)
    cb[:, B_ONES:B_ONES + 128] = 1.0
    cb[:64, B_BLK:B_BLK + 64] = 1.0
    cb[64:, B_BLK + 64:B_BLK + 128] = 1.0
    cb[:, B_ID:B_ID + 128] = np.eye(128, dtype=np.float32)
    c = np.arange(64)
    a = 2 * np.pi * np.outer(c, c) / 64.0
    for gq in range(2):
        cb[gq * 64:(gq + 1) * 64, B_DFT + gq * 64:B_DFT + (gq + 1) * 64] = np.cos(a)
        cb[gq * 64:(gq + 1) * 64, B_DFT + 128 + gq * 64:B_DFT + 128 + (gq + 1) * 64] = np.sin(a)
    ki = np.arange(128)[:, None]
    qi = np.arange(128)[None, :]
    for jt in range(8):
        for s in range(3):
            if sample_mode:
                m = (ki >= qi) if s == 0 else (np.ones((128, 128), bool) if s == 1 else (ki <= qi))
            else:
                if s == 1:
                    m = np.ones((128, 128), bool)
                elif s == 0:
                    m = np.full((128, 128), jt % 2 == 1)
                else:
                    m = np.full((128, 128), jt % 2 == 0)
            cb[:, B_MASK + (jt * 3 + s) * 128:B_MASK + (jt * 3 + s + 1) * 128] = m.astype(np.float32)

    def dft(L):
        ll = np.arange(L)
        aa = 2 * np.pi * np.outer(ll, ll) / L
        sc_ = 1.0 / np.sqrt(L * 64.0)
        return (np.cos(aa) * sc_).astype(np.float32), (-np.sin(aa) * sc_).astype(np.float32)

    dftp = np.stack(dft(256))
    if sample_mode:
        dftl = np.stack(dft(1024))
    else:
        dftl = np.zeros((2, 1024, 1024), np.float32)
        for i in range(4):
            dftl[:, i * 256:(i + 1) * 256, i * 256:(i + 1) * 256] = dftp
    return cf, cb, dftl, dftp


_NC_CACHE = {}


def _prep(inputs):
    I = {k: np.asarray(v) for k, v in inputs.items()}
    xp, xs = I["x_prompt"], I["x_sample"]
    consts = {True: _consts(True), False: _consts(False)}
    fm = lambda v: np.ascontiguousarray(v.reshape(-1, 128).T)
    in_maps = []
    assign = []
    for core in range(8):
        sm = core < 4
        if sm:
            gtok = xs[core]
            pseq = [2 * core, 2 * core + 1]
            gseq = []
            cond = np.stack([I["c"][core], I["c_ctx"]])
        else:
            base = 8 + 6 * (core - 4)
            gseq = [base, base + 1, base + 2, base + 3]
            gtok = xp[gseq].reshape(1024, D)
            pseq = [base + 4, base + 5]
            cond = np.stack([I["c_ctx"], I["c_ctx"]])
        assign.append((sm, gseq, pseq))
        xin = np.concatenate([gtok, xp[pseq].reshape(512, D)], axis=0)
        par = np.zeros((128, NPAR), np.float32)
        par[:, P_COND:P_COND + 16] = cond.reshape(2, 8, 128).transpose(2, 1, 0).reshape(128, 16)
        for l in range(2):
            par[:, P_BADA + l * 72:P_BADA + (l + 1) * 72] = fm(I["b_ada"][l])
            for w3, nm in enumerate(["norm_ffn1", "norm_mix", "norm_ffn2"]):
                par[:, P_NRM + l * 24 + w3 * 8:P_NRM + l * 24 + w3 * 8 + 8] = fm(I[nm][l])
            for dr in range(2):
                par[:, P_LBS + l * 4 + dr * 2:P_LBS + l * 4 + dr * 2 + 2] = fm(I["hgrn_lower_bounds"][l, dr])
            par[:, P_HNRM + l * 2:P_HNRM + l * 2 + 2] = fm(I["hgrn_norm"][l])
            par[:, P_QN + l] = np.tile(I["q_norm"][l], 2)
            par[:, P_KN + l] = np.tile(I["k_norm"][l], 2)
            par[:, P_SINK + l * 8:P_SINK + (l + 1) * 8] = I["attn_sink"][l][None, :]
        par[:, P_KEEP] = 1.0 if sm else 0.0
        par[:, P_CBIAS] = 0.0 if sm else -30000.0
        cf, cb, dftl, dftp = consts[sm]
        if sm:
            ck = I["cache_attn_k"][core].reshape(2, 512, 128)
            cv = I["cache_attn_v"][core].reshape(2, 512, 128)
            s0 = I["state_hgrn"][core]
        else:
            ck = np.zeros((2, 512, 128), np.float32)
            cv = np.zeros((2, 512, 128), np.float32)
            s0 = np.zeros((2, 2, 4, 64, 64), np.float32)
        in_maps.append(dict(
            xin=np.ascontiguousarray(xin), par=par, cf=cf, cb=cb, dftl=dftl, dftp=dftp,
            ck=np.ascontiguousarray(ck), cv=np.ascontiguousarray(cv), s0=np.ascontiguousarray(s0),
            w_ada=I["w_ada"], ffn1_gu=I["ffn1_w_gate_up"], ffn2_gu=I["ffn2_w_gate_up"],
            ffn1_d=I["ffn1_w_down"], ffn2_d=I["ffn2_w_down"], w_in=I["w_in"], w_out=I["w_out"]))
    return in_maps, assign


def _post(results, assign):
    y_p = np.zeros((32, 256, D), np.float32)
    y_s = np.zeros((4, 1024, D), np.float32)
    nk = np.zeros((32, 2, 256, 2, 64), np.float32)
    nv = np.zeros((32, 2, 256, 2, 64), np.float32)
    ns = np.zeros((32, 2, 2, 4, 64, 64), np.float32)
    for core in range(8):
        r = results[core]
        sm, gseq, pseq = assign[core]
        y = r["yout"]
        ko = r["kout"].reshape(2, TOK, 2, 64)
        vo = r["vout"].reshape(2, TOK, 2, 64)
        so = r["sout"]
        if sm:
            y_s[core] = y[:1024]
        seqs = [(s, i) for i, s in enumerate(gseq)] + [(s, 4 + i) for i, s in enumerate(pseq)]
        for s, slot in seqs:
            y_p[s] = y[slot * 256:(slot + 1) * 256]
            nk[s] = ko[:, slot * 256:(slot + 1) * 256]
            nv[s] = vo[:, slot * 256:(slot + 1) * 256]
            ns[s] = so[:, slot]
    return (y_p, y_s, nk, nv, ns)


def kernel(**inputs):
    in_maps, assign = _prep(inputs)
    if "nc" not in _NC_CACHE:
        _NC_CACHE["nc"] = build()
    res = run_bass_kernel_spmd(_NC_CACHE["nc"], in_maps, core_ids=list(range(8)))
    return _post(res.results, assign)
```

```python
import numpy as np
from contextlib import ExitStack
import concourse.bass as bass
import concourse.mybir as mybir
from concourse.bass_utils import run_bass_kernel_spmd

F32 = mybir.dt.float32
BF16 = mybir.dt.bfloat16
AF = mybir.ActivationFunctionType
ALU = mybir.AluOpType

EP = 8192
ENGS = ("pe", "act", "dve", "pool", "sp")
SAME_ENGINE_SYNC = ("act", "dve", "pool")


class Buf:
    def __init__(self, name, fence=None):
        self.name = name
        self.w = None
        self.r = list(fence) if fence else []
        self.dsem = None
        self.dcnt = 0
        self.excl = False


class Op:
    pass


class Tracker:
    def __init__(self):
        self.ops = []
        self.per_eng = {e: [] for e in ENGS}
        self.dma_bufs = []
        self.dma_ids = []

    def fence(self):
        ids = [l[-1].id for l in self.per_eng.values() if l]
        return ids + list(self.dma_ids)

    def op(self, eng, fn, reads=(), writes=(), dma_buf=None):
        o = Op()
        o.eng = eng
        o.fn = fn
        o.dma_buf = dma_buf
        o.marked = False
        o.id = len(self.ops)
        o.pos = len(self.per_eng[eng])
        deps = set()
        for b in reads:
            if b.w is not None:
                deps.add(b.w)
            if b.excl:
                deps.update(b.r)
        for b in writes:
            if b.w is not None:
                deps.add(b.w)
            deps.update(b.r)
        o.deps = deps
        for b in reads:
            b.r.append(o.id)
        for b in writes:
            b.w = o.id
            b.r = []
        if dma_buf is not None:
            if dma_buf.dsem is None:
                dma_buf.dsem = True
                self.dma_bufs.append(dma_buf)
            dma_buf.dcnt += 16
            o.dval = dma_buf.dcnt
            self.dma_ids.append(o.id)
        self.ops.append(o)
        self.per_eng[eng].append(o)
        return o

    def plan(self):
        ops = self.ops
        seen = {e: {f: -1 for f in ENGS} for e in ENGS}
        seen_dma = {e: {} for e in ENGS}
        for o in ops:
            w_eng = {}
            w_dma = {}
            for d in o.deps:
                p = ops[d]
                if p.dma_buf is not None:
                    b = p.dma_buf
                    if seen_dma[o.eng].get(id(b), 0) >= p.dval:
                        continue
                    w_dma[id(b)] = (b, max(p.dval, w_dma.get(id(b), (None, 0))[1]))
                else:
                    if p.eng == o.eng:
                        if p.eng not in SAME_ENGINE_SYNC:
                            continue
                        if p.pos < o.pos - 2:
                            continue
                    if seen[o.eng][p.eng] >= d:
                        continue
                    w_eng[p.eng] = max(w_eng.get(p.eng, -1), d)
            for f, d in w_eng.items():
                seen[o.eng][f] = d
                ops[d].marked = True
            for k, (b, v) in w_dma.items():
                seen_dma[o.eng][k] = v
            o.w_eng = w_eng
            o.w_dma = list(w_dma.values())
        cnt = {e: 0 for e in ENGS}
        for o in ops:
            if o.dma_buf is None and o.marked:
                cnt[o.eng] += 1
            o.cnt = cnt[o.eng]
        self.final_cnt = cnt

    def emit(self, nc, final_eng="sp"):
        self.plan()
        ops = self.ops
        with ExitStack() as st:
            esems = {}
            for e in ENGS:
                n = max(1, (self.final_cnt[e] + EP - 1) // EP)
                esems[e] = [st.enter_context(nc.semaphore(f"s_{e}_{i}")) for i in range(n)]
            for i, b in enumerate(self.dma_bufs):
                b.dsem = st.enter_context(nc.semaphore(f"d_{i}"))
            block = st.enter_context(nc.Block())

            def run(eng_name, eng):
                for o in self.per_eng[eng_name]:
                    for f, d in o.w_eng.items():
                        c = ops[d].cnt
                        eng.wait_ge(esems[f][(c - 1) // EP], (c - 1) % EP + 1)
                    for (b, v) in o.w_dma:
                        eng.wait_ge(b.dsem, v)
                    ins = o.fn(eng)
                    if o.dma_buf is not None:
                        ins.then_inc(o.dma_buf.dsem, 16)
                    elif o.marked:
                        c = o.cnt
                        ins.then_inc(esems[eng_name][(c - 1) // EP], 1)
                if eng_name == final_eng:
                    for b in self.dma_bufs:
                        eng.wait_ge(b.dsem, b.dcnt)
                    for f in ENGS:
                        c = self.final_cnt[f]
                        if c > 0 and f != eng_name:
                            eng.wait_ge(esems[f][(c - 1) // EP], (c - 1) % EP + 1)

            @block.tensor
            def _(e):
                run("pe", e)

            @block.scalar
            def _(e):
                run("act", e)

            @block.vector
            def _(e):
                run("dve", e)

            @block.gpsimd
            def _(e):
                run("pool", e)

            @block.sync
            def _(e):
                run("sp", e)


TOK = 1536
NT = 12
NG = 3
D = 1024
DFF = 2816
EPS = 1e-6
P_COND, P_BADA, P_NRM, P_LBS, P_HNRM, P_QN, P_KN, P_SINK, P_KEEP, P_CBIAS, NPAR = 0, 16, 160, 208, 216, 220, 222, 224, 240, 241, 242
C_ID, C_PERM, C_COS, C_SIN, C_MF, C_MB, NCF = 0, 128, 256, 1280, 2304, 2368, 2432
B_ONES, B_BLK, B_ID, B_DFT, B_MASK, NCB = 0, 128, 256, 384, 640, 640 + 24 * 128
SLOT = 4096
ENABLE_HGRN = True
DEBUG_PAIR = None
ATTN_CORE = True
SKIP = set()
ENABLE_ATTN = True
NSL = 3
ARENA = 23400


def build(stop_after=None):
    nc = bass.Bass("TRN2", target_bir_lowering=False)
    T = Tracker()
    di = lambda n, s: nc.dram_tensor(n, s, F32, kind="ExternalInput").ap()
    do = lambda n, s: nc.dram_tensor(n, s, F32, kind="ExternalOutput").ap()
    xin = di("xin", [TOK, D]); par_d = di("par", [128, NPAR]); cf_d = di("cf", [128, NCF]); cb_d = di("cb", [128, NCB])
    dftl_d = di("dftl", [2, 1024, 1024]); dftp_d = di("dftp", [2, 256, 256])
    ck_d = di("ck", [2, 512, 128]); cv_d = di("cv", [2, 512, 128]); s0_d = di("s0", [2, 2, 4, 64, 64])
    wada_d = di("w_ada", [2, D, 9 * D])
    wgu_d = [di("ffn1_gu", [2, D, 2 * DFF]), di("ffn2_gu", [2, D, 2 * DFF])]
    wdn_d = [di("ffn1_d", [2, DFF, D]), di("ffn2_d", [2, DFF, D])]
    win_d = di("w_in", [2, D, 2304]); wout_d = di("w_out", [2, D, D])
    yout = do("yout", [TOK, D]); kout = do("kout", [2, TOK, 128]); vout = do("vout", [2, TOK, 128])
    sout = do("sout", [2, 6, 2, 4, 64, 64])

    st = ExitStack()
    sbt = lambda n, s, dt: st.enter_context(nc.sbuf_tensor("sb_" + n, s, dt))
    xT = sbt("xT", [128, 8, TOK], F32)
    hT = sbt("hT", [128, 8, TOK], BF16)
    par = sbt("par", [128, NPAR], F32)
    cf = sbt("cf", [128, NCF], F32)
    cb = sbt("cb", [128, NCB], BF16)
    modv = sbt("modv", [128, 2, 72, 2], F32)
    mA = sbt("mA", [128, 2, 3, 8, 2], F32)
    mG = sbt("mG", [128, 2, 3, 8, 2], F32)
    sml = sbt("sml", [128, 64], F32)
    slots = [sbt(f"slot{i}", [128, SLOT], BF16) for i in range(NSL)]
    arena = sbt("arena", [128, ARENA], F32)
    psum = [st.enter_context(nc.psum_tensor(f"ps{i}", [128, 512], F32)) for i in range(8)]
    bps = [Buf(f"ps{i}") for i in range(8)]
    for b_ in bps:
        b_.excl = True
    bslot = [Buf(f"slot{i}") for i in range(NSL)]
    bx = [[Buf(f"x{c}_{g}") for g in range(NG)] for c in range(8)]
    bh = [[Buf(f"h{c}_{g}") for g in range(NG)] for c in range(8)]
    bpar, bcf, bcb, bmod = Buf("par"), Buf("cf"), Buf("cb"), Buf("mod")
    bsml = Buf("sml")

    state = {"ps": 0, "aoff": 0}

    def nps():
        i = state["ps"] % 8
        state["ps"] += 1
        return psum[i], bps[i]

    def aalloc(n32):
        o = state["aoff"]
        state["aoff"] += n32
        assert state["aoff"] <= ARENA, state["aoff"]
        return arena[:, o:o + n32]

    def areset():
        state["aoff"] = 0

    def nb(name):
        return Buf(name, T.fence())

    PE = lambda fn, r, w: T.op("pe", fn, r, w)
    ACT = lambda fn, r, w: T.op("act", fn, r, w)
    DVE = lambda fn, r, w: T.op("dve", fn, r, w)
    POOL = lambda fn, r, w: T.op("pool", fn, r, w)

    def DMA(eng, out, in_, buf, reads=(), writes=()):
        T.op(eng, lambda q: q.dma_start(out=out, in_=in_), reads, writes, dma_buf=buf)

    wq = []
    wstate = {"loaded": 0, "used": 0}

    def witem(parts):
        wq.append(parts)

    def wnext(keep_prev=False):
        i = wstate["used"]
        oldest = i - 1 if keep_prev else i
        while wstate["loaded"] < min(len(wq), oldest + NSL):
            k = wstate["loaded"]
            s = k % NSL
            for (fn, dap) in wq[k]:
                DMA("pool", fn(slots[s]), dap, bslot[s], writes=[bslot[s]])
            wstate["loaded"] += 1
        wstate["used"] += 1
        return slots[i % NSL], bslot[i % NSL]

    def kview(ap2d):
        return ap2d.rearrange("(c p) n -> p c n", p=128)

    def sched_weights():
        for l in range(2):
            for i in range(18):
                witem([(lambda s: s[:, 0:4096].rearrange("p (c n) -> p c n", c=8), kview(wada_d[l][:, i * 512:(i + 1) * 512]))])
        for l in range(2):
            for f in range(2):
                sched_ffn(l, f)
                if f == 0:
                    sched_mixer(l)

    def sched_ffn(l, f):
        for h in range(2):
            for j0 in range(0, 11, 2):
                nf = min(2, 11 - j0)
                c0 = (h * 11 + j0) * 128
                witem([
                    (lambda s, nf=nf: s[:, 0:8 * nf * 128].rearrange("p (c n) -> p c n", c=8), kview(wgu_d[f][l][:, c0:c0 + nf * 128])),
                    (lambda s, nf=nf: s[:, 2048:2048 + 8 * nf * 128].rearrange("p (c n) -> p c n", c=8), kview(wgu_d[f][l][:, DFF + c0:DFF + c0 + nf * 128])),
                ])

    def sched_mixer(l):
        w = win_d[l]
        witem([(lambda s: s[:, 0:2048].rearrange("p (c n) -> p c n", c=8), kview(w[:, 0:256]))])
        for hf in range(2):
            for cs in range(2):
                witem([(lambda s: s[:, 0:4096].rearrange("p (c n) -> p c n", c=8), kview(dftl_d[cs][:, hf * 512:(hf + 1) * 512]))])
        witem([(lambda s, cs=cs: s[:, cs * 512:(cs + 1) * 512].rearrange("p (c n) -> p c n", c=2), kview(dftp_d[cs])) for cs in range(2)])
        for pr in range(2):
            cols = [256 + pr * 128, 512 + pr * 128, 768 + pr * 128, 1024 + pr * 128, 1280 + pr * 128]
            witem([(lambda s, k=k: s[:, k * 1024:(k + 1) * 1024].rearrange("p (c n) -> p c n", c=8), kview(w[:, cols[k]:cols[k] + 128])) for k in range(3)])
            witem([(lambda s, k=k: s[:, k * 1024:(k + 1) * 1024].rearrange("p (c n) -> p c n", c=8), kview(w[:, cols[3 + k]:cols[3 + k] + 128])) for k in range(2)])
        witem([(lambda s: s[:, 0:4096].rearrange("p (c n) -> p c n", c=8), kview(w[:, 1536:2048]))])
        parts = []
        for kv in range(2):
            for dup in range(2):
                parts.append((lambda s, kv=kv, dup=dup: s[:, kv * 1024:(kv + 1) * 1024].rearrange("p (c n) -> p c n", c=8)[:, :, dup * 64:(dup + 1) * 64],
                              kview(w[:, 2048 + kv * 64:2048 + (kv + 1) * 64])))
        parts.append((lambda s: s[:, 2048:3072].rearrange("p (c n) -> p c n", c=8), kview(w[:, 2176:2304])))
        witem(parts)
        for i in range(2):
            witem([(lambda s: s[:, 0:4096].rearrange("p (c n) -> p c n", c=8), kview(wout_d[l][:, i * 512:(i + 1) * 512]))])

    sched_weights()

    DMA("sp", par[:], par_d, bpar, writes=[bpar])
    DMA("sp", cf[:], cf_d, bcf, writes=[bcf])
    DMA("pool", cb[:], cb_d, bcb, writes=[bcb])
    ident = cf[:, C_ID:C_ID + 128]
    ones_b = cb[:, B_ONES:B_ONES + 128]
    blk_b = cb[:, B_BLK:B_BLK + 128]
    ident_b = cb[:, B_ID:B_ID + 128]

    areset()
    xst = [aalloc(1024), aalloc(1024)]
    bxst = [nb("xst0"), nb("xst1")]
    for t in range(NT):
        s = t % 2
        DMA("sp", xst[s], xin[t * 128:(t + 1) * 128, :], bxst[s], writes=[bxst[s]])
        for half in range(2):
            ps, bp = nps()
            for k in range(4):
                c = half * 4 + k
                PE(lambda q, ps=ps, k=k, c=c, s=s: q.transpose(ps[:, k * 128:(k + 1) * 128], xst[s][:, c * 128:(c + 1) * 128], ident),
                   [bxst[s], bcf], [bp])
            g = t // 4
            eng = ACT if half == 0 else DVE
            fn = (lambda q, ps=ps, half=half, t=t: q.activation(out=xT[:, half * 4:half * 4 + 4, t * 128:(t + 1) * 128], in_=ps[:].rearrange("p (k n) -> p k n", k=4), func=AF.Copy)) if half == 0 else \
                 (lambda q, ps=ps, half=half, t=t: q.tensor_copy(out=xT[:, half * 4:half * 4 + 4, t * 128:(t + 1) * 128], in_=ps[:].rearrange("p (k n) -> p k n", k=4)))
            eng(fn, [bp], [bx[c][g] for c in range(half * 4, half * 4 + 4)])

    areset()
    sc = aalloc(16)
    scb = aalloc(8).bitcast(BF16)
    bsc = nb("sc")
    ACT(lambda q: q.activation(out=sc, in_=par[:, P_COND:P_COND + 16], func=AF.Exp, scale=-1.0), [bpar], [bsc])
    DVE(lambda q: q.tensor_scalar_add(out=sc, in0=sc, scalar1=1.0), [bsc], [bsc])
    DVE(lambda q: q.reciprocal(out=sc, in_=sc), [bsc], [bsc])
    DVE(lambda q: q.tensor_tensor(out=scb, in0=sc, in1=par[:, P_COND:P_COND + 16], op=ALU.mult), [bsc, bpar], [bsc])
    scb3 = scb.rearrange("p (c r) -> p c r", c=8)
    for l in range(2):
        ps, bp = nps()
        for i in range(18):
            sl, bs = wnext()
            wv = sl[:, 0:4096].rearrange("p (c n) -> p c n", c=8)
            for k in range(4):
                nn = i * 4 + k
                for c in range(8):
                    PE(lambda q, ps=ps, wv=wv, k=k, c=c, nn=nn: q.matmul(ps[:, nn * 2:nn * 2 + 2], lhsT=wv[:, c, k * 128:(k + 1) * 128], rhs=scb3[:, c, :], start=(c == 0), stop=(c == 7)),
                       [bs, bsc], [bp])
        DVE(lambda q, ps=ps, l=l: q.tensor_tensor(out=modv[:, l, :, :], in0=ps[:, 0:144].rearrange("p (n r) -> p n r", r=2),
                                                 in1=par[:, P_BADA + l * 72:P_BADA + (l + 1) * 72].unsqueeze(2).to_broadcast([128, 72, 2]), op=ALU.add),
            [bp, bpar], [bmod])
        for w3 in range(3):
            nrm = par[:, P_NRM + l * 24 + w3 * 8:P_NRM + l * 24 + w3 * 8 + 8].unsqueeze(2).to_broadcast([128, 8, 2])
            DVE(lambda q, l=l, w3=w3, nrm=nrm: q.scalar_tensor_tensor(out=mA[:, l, w3, :, :], in0=modv[:, l, (3 * w3 + 1) * 8:(3 * w3 + 2) * 8, :], scalar=1.0, in1=nrm, op0=ALU.add, op1=ALU.mult),
                [bmod, bpar], [bmod])
            DVE(lambda q, l=l, w3=w3: q.tensor_scalar_mul(out=mG[:, l, w3, :, :], in0=modv[:, l, (3 * w3 + 2) * 8:(3 * w3 + 3) * 8, :], scalar1=(1.0 if w3 == 1 else 0.5)),
                [bmod], [bmod])

    def mod_A(l, w3, c, g):
        r = 0 if g < 2 else 1
        return mA[:, l, w3, c, r:r + 1]

    def mod_B(l, w3, c, g):
        r = 0 if g < 2 else 1
        return modv[:, l, (3 * w3) * 8 + c, r:r + 1]

    def mod_G(l, w3, c, g):
        r = 0 if g < 2 else 1
        return mG[:, l, w3, c, r:r + 1]

    def norm_mod(l, w3):
        sq = [aalloc(256).bitcast(BF16), aalloc(256).bitcast(BF16)]
        bsq = [nb("sq0"), nb("sq1")]
        tmp = [aalloc(512), aalloc(512)]
        btmp = [nb("tmp0"), nb("tmp1")]
        rs = aalloc(512)
        brs = nb("rs")
        for g in range(NG):
            ps, bp = nps()
            for c in range(8):
                s = c % 2
                ACT(lambda q, s=s, c=c, g=g: q.activation(out=sq[s], in_=xT[:, c, g * 512:(g + 1) * 512], func=AF.Square), [bx[c][g]], [bsq[s]])
                PE(lambda q, ps=ps, s=s, c=c: q.matmul(ps[:], lhsT=ones_b, rhs=sq[s], start=(c == 0), stop=(c == 7)), [bsq[s], bcb], [bp])
            DVE(lambda q, ps=ps: q.tensor_scalar(out=rs, in0=ps[:], scalar1=1.0 / D, scalar2=EPS, op0=ALU.mult, op1=ALU.add), [bp], [brs])
            ACT(lambda q: q.activation(out=rs, in_=rs, func=AF.Ln), [brs], [brs])
            ACT(lambda q: q.activation(out=rs, in_=rs, func=AF.Exp, scale=-0.5), [brs], [brs])
            for c in range(8):
                s = c % 2
                DVE(lambda q, s=s, c=c, g=g: q.tensor_tensor(out=tmp[s], in0=xT[:, c, g * 512:(g + 1) * 512], in1=rs, op=ALU.mult), [bx[c][g], brs], [btmp[s]])
                ACT(lambda q, s=s, c=c, g=g: q.activation(out=hT[:, c, g * 512:(g + 1) * 512], in_=tmp[s], func=AF.Identity,
                                                          scale=mod_A(l, w3, c, g), bias=mod_B(l, w3, c, g)), [btmp[s], bmod], [bh[c][g]])

    def ffn(l, f):
        w3 = 0 if f == 0 else 2
        areset()
        norm_mod(l, w3)
        actT = aalloc(11 * TOK // 2).bitcast(BF16).rearrange("p (j n) -> p j n", j=11)
        bact = [nb(f"act{g}") for g in range(NG)]
        wd = aalloc(11 * 1024 // 2).bitcast(BF16).rearrange("p (j n) -> p j n", j=11)
        bwd = nb("wd")
        ebuf = [aalloc(512), aalloc(512)]
        bebuf = [nb("e0"), nb("e1")]
        tbuf = [aalloc(512), aalloc(512)]
        btbuf = [nb("t0"), nb("t1")]
        k = 0
        for h in range(2):
            DMA("pool", wd, wdn_d[f][l][h * 1408:(h + 1) * 1408, :].rearrange("(j p) n -> p j n", p=128), bwd, writes=[bwd])
            for j0 in range(0, 11, 2):
                nf = min(2, 11 - j0)
                sl, bs = wnext()
                wg = sl[:, 0:8 * nf * 128].rearrange("p (c n) -> p c n", c=8)
                wu = sl[:, 2048:2048 + 8 * nf * 128].rearrange("p (c n) -> p c n", c=8)
                for jj in range(nf):
                    j = j0 + jj
                    for g in range(NG):
                        pg, bpg = nps()
                        pu, bpu = nps()
                        for c in range(8):
                            PE(lambda q, pg=pg, wg=wg, c=c, jj=jj, g=g: q.matmul(pg[:], lhsT=wg[:, c, jj * 128:(jj + 1) * 128], rhs=hT[:, c, g * 512:(g + 1) * 512], start=(c == 0), stop=(c == 7)),
                               [bs, bh[c][g]], [bpg])
                        for c in range(8):
                            PE(lambda q, pu=pu, wu=wu, c=c, jj=jj, g=g: q.matmul(pu[:], lhsT=wu[:, c, jj * 128:(jj + 1) * 128], rhs=hT[:, c, g * 512:(g + 1) * 512], start=(c == 0), stop=(c == 7)),
                               [bs, bh[c][g]], [bpu])
                        s = k % 2
                        k += 1
                        ACT(lambda q, pg=pg, s=s: q.activation(out=tbuf[s], in_=pg[:], func=AF.Silu), [bpg], [btbuf[s]])
                        DVE(lambda q, pu=pu, s=s, j=j, g=g: q.tensor_tensor(out=actT[:, j, g * 512:(g + 1) * 512], in0=pu[:], in1=tbuf[s], op=ALU.mult), [bpu, btbuf[s]], [bact[g]])
            for g in range(NG):
                for dc in range(8):
                    ps, bp = nps()
                    for j in range(11):
                        PE(lambda q, ps=ps, j=j, dc=dc, g=g: q.matmul(ps[:], lhsT=wd[:, j, dc * 128:(dc + 1) * 128], rhs=actT[:, j, g * 512:(g + 1) * 512], start=(j == 0), stop=(j == 10)),
                           [bwd, bact[g]], [bp])
                    DVE(lambda q, ps=ps, dc=dc, g=g: q.scalar_tensor_tensor(out=xT[:, dc, g * 512:(g + 1) * 512], in0=ps[:], scalar=mod_G(l, w3, dc, g), in1=xT[:, dc, g * 512:(g + 1) * 512], op0=ALU.mult, op1=ALU.add),
                        [bp, bmod, bx[dc][g]], [bx[dc][g]])

    def mixer(l):
        areset()
        norm_mod(l, 1)
        areset()
        catT = aalloc(8 * TOK // 2).bitcast(BF16).rearrange("p (c n) -> p c n", c=8)
        bcat = [[nb(f"cat{c}_{g}") for g in range(NG)] for c in range(8)]
        base_off = state["aoff"]

        def inproj(wv, k, g, ps):
            for c in range(8):
                PE(lambda q, c=c: q.matmul(ps[0][:], lhsT=wv[:, c, k * 128:(k + 1) * 128], rhs=hT[:, c, g * 512:(g + 1) * 512], start=(c == 0), stop=(c == 7)),
                   [ps[2], bh[c][g]], [ps[1]])

        uT = aalloc(2 * TOK // 2).bitcast(BF16).rearrange("p (c n) -> p c n", c=2)
        buT = nb("uT")
        ucs = aalloc(NT * 2 * 256 // 2).bitcast(BF16).rearrange("p (t c n) -> p t c n", t=NT, c=2)
        bucs = nb("ucs")
        sl, bs = wnext()
        wv = sl[:, 0:2048].rearrange("p (c n) -> p c n", c=8)
        for k in range(2):
            for g in range(NG):
                ps, bp = nps()
                inproj(wv, k, g, (ps, bp, bs))
                ACT(lambda q, ps=ps, k=k, g=g: q.activation(out=uT[:, k, g * 512:(g + 1) * 512], in_=ps[:], func=AF.Copy), [bp], [buT])
        dft64 = cb[:, B_DFT:B_DFT + 256]
        for t in range(NT):
            ps, bp = nps()
            for k in range(2):
                PE(lambda q, ps=ps, k=k, t=t: q.matmul(ps[:, k * 256:(k + 1) * 256], lhsT=uT[:, k, t * 128:(t + 1) * 128], rhs=dft64, start=True, stop=True), [buT, bcb], [bp])
            DVE(lambda q, ps=ps, t=t: q.tensor_copy(out=ucs[:, t, :, :], in_=ps[:].rearrange("p (c n) -> p c n", c=2)), [bp], [bucs])
        for hf in range(2):
            slc, bsc_ = wnext()
            sls, bss = wnext(keep_prev=True)
            cl = slc[:, 0:4096].rearrange("p (c n) -> p c n", c=8)
            sn = sls[:, 0:4096].rearrange("p (c n) -> p c n", c=8)
            for k in range(2):
                ps, bp = nps()
                for lc in range(8):
                    PE(lambda q, ps=ps, k=k, lc=lc, cl=cl: q.matmul(ps[:], lhsT=ucs[:, lc, k, 0:128], rhs=cl[:, lc, :], start=(lc == 0), stop=False), [bucs, bsc_], [bp])
                    PE(lambda q, ps=ps, k=k, lc=lc, sn=sn: q.matmul(ps[:], lhsT=ucs[:, lc, k, 128:256], rhs=sn[:, lc, :], start=False, stop=(lc == 7)), [bucs, bss], [bp])
                ACT(lambda q, ps=ps, k=k, hf=hf: q.activation(out=catT[:, k, hf * 512:(hf + 1) * 512], in_=ps[:], func=AF.Copy), [bp], [bcat[k][hf]])
        slp, bsp = wnext()
        pc = slp[:, 0:512].rearrange("p (c n) -> p c n", c=2)
        pn = slp[:, 512:1024].rearrange("p (c n) -> p c n", c=2)
        for k in range(2):
            ps, bp = nps()
            for sq_ in range(2):
                for lc in range(2):
                    tt = 8 + sq_ * 2 + lc
                    PE(lambda q, ps=ps, k=k, lc=lc, tt=tt, sq_=sq_: q.matmul(ps[:, sq_ * 256:(sq_ + 1) * 256], lhsT=ucs[:, tt, k, 0:128], rhs=pc[:, lc, :], start=(lc == 0), stop=False), [bucs, bsp], [bp])
                    PE(lambda q, ps=ps, k=k, lc=lc, tt=tt, sq_=sq_: q.matmul(ps[:, sq_ * 256:(sq_ + 1) * 256], lhsT=ucs[:, tt, k, 128:256], rhs=pn[:, lc, :], start=False, stop=(lc == 1)), [bucs, bsp], [bp])
            ACT(lambda q, ps=ps, k=k: q.activation(out=catT[:, k, 1024:1536], in_=ps[:], func=AF.Copy), [bp], [bcat[k][2]])

        if ENABLE_HGRN:
            for pr in range(2):
                if DEBUG_PAIR is not None and pr != DEBUG_PAIR:
                    wnext(); wnext()
                    for g in range(NG):
                        POOL(lambda q, pr=pr, g=g: q.memset(catT[:, 2 + pr, g * 512:(g + 1) * 512], 0.0), [], [bcat[2 + pr][g]])
                    continue
                state["aoff"] = base_off
                qf, sF, sB, tmpb, kb, hoT = [aalloc(TOK) for _ in range(6)]
                bqf, bsF, bsB, btmpb, bkb, bho = [nb(n) for n in ("qf", "sF", "sB", "tmpb", "kb", "hoT")]
                qt = [aalloc(TOK // 2).bitcast(BF16) for _ in range(2)]
                kt_ = [aalloc(TOK // 2).bitcast(BF16) for _ in range(2)]
                vT = aalloc(TOK // 2).bitcast(BF16)
                sgT = aalloc(TOK // 2).bitcast(BF16)
                bqt, bkt = [nb("qt0"), nb("qt1")], [nb("kt0"), nb("kt1")]
                bvT, bsg = nb("vT"), nb("sgT")
                ktp = [aalloc(128).bitcast(BF16).rearrange("p (a n) -> p a n", a=2) for _ in range(3)]
                vpd = [aalloc(128).bitcast(BF16).rearrange("p (a n) -> p a n", a=2) for _ in range(3)]
                bktp, bvpd = [nb("ktp0"), nb("ktp1"), nb("ktp2")], [nb("vpd0"), nb("vpd1"), nb("vpd2")]
                atb = [aalloc(64).bitcast(BF16) for _ in range(2)]
                batb = [nb("at0"), nb("at1")]
                S = aalloc(128); tS = aalloc(128); S0m = aalloc(64).bitcast(BF16)
                bS, btS, bS0m = nb("S"), nb("tS"), nb("S0m")
                sst = [aalloc(128), aalloc(128)]
                bsst = [nb("sst0"), nb("sst1")]
                scl = [[aalloc(48) for _ in range(4)] for _ in range(2)]
                bscl = [nb("scl0"), nb("scl1")]
                lbp = aalloc(8)
                blbp = nb("lbp")
                for i in range(3):
                    POOL(lambda q, i=i: q.memset(ktp[i], 0.0), [], [bktp[i]])
                    POOL(lambda q, i=i: q.memset(vpd[i], 0.0), [], [bvpd[i]])
                for dr in range(2):
                    c0 = dr * 3
                    if l == 0:
                        POOL(lambda q, c0=c0: q.memset(lbp[:, c0:c0 + 1], 1.0), [], [blbp])
                        POOL(lambda q, c0=c0: q.memset(lbp[:, c0 + 1:c0 + 2], 1e-30), [], [blbp])
                        POOL(lambda q, c0=c0: q.memset(lbp[:, c0 + 2:c0 + 3], -1.0), [], [blbp])
                    else:
                        a0 = P_LBS + 0 * 4 + dr * 2 + pr
                        a1 = P_LBS + 1 * 4 + dr * 2 + pr
                        DVE(lambda q, c0=c0, a0=a0, a1=a1: q.tensor_tensor(out=lbp[:, c0 + 1:c0 + 2], in0=par[:, a0:a0 + 1], in1=par[:, a1:a1 + 1], op=ALU.subtract), [bpar], [blbp])
                        ACT(lambda q, c0=c0: q.activation(out=lbp[:, c0 + 1:c0 + 2], in_=lbp[:, c0 + 1:c0 + 2], func=AF.Exp), [blbp], [blbp])
                        DVE(lambda q, c0=c0: q.tensor_scalar_add(out=lbp[:, c0 + 1:c0 + 2], in0=lbp[:, c0 + 1:c0 + 2], scalar1=1.0), [blbp], [blbp])
                        DVE(lambda q, c0=c0: q.reciprocal(out=lbp[:, c0 + 1:c0 + 2], in_=lbp[:, c0 + 1:c0 + 2]), [blbp], [blbp])
                        DVE(lambda q, c0=c0: q.tensor_scalar(out=lbp[:, c0:c0 + 1], in0=lbp[:, c0 + 1:c0 + 2], scalar1=-1.0, scalar2=1.0, op0=ALU.mult, op1=ALU.add), [blbp], [blbp])
                        DVE(lambda q, c0=c0: q.tensor_scalar_mul(out=lbp[:, c0 + 2:c0 + 3], in0=lbp[:, c0:c0 + 1], scalar1=-1.0), [blbp], [blbp])
                        DVE(lambda q, c0=c0: q.tensor_scalar_max(out=lbp[:, c0 + 1:c0 + 2], in0=lbp[:, c0 + 1:c0 + 2], scalar1=1e-30), [blbp], [blbp])
                sl, bs = wnext()
                for k in range(3):
                    wv = sl[:, k * 1024:(k + 1) * 1024].rearrange("p (c n) -> p c n", c=8)
                    for g in range(NG):
                        ps, bp = nps()
                        inproj(wv, 0, g, (ps, bp, bs))
                        gs = slice(g * 512, (g + 1) * 512)
                        if k == 0:
                            ACT(lambda q, ps=ps, gs=gs: q.activation(out=qf[:, gs], in_=ps[:], func=AF.Copy, scale=0.125), [bp], [bqf])
                        elif k == 1:
                            ACT(lambda q, ps=ps, gs=gs: q.activation(out=vT[:, gs], in_=ps[:], func=AF.Copy), [bp], [bvT])
                        else:
                            ACT(lambda q, ps=ps, gs=gs: q.activation(out=sF[:, gs], in_=ps[:], func=AF.Exp, scale=-1.0), [bp], [bsF])
                sl, bs = wnext()
                for k in range(2):
                    wv = sl[:, k * 1024:(k + 1) * 1024].rearrange("p (c n) -> p c n", c=8)
                    for g in range(NG):
                        ps, bp = nps()
                        inproj(wv, 0, g, (ps, bp, bs))
                        gs = slice(g * 512, (g + 1) * 512)
                        if k == 0:
                            ACT(lambda q, ps=ps, gs=gs: q.activation(out=sB[:, gs], in_=ps[:], func=AF.Exp, scale=-1.0), [bp], [bsB])
                        else:
                            ACT(lambda q, ps=ps, gs=gs: q.activation(out=tmpb[:, gs], in_=ps[:], func=AF.Exp, scale=-1.0), [bp], [btmpb])
                            DVE(lambda q, gs=gs: q.tensor_scalar_add(out=tmpb[:, gs], in0=tmpb[:, gs], scalar1=1.0), [btmpb], [btmpb])
                            DVE(lambda q, gs=gs: q.reciprocal(out=tmpb[:, gs], in_=tmpb[:, gs]), [btmpb], [btmpb])
                            DVE(lambda q, ps=ps, gs=gs: q.tensor_tensor(out=sgT[:, gs], in0=ps[:], in1=tmpb[:, gs], op=ALU.mult), [bp, btmpb], [bsg])
                for (sx, bsx) in ((sF, bsF), (sB, bsB)):
                    DVE(lambda q, sx=sx: q.tensor_scalar_add(out=sx, in0=sx, scalar1=1.0), [bsx], [bsx])
                    DVE(lambda q, sx=sx: q.reciprocal(out=sx, in_=sx), [bsx], [bsx])
                CH = 32
                NCH = TOK // CH
                SPC = 256 // CH
                for dr, (sx, bsx) in enumerate(((sF, bsF), (sB, bsB))):
                    c0 = dr * 3
                    DVE(lambda q, sx=sx, c0=c0: q.tensor_scalar(out=kb, in0=sx, scalar1=lbp[:, c0 + 2:c0 + 3], scalar2=lbp[:, c0:c0 + 1], op0=ALU.mult, op1=ALU.add), [bsx, blbp], [bkb])
                    DVE(lambda q, sx=sx, c0=c0: q.tensor_scalar(out=sx, in0=sx, scalar1=lbp[:, c0:c0 + 1], scalar2=lbp[:, c0 + 1:c0 + 2], op0=ALU.mult, op1=ALU.add), [bsx, blbp], [bsx])
                    ACT(lambda q, sx=sx: q.activation(out=sx, in_=sx, func=AF.Ln), [bsx], [bsx])
                    A_, bA_, B_, bB_ = sx, bsx, tmpb, btmpb
                    sh = 1
                    while sh < CH:
                        A3 = A_.rearrange("p (c n) -> p c n", n=CH)
                        B3 = B_.rearrange("p (c n) -> p c n", n=CH)
                        if dr == 0:
                            ACT(lambda q, A3=A3, B3=B3, sh=sh: q.activation(out=B3[:, :, 0:sh], in_=A3[:, :, 0:sh], func=AF.Copy), [bA_], [bB_])
                            DVE(lambda q, A3=A3, B3=B3, sh=sh: q.tensor_tensor(out=B3[:, :, sh:CH], in0=A3[:, :, sh:CH], in1=A3[:, :, 0:CH - sh], op=ALU.add), [bA_], [bB_])
                        else:
                            ACT(lambda q, A3=A3, B3=B3, sh=sh: q.activation(out=B3[:, :, CH - sh:CH], in_=A3[:, :, CH - sh:CH], func=AF.Copy), [bA_], [bB_])
                            DVE(lambda q, A3=A3, B3=B3, sh=sh: q.tensor_tensor(out=B3[:, :, 0:CH - sh], in0=A3[:, :, 0:CH - sh], in1=A3[:, :, sh:CH], op=ALU.add), [bA_], [bB_])
                        A_, bA_, B_, bB_ = B_, bB_, A_, bA_
                        sh *= 2
                    bq_, bbq_, tq_, btq_ = A_, bA_, B_, bB_
                    b3 = bq_.rearrange("p (c n) -> p c n", n=CH)
                    mid, Ti = (CH // 2 - 1, CH - 1) if dr == 0 else (CH // 2, 0)
                    bm, em, eT, eTm = scl[dr]
                    DVE(lambda q, b3=b3, mid=mid, bm=bm: q.tensor_copy(out=bm, in_=b3[:, :, mid]), [bbq_], [bscl[dr]])
                    ACT(lambda q, bm=bm, em=em: q.activation(out=em, in_=bm, func=AF.Exp), [bscl[dr]], [bscl[dr]])
                    ACT(lambda q, b3=b3, Ti=Ti, eT=eT: q.activation(out=eT, in_=b3[:, :, Ti], func=AF.Exp), [bbq_], [bscl[dr]])
                    DVE(lambda q, b3=b3, Ti=Ti, bm=bm, eTm=eTm: q.tensor_tensor(out=eTm, in0=b3[:, :, Ti], in1=bm, op=ALU.subtract), [bbq_, bscl[dr]], [bscl[dr]])
                    ACT(lambda q, eTm=eTm: q.activation(out=eTm, in_=eTm, func=AF.Exp), [bscl[dr]], [bscl[dr]])
                    DVE(lambda q, b3=b3, bm=bm: q.tensor_tensor(out=b3, in0=b3, in1=bm.unsqueeze(2).to_broadcast([128, NCH, CH]), op=ALU.subtract), [bbq_, bscl[dr]], [bbq_])
                    ACT(lambda q, bq_=bq_, tq_=tq_: q.activation(out=tq_, in_=bq_, func=AF.Exp), [bbq_], [btq_])
                    DVE(lambda q, dr=dr, tq_=tq_: q.tensor_tensor(out=qt[dr], in0=qf, in1=tq_, op=ALU.mult), [bqf, btq_], [bqt[dr]])
                    ACT(lambda q, bq_=bq_, tq_=tq_: q.activation(out=tq_, in_=bq_, func=AF.Exp, scale=-1.0), [bbq_], [btq_])
                    DVE(lambda q, dr=dr, tq_=tq_: q.tensor_tensor(out=kt_[dr], in0=kb, in1=tq_, op=ALU.mult), [bkb, btq_], [bkt[dr]])
                ev = [0]

                def emit_state(slot, dr):
                    i = ev[0] % 2
                    ev[0] += 1
                    ACT(lambda q, i=i: q.activation(out=sst[i], in_=S, func=AF.Copy), [bS], [bsst[i]])
                    for hd in range(2):
                        DMA("sp", sout[l, slot, dr, 2 * pr + hd], sst[i][hd * 64:(hd + 1) * 64, hd * 64:(hd + 1) * 64], bsst[i], reads=[bsst[i]])

                def load_state(dr):
                    POOL(lambda q: q.memset(S, 0.0), [bS], [bS])
                    for hd in range(2):
                        DMA("sp", S[hd * 64:(hd + 1) * 64, hd * 64:(hd + 1) * 64], s0_d[l, dr, 2 * pr + hd], bS, reads=[bS], writes=[bS])

                Xb = [aalloc(128), aalloc(128)]
                bXb = [nb("X0"), nb("X1")]
                for dr in range(2):
                    bm, em, eT, eTm = scl[dr]
                    order = list(range(NCH)) if dr == 0 else list(range(NCH - 1, -1, -1))
                    mcol = C_MF if dr == 0 else C_MB
                    pst = {}

                    def stageA(n_, dr=dr, order=order):
                        ch = order[n_]
                        i3 = n_ % 3
                        cs = slice(ch * CH, (ch + 1) * CH)
                        ptk, bptk = nps()
                        ptkb = ptk[:].bitcast(BF16)
                        PE(lambda q: q.transpose(ptkb[0:CH, 0:128], kt_[dr][:, cs], ident_b), [bkt[dr], bcb], [bptk])
                        PE(lambda q: q.transpose(ptkb[0:CH, 128:256], vT[:, cs], ident_b), [bvT, bcb], [bptk])
                        ACT(lambda q: q.activation(out=ktp[i3][0:CH, 0, 0:64], in_=ptkb[0:CH, 0:64], func=AF.Copy), [bptk], [bktp[i3]])
                        ACT(lambda q: q.activation(out=ktp[i3][0:CH, 1, 64:128], in_=ptkb[0:CH, 64:128], func=AF.Copy), [bptk], [bktp[i3]])
                        DVE(lambda q: q.tensor_copy(out=vpd[i3][0:CH, 0, 0:64], in_=ptkb[0:CH, 128:192]), [bptk], [bvpd[i3]])
                        DVE(lambda q: q.tensor_copy(out=vpd[i3][0:CH, 1, 64:128], in_=ptkb[0:CH, 192:256]), [bptk], [bvpd[i3]])

                    def stageB(n_, dr=dr, eTm=eTm, mcol=mcol, order=order):
                        ch = order[n_]
                        i = n_ % 2
                        i3 = n_ % 3
                        cs = slice(ch * CH, (ch + 1) * CH)
                        for hd in range(2):
                            hs = slice(hd * 64, (hd + 1) * 64)
                            pA, bpA = nps()
                            PE(lambda q, pA=pA, hs=hs: q.matmul(pA[0:CH, 0:CH], lhsT=kt_[dr][hs, cs], rhs=qt[dr][hs, cs], start=True, stop=True), [bkt[dr], bqt[dr]], [bpA])
                            DVE(lambda q, pA=pA, hd=hd: q.tensor_tensor(out=atb[i][0:CH, hd * CH:(hd + 1) * CH], in0=pA[0:CH, 0:CH], in1=cf[0:CH, mcol:mcol + CH], op=ALU.mult), [bpA, bcf], [batb[i]])
                        pS2, bpS2 = nps()
                        PE(lambda q: q.matmul(pS2[:, 0:128], lhsT=ktp[i3][0:CH, 0, :], rhs=vpd[i3][0:CH, 0, :], start=True, stop=False), [bktp[i3], bvpd[i3]], [bpS2])
                        PE(lambda q: q.matmul(pS2[:, 0:128], lhsT=ktp[i3][0:CH, 1, :], rhs=vpd[i3][0:CH, 1, :], start=False, stop=True), [bktp[i3], bvpd[i3]], [bpS2])
                        ACT(lambda q: q.activation(out=Xb[i], in_=pS2[:, 0:128], func=AF.Copy, scale=eTm[:, ch:ch + 1]), [bpS2, bscl[dr]], [bXb[i]])

                    def chain(n_, dr=dr, em=em, eT=eT, order=order):
                        ch = order[n_]
                        i = n_ % 2
                        i3 = n_ % 3
                        cs = slice(ch * CH, (ch + 1) * CH)
                        first_of_seq = (ch % SPC == 0) if dr == 0 else (ch % SPC == SPC - 1)
                        if first_of_seq:
                            seq = ch // SPC
                            prev = seq - 1 if dr == 0 else seq + 1
                            if n_ > 0:
                                emit_state(prev, dr)
                            if (dr == 0 and seq == 0) or (dr == 1 and seq == 3):
                                load_state(dr)
                            elif seq >= 4:
                                POOL(lambda q: q.memset(S, 0.0), [bS], [bS])
                            else:
                                DVE(lambda q: q.tensor_scalar_mul(out=S, in0=S, scalar1=par[:, P_KEEP:P_KEEP + 1]), [bS, bpar], [bS])
                        DVE(lambda q: q.tensor_scalar_mul(out=S0m, in0=S, scalar1=em[:, ch:ch + 1]), [bS, bscl[dr]], [bS0m])
                        DVE(lambda q: q.scalar_tensor_tensor(out=S, in0=S, scalar=eT[:, ch:ch + 1], in1=Xb[i], op0=ALU.mult, op1=ALU.add), [bS, bscl[dr], bXb[i]], [bS])
                        po, bpo = nps()
                        PE(lambda q: q.matmul(po[:, 0:CH], lhsT=vpd[i3][0:CH, 0, :], rhs=atb[i][0:CH, 0:CH], start=True, stop=False), [bvpd[i3], batb[i]], [bpo])
                        PE(lambda q: q.matmul(po[:, 0:CH], lhsT=vpd[i3][0:CH, 1, :], rhs=atb[i][0:CH, CH:2 * CH], start=False, stop=False), [bvpd[i3], batb[i]], [bpo])
                        PE(lambda q: q.matmul(po[:, 0:CH], lhsT=S0m, rhs=qt[dr][:, cs], start=False, stop=True), [bS0m, bqt[dr]], [bpo])
                        if dr == 0:
                            ACT(lambda q: q.activation(out=hoT[:, cs], in_=po[:, 0:CH], func=AF.Copy), [bpo], [bho])
                        else:
                            DVE(lambda q: q.tensor_tensor(out=hoT[:, cs], in0=po[:, 0:CH], in1=hoT[:, cs], op=ALU.add), [bpo, bho], [bho])

                    stageA(0)
                    stageA(1)
                    stageB(0)
                    for n_ in range(NCH):
                        if n_ + 2 < NCH:
                            stageA(n_ + 2)
                        if n_ + 1 < NCH:
                            stageB(n_ + 1)
                        chain(n_)
                    emit_state(0 if dr == 1 else 5, dr)
                for g in range(NG):
                    gs = slice(g * 512, (g + 1) * 512)
                    sqh = tmpb[:, 0:256].bitcast(BF16)
                    ACT(lambda q, gs=gs, sqh=sqh: q.activation(out=sqh, in_=hoT[:, gs], func=AF.Square), [bho], [btmpb])
                    p2, bp2 = nps()
                    PE(lambda q, p2=p2, sqh=sqh: q.matmul(p2[:], lhsT=blk_b, rhs=sqh, start=True, stop=True), [btmpb, bcb], [bp2])
                    rv = kb[:, 0:512]
                    DVE(lambda q, p2=p2, rv=rv: q.tensor_scalar(out=rv, in0=p2[:], scalar1=1.0 / 64, scalar2=EPS, op0=ALU.mult, op1=ALU.add), [bp2], [bkb])
                    ACT(lambda q, rv=rv: q.activation(out=rv, in_=rv, func=AF.Ln), [bkb], [bkb])
                    ACT(lambda q, rv=rv: q.activation(out=rv, in_=rv, func=AF.Exp, scale=-0.5), [bkb], [bkb])
                    DVE(lambda q, rv=rv, gs=gs, pr=pr: q.scalar_tensor_tensor(out=rv, in0=hoT[:, gs], scalar=par[:, P_HNRM + l * 2 + pr:P_HNRM + l * 2 + pr + 1], in1=rv, op0=ALU.mult, op1=ALU.mult), [bho, bkb, bpar], [bkb])
                    DVE(lambda q, rv=rv, gs=gs, g=g, pr=pr: q.tensor_tensor(out=catT[:, 2 + pr, gs], in0=rv, in1=sgT[:, gs], op=ALU.mult), [bkb, bsg], [bcat[2 + pr][g]])
        else:
            for pr in range(2):
                wnext()
                wnext()
                for g in range(NG):
                    POOL(lambda q, pr=pr, g=g: q.memset(catT[:, 2 + pr, g * 512:(g + 1) * 512], 0.0), [], [bcat[2 + pr][g]])
        if ENABLE_ATTN:
            state["aoff"] = base_off
            qT = aalloc(4 * TOK // 2).bitcast(BF16).rearrange("p (c n) -> p c n", c=4)
            kdT = aalloc(4 * TOK // 2).bitcast(BF16).rearrange("p (c v n) -> p c v n", c=2, v=2)
            kcT = aalloc(4 * 512 // 2).bitcast(BF16).rearrange("p (c v n) -> p c v n", c=2, v=2)
            vaug = aalloc(16 * 2 * 96 // 2).bitcast(BF16).rearrange("p (t k n) -> p t k n", t=16, k=2)
            kst = aalloc(NT * 128).rearrange("p (t n) -> p t n", t=NT)
            vst = aalloc(NT * 128).rearrange("p (t n) -> p t n", t=NT)
            kcd = aalloc(4 * 256).rearrange("p (t k n) -> p t k n", t=4, k=2)
            zq = [aalloc(512), aalloc(512)]
            rr = [aalloc(512), aalloc(512)]
            t1 = [aalloc(512)] * 2
            sqa = [aalloc(256).bitcast(BF16), aalloc(256).bitcast(BF16)]
            pT = [aalloc(256).bitcast(BF16) for _ in range(3)]
            otok = [aalloc(256).bitcast(BF16).rearrange("p (h n) -> p h n", h=8), aalloc(256).bitcast(BF16).rearrange("p (h n) -> p h n", h=8)]
            den = [aalloc(4), aalloc(4)]
            esink = aalloc(8)
            bq = [[nb("q") for g in range(NG)] for c in range(4)]
            bkd = [[nb("kd") for g in range(NG)] for c in range(2)]
            bkc, bva, bkst, bvst, bkcd, besk = nb("kc"), [nb("va") for t in range(16)], nb("kst"), nb("vst"), nb("kcd"), nb("esk")
            bzq, brr, bt1, bsqa, bpT, botok, bden = [[nb(n + str(i)) for i in range(3)] for n in ("zq", "rr", "t1", "sqa", "pT", "otok", "den")]
            bt1[1] = bt1[0]
            ACT(lambda q: q.activation(out=esink, in_=par[:, P_SINK + l * 8:P_SINK + l * 8 + 8], func=AF.Exp), [bpar], [besk])
            POOL(lambda q: q.memset(vaug[:, :, :, 64:66], 1.0), [], list(bva))
            allkd = [bkd[c_][g_] for c_ in range(2) for g_ in range(NG)]
            POOL(lambda q: q.memset(kdT[64:128, :, 0, :], 0.0), [], allkd)
            POOL(lambda q: q.memset(kdT[0:64, :, 1, :], 0.0), [], allkd)
            POOL(lambda q: q.memset(kcT[64:128, :, 0, :], 0.0), [], [bkc])
            POOL(lambda q: q.memset(kcT[0:64, :, 1, :], 0.0), [], [bkc])
            for kv in (range(2) if "cachek" not in SKIP else []):
                for dup in range(2):
                    DMA("sp", kcd[:, :, kv, dup * 64:(dup + 1) * 64], ck_d[l][:, kv * 64:(kv + 1) * 64].rearrange("(t p) d -> p t d", p=128), bkcd, writes=[bkcd])
            for t in (range(4) if "cachev" not in SKIP else []):
                DMA("pool", vaug[:, 12 + t, :, 0:64], cv_d[l][t * 128:(t + 1) * 128, :].rearrange("p (k d) -> p k d", k=2), bva[12 + t], writes=[bva[12 + t]])
            for kv in (range(2) if "cachek" not in SKIP else []):
                ps, bp = nps()
                for t in range(4):
                    PE(lambda q, ps=ps, t=t, kv=kv: q.transpose(ps[:, t * 128:(t + 1) * 128], kcd[:, t, kv, :], ident), [bkcd, bcf], [bp])
                ACT(lambda q, ps=ps, kv=kv: q.activation(out=kcT[0:64, kv, 0, :], in_=ps[0:64, :], func=AF.Copy), [bp], [bkc])
                ACT(lambda q, ps=ps, kv=kv: q.activation(out=kcT[64:128, kv, 1, :], in_=ps[64:128, :], func=AF.Copy), [bp], [bkc])
            cnt = [0]

            def qk_post(ps, bp, g, nw_col, dst, bdst, scale_extra, kfin=None):
                s = cnt[0] % 2
                cnt[0] += 1
                ACT(lambda q: q.activation(out=zq[s], in_=ps[:], func=AF.Copy), [bp], [bzq[s]])
                ACT(lambda q: q.activation(out=sqa[s], in_=ps[:], func=AF.Square), [bp], [bsqa[s]])
                p2, bp2 = nps()
                PE(lambda q: q.matmul(p2[:], lhsT=blk_b, rhs=sqa[s], start=True, stop=True), [bsqa[s], bcb], [bp2])
                DVE(lambda q: q.tensor_scalar(out=rr[s], in0=p2[:], scalar1=1.0 / 64, scalar2=EPS, op0=ALU.mult, op1=ALU.add), [bp2], [brr[s]])
                ACT(lambda q: q.activation(out=rr[s], in_=rr[s], func=AF.Ln), [brr[s]], [brr[s]])
                ACT(lambda q: q.activation(out=rr[s], in_=rr[s], func=AF.Exp, scale=-0.5), [brr[s]], [brr[s]])
                DVE(lambda q: q.scalar_tensor_tensor(out=zq[s], in0=zq[s], scalar=par[:, nw_col:nw_col + 1], in1=rr[s], op0=ALU.mult, op1=ALU.mult), [bzq[s], brr[s], bpar], [bzq[s]])
                if g < 2 and "rope" not in SKIP:
                    p3, bp3 = nps()
                    PE(lambda q: q.matmul(p3[:], lhsT=cf[:, C_PERM:C_PERM + 128], rhs=zq[s], start=True, stop=True), [bzq[s], bcf], [bp3])
                    DVE(lambda q: q.tensor_tensor(out=t1[s], in0=p3[:], in1=cf[:, C_SIN + g * 512:C_SIN + (g + 1) * 512], op=ALU.mult), [bp3, bcf], [bt1[s]])
                    DVE(lambda q: q.tensor_tensor(out=zq[s], in0=zq[s], in1=cf[:, C_COS + g * 512:C_COS + (g + 1) * 512], op=ALU.mult), [bzq[s], bcf], [bzq[s]])
                    DVE(lambda q: q.tensor_tensor(out=zq[s], in0=zq[s], in1=t1[s], op=ALU.add), [bzq[s], bt1[s]], [bzq[s]])
                if kfin is None:
                    ACT(lambda q: q.activation(out=dst, in_=zq[s], func=AF.Copy, scale=scale_extra), [bzq[s]], [bdst])
                else:
                    ACT(lambda q: q.activation(out=dst[0][0:64, :], in_=zq[s][0:64, :], func=AF.Copy), [bzq[s]], [bdst])
                    ACT(lambda q: q.activation(out=dst[1][64:128, :], in_=zq[s][64:128, :], func=AF.Copy), [bzq[s]], [bdst])
                if kfin is not None and "ktr" not in SKIP:
                    kv = kfin
                    p4, bp4 = nps()
                    for tt in range(4):
                        PE(lambda q, tt=tt: q.transpose(p4[:, tt * 64:(tt + 1) * 64], zq[s][0:64, tt * 128:(tt + 1) * 128], ident[0:64, 0:64]), [bzq[s], bcf], [bp4])
                    DVE(lambda q: q.tensor_copy(out=kst[:, g * 4:(g + 1) * 4, kv * 64:(kv + 1) * 64], in_=p4[:, 0:256].rearrange("p (t n) -> p t n", t=4)), [bp4], [bkst])

            sl, bs = wnext()
            wv = sl[:, 0:4096].rearrange("p (c n) -> p c n", c=8)
            for c in range(4):
                for g in range(NG):
                    ps, bp = nps()
                    inproj(wv, c, g, (ps, bp, bs))
                    qk_post(ps, bp, g, P_QN + l, qT[:, c, g * 512:(g + 1) * 512], bq[c][g], 0.125)
            sl, bs = wnext()
            for kv in range(2):
                wv = sl[:, kv * 1024:(kv + 1) * 1024].rearrange("p (c n) -> p c n", c=8)
                for g in range(NG):
                    ps, bp = nps()
                    inproj(wv, 0, g, (ps, bp, bs))
                    qk_post(ps, bp, g, P_KN + l, (kdT[:, kv, 0, g * 512:(g + 1) * 512], kdT[:, kv, 1, g * 512:(g + 1) * 512]), bkd[kv][g], 1.0, kfin=kv)
            wv = sl[:, 2048:3072].rearrange("p (c n) -> p c n", c=8)
            for g in range(NG):
                ps, bp = nps()
                inproj(wv, 0, g, (ps, bp, bs))
                s = cnt[0] % 2
                cnt[0] += 1
                ACT(lambda q, ps=ps, s=s: q.activation(out=zq[s], in_=ps[:], func=AF.Copy), [bp], [bzq[s]])
                if "vtr" in SKIP:
                    continue
                p4, bp4 = nps()
                for tt in range(4):
                    PE(lambda q, tt=tt, s=s, p4=p4: q.transpose(p4[:, tt * 128:(tt + 1) * 128], zq[s][:, tt * 128:(tt + 1) * 128], ident), [bzq[s], bcf], [bp4])
                if "vtr_dve" not in SKIP:
                    DVE(lambda q, p4=p4, g=g: q.tensor_copy(out=vst[:, g * 4:(g + 1) * 4, :], in_=p4[:].rearrange("p (t n) -> p t n", t=4)), [bp4], [bvst])
                for tt in (range(4) if "vtr_act" not in SKIP else []):
                    ACT(lambda q, p4=p4, g=g, tt=tt: q.activation(out=vaug[:, g * 4 + tt, :, 0:64], in_=p4[:, tt * 128:(tt + 1) * 128].rearrange("p (k d) -> p k d", k=2), func=AF.Copy), [bp4], [bva[g * 4 + tt]])
            if "kvout" not in SKIP:
                DMA("sp", kout[l].rearrange("(t p) n -> p t n", p=128), kst, bkst, reads=[bkst])
            if "kvout" not in SKIP:
                DMA("sp", vout[l].rearrange("(t p) n -> p t n", p=128), vst, bvst, reads=[bvst])
            it = 0
            if not ATTN_CORE:
                for c in range(4):
                    for g in range(NG):
                        POOL(lambda q, c=c, g=g: q.memset(catT[:, 4 + c, g * 512:(g + 1) * 512], 0.0), [], [bcat[4 + c][g]])
            for j in (range(NT) if ATTN_CORE else []):
                g = j // 4
                if j < 8:
                    kts = [("n", j + d_, (j * 3 + 1 + d_) if d_ != 0 else None) for d_ in (-1, 0, 1) if 0 <= j + d_ < 8] + [("c", t, None) for t in range(4)]
                else:
                    b0 = 8 + 2 * ((j - 8) // 2)
                    kts = [("n", b0, None), ("n", b0 + 1, None)]
                so = j % 2
                for kv in range(2):
                    po, bpo = nps()
                    pov = po[:, 0:264].rearrange("p (h n) -> p h n", h=4)
                    nk = len(kts)

                    def scores(ki, j=j, kv=kv, g=g):
                        kind, kt, mi = kts[ki]
                        pS, bpS = nps()
                        for hh in range(4):
                            cq = kv * 2 + hh // 2
                            hf = hh % 2
                            if kind == "n":
                                lhs = kdT[:, kv, hf, kt * 128:(kt + 1) * 128]
                                rb = bkd[kv][kt // 4]
                            else:
                                lhs = kcT[:, kv, hf, kt * 128:(kt + 1) * 128]
                                rb = bkc
                            PE(lambda q, pS=pS, hh=hh, lhs=lhs, cq=cq, j=j: q.matmul(pS[:, hh * 128:(hh + 1) * 128], lhsT=lhs, rhs=qT[:, cq, j * 128:(j + 1) * 128], start=True, stop=True),
                               [rb, bq[cq][g]], [bpS])
                        return pS, bpS

                    LOOK = 2
                    sc = {}
                    for ki in range(min(LOOK, nk)):
                        sc[ki] = scores(ki)
                    for ki, (kind, kt, mi) in enumerate(kts):
                        pS, bpS = sc.pop(ki)
                        s = it % 3
                        it += 1
                        if kind == "c":
                            ACT(lambda q, pS=pS, s=s: q.activation(out=pT[s], in_=pS[:], func=AF.Exp, bias=par[:, P_CBIAS:P_CBIAS + 1]), [bpS, bpar], [bpT[s]])
                        else:
                            ACT(lambda q, pS=pS, s=s: q.activation(out=pT[s], in_=pS[:], func=AF.Exp), [bpS], [bpT[s]])
                        if mi is not None:
                            DVE(lambda q, s=s, mi=mi: q.tensor_tensor(out=pT[s].rearrange("p (h n) -> p h n", h=4), in0=pT[s].rearrange("p (h n) -> p h n", h=4),
                                                                      in1=cb[:, B_MASK + mi * 128:B_MASK + (mi + 1) * 128].unsqueeze(1).to_broadcast([128, 4, 128]), op=ALU.mult), [bpT[s], bcb], [bpT[s]])
                        if ki + LOOK < nk:
                            sc[ki + LOOK] = scores(ki + LOOK)
                        vt = kt if kind == "n" else 12 + kt
                        for hh in range(4):
                            PE(lambda q, pov=pov, hh=hh, s=s, vt=vt, kv=kv, ki=ki, n=nk: q.matmul(pov[:, hh, 0:66], lhsT=pT[s][:, hh * 128:(hh + 1) * 128], rhs=vaug[:, vt, kv, 0:66], start=(ki == 0 and hh == 0), stop=(ki == n - 1 and hh == 3)),
                               [bpT[s], bva[vt]], [bpo])
                    DVE(lambda q, pov=pov, so=so, kv=kv: q.tensor_tensor(out=den[so], in0=pov[:, :, 64], in1=esink[:, kv * 4:(kv + 1) * 4], op=ALU.add), [bpo, besk], [bden[so]])
                    DVE(lambda q, so=so: q.reciprocal(out=den[so], in_=den[so]), [bden[so]], [bden[so]])
                    DVE(lambda q, pov=pov, so=so, kv=kv: q.tensor_tensor(out=otok[so][:, kv * 4:(kv + 1) * 4, :], in0=pov[:, :, 0:64], in1=den[so].unsqueeze(2).to_broadcast([128, 4, 64]), op=ALU.mult), [bpo, bden[so]], [botok[so]])
                pt_, bpt = nps()
                ptb = pt_[:].bitcast(BF16)
                for c in range(4):
                    PE(lambda q, ptb=ptb, c=c, so=so: q.transpose(ptb[:, c * 128:(c + 1) * 128], otok[so][:, 2 * c:2 * c + 2, :].rearrange("p h n -> p (h n)"), ident_b), [botok[so], bcb], [bpt])
                ACT(lambda q, ptb=ptb, j=j: q.activation(out=catT[:, 4:8, j * 128:(j + 1) * 128], in_=ptb[:, 0:512].rearrange("p (c n) -> p c n", c=4), func=AF.Copy), [bpt], [bcat[4 + c][g] for c in range(4)])

        else:
            wnext()
            wnext()
            for c in range(4):
                for g in range(NG):
                    POOL(lambda q, c=c, g=g: q.memset(catT[:, 4 + c, g * 512:(g + 1) * 512], 0.0), [], [bcat[4 + c][g]])
        for i in range(2):
            sl, bs = wnext()
            wv = sl[:, 0:4096].rearrange("p (c n) -> p c n", c=8)
            for k in range(4):
                dc = i * 4 + k
                for g in range(NG):
                    ps, bp = nps()
                    for c in range(8):
                        PE(lambda q, ps=ps, wv=wv, c=c, k=k, g=g: q.matmul(ps[:], lhsT=wv[:, c, k * 128:(k + 1) * 128], rhs=catT[:, c, g * 512:(g + 1) * 512], start=(c == 0), stop=(c == 7)),
                           [bs, bcat[c][g]], [bp])
                    DVE(lambda q, ps=ps, dc=dc, g=g: q.scalar_tensor_tensor(out=xT[:, dc, g * 512:(g + 1) * 512], in0=ps[:], scalar=mod_G(l, 1, dc, g), in1=xT[:, dc, g * 512:(g + 1) * 512], op0=ALU.mult, op1=ALU.add),
                        [bp, bmod, bx[dc][g]], [bx[dc][g]])

    for l in range(2):
        ffn(l, 0)
        mixer(l)
        ffn(l, 1)

    areset()
    yst = [aalloc(1024), aalloc(1024)]
    byst = [nb("yst0"), nb("yst1")]
    for t in range(NT):
        s = t % 2
        g = t // 4
        for half in range(2):
            ps, bp = nps()
            for k in range(4):
                c = half * 4 + k
                PE(lambda q, ps=ps, k=k, c=c, t=t: q.transpose(ps[:, k * 128:(k + 1) * 128], xT[:, c, t * 128:(t + 1) * 128], ident), [bx[c][g], bcf], [bp])
            if half == 0:
                ACT(lambda q, ps=ps, s=s: q.activation(out=yst[s][:, 0:512], in_=ps[:], func=AF.Copy), [bp], [byst[s]])
            else:
                DVE(lambda q, ps=ps, s=s: q.tensor_copy(out=yst[s][:, 512:1024], in_=ps[:]), [bp], [byst[s]])
        DMA("sp", yout[t * 128:(t + 1) * 128, :], yst[s], byst[s], reads=[byst[s]])

    T.emit(nc)
    st.close()
    return nc


def _consts(sample_mode):
    cf = np.zeros((128, NCF), np.float32)
    cf[:, C_ID:C_ID + 128] = np.eye(128, dtype=np.float32)
    Pm = np.zeros((64, 64), np.float32)
    for d in range(64):
        blk = d // 16
        if blk % 2 == 0:
            Pm[d, d + 16] = -1.0
        else:
            Pm[d, d - 16] = 1.0
    P2 = np.zeros((128, 128), np.float32)
    P2[:64, :64] = Pm
    P2[64:, 64:] = Pm
    cf[:, C_PERM:C_PERM + 128] = P2.T
    t = np.arange(1024)
    inv = (10000.0 ** (-np.arange(0, 32, 2, dtype=np.float32) / 32)).astype(np.float32)
    d = np.arange(128) % 64
    pos = np.where((d < 32)[:, None], (t // 64)[None, :], (t % 64)[None, :]).astype(np.float32)
    ang = pos * inv[d % 16][:, None]
    if sample_mode:
        cf[:, C_COS:C_COS + 1024] = np.cos(ang)
        cf[:, C_SIN:C_SIN + 1024] = np.sin(ang)
    else:
        cf[:, C_COS:C_COS + 1024] = 1.0
    j = np.arange(64)
    cf[:64, C_MF:C_MF + 64] = (j[:, None] <= j[None, :]).astype(np.float32)
    cf[:64, C_MB:C_MB + 64] = (j[:, None] >= j[None, :]).astype(np.float32)
    cb = np.zeros((128, NCB), np.float32)
    cb[:, B_ONES:B_ONES + 128] = 1.0
    cb[:64, B_BLK:B_BLK + 64] = 1.0
    cb[64:, B_BLK + 64:B_BLK + 128] = 1.0
    cb[:, B_ID:B_ID + 128] = np.eye(128, dtype=np.float32)
    c = np.arange(64)
    a = 2 * np.pi * np.outer(c, c) / 64.0
    for gq in range(2):
        cb[gq * 64:(gq + 1) * 64, B_DFT + gq * 64:B_DFT + (gq + 1) * 64] = np.cos(a)
        cb[gq * 64:(gq + 1) * 64, B_DFT + 128 + gq * 64:B_DFT + 128 + (gq + 1) * 64] = np.sin(a)
    ki = np.arange(128)[:, None]
    qi = np.arange(128)[None, :]
    for jt in range(8):
        for s in range(3):
            if sample_mode:
                m = (ki >= qi) if s == 0 else (np.ones((128, 128), bool) if s == 1 else (ki <= qi))
            else:
                if s == 1:
                    m = np.ones((128, 128), bool)
                elif s == 0:
                    m = np.full((128, 128), jt % 2 == 1)
                else:
                    m = np.full((128, 128), jt % 2 == 0)
            cb[:, B_MASK + (jt * 3 + s) * 128:B_MASK + (jt * 3 + s + 1) * 128] = m.astype(np.float32)

    def dft(L):
        ll = np.arange(L)
        aa = 2 * np.pi * np.outer(ll, ll) / L
        sc_ = 1.0 / np.sqrt(L * 64.0)
        return (np.cos(aa) * sc_).astype(np.float32), (-np.sin(aa) * sc_).astype(np.float32)

    dftp = np.stack(dft(256))
    if sample_mode:
        dftl = np.stack(dft(1024))
    else:
        dftl = np.zeros((2, 1024, 1024), np.float32)
        for i in range(4):
            dftl[:, i * 256:(i + 1) * 256, i * 256:(i + 1) * 256] = dftp
    return cf, cb, dftl, dftp


_NC_CACHE = {}


def _prep(inputs):
    I = {k: np.asarray(v) for k, v in inputs.items()}
    xp, xs = I["x_prompt"], I["x_sample"]
    consts = {True: _consts(True), False: _consts(False)}
    fm = lambda v: np.ascontiguousarray(v.reshape(-1, 128).T)
    in_maps = []
    assign = []
    for core in range(8):
        sm = core < 4
        if sm:
            gtok = xs[core]
            pseq = [2 * core, 2 * core + 1]
            gseq = []
            cond = np.stack([I["c"][core], I["c_ctx"]])
        else:
            base = 8 + 6 * (core - 4)
            gseq = [base, base + 1, base + 2, base + 3]
            gtok = xp[gseq].reshape(1024, D)
            pseq = [base + 4, base + 5]
            cond = np.stack([I["c_ctx"], I["c_ctx"]])
        assign.append((sm, gseq, pseq))
        xin = np.concatenate([gtok, xp[pseq].reshape(512, D)], axis=0)
        par = np.zeros((128, NPAR), np.float32)
        par[:, P_COND:P_COND + 16] = cond.reshape(2, 8, 128).transpose(2, 1, 0).reshape(128, 16)
        for l in range(2):
            par[:, P_BADA + l * 72:P_BADA + (l + 1) * 72] = fm(I["b_ada"][l])
            for w3, nm in enumerate(["norm_ffn1", "norm_mix", "norm_ffn2"]):
                par[:, P_NRM + l * 24 + w3 * 8:P_NRM + l * 24 + w3 * 8 + 8] = fm(I[nm][l])
            for dr in range(2):
                par[:, P_LBS + l * 4 + dr * 2:P_LBS + l * 4 + dr * 2 + 2] = fm(I["hgrn_lower_bounds"][l, dr])
            par[:, P_HNRM + l * 2:P_HNRM + l * 2 + 2] = fm(I["hgrn_norm"][l])
            par[:, P_QN + l] = np.tile(I["q_norm"][l], 2)
            par[:, P_KN + l] = np.tile(I["k_norm"][l], 2)
            par[:, P_SINK + l * 8:P_SINK + (l + 1) * 8] = I["attn_sink"][l][None, :]
        par[:, P_KEEP] = 1.0 if sm else 0.0
        par[:, P_CBIAS] = 0.0 if sm else -30000.0
        cf, cb, dftl, dftp = consts[sm]
        if sm:
            ck = I["cache_attn_k"][core].reshape(2, 512, 128)
            cv = I["cache_attn_v"][core].reshape(2, 512, 128)
            s0 = I["state_hgrn"][core]
        else:
            ck = np.zeros((2, 512, 128), np.float32)
            cv = np.zeros((2, 512, 128), np.float32)
            s0 = np.zeros((2, 2, 4, 64, 64), np.float32)
        in_maps.append(dict(
            xin=np.ascontiguousarray(xin), par=par, cf=cf, cb=cb, dftl=dftl, dftp=dftp,
            ck=np.ascontiguousarray(ck), cv=np.ascontiguousarray(cv), s0=np.ascontiguousarray(s0),
            w_ada=I["w_ada"], ffn1_gu=I["ffn1_w_gate_up"], ffn2_gu=I["ffn2_w_gate_up"],
            ffn1_d=I["ffn1_w_down"], ffn2_d=I["ffn2_w_down"], w_in=I["w_in"], w_out=I["w_out"]))
    return in_maps, assign


def _post(results, assign):
    y_p = np.zeros((32, 256, D), np.float32)
    y_s = np.zeros((4, 1024, D), np.float32)
    nk = np.zeros((32, 2, 256, 2, 64), np.float32)
    nv = np.zeros((32, 2, 256, 2, 64), np.float32)
    ns = np.zeros((32, 2, 2, 4, 64, 64), np.float32)
    for core in range(8):
        r = results[core]
        sm, gseq, pseq = assign[core]
        y = r["yout"]
        ko = r["kout"].reshape(2, TOK, 2, 64)
        vo = r["vout"].reshape(2, TOK, 2, 64)
        so = r["sout"]
        if sm:
            y_s[core] = y[:1024]
        seqs = [(s, i) for i, s in enumerate(gseq)] + [(s, 4 + i) for i, s in enumerate(pseq)]
        for s, slot in seqs:
            y_p[s] = y[slot * 256:(slot + 1) * 256]
            nk[s] = ko[:, slot * 256:(slot + 1) * 256]
            nv[s] = vo[:, slot * 256:(slot + 1) * 256]
            ns[s] = so[:, slot]
    return (y_p, y_s, nk, nv, ns)


def kernel(**inputs):
    in_maps, assign = _prep(inputs)
    if "nc" not in _NC_CACHE:
        _NC_CACHE["nc"] = build()
    res = run_bass_kernel_spmd(_NC_CACHE["nc"], in_maps, core_ids=list(range(8)))
    return _post(res.results, assign)
```

```python
import numpy as np
from contextlib import ExitStack
import concourse.bass as bass
import concourse.mybir as mybir
from concourse.bass_utils import run_bass_kernel_spmd

F32 = mybir.dt.float32
BF16 = mybir.dt.bfloat16
AF = mybir.ActivationFunctionType
ALU = mybir.AluOpType

EP = 8192
ENGS = ("pe", "act", "dve", "pool", "sp")
SAME_ENGINE_SYNC = ("act", "dve", "pool")


class Buf:
    def __init__(self, name, fence=None):
        self.name = name
        self.w = None
        self.r = list(fence) if fence else []
        self.dsem = None
        self.dcnt = 0
        self.excl = False


class Op:
    pass


class Tracker:
    def __init__(self):
        self.ops = []
        self.per_eng = {e: [] for e in ENGS}
        self.dma_bufs = []
        self.dma_ids = []

    def fence(self):
        ids = [l[-1].id for l in self.per_eng.values() if l]
        return ids + list(self.dma_ids)

    def op(self, eng, fn, reads=(), writes=(), dma_buf=None):
        o = Op()
        o.eng = eng
        o.fn = fn
        o.dma_buf = dma_buf
        o.marked = False
        o.id = len(self.ops)
        o.pos = len(self.per_eng[eng])
        deps = set()
        for b in reads:
            if b.w is not None:
                deps.add(b.w)
            if b.excl:
                deps.update(b.r)
        for b in writes:
            if b.w is not None:
                deps.add(b.w)
            deps.update(b.r)
        o.deps = deps
        for b in reads:
            b.r.append(o.id)
        for b in writes:
            b.w = o.id
            b.r = []
        if dma_buf is not None:
            if dma_buf.dsem is None:
                dma_buf.dsem = True
                self.dma_bufs.append(dma_buf)
            dma_buf.dcnt += 16
            o.dval = dma_buf.dcnt
            self.dma_ids.append(o.id)
        self.ops.append(o)
        self.per_eng[eng].append(o)
        return o

    def plan(self):
        ops = self.ops
        seen = {e: {f: -1 for f in ENGS} for e in ENGS}
        seen_dma = {e: {} for e in ENGS}
        for o in ops:
            w_eng = {}
            w_dma = {}
            for d in o.deps:
                p = ops[d]
                if p.dma_buf is not None:
                    b = p.dma_buf
                    if seen_dma[o.eng].get(id(b), 0) >= p.dval:
                        continue
                    w_dma[id(b)] = (b, max(p.dval, w_dma.get(id(b), (None, 0))[1]))
                else:
                    if p.eng == o.eng:
                        if p.eng not in SAME_ENGINE_SYNC:
                            continue
                        if p.pos < o.pos - 2:
                            continue
                    if seen[o.eng][p.eng] >= d:
                        continue
                    w_eng[p.eng] = max(w_eng.get(p.eng, -1), d)
            for f, d in w_eng.items():
                seen[o.eng][f] = d
                ops[d].marked = True
            for k, (b, v) in w_dma.items():
                seen_dma[o.eng][k] = v
            o.w_eng = w_eng
            o.w_dma = list(w_dma.values())
        cnt = {e: 0 for e in ENGS}
        for o in ops:
            if o.dma_buf is None and o.marked:
                cnt[o.eng] += 1
            o.cnt = cnt[o.eng]
        self.final_cnt = cnt

    def emit(self, nc, final_eng="sp"):
        self.plan()
        ops = self.ops
        with ExitStack() as st:
            esems = {}
            for e in ENGS:
                n = max(1, (self.final_cnt[e] + EP - 1) // EP)
                esems[e] = [st.enter_context(nc.semaphore(f"s_{e}_{i}")) for i in range(n)]
            for i, b in enumerate(self.dma_bufs):
                b.dsem = st.enter_context(nc.semaphore(f"d_{i}"))
            block = st.enter_context(nc.Block())

            def run(eng_name, eng):
                for o in self.per_eng[eng_name]:
                    for f, d in o.w_eng.items():
                        c = ops[d].cnt
                        eng.wait_ge(esems[f][(c - 1) // EP], (c - 1) % EP + 1)
                    for (b, v) in o.w_dma:
                        eng.wait_ge(b.dsem, v)
                    ins = o.fn(eng)
                    if o.dma_buf is not None:
                        ins.then_inc(o.dma_buf.dsem, 16)
                    elif o.marked:
                        c = o.cnt
                        ins.then_inc(esems[eng_name][(c - 1) // EP], 1)
                if eng_name == final_eng:
                    for b in self.dma_bufs:
                        eng.wait_ge(b.dsem, b.dcnt)
                    for f in ENGS:
                        c = self.final_cnt[f]
                        if c > 0 and f != eng_name:
                            eng.wait_ge(esems[f][(c - 1) // EP], (c - 1) % EP + 1)

            @block.tensor
            def _(e):
                run("pe", e)

            @block.scalar
            def _(e):
                run("act", e)

            @block.vector
            def _(e):
                run("dve", e)

            @block.gpsimd
            def _(e):
                run("pool", e)

            @block.sync
            def _(e):
                run("sp", e)


TOK = 1536
NT = 12
NG = 3
D = 1024
DFF = 2816
EPS = 1e-6
P_COND, P_BADA, P_NRM, P_LBS, P_HNRM, P_QN, P_KN, P_SINK, P_KEEP, P_CBIAS, NPAR = 0, 16, 160, 208, 216, 220, 222, 224, 240, 241, 242
C_ID, C_PERM, C_COS, C_SIN, C_MF, C_MB, NCF = 0, 128, 256, 1280, 2304, 2368, 2432
B_ONES, B_BLK, B_ID, B_DFT, B_MASK, NCB = 0, 128, 256, 384, 640, 640 + 24 * 128
SLOT = 4096
ENABLE_HGRN = True
DEBUG_PAIR = None
ATTN_CORE = True
SKIP = set()
ENABLE_ATTN = True
NSL = 3
ARENA = 23400


def build(stop_after=None):
    nc = bass.Bass("TRN2", target_bir_lowering=False)
    T = Tracker()
    di = lambda n, s: nc.dram_tensor(n, s, F32, kind="ExternalInput").ap()
    do = lambda n, s: nc.dram_tensor(n, s, F32, kind="ExternalOutput").ap()
    xin = di("xin", [TOK, D]); par_d = di("par", [128, NPAR]); cf_d = di("cf", [128, NCF]); cb_d = di("cb", [128, NCB])
    dftl_d = di("dftl", [2, 1024, 1024]); dftp_d = di("dftp", [2, 256, 256])
    ck_d = di("ck", [2, 512, 128]); cv_d = di("cv", [2, 512, 128]); s0_d = di("s0", [2, 2, 4, 64, 64])
    wada_d = di("w_ada", [2, D, 9 * D])
    wgu_d = [di("ffn1_gu", [2, D, 2 * DFF]), di("ffn2_gu", [2, D, 2 * DFF])]
    wdn_d = [di("ffn1_d", [2, DFF, D]), di("ffn2_d", [2, DFF, D])]
    win_d = di("w_in", [2, D, 2304]); wout_d = di("w_out", [2, D, D])
    yout = do("yout", [TOK, D]); kout = do("kout", [2, TOK, 128]); vout = do("vout", [2, TOK, 128])
    sout = do("sout", [2, 6, 2, 4, 64, 64])

    st = ExitStack()
    sbt = lambda n, s, dt: st.enter_context(nc.sbuf_tensor("sb_" + n, s, dt))
    xT = sbt("xT", [128, 8, TOK], F32)
    hT = sbt("hT", [128, 8, TOK], BF16)
    par = sbt("par", [128, NPAR], F32)
    cf = sbt("cf", [128, NCF], F32)
    cb = sbt("cb", [128, NCB], BF16)
    modv = sbt("modv", [128, 2, 72, 2], F32)
    mA = sbt("mA", [128, 2, 3, 8, 2], F32)
    mG = sbt("mG", [128, 2, 3, 8, 2], F32)
    sml = sbt("sml", [128, 64], F32)
    slots = [sbt(f"slot{i}", [128, SLOT], BF16) for i in range(NSL)]
    arena = sbt("arena", [128, ARENA], F32)
    psum = [st.enter_context(nc.psum_tensor(f"ps{i}", [128, 512], F32)) for i in range(8)]
    bps = [Buf(f"ps{i}") for i in range(8)]
    for b_ in bps:
        b_.excl = True
    bslot = [Buf(f"slot{i}") for i in range(NSL)]
    bx = [[Buf(f"x{c}_{g}") for g in range(NG)] for c in range(8)]
    bh = [[Buf(f"h{c}_{g}") for g in range(NG)] for c in range(8)]
    bpar, bcf, bcb, bmod = Buf("par"), Buf("cf"), Buf("cb"), Buf("mod")
    bsml = Buf("sml")

    state = {"ps": 0, "aoff": 0}

    def nps():
        i = state["ps"] % 8
        state["ps"] += 1
        return psum[i], bps[i]

    def aalloc(n32):
        o = state["aoff"]
        state["aoff"] += n32
        assert state["aoff"] <= ARENA, state["aoff"]
        return arena[:, o:o + n32]

    def areset():
        state["aoff"] = 0

    def nb(name):
        return Buf(name, T.fence())

    PE = lambda fn, r, w: T.op("pe", fn, r, w)
    ACT = lambda fn, r, w: T.op("act", fn, r, w)
    DVE = lambda fn, r, w: T.op("dve", fn, r, w)
    POOL = lambda fn, r, w: T.op("pool", fn, r, w)

    def DMA(eng, out, in_, buf, reads=(), writes=()):
        T.op(eng, lambda q: q.dma_start(out=out, in_=in_), reads, writes, dma_buf=buf)

    wq = []
    wstate = {"loaded": 0, "used": 0}

    def witem(parts):
        wq.append(parts)

    def wnext(keep_prev=False):
        i = wstate["used"]
        oldest = i - 1 if keep_prev else i
        while wstate["loaded"] < min(len(wq), oldest + NSL):
            k = wstate["loaded"]
            s = k % NSL
            for (fn, dap) in wq[k]:
                DMA("pool", fn(slots[s]), dap, bslot[s], writes=[bslot[s]])
            wstate["loaded"] += 1
        wstate["used"] += 1
        return slots[i % NSL], bslot[i % NSL]

    def kview(ap2d):
        return ap2d.rearrange("(c p) n -> p c n", p=128)

    def sched_weights():
        for l in range(2):
            for i in range(18):
                witem([(lambda s: s[:, 0:4096].rearrange("p (c n) -> p c n", c=8), kview(wada_d[l][:, i * 512:(i + 1) * 512]))])
        for l in range(2):
            for f in range(2):
                sched_ffn(l, f)
                if f == 0:
                    sched_mixer(l)

    def sched_ffn(l, f):
        for h in range(2):
            for j0 in range(0, 11, 2):
                nf = min(2, 11 - j0)
                c0 = (h * 11 + j0) * 128
                witem([
                    (lambda s, nf=nf: s[:, 0:8 * nf * 128].rearrange("p (c n) -> p c n", c=8), kview(wgu_d[f][l][:, c0:c0 + nf * 128])),
                    (lambda s, nf=nf: s[:, 2048:2048 + 8 * nf * 128].rearrange("p (c n) -> p c n", c=8), kview(wgu_d[f][l][:, DFF + c0:DFF + c0 + nf * 128])),
                ])

    def sched_mixer(l):
        w = win_d[l]
        witem([(lambda s: s[:, 0:2048].rearrange("p (c n) -> p c n", c=8), kview(w[:, 0:256]))])
        for hf in range(2):
            for cs in range(2):
                witem([(lambda s: s[:, 0:4096].rearrange("p (c n) -> p c n", c=8), kview(dftl_d[cs][:, hf * 512:(hf + 1) * 512]))])
        witem([(lambda s, cs=cs: s[:, cs * 512:(cs + 1) * 512].rearrange("p (c n) -> p c n", c=2), kview(dftp_d[cs])) for cs in range(2)])
        for pr in range(2):
            cols = [256 + pr * 128, 512 + pr * 128, 768 + pr * 128, 1024 + pr * 128, 1280 + pr * 128]
            witem([(lambda s, k=k: s[:, k * 1024:(k + 1) * 1024].rearrange("p (c n) -> p c n", c=8), kview(w[:, cols[k]:cols[k] + 128])) for k in range(3)])
            witem([(lambda s, k=k: s[:, k * 1024:(k + 1) * 1024].rearrange("p (c n) -> p c n", c=8), kview(w[:, cols[3 + k]:cols[3 + k] + 128])) for k in range(2)])
        witem([(lambda s: s[:, 0:4096].rearrange("p (c n) -> p c n", c=8), kview(w[:, 1536:2048]))])
        parts = []
        for kv in range(2):
            for dup in range(2):
                parts.append((lambda s, kv=kv, dup=dup: s[:, kv * 1024:(kv + 1) * 1024].rearrange("p (c n) -> p c n", c=8)[:, :, dup * 64:(dup + 1) * 64],
                              kview(w[:, 2048 + kv * 64:2048 + (kv + 1) * 64])))
        parts.append((lambda s: s[:, 2048:3072].rearrange("p (c n) -> p c n", c=8), kview(w[:, 2176:2304])))
        witem(parts)
        for i in range(2):
            witem([(lambda s: s[:, 0:4096].rearrange("p (c n) -> p c n", c=8), kview(wout_d[l][:, i * 512:(i + 1) * 512]))])

    sched_weights()

    DMA("sp", par[:], par_d, bpar, writes=[bpar])
    DMA("sp", cf[:], cf_d, bcf, writes=[bcf])
    DMA("pool", cb[:], cb_d, bcb, writes=[bcb])
    ident = cf[:, C_ID:C_ID + 128]
    ones_b = cb[:, B_ONES:B_ONES + 128]
    blk_b = cb[:, B_BLK:B_BLK + 128]
    ident_b = cb[:, B_ID:B_ID + 128]

    areset()
    xst = [aalloc(1024), aalloc(1024)]
    bxst = [nb("xst0"), nb("xst1")]
    for t in range(NT):
        s = t % 2
        DMA("sp", xst[s], xin[t * 128:(t + 1) * 128, :], bxst[s], writes=[bxst[s]])
        for half in range(2):
            ps, bp = nps()
            for k in range(4):
                c = half * 4 + k
                PE(lambda q, ps=ps, k=k, c=c, s=s: q.transpose(ps[:, k * 128:(k + 1) * 128], xst[s][:, c * 128:(c + 1) * 128], ident),
                   [bxst[s], bcf], [bp])
            g = t // 4
            eng = ACT if half == 0 else DVE
            fn = (lambda q, ps=ps, half=half, t=t: q.activation(out=xT[:, half * 4:half * 4 + 4, t * 128:(t + 1) * 128], in_=ps[:].rearrange("p (k n) -> p k n", k=4), func=AF.Copy)) if half == 0 else \
                 (lambda q, ps=ps, half=half, t=t: q.tensor_copy(out=xT[:, half * 4:half * 4 + 4, t * 128:(t + 1) * 128], in_=ps[:].rearrange("p (k n) -> p k n", k=4)))
            eng(fn, [bp], [bx[c][g] for c in range(half * 4, half * 4 + 4)])

    areset()
    sc = aalloc(16)
    scb = aalloc(8).bitcast(BF16)
    bsc = nb("sc")
    ACT(lambda q: q.activation(out=sc, in_=par[:, P_COND:P_COND + 16], func=AF.Exp, scale=-1.0), [bpar], [bsc])
    DVE(lambda q: q.tensor_scalar_add(out=sc, in0=sc, scalar1=1.0), [bsc], [bsc])
    DVE(lambda q: q.reciprocal(out=sc, in_=sc), [bsc], [bsc])
    DVE(lambda q: q.tensor_tensor(out=scb, in0=sc, in1=par[:, P_COND:P_COND + 16], op=ALU.mult), [bsc, bpar], [bsc])
    scb3 = scb.rearrange("p (c r) -> p c r", c=8)
    for l in range(2):
        ps, bp = nps()
        for i in range(18):
            sl, bs = wnext()
            wv = sl[:, 0:4096].rearrange("p (c n) -> p c n", c=8)
            for k in range(4):
                nn = i * 4 + k
                for c in range(8):
                    PE(lambda q, ps=ps, wv=wv, k=k, c=c, nn=nn: q.matmul(ps[:, nn * 2:nn * 2 + 2], lhsT=wv[:, c, k * 128:(k + 1) * 128], rhs=scb3[:, c, :], start=(c == 0), stop=(c == 7)),
                       [bs, bsc], [bp])
        DVE(lambda q, ps=ps, l=l: q.tensor_tensor(out=modv[:, l, :, :], in0=ps[:, 0:144].rearrange("p (n r) -> p n r", r=2),
                                                 in1=par[:, P_BADA + l * 72:P_BADA + (l + 1) * 72].unsqueeze(2).to_broadcast([128, 72, 2]), op=ALU.add),
            [bp, bpar], [bmod])
        for w3 in range(3):
            nrm = par[:, P_NRM + l * 24 + w3 * 8:P_NRM + l * 24 + w3 * 8 + 8].unsqueeze(2).to_broadcast([128, 8, 2])
            DVE(lambda q, l=l, w3=w3, nrm=nrm: q.scalar_tensor_tensor(out=mA[:, l, w3, :, :], in0=modv[:, l, (3 * w3 + 1) * 8:(3 * w3 + 2) * 8, :], scalar=1.0, in1=nrm, op0=ALU.add, op1=ALU.mult),
                [bmod, bpar], [bmod])
            DVE(lambda q, l=l, w3=w3: q.tensor_scalar_mul(out=mG[:, l, w3, :, :], in0=modv[:, l, (3 * w3 + 2) * 8:(3 * w3 + 3) * 8, :], scalar1=(1.0 if w3 == 1 else 0.5)),
                [bmod], [bmod])

    def mod_A(l, w3, c, g):
        r = 0 if g < 2 else 1
        return mA[:, l, w3, c, r:r + 1]

    def mod_B(l, w3, c, g):
        r = 0 if g < 2 else 1
        return modv[:, l, (3 * w3) * 8 + c, r:r + 1]

    def mod_G(l, w3, c, g):
        r = 0 if g < 2 else 1
        return mG[:, l, w3, c, r:r + 1]

    def norm_mod(l, w3):
        sq = [aalloc(256).bitcast(BF16), aalloc(256).bitcast(BF16)]
        bsq = [nb("sq0"), nb("sq1")]
        tmp = [aalloc(512), aalloc(512)]
        btmp = [nb("tmp0"), nb("tmp1")]
        rs = aalloc(512)
        brs = nb("rs")
        for g in range(NG):
            ps, bp = nps()
            for c in range(8):
                s = c % 2
                ACT(lambda q, s=s, c=c, g=g: q.activation(out=sq[s], in_=xT[:, c, g * 512:(g + 1) * 512], func=AF.Square), [bx[c][g]], [bsq[s]])
                PE(lambda q, ps=ps, s=s, c=c: q.matmul(ps[:], lhsT=ones_b, rhs=sq[s], start=(c == 0), stop=(c == 7)), [bsq[s], bcb], [bp])
            DVE(lambda q, ps=ps: q.tensor_scalar(out=rs, in0=ps[:], scalar1=1.0 / D, scalar2=EPS, op0=ALU.mult, op1=ALU.add), [bp], [brs])
            ACT(lambda q: q.activation(out=rs, in_=rs, func=AF.Ln), [brs], [brs])
            ACT(lambda q: q.activation(out=rs, in_=rs, func=AF.Exp, scale=-0.5), [brs], [brs])
            for c in range(8):
                s = c % 2
                DVE(lambda q, s=s, c=c, g=g: q.tensor_tensor(out=tmp[s], in0=xT[:, c, g * 512:(g + 1) * 512], in1=rs, op=ALU.mult), [bx[c][g], brs], [btmp[s]])
                ACT(lambda q, s=s, c=c, g=g: q.activation(out=hT[:, c, g * 512:(g + 1) * 512], in_=tmp[s], func=AF.Identity,
                                                          scale=mod_A(l, w3, c, g), bias=mod_B(l, w3, c, g)), [btmp[s], bmod], [bh[c][g]])

    def ffn(l, f):
        w3 = 0 if f == 0 else 2
        areset()
        norm_mod(l, w3)
        actT = aalloc(11 * TOK // 2).bitcast(BF16).rearrange("p (j n) -> p j n", j=11)
        bact = [nb(f"act{g}") for g in range(NG)]
        wd = aalloc(11 * 1024 // 2).bitcast(BF16).rearrange("p (j n) -> p j n", j=11)
        bwd = nb("wd")
        ebuf = [aalloc(512), aalloc(512)]
        bebuf = [nb("e0"), nb("e1")]
        tbuf = [aalloc(512), aalloc(512)]
        btbuf = [nb("t0"), nb("t1")]
        k = 0
        for h in range(2):
            DMA("pool", wd, wdn_d[f][l][h * 1408:(h + 1) * 1408, :].rearrange("(j p) n -> p j n", p=128), bwd, writes=[bwd])
            for j0 in range(0, 11, 2):
                nf = min(2, 11 - j0)
                sl, bs = wnext()
                wg = sl[:, 0:8 * nf * 128].rearrange("p (c n) -> p c n", c=8)
                wu = sl[:, 2048:2048 + 8 * nf * 128].rearrange("p (c n) -> p c n", c=8)
                for jj in range(nf):
                    j = j0 + jj
                    for g in range(NG):
                        pg, bpg = nps()
                        pu, bpu = nps()
                        for c in range(8):
                            PE(lambda q, pg=pg, wg=wg, c=c, jj=jj, g=g: q.matmul(pg[:], lhsT=wg[:, c, jj * 128:(jj + 1) * 128], rhs=hT[:, c, g * 512:(g + 1) * 512], start=(c == 0), stop=(c == 7)),
                               [bs, bh[c][g]], [bpg])
                        for c in range(8):
                            PE(lambda q, pu=pu, wu=wu, c=c, jj=jj, g=g: q.matmul(pu[:], lhsT=wu[:, c, jj * 128:(jj + 1) * 128], rhs=hT[:, c, g * 512:(g + 1) * 512], start=(c == 0), stop=(c == 7)),
                               [bs, bh[c][g]], [bpu])
                        s = k % 2
                        k += 1
                        ACT(lambda q, pg=pg, s=s: q.activation(out=tbuf[s], in_=pg[:], func=AF.Silu), [bpg], [btbuf[s]])
                        DVE(lambda q, pu=pu, s=s, j=j, g=g: q.tensor_tensor(out=actT[:, j, g * 512:(g + 1) * 512], in0=pu[:], in1=tbuf[s], op=ALU.mult), [bpu, btbuf[s]], [bact[g]])
            for g in range(NG):
                for dc in range(8):
                    ps, bp = nps()
                    for j in range(11):
                        PE(lambda q, ps=ps, j=j, dc=dc, g=g: q.matmul(ps[:], lhsT=wd[:, j, dc * 128:(dc + 1) * 128], rhs=actT[:, j, g * 512:(g + 1) * 512], start=(j == 0), stop=(j == 10)),
                           [bwd, bact[g]], [bp])
                    DVE(lambda q, ps=ps, dc=dc, g=g: q.scalar_tensor_tensor(out=xT[:, dc, g * 512:(g + 1) * 512], in0=ps[:], scalar=mod_G(l, w3, dc, g), in1=xT[:, dc, g * 512:(g + 1) * 512], op0=ALU.mult, op1=ALU.add),
                        [bp, bmod, bx[dc][g]], [bx[dc][g]])

    def mixer(l):
        areset()
        norm_mod(l, 1)
        areset()
        catT = aalloc(8 * TOK // 2).bitcast(BF16).rearrange("p (c n) -> p c n", c=8)
        bcat = [[nb(f"cat{c}_{g}") for g in range(NG)] for c in range(8)]
        base_off = state["aoff"]

        def inproj(wv, k, g, ps):
            for c in range(8):
                PE(lambda q, c=c: q.matmul(ps[0][:], lhsT=wv[:, c, k * 128:(k + 1) * 128], rhs=hT[:, c, g * 512:(g + 1) * 512], start=(c == 0), stop=(c == 7)),
                   [ps[2], bh[c][g]], [ps[1]])

        uT = aalloc(2 * TOK // 2).bitcast(BF16).rearrange("p (c n) -> p c n", c=2)
        buT = nb("uT")
        ucs = aalloc(NT * 2 * 256 // 2).bitcast(BF16).rearrange("p (t c n) -> p t c n", t=NT, c=2)
        bucs = nb("ucs")
        sl, bs = wnext()
        wv = sl[:, 0:2048].rearrange("p (c n) -> p c n", c=8)
        for k in range(2):
            for g in range(NG):
                ps, bp = nps()
                inproj(wv, k, g, (ps, bp, bs))
                ACT(lambda q, ps=ps, k=k, g=g: q.activation(out=uT[:, k, g * 512:(g + 1) * 512], in_=ps[:], func=AF.Copy), [bp], [buT])
        dft64 = cb[:, B_DFT:B_DFT + 256]
        for t in range(NT):
            ps, bp = nps()
            for k in range(2):
                PE(lambda q, ps=ps, k=k, t=t: q.matmul(ps[:, k * 256:(k + 1) * 256], lhsT=uT[:, k, t * 128:(t + 1) * 128], rhs=dft64, start=True, stop=True), [buT, bcb], [bp])
            DVE(lambda q, ps=ps, t=t: q.tensor_copy(out=ucs[:, t, :, :], in_=ps[:].rearrange("p (c n) -> p c n", c=2)), [bp], [bucs])
        for hf in range(2):
            slc, bsc_ = wnext()
            sls, bss = wnext(keep_prev=True)
            cl = slc[:, 0:4096].rearrange("p (c n) -> p c n", c=8)
            sn = sls[:, 0:4096].rearrange("p (c n) -> p c n", c=8)
            for k in range(2):
                ps, bp = nps()
                for lc in range(8):
                    PE(lambda q, ps=ps, k=k, lc=lc, cl=cl: q.matmul(ps[:], lhsT=ucs[:, lc, k, 0:128], rhs=cl[:, lc, :], start=(lc == 0), stop=False), [bucs, bsc_], [bp])
                    PE(lambda q, ps=ps, k=k, lc=lc, sn=sn: q.matmul(ps[:], lhsT=ucs[:, lc, k, 128:256], rhs=sn[:, lc, :], start=False, stop=(lc == 7)), [bucs, bss], [bp])
                ACT(lambda q, ps=ps, k=k, hf=hf: q.activation(out=catT[:, k, hf * 512:(hf + 1) * 512], in_=ps[:], func=AF.Copy), [bp], [bcat[k][hf]])
        slp, bsp = wnext()
        pc = slp[:, 0:512].rearrange("p (c n) -> p c n", c=2)
        pn = slp[:, 512:1024].rearrange("p (c n) -> p c n", c=2)
        for k in range(2):
            ps, bp = nps()
            for sq_ in range(2):
                for lc in range(2):
                    tt = 8 + sq_ * 2 + lc
                    PE(lambda q, ps=ps, k=k, lc=lc, tt=tt, sq_=sq_: q.matmul(ps[:, sq_ * 256:(sq_ + 1) * 256], lhsT=ucs[:, tt, k, 0:128], rhs=pc[:, lc, :], start=(lc == 0), stop=False), [bucs, bsp], [bp])
                    PE(lambda q, ps=ps, k=k, lc=lc, tt=tt, sq_=sq_: q.matmul(ps[:, sq_ * 256:(sq_ + 1) * 256], lhsT=ucs[:, tt, k, 128:256], rhs=pn[:, lc, :], start=False, stop=(lc == 1)), [bucs, bsp], [bp])
            ACT(lambda q, ps=ps, k=k: q.activation(out=catT[:, k, 1024:1536], in_=ps[:], func=AF.Copy), [bp], [bcat[k][2]])

        if ENABLE_HGRN:
            for pr in range(2):
                if DEBUG_PAIR is not None and pr != DEBUG_PAIR:
                    wnext(); wnext()
                    for g in range(NG):
                        POOL(lambda q, pr=pr, g=g: q.memset(catT[:, 2 + pr, g * 512:(g + 1) * 512], 0.0), [], [bcat[2 + pr][g]])
                    continue
                state["aoff"] = base_off
                qf, sF, sB, tmpb, kb, hoT = [aalloc(TOK) for _ in range(6)]
                bqf, bsF, bsB, btmpb, bkb, bho = [nb(n) for n in ("qf", "sF", "sB", "tmpb", "kb", "hoT")]
                qt = [aalloc(TOK // 2).bitcast(BF16) for _ in range(2)]
                kt_ = [aalloc(TOK // 2).bitcast(BF16) for _ in range(2)]
                vT = aalloc(TOK // 2).bitcast(BF16)
                sgT = aalloc(TOK // 2).bitcast(BF16)
                bqt, bkt = [nb("qt0"), nb("qt1")], [nb("kt0"), nb("kt1")]
                bvT, bsg = nb("vT"), nb("sgT")
                ktp = [aalloc(128).bitcast(BF16).rearrange("p (a n) -> p a n", a=2) for _ in range(3)]
                vpd = [aalloc(128).bitcast(BF16).rearrange("p (a n) -> p a n", a=2) for _ in range(3)]
                bktp, bvpd = [nb("ktp0"), nb("ktp1"), nb("ktp2")], [nb("vpd0"), nb("vpd1"), nb("vpd2")]
                atb = [aalloc(64).bitcast(BF16) for _ in range(2)]
                batb = [nb("at0"), nb("at1")]
                S = aalloc(128); tS = aalloc(128); S0m = aalloc(64).bitcast(BF16)
                bS, btS, bS0m = nb("S"), nb("tS"), nb("S0m")
                sst = [aalloc(128), aalloc(128)]
                bsst = [nb("sst0"), nb("sst1")]
                scl = [[aalloc(48) for _ in range(4)] for _ in range(2)]
                bscl = [nb("scl0"), nb("scl1")]
                lbp = aalloc(8)
                blbp = nb("lbp")
                for i in range(3):
                    POOL(lambda q, i=i: q.memset(ktp[i], 0.0), [], [bktp[i]])
                    POOL(lambda q, i=i: q.memset(vpd[i], 0.0), [], [bvpd[i]])
                for dr in range(2):
                    c0 = dr * 3
                    if l == 0:
                        POOL(lambda q, c0=c0: q.memset(lbp[:, c0:c0 + 1], 1.0), [], [blbp])
                        POOL(lambda q, c0=c0: q.memset(lbp[:, c0 + 1:c0 + 2], 1e-30), [], [blbp])
                        POOL(lambda q, c0=c0: q.memset(lbp[:, c0 + 2:c0 + 3], -1.0), [], [blbp])
                    else:
                        a0 = P_LBS + 0 * 4 + dr * 2 + pr
                        a1 = P_LBS + 1 * 4 + dr * 2 + pr
                        DVE(lambda q, c0=c0, a0=a0, a1=a1: q.tensor_tensor(out=lbp[:, c0 + 1:c0 + 2], in0=par[:, a0:a0 + 1], in1=par[:, a1:a1 + 1], op=ALU.subtract), [bpar], [blbp])
                        ACT(lambda q, c0=c0: q.activation(out=lbp[:, c0 + 1:c0 + 2], in_=lbp[:, c0 + 1:c0 + 2], func=AF.Exp), [blbp], [blbp])
                        DVE(lambda q, c0=c0: q.tensor_scalar_add(out=lbp[:, c0 + 1:c0 + 2], in0=lbp[:, c0 + 1:c0 + 2], scalar1=1.0), [blbp], [blbp])
                        DVE(lambda q, c0=c0: q.reciprocal(out=lbp[:, c0 + 1:c0 + 2], in_=lbp[:, c0 + 1:c0 + 2]), [blbp], [blbp])
                        DVE(lambda q, c0=c0: q.tensor_scalar(out=lbp[:, c0:c0 + 1], in0=lbp[:, c0 + 1:c0 + 2], scalar1=-1.0, scalar2=1.0, op0=ALU.mult, op1=ALU.add), [blbp], [blbp])
                        DVE(lambda q, c0=c0: q.tensor_scalar_mul(out=lbp[:, c0 + 2:c0 + 3], in0=lbp[:, c0:c0 + 1], scalar1=-1.0), [blbp], [blbp])
                        DVE(lambda q, c0=c0: q.tensor_scalar_max(out=lbp[:, c0 + 1:c0 + 2], in0=lbp[:, c0 + 1:c0 + 2], scalar1=1e-30), [blbp], [blbp])
                sl, bs = wnext()
                for k in range(3):
                    wv = sl[:, k * 1024:(k + 1) * 1024].rearrange("p (c n) -> p c n", c=8)
                    for g in range(NG):
                        ps, bp = nps()
                        inproj(wv, 0, g, (ps, bp, bs))
                        gs = slice(g * 512, (g + 1) * 512)
                        if k == 0:
                            ACT(lambda q, ps=ps, gs=gs: q.activation(out=qf[:, gs], in_=ps[:], func=AF.Copy, scale=0.125), [bp], [bqf])
                        elif k == 1:
                            ACT(lambda q, ps=ps, gs=gs: q.activation(out=vT[:, gs], in_=ps[:], func=AF.Copy), [bp], [bvT])
                        else:
                            ACT(lambda q, ps=ps, gs=gs: q.activation(out=sF[:, gs], in_=ps[:], func=AF.Exp, scale=-1.0), [bp], [bsF])
                sl, bs = wnext()
                for k in range(2):
                    wv = sl[:, k * 1024:(k + 1) * 1024].rearrange("p (c n) -> p c n", c=8)
                    for g in range(NG):
                        ps, bp = nps()
                        inproj(wv, 0, g, (ps, bp, bs))
                        gs = slice(g * 512, (g + 1) * 512)
                        if k == 0:
                            ACT(lambda q, ps=ps, gs=gs: q.activation(out=sB[:, gs], in_=ps[:], func=AF.Exp, scale=-1.0), [bp], [bsB])
                        else:
                            ACT(lambda q, ps=ps, gs=gs: q.activation(out=tmpb[:, gs], in_=ps[:], func=AF.Exp, scale=-1.0), [bp], [btmpb])
                            DVE(lambda q, gs=gs: q.tensor_scalar_add(out=tmpb[:, gs], in0=tmpb[:, gs], scalar1=1.0), [btmpb], [btmpb])
                            DVE(lambda q, gs=gs: q.reciprocal(out=tmpb[:, gs], in_=tmpb[:, gs]), [btmpb], [btmpb])
                            DVE(lambda q, ps=ps, gs=gs: q.tensor_tensor(out=sgT[:, gs], in0=ps[:], in1=tmpb[:, gs], op=ALU.mult), [bp, btmpb], [bsg])
                for (sx, bsx) in ((sF, bsF), (sB, bsB)):
                    DVE(lambda q, sx=sx: q.tensor_scalar_add(out=sx, in0=sx, scalar1=1.0), [bsx], [bsx])
                    DVE(lambda q, sx=sx: q.reciprocal(out=sx, in_=sx), [bsx], [bsx])
                CH = 32
                NCH = TOK // CH
                SPC = 256 // CH
                for dr, (sx, bsx) in enumerate(((sF, bsF), (sB, bsB))):
                    c0 = dr * 3
                    DVE(lambda q, sx=sx, c0=c0: q.tensor_scalar(out=kb, in0=sx, scalar1=lbp[:, c0 + 2:c0 + 3], scalar2=lbp[:, c0:c0 + 1], op0=ALU.mult, op1=ALU.add), [bsx, blbp], [bkb])
                    DVE(lambda q, sx=sx, c0=c0: q.tensor_scalar(out=sx, in0=sx, scalar1=lbp[:, c0:c0 + 1], scalar2=lbp[:, c0 + 1:c0 + 2], op0=ALU.mult, op1=ALU.add), [bsx, blbp], [bsx])
                    ACT(lambda q, sx=sx: q.activation(out=sx, in_=sx, func=AF.Ln), [bsx], [bsx])
                    A_, bA_, B_, bB_ = sx, bsx, tmpb, btmpb
                    sh = 1
                    while sh < CH:
                        A3 = A_.rearrange("p (c n) -> p c n", n=CH)
                        B3 = B_.rearrange("p (c n) -> p c n", n=CH)
                        if dr == 0:
                            ACT(lambda q, A3=A3, B3=B3, sh=sh: q.activation(out=B3[:, :, 0:sh], in_=A3[:, :, 0:sh], func=AF.Copy), [bA_], [bB_])
                            DVE(lambda q, A3=A3, B3=B3, sh=sh: q.tensor_tensor(out=B3[:, :, sh:CH], in0=A3[:, :, sh:CH], in1=A3[:, :, 0:CH - sh], op=ALU.add), [bA_], [bB_])
                        else:
                            ACT(lambda q, A3=A3, B3=B3, sh=sh: q.activation(out=B3[:, :, CH - sh:CH], in_=A3[:, :, CH - sh:CH], func=AF.Copy), [bA_], [bB_])
                            DVE(lambda q, A3=A3, B3=B3, sh=sh: q.tensor_tensor(out=B3[:, :, 0:CH - sh], in0=A3[:, :, 0:CH - sh], in1=A3[:, :, sh:CH], op=ALU.add), [bA_], [bB_])
                        A_, bA_, B_, bB_ = B_, bB_, A_, bA_
                        sh *= 2
                    bq_, bbq_, tq_, btq_ = A_, bA_, B_, bB_
                    b3 = bq_.rearrange("p (c n) -> p c n", n=CH)
                    mid, Ti = (CH // 2 - 1, CH - 1) if dr == 0 else (CH // 2, 0)
                    bm, em, eT, eTm = scl[dr]
                    DVE(lambda q, b3=b3, mid=mid, bm=bm: q.tensor_copy(out=bm, in_=b3[:, :, mid]), [bbq_], [bscl[dr]])
                    ACT(lambda q, bm=bm, em=em: q.activation(out=em, in_=bm, func=AF.Exp), [bscl[dr]], [bscl[dr]])
                    ACT(lambda q, b3=b3, Ti=Ti, eT=eT: q.activation(out=eT, in_=b3[:, :, Ti], func=AF.Exp), [bbq_], [bscl[dr]])
                    DVE(lambda q, b3=b3, Ti=Ti, bm=bm, eTm=eTm: q.tensor_tensor(out=eTm, in0=b3[:, :, Ti], in1=bm, op=ALU.subtract), [bbq_, bscl[dr]], [bscl[dr]])
                    ACT(lambda q, eTm=eTm: q.activation(out=eTm, in_=eTm, func=AF.Exp), [bscl[dr]], [bscl[dr]])
                    DVE(lambda q, b3=b3, bm=bm: q.tensor_tensor(out=b3, in0=b3, in1=bm.unsqueeze(2).to_broadcast([128, NCH, CH]), op=ALU.subtract), [bbq_, bscl[dr]], [bbq_])
                    ACT(lambda q, bq_=bq_, tq_=tq_: q.activation(out=tq_, in_=bq_, func=AF.Exp), [bbq_], [btq_])
                    DVE(lambda q, dr=dr, tq_=tq_: q.tensor_tensor(out=qt[dr], in0=qf, in1=tq_, op=ALU.mult), [bqf, btq_], [bqt[dr]])
                    ACT(lambda q, bq_=bq_, tq_=tq_: q.activation(out=tq_, in_=bq_, func=AF.Exp, scale=-1.0), [bbq_], [btq_])
                    DVE(lambda q, dr=dr, tq_=tq_: q.tensor_tensor(out=kt_[dr], in0=kb, in1=tq_, op=ALU.mult), [bkb, btq_], [bkt[dr]])
                ev = [0]

                def emit_state(slot, dr):
                    i = ev[0] % 2
                    ev[0] += 1
                    ACT(lambda q, i=i: q.activation(out=sst[i], in_=S, func=AF.Copy), [bS], [bsst[i]])
                    for hd in range(2):
                        DMA("sp", sout[l, slot, dr, 2 * pr + hd], sst[i][hd * 64:(hd + 1) * 64, hd * 64:(hd + 1) * 64], bsst[i], reads=[bsst[i]])

                def load_state(dr):
                    POOL(lambda q: q.memset(S, 0.0), [bS], [bS])
                    for hd in range(2):
                        DMA("sp", S[hd * 64:(hd + 1) * 64, hd * 64:(hd + 1) * 64], s0_d[l, dr, 2 * pr + hd], bS, reads=[bS], writes=[bS])

                Xb = [aalloc(128), aalloc(128)]
                bXb = [nb("X0"), nb("X1")]
                for dr in range(2):
                    bm, em, eT, eTm = scl[dr]
                    order = list(range(NCH)) if dr == 0 else list(range(NCH - 1, -1, -1))
                    mcol = C_MF if dr == 0 else C_MB
                    pst = {}

                    def stageA(n_, dr=dr, order=order):
                        ch = order[n_]
                        i3 = n_ % 3
                        cs = slice(ch * CH, (ch + 1) * CH)
                        ptk, bptk = nps()
                        ptkb = ptk[:].bitcast(BF16)
                        PE(lambda q: q.transpose(ptkb[0:CH, 0:128], kt_[dr][:, cs], ident_b), [bkt[dr], bcb], [bptk])
                        PE(lambda q: q.transpose(ptkb[0:CH, 128:256], vT[:, cs], ident_b), [bvT, bcb], [bptk])
                        ACT(lambda q: q.activation(out=ktp[i3][0:CH, 0, 0:64], in_=ptkb[0:CH, 0:64], func=AF.Copy), [bptk], [bktp[i3]])
                        ACT(lambda q: q.activation(out=ktp[i3][0:CH, 1, 64:128], in_=ptkb[0:CH, 64:128], func=AF.Copy), [bptk], [bktp[i3]])
                        DVE(lambda q: q.tensor_copy(out=vpd[i3][0:CH, 0, 0:64], in_=ptkb[0:CH, 128:192]), [bptk], [bvpd[i3]])
                        DVE(lambda q: q.tensor_copy(out=vpd[i3][0:CH, 1, 64:128], in_=ptkb[0:CH, 192:256]), [bptk], [bvpd[i3]])

                    def stageB(n_, dr=dr, eTm=eTm, mcol=mcol, order=order):
                        ch = order[n_]
                        i = n_ % 2
                        i3 = n_ % 3
                        cs = slice(ch * CH, (ch + 1) * CH)
                        for hd in range(2):
                            hs = slice(hd * 64, (hd + 1) * 64)
                            pA, bpA = nps()
                            PE(lambda q, pA=pA, hs=hs: q.matmul(pA[0:CH, 0:CH], lhsT=kt_[dr][hs, cs], rhs=qt[dr][hs, cs], start=True, stop=True), [bkt[dr], bqt[dr]], [bpA])
                            DVE(lambda q, pA=pA, hd=hd: q.tensor_tensor(out=atb[i][0:CH, hd * CH:(hd + 1) * CH], in0=pA[0:CH, 0:CH], in1=cf[0:CH, mcol:mcol + CH], op=ALU.mult), [bpA, bcf], [batb[i]])
                        pS2, bpS2 = nps()
                        PE(lambda q: q.matmul(pS2[:, 0:128], lhsT=ktp[i3][0:CH, 0, :], rhs=vpd[i3][0:CH, 0, :], start=True, stop=False), [bktp[i3], bvpd[i3]], [bpS2])
                        PE(lambda q: q.matmul(pS2[:, 0:128], lhsT=ktp[i3][0:CH, 1, :], rhs=vpd[i3][0:CH, 1, :], start=False, stop=True), [bktp[i3], bvpd[i3]], [bpS2])
                        ACT(lambda q: q.activation(out=Xb[i], in_=pS2[:, 0:128], func=AF.Copy, scale=eTm[:, ch:ch + 1]), [bpS2, bscl[dr]], [bXb[i]])

                    def chain(n_, dr=dr, em=em, eT=eT, order=order):
                        ch = order[n_]
                        i = n_ % 2
                        i3 = n_ % 3
                        cs = slice(ch * CH, (ch + 1) * CH)
                        first_of_seq = (ch % SPC == 0) if dr == 0 else (ch % SPC == SPC - 1)
                        if first_of_seq:
                            seq = ch // SPC
                            prev = seq - 1 if dr == 0 else seq + 1
                            if n_ > 0:
                                emit_state(prev, dr)
                            if (dr == 0 and seq == 0) or (dr == 1 and seq == 3):
                                load_state(dr)
                            elif seq >= 4:
                                POOL(lambda q: q.memset(S, 0.0), [bS], [bS])
                            else:
                                DVE(lambda q: q.tensor_scalar_mul(out=S, in0=S, scalar1=par[:, P_KEEP:P_KEEP + 1]), [bS, bpar], [bS])
                        DVE(lambda q: q.tensor_scalar_mul(out=S0m, in0=S, scalar1=em[:, ch:ch + 1]), [bS, bscl[dr]], [bS0m])
                        DVE(lambda q: q.scalar_tensor_tensor(out=S, in0=S, scalar=eT[:, ch:ch + 1], in1=Xb[i], op0=ALU.mult, op1=ALU.add), [bS, bscl[dr], bXb[i]], [bS])
                        po, bpo = nps()
                        PE(lambda q: q.matmul(po[:, 0:CH], lhsT=vpd[i3][0:CH, 0, :], rhs=atb[i][0:CH, 0:CH], start=True, stop=False), [bvpd[i3], batb[i]], [bpo])
                        PE(lambda q: q.matmul(po[:, 0:CH], lhsT=vpd[i3][0:CH, 1, :], rhs=atb[i][0:CH, CH:2 * CH], start=False, stop=False), [bvpd[i3], batb[i]], [bpo])
                        PE(lambda q: q.matmul(po[:, 0:CH], lhsT=S0m, rhs=qt[dr][:, cs], start=False, stop=True), [bS0m, bqt[dr]], [bpo])
                        if dr == 0:
                            ACT(lambda q: q.activation(out=hoT[:, cs], in_=po[:, 0:CH], func=AF.Copy), [bpo], [bho])
                        else:
                            DVE(lambda q: q.tensor_tensor(out=hoT[:, cs], in0=po[:, 0:CH], in1=hoT[:, cs], op=ALU.add), [bpo, bho], [bho])

                    stageA(0)
                    stageA(1)
                    stageB(0)
                    for n_ in range(NCH):
                        chain(n_)
                        if n_ + 1 < NCH:
                            stageB(n_ + 1)
                        if n_ + 2 < NCH:
                            stageA(n_ + 2)
                    emit_state(0 if dr == 1 else 5, dr)
                for g in range(NG):
                    gs = slice(g * 512, (g + 1) * 512)
                    sqh = tmpb[:, 0:256].bitcast(BF16)
                    ACT(lambda q, gs=gs, sqh=sqh: q.activation(out=sqh, in_=hoT[:, gs], func=AF.Square), [bho], [btmpb])
                    p2, bp2 = nps()
                    PE(lambda q, p2=p2, sqh=sqh: q.matmul(p2[:], lhsT=blk_b, rhs=sqh, start=True, stop=True), [btmpb, bcb], [bp2])
                    rv = kb[:, 0:512]
                    DVE(lambda q, p2=p2, rv=rv: q.tensor_scalar(out=rv, in0=p2[:], scalar1=1.0 / 64, scalar2=EPS, op0=ALU.mult, op1=ALU.add), [bp2], [bkb])
                    ACT(lambda q, rv=rv: q.activation(out=rv, in_=rv, func=AF.Ln), [bkb], [bkb])
                    ACT(lambda q, rv=rv: q.activation(out=rv, in_=rv, func=AF.Exp, scale=-0.5), [bkb], [bkb])
                    DVE(lambda q, rv=rv, gs=gs, pr=pr: q.scalar_tensor_tensor(out=rv, in0=hoT[:, gs], scalar=par[:, P_HNRM + l * 2 + pr:P_HNRM + l * 2 + pr + 1], in1=rv, op0=ALU.mult, op1=ALU.mult), [bho, bkb, bpar], [bkb])
                    DVE(lambda q, rv=rv, gs=gs, g=g, pr=pr: q.tensor_tensor(out=catT[:, 2 + pr, gs], in0=rv, in1=sgT[:, gs], op=ALU.mult), [bkb, bsg], [bcat[2 + pr][g]])
        else:
            for pr in range(2):
                wnext()
                wnext()
                for g in range(NG):
                    POOL(lambda q, pr=pr, g=g: q.memset(catT[:, 2 + pr, g * 512:(g + 1) * 512], 0.0), [], [bcat[2 + pr][g]])
        if ENABLE_ATTN:
            state["aoff"] = base_off
            qT = aalloc(4 * TOK // 2).bitcast(BF16).rearrange("p (c n) -> p c n", c=4)
            kdT = aalloc(4 * TOK // 2).bitcast(BF16).rearrange("p (c v n) -> p c v n", c=2, v=2)
            kcT = aalloc(4 * 512 // 2).bitcast(BF16).rearrange("p (c v n) -> p c v n", c=2, v=2)
            vaug = aalloc(16 * 2 * 96 // 2).bitcast(BF16).rearrange("p (t k n) -> p t k n", t=16, k=2)
            kst = aalloc(NT * 128).rearrange("p (t n) -> p t n", t=NT)
            vst = aalloc(NT * 128).rearrange("p (t n) -> p t n", t=NT)
            kcd = aalloc(4 * 256).rearrange("p (t k n) -> p t k n", t=4, k=2)
            zq = [aalloc(512), aalloc(512)]
            rr = [aalloc(512), aalloc(512)]
            t1 = [aalloc(512)] * 2
            sqa = [aalloc(256).bitcast(BF16), aalloc(256).bitcast(BF16)]
            pT = [aalloc(256).bitcast(BF16) for _ in range(3)]
            otok = [aalloc(256).bitcast(BF16).rearrange("p (h n) -> p h n", h=8), aalloc(256).bitcast(BF16).rearrange("p (h n) -> p h n", h=8)]
            den = [aalloc(4), aalloc(4)]
            esink = aalloc(8)
            bq = [[nb("q") for g in range(NG)] for c in range(4)]
            bkd = [[nb("kd") for g in range(NG)] for c in range(2)]
            bkc, bva, bkst, bvst, bkcd, besk = nb("kc"), [nb("va") for t in range(16)], nb("kst"), nb("vst"), nb("kcd"), nb("esk")
            bzq, brr, bt1, bsqa, bpT, botok, bden = [[nb(n + str(i)) for i in range(3)] for n in ("zq", "rr", "t1", "sqa", "pT", "otok", "den")]
            bt1[1] = bt1[0]
            ACT(lambda q: q.activation(out=esink, in_=par[:, P_SINK + l * 8:P_SINK + l * 8 + 8], func=AF.Exp), [bpar], [besk])
            POOL(lambda q: q.memset(vaug[:, :, :, 64:66], 1.0), [], list(bva))
            allkd = [bkd[c_][g_] for c_ in range(2) for g_ in range(NG)]
            POOL(lambda q: q.memset(kdT[64:128, :, 0, :], 0.0), [], allkd)
            POOL(lambda q: q.memset(kdT[0:64, :, 1, :], 0.0), [], allkd)
            POOL(lambda q: q.memset(kcT[64:128, :, 0, :], 0.0), [], [bkc])
            POOL(lambda q: q.memset(kcT[0:64, :, 1, :], 0.0), [], [bkc])
            for kv in (range(2) if "cachek" not in SKIP else []):
                for dup in range(2):
                    DMA("sp", kcd[:, :, kv, dup * 64:(dup + 1) * 64], ck_d[l][:, kv * 64:(kv + 1) * 64].rearrange("(t p) d -> p t d", p=128), bkcd, writes=[bkcd])
            for t in (range(4) if "cachev" not in SKIP else []):
                DMA("pool", vaug[:, 12 + t, :, 0:64], cv_d[l][t * 128:(t + 1) * 128, :].rearrange("p (k d) -> p k d", k=2), bva[12 + t], writes=[bva[12 + t]])
            for kv in (range(2) if "cachek" not in SKIP else []):
                ps, bp = nps()
                for t in range(4):
                    PE(lambda q, ps=ps, t=t, kv=kv: q.transpose(ps[:, t * 128:(t + 1) * 128], kcd[:, t, kv, :], ident), [bkcd, bcf], [bp])
                ACT(lambda q, ps=ps, kv=kv: q.activation(out=kcT[0:64, kv, 0, :], in_=ps[0:64, :], func=AF.Copy), [bp], [bkc])
                ACT(lambda q, ps=ps, kv=kv: q.activation(out=kcT[64:128, kv, 1, :], in_=ps[64:128, :], func=AF.Copy), [bp], [bkc])
            cnt = [0]

            def qk_post(ps, bp, g, nw_col, dst, bdst, scale_extra, kfin=None):
                s = cnt[0] % 2
                cnt[0] += 1
                ACT(lambda q: q.activation(out=zq[s], in_=ps[:], func=AF.Copy), [bp], [bzq[s]])
                ACT(lambda q: q.activation(out=sqa[s], in_=ps[:], func=AF.Square), [bp], [bsqa[s]])
                p2, bp2 = nps()
                PE(lambda q: q.matmul(p2[:], lhsT=blk_b, rhs=sqa[s], start=True, stop=True), [bsqa[s], bcb], [bp2])
                DVE(lambda q: q.tensor_scalar(out=rr[s], in0=p2[:], scalar1=1.0 / 64, scalar2=EPS, op0=ALU.mult, op1=ALU.add), [bp2], [brr[s]])
                ACT(lambda q: q.activation(out=rr[s], in_=rr[s], func=AF.Ln), [brr[s]], [brr[s]])
                ACT(lambda q: q.activation(out=rr[s], in_=rr[s], func=AF.Exp, scale=-0.5), [brr[s]], [brr[s]])
                DVE(lambda q: q.scalar_tensor_tensor(out=zq[s], in0=zq[s], scalar=par[:, nw_col:nw_col + 1], in1=rr[s], op0=ALU.mult, op1=ALU.mult), [bzq[s], brr[s], bpar], [bzq[s]])
                if g < 2 and "rope" not in SKIP:
                    p3, bp3 = nps()
                    PE(lambda q: q.matmul(p3[:], lhsT=cf[:, C_PERM:C_PERM + 128], rhs=zq[s], start=True, stop=True), [bzq[s], bcf], [bp3])
                    DVE(lambda q: q.tensor_tensor(out=t1[s], in0=p3[:], in1=cf[:, C_SIN + g * 512:C_SIN + (g + 1) * 512], op=ALU.mult), [bp3, bcf], [bt1[s]])
                    DVE(lambda q: q.tensor_tensor(out=zq[s], in0=zq[s], in1=cf[:, C_COS + g * 512:C_COS + (g + 1) * 512], op=ALU.mult), [bzq[s], bcf], [bzq[s]])
                    DVE(lambda q: q.tensor_tensor(out=zq[s], in0=zq[s], in1=t1[s], op=ALU.add), [bzq[s], bt1[s]], [bzq[s]])
                if kfin is None:
                    ACT(lambda q: q.activation(out=dst, in_=zq[s], func=AF.Copy, scale=scale_extra), [bzq[s]], [bdst])
                else:
                    ACT(lambda q: q.activation(out=dst[0][0:64, :], in_=zq[s][0:64, :], func=AF.Copy), [bzq[s]], [bdst])
                    ACT(lambda q: q.activation(out=dst[1][64:128, :], in_=zq[s][64:128, :], func=AF.Copy), [bzq[s]], [bdst])
                if kfin is not None and "ktr" not in SKIP:
                    kv = kfin
                    p4, bp4 = nps()
                    for tt in range(4):
                        PE(lambda q, tt=tt: q.transpose(p4[:, tt * 64:(tt + 1) * 64], zq[s][0:64, tt * 128:(tt + 1) * 128], ident[0:64, 0:64]), [bzq[s], bcf], [bp4])
                    DVE(lambda q: q.tensor_copy(out=kst[:, g * 4:(g + 1) * 4, kv * 64:(kv + 1) * 64], in_=p4[:, 0:256].rearrange("p (t n) -> p t n", t=4)), [bp4], [bkst])

            sl, bs = wnext()
            wv = sl[:, 0:4096].rearrange("p (c n) -> p c n", c=8)
            for c in range(4):
                for g in range(NG):
                    ps, bp = nps()
                    inproj(wv, c, g, (ps, bp, bs))
                    qk_post(ps, bp, g, P_QN + l, qT[:, c, g * 512:(g + 1) * 512], bq[c][g], 0.125)
            sl, bs = wnext()
            for kv in range(2):
                wv = sl[:, kv * 1024:(kv + 1) * 1024].rearrange("p (c n) -> p c n", c=8)
                for g in range(NG):
                    ps, bp = nps()
                    inproj(wv, 0, g, (ps, bp, bs))
                    qk_post(ps, bp, g, P_KN + l, (kdT[:, kv, 0, g * 512:(g + 1) * 512], kdT[:, kv, 1, g * 512:(g + 1) * 512]), bkd[kv][g], 1.0, kfin=kv)
            wv = sl[:, 2048:3072].rearrange("p (c n) -> p c n", c=8)
            for g in range(NG):
                ps, bp = nps()
                inproj(wv, 0, g, (ps, bp, bs))
                s = cnt[0] % 2
                cnt[0] += 1
                ACT(lambda q, ps=ps, s=s: q.activation(out=zq[s], in_=ps[:], func=AF.Copy), [bp], [bzq[s]])
                if "vtr" in SKIP:
                    continue
                p4, bp4 = nps()
                for tt in range(4):
                    PE(lambda q, tt=tt, s=s, p4=p4: q.transpose(p4[:, tt * 128:(tt + 1) * 128], zq[s][:, tt * 128:(tt + 1) * 128], ident), [bzq[s], bcf], [bp4])
                if "vtr_dve" not in SKIP:
                    DVE(lambda q, p4=p4, g=g: q.tensor_copy(out=vst[:, g * 4:(g + 1) * 4, :], in_=p4[:].rearrange("p (t n) -> p t n", t=4)), [bp4], [bvst])
                for tt in (range(4) if "vtr_act" not in SKIP else []):
                    ACT(lambda q, p4=p4, g=g, tt=tt: q.activation(out=vaug[:, g * 4 + tt, :, 0:64], in_=p4[:, tt * 128:(tt + 1) * 128].rearrange("p (k d) -> p k d", k=2), func=AF.Copy), [bp4], [bva[g * 4 + tt]])
            if "kvout" not in SKIP:
                DMA("sp", kout[l].rearrange("(t p) n -> p t n", p=128), kst, bkst, reads=[bkst])
            if "kvout" not in SKIP:
                DMA("sp", vout[l].rearrange("(t p) n -> p t n", p=128), vst, bvst, reads=[bvst])
            it = 0
            if not ATTN_CORE:
                for c in range(4):
                    for g in range(NG):
                        POOL(lambda q, c=c, g=g: q.memset(catT[:, 4 + c, g * 512:(g + 1) * 512], 0.0), [], [bcat[4 + c][g]])
            for j in (range(NT) if ATTN_CORE else []):
                g = j // 4
                if j < 8:
                    kts = [("n", j + d_, (j * 3 + 1 + d_) if d_ != 0 else None) for d_ in (-1, 0, 1) if 0 <= j + d_ < 8] + [("c", t, None) for t in range(4)]
                else:
                    b0 = 8 + 2 * ((j - 8) // 2)
                    kts = [("n", b0, None), ("n", b0 + 1, None)]
                so = j % 2
                for kv in range(2):
                    po, bpo = nps()
                    pov = po[:, 0:264].rearrange("p (h n) -> p h n", h=4)
                    nk = len(kts)

                    def scores(ki, j=j, kv=kv, g=g):
                        kind, kt, mi = kts[ki]
                        pS, bpS = nps()
                        for hh in range(4):
                            cq = kv * 2 + hh // 2
                            hf = hh % 2
                            if kind == "n":
                                lhs = kdT[:, kv, hf, kt * 128:(kt + 1) * 128]
                                rb = bkd[kv][kt // 4]
                            else:
                                lhs = kcT[:, kv, hf, kt * 128:(kt + 1) * 128]
                                rb = bkc
                            PE(lambda q, pS=pS, hh=hh, lhs=lhs, cq=cq, j=j: q.matmul(pS[:, hh * 128:(hh + 1) * 128], lhsT=lhs, rhs=qT[:, cq, j * 128:(j + 1) * 128], start=True, stop=True),
                               [rb, bq[cq][g]], [bpS])
                        return pS, bpS

                    LOOK = 2
                    sc = {}
                    for ki in range(min(LOOK, nk)):
                        sc[ki] = scores(ki)
                    for ki, (kind, kt, mi) in enumerate(kts):
                        pS, bpS = sc.pop(ki)
                        s = it % 3
                        it += 1
                        if kind == "c":
                            ACT(lambda q, pS=pS, s=s: q.activation(out=pT[s], in_=pS[:], func=AF.Exp, bias=par[:, P_CBIAS:P_CBIAS + 1]), [bpS, bpar], [bpT[s]])
                        else:
                            ACT(lambda q, pS=pS, s=s: q.activation(out=pT[s], in_=pS[:], func=AF.Exp), [bpS], [bpT[s]])
                        if mi is not None:
                            DVE(lambda q, s=s, mi=mi: q.tensor_tensor(out=pT[s].rearrange("p (h n) -> p h n", h=4), in0=pT[s].rearrange("p (h n) -> p h n", h=4),
                                                                      in1=cb[:, B_MASK + mi * 128:B_MASK + (mi + 1) * 128].unsqueeze(1).to_broadcast([128, 4, 128]), op=ALU.mult), [bpT[s], bcb], [bpT[s]])
                        if ki + LOOK < nk:
                            sc[ki + LOOK] = scores(ki + LOOK)
                        vt = kt if kind == "n" else 12 + kt
                        for hh in range(4):
                            PE(lambda q, pov=pov, hh=hh, s=s, vt=vt, kv=kv, ki=ki, n=nk: q.matmul(pov[:, hh, 0:66], lhsT=pT[s][:, hh * 128:(hh + 1) * 128], rhs=vaug[:, vt, kv, 0:66], start=(ki == 0 and hh == 0), stop=(ki == n - 1 and hh == 3)),
                               [bpT[s], bva[vt]], [bpo])
                    DVE(lambda q, pov=pov, so=so, kv=kv: q.tensor_tensor(out=den[so], in0=pov[:, :, 64], in1=esink[:, kv * 4:(kv + 1) * 4], op=ALU.add), [bpo, besk], [bden[so]])
                    DVE(lambda q, so=so: q.reciprocal(out=den[so], in_=den[so]), [bden[so]], [bden[so]])
                    DVE(lambda q, pov=pov, so=so, kv=kv: q.tensor_tensor(out=otok[so][:, kv * 4:(kv + 1) * 4, :], in0=pov[:, :, 0:64], in1=den[so].unsqueeze(2).to_broadcast([128, 4, 64]), op=ALU.mult), [bpo, bden[so]], [botok[so]])
                pt_, bpt = nps()
                ptb = pt_[:].bitcast(BF16)
                for c in range(4):
                    PE(lambda q, ptb=ptb, c=c, so=so: q.transpose(ptb[:, c * 128:(c + 1) * 128], otok[so][:, 2 * c:2 * c + 2, :].rearrange("p h n -> p (h n)"), ident_b), [botok[so], bcb], [bpt])
                ACT(lambda q, ptb=ptb, j=j: q.activation(out=catT[:, 4:8, j * 128:(j + 1) * 128], in_=ptb[:, 0:512].rearrange("p (c n) -> p c n", c=4), func=AF.Copy), [bpt], [bcat[4 + c][g] for c in range(4)])

        else:
            wnext()
            wnext()
            for c in range(4):
                for g in range(NG):
                    POOL(lambda q, c=c, g=g: q.memset(catT[:, 4 + c, g * 512:(g + 1) * 512], 0.0), [], [bcat[4 + c][g]])
        for i in range(2):
            sl, bs = wnext()
            wv = sl[:, 0:4096].rearrange("p (c n) -> p c n", c=8)
            for k in range(4):
                dc = i * 4 + k
                for g in range(NG):
                    ps, bp = nps()
                    for c in range(8):
                        PE(lambda q, ps=ps, wv=wv, c=c, k=k, g=g: q.matmul(ps[:], lhsT=wv[:, c, k * 128:(k + 1) * 128], rhs=catT[:, c, g * 512:(g + 1) * 512], start=(c == 0), stop=(c == 7)),
                           [bs, bcat[c][g]], [bp])
                    DVE(lambda q, ps=ps, dc=dc, g=g: q.scalar_tensor_tensor(out=xT[:, dc, g * 512:(g + 1) * 512], in0=ps[:], scalar=mod_G(l, 1, dc, g), in1=xT[:, dc, g * 512:(g + 1) * 512], op0=ALU.mult, op1=ALU.add),
                        [bp, bmod, bx[dc][g]], [bx[dc][g]])

    for l in range(2):
        ffn(l, 0)
        mixer(l)
        ffn(l, 1)

    areset()
    yst = [aalloc(1024), aalloc(1024)]
    byst = [nb("yst0"), nb("yst1")]
    for t in range(NT):
        s = t % 2
        g = t // 4
        for half in range(2):
            ps, bp = nps()
            for k in range(4):
                c = half * 4 + k
                PE(lambda q, ps=ps, k=k, c=c, t=t: q.transpose(ps[:, k * 128:(k + 1) * 128], xT[:, c, t * 128:(t + 1) * 128], ident), [bx[c][g], bcf], [bp])
            if half == 0:
                ACT(lambda q, ps=ps, s=s: q.activation(out=yst[s][:, 0:512], in_=ps[:], func=AF.Copy), [bp], [byst[s]])
            else:
                DVE(lambda q, ps=ps, s=s: q.tensor_copy(out=yst[s][:, 512:1024], in_=ps[:]), [bp], [byst[s]])
        DMA("sp", yout[t * 128:(t + 1) * 128, :], yst[s], byst[s], reads=[byst[s]])

    T.emit(nc)
    st.close()
    return nc


def _consts(sample_mode):
    cf = np.zeros((128, NCF), np.float32)
    cf[:, C_ID:C_ID + 128] = np.eye(128, dtype=np.float32)
    Pm = np.zeros((64, 64), np.float32)
    for d in range(64):
        blk = d // 16
        if blk % 2 == 0:
            Pm[d, d + 16] = -1.0
        else:
            Pm[d, d - 16] = 1.0
    P2 = np.zeros((128, 128), np.float32)
    P2[:64, :64] = Pm
    P2[64:, 64:] = Pm
    cf[:, C_PERM:C_PERM + 128] = P2.T
    t = np.arange(1024)
    inv = (10000.0 ** (-np.arange(0, 32, 2, dtype=np.float32) / 32)).astype(np.float32)
    d = np.arange(128) % 64
    pos = np.where((d < 32)[:, None], (t // 64)[None, :], (t % 64)[None, :]).astype(np.float32)
    ang = pos * inv[d % 16][:, None]
    if sample_mode:
        cf[:, C_COS:C_COS + 1024] = np.cos(ang)
        cf[:, C_SIN:C_SIN + 1024] = np.sin(ang)
    else:
        cf[:, C_COS:C_COS + 1024] = 1.0
    j = np.arange(64)
    cf[:64, C_MF:C_MF + 64] = (j[:, None] <= j[None, :]).astype(np.float32)
    cf[:64, C_MB:C_MB + 64] = (j[:, None] >= j[None, :]).astype(np.float32)
    cb = np.zeros((128, NCB), np.float32)
    cb[:, B_ONES:B_ONES + 128] = 1.0
    cb[:64, B_BLK:B_BLK + 64] = 1.0
    cb[64:, B_BLK + 64:B_BLK + 128] = 1.0
    cb[:, B_ID:B_ID + 128] = np.eye(128, dtype=np.float32)
    c = np.arange(64)
    a = 2 * np.pi * np.outer(c, c) / 64.0
    for gq in range(2):
        cb[gq * 64:(gq + 1) * 64, B_DFT + gq * 64:B_DFT + (gq + 1) * 64] = np.cos(a)
        cb[gq * 64:(gq + 1) * 64, B_DFT + 128 + gq * 64:B_DFT + 128 + (gq + 1) * 64] = np.sin(a)
    ki = np.arange(128)[:, None]
    qi = np.arange(128)[None, :]
    for jt in range(8):
        for s in range(3):
            if sample_mode:
                m = (ki >= qi) if s == 0 else (np.ones((128, 128), bool) if s == 1 else (ki <= qi))
            else:
                if s == 1:
                    m = np.ones((128, 128), bool)
                elif s == 0:
                    m = np.full((128, 128), jt % 2 == 1)
                else:
                    m = np.full((128, 128), jt % 2 == 0)
            cb[:, B_MASK + (jt * 3 + s) * 128:B_MASK + (jt * 3 + s + 1) * 128] = m.astype(np.float32)

    def dft(L):
        ll = np.arange(L)
        aa = 2 * np.pi * np.outer(ll, ll) / L
        sc_ = 1.0 / np.sqrt(L * 64.0)
        return (np.cos(aa) * sc_).astype(np.float32), (-np.sin(aa) * sc_).astype(np.float32)

    dftp = np.stack(dft(256))
    if sample_mode:
        dftl = np.stack(dft(1024))
    else:
        dftl = np.zeros((2, 1024, 1024), np.float32)
        for i in range(4):
            dftl[:, i * 256:(i + 1) * 256, i * 256:(i + 1) * 256] = dftp
    return cf, cb, dftl, dftp


_NC_CACHE = {}


def _prep(inputs):
    I = {k: np.asarray(v) for k, v in inputs.items()}
    xp, xs = I["x_prompt"], I["x_sample"]
    consts = {True: _consts(True), False: _consts(False)}
    fm = lambda v: np.ascontiguousarray(v.reshape(-1, 128).T)
    in_maps = []
    assign = []
    for core in range(8):
        sm = core < 4
        if sm:
            gtok = xs[core]
            pseq = [2 * core, 2 * core + 1]
            gseq = []
            cond = np.stack([I["c"][core], I["c_ctx"]])
        else:
            base = 8 + 6 * (core - 4)
            gseq = [base, base + 1, base + 2, base + 3]
            gtok = xp[gseq].reshape(1024, D)
            pseq = [base + 4, base + 5]
            cond = np.stack([I["c_ctx"], I["c_ctx"]])
        assign.append((sm, gseq, pseq))
        xin = np.concatenate([gtok, xp[pseq].reshape(512, D)], axis=0)
        par = np.zeros((128, NPAR), np.float32)
        par[:, P_COND:P_COND + 16] = cond.reshape(2, 8, 128).transpose(2, 1, 0).reshape(128, 16)
        for l in range(2):
            par[:, P_BADA + l * 72:P_BADA + (l + 1) * 72] = fm(I["b_ada"][l])
            for w3, nm in enumerate(["norm_ffn1", "norm_mix", "norm_ffn2"]):
                par[:, P_NRM + l * 24 + w3 * 8:P_NRM + l * 24 + w3 * 8 + 8] = fm(I[nm][l])
            for dr in range(2):
                par[:, P_LBS + l * 4 + dr * 2:P_LBS + l * 4 + dr * 2 + 2] = fm(I["hgrn_lower_bounds"][l, dr])
            par[:, P_HNRM + l * 2:P_HNRM + l * 2 + 2] = fm(I["hgrn_norm"][l])
            par[:, P_QN + l] = np.tile(I["q_norm"][l], 2)
            par[:, P_KN + l] = np.tile(I["k_norm"][l], 2)
            par[:, P_SINK + l * 8:P_SINK + (l + 1) * 8] = I["attn_sink"][l][None, :]
        par[:, P_KEEP] = 1.0 if sm else 0.0
        par[:, P_CBIAS] = 0.0 if sm else -30000.0
        cf, cb, dftl, dftp = consts[sm]
        if sm:
            ck = I["cache_attn_k"][core].reshape(2, 512, 128)
            cv = I["cache_attn_v"][core].reshape(2, 512, 128)
            s0 = I["state_hgrn"][core]
        else:
            ck = np.zeros((2, 512, 128), np.float32)
            cv = np.zeros((2, 512, 128), np.float32)
            s0 = np.zeros((2, 2, 4, 64, 64), np.float32)
        in_maps.append(dict(
            xin=np.ascontiguousarray(xin), par=par, cf=cf, cb=cb, dftl=dftl, dftp=dftp,
            ck=np.ascontiguousarray(ck), cv=np.ascontiguousarray(cv), s0=np.ascontiguousarray(s0),
            w_ada=I["w_ada"], ffn1_gu=I["ffn1_w_gate_up"], ffn2_gu=I["ffn2_w_gate_up"],
            ffn1_d=I["ffn1_w_down"], ffn2_d=I["ffn2_w_down"], w_in=I["w_in"], w_out=I["w_out"]))
    return in_maps, assign


def _post(results, assign):
    y_p = np.zeros((32, 256, D), np.float32)
    y_s = np.zeros((4, 1024, D), np.float32)
    nk = np.zeros((32, 2, 256, 2, 64), np.float32)
    nv = np.zeros((32, 2, 256, 2, 64), np.float32)
    ns = np.zeros((32, 2, 2, 4, 64, 64), np.float32)
    for core in range(8):
        r = results[core]
        sm, gseq, pseq = assign[core]
        y = r["yout"]
        ko = r["kout"].reshape(2, TOK, 2, 64)
        vo = r["vout"].reshape(2, TOK, 2, 64)
        so = r["sout"]
        if sm:
            y_s[core] = y[:1024]
        seqs = [(s, i) for i, s in enumerate(gseq)] + [(s, 4 + i) for i, s in enumerate(pseq)]
        for s, slot in seqs:
            y_p[s] = y[slot * 256:(slot + 1) * 256]
            nk[s] = ko[:, slot * 256:(slot + 1) * 256]
            nv[s] = vo[:, slot * 256:(slot + 1) * 256]
            ns[s] = so[:, slot]
    return (y_p, y_s, nk, nv, ns)


def kernel(**inputs):
    in_maps, assign = _prep(inputs)
    if "nc" not in _NC_CACHE:
        _NC_CACHE["nc"] = build()
    res = run_bass_kernel_spmd(_NC_CACHE["nc"], in_maps, core_ids=list(range(8)))
    return _post(res.results, assign)
```

```python
import numpy as np
from contextlib import ExitStack
import concourse.bass as bass
import concourse.mybir as mybir
from concourse.bass_utils import run_bass_kernel_spmd

F32 = mybir.dt.float32
BF16 = mybir.dt.bfloat16
AF = mybir.ActivationFunctionType
ALU = mybir.AluOpType

EP = 8192
ENGS = ("pe", "act", "dve", "pool", "sp")
SAME_ENGINE_SYNC = ("act", "dve", "pool")


class Buf:
    def __init__(self, name, fence=None):
        self.name = name
        self.w = None
        self.r = list(fence) if fence else []
        self.dsem = None
        self.dcnt = 0
        self.excl = False


class Op:
    pass


class Tracker:
    def __init__(self):
        self.ops = []
        self.per_eng = {e: [] for e in ENGS}
        self.dma_bufs = []
        self.dma_ids = []

    def fence(self):
        ids = [l[-1].id for l in self.per_eng.values() if l]
        return ids + list(self.dma_ids)

    def op(self, eng, fn, reads=(), writes=(), dma_buf=None):
        o = Op()
        o.eng = eng
        o.fn = fn
        o.dma_buf = dma_buf
        o.marked = False
        o.id = len(self.ops)
        o.pos = len(self.per_eng[eng])
        deps = set()
        for b in reads:
            if b.w is not None:
                deps.add(b.w)
            if b.excl:
                deps.update(b.r)
        for b in writes:
            if b.w is not None:
                deps.add(b.w)
            deps.update(b.r)
        o.deps = deps
        for b in reads:
            b.r.append(o.id)
        for b in writes:
            b.w = o.id
            b.r = []
        if dma_buf is not None:
            if dma_buf.dsem is None:
                dma_buf.dsem = True
                self.dma_bufs.append(dma_buf)
            dma_buf.dcnt += 16
            o.dval = dma_buf.dcnt
            self.dma_ids.append(o.id)
        self.ops.append(o)
        self.per_eng[eng].append(o)
        return o

    def plan(self):
        ops = self.ops
        seen = {e: {f: -1 for f in ENGS} for e in ENGS}
        seen_dma = {e: {} for e in ENGS}
        for o in ops:
            w_eng = {}
            w_dma = {}
            for d in o.deps:
                p = ops[d]
                if p.dma_buf is not None:
                    b = p.dma_buf
                    if seen_dma[o.eng].get(id(b), 0) >= p.dval:
                        continue
                    w_dma[id(b)] = (b, max(p.dval, w_dma.get(id(b), (None, 0))[1]))
                else:
                    if p.eng == o.eng:
                        if p.eng not in SAME_ENGINE_SYNC:
                            continue
                        if p.pos < o.pos - 2:
                            continue
                    if seen[o.eng][p.eng] >= d:
                        continue
                    w_eng[p.eng] = max(w_eng.get(p.eng, -1), d)
            for f, d in w_eng.items():
                seen[o.eng][f] = d
                ops[d].marked = True
            for k, (b, v) in w_dma.items():
                seen_dma[o.eng][k] = v
            o.w_eng = w_eng
            o.w_dma = list(w_dma.values())
        cnt = {e: 0 for e in ENGS}
        for o in ops:
            if o.dma_buf is None and o.marked:
                cnt[o.eng] += 1
            o.cnt = cnt[o.eng]
        self.final_cnt = cnt

    def emit(self, nc, final_eng="sp"):
        self.plan()
        ops = self.ops
        with ExitStack() as st:
            esems = {}
            for e in ENGS:
                n = max(1, (self.final_cnt[e] + EP - 1) // EP)
                esems[e] = [st.enter_context(nc.semaphore(f"s_{e}_{i}")) for i in range(n)]
            for i, b in enumerate(self.dma_bufs):
                b.dsem = st.enter_context(nc.semaphore(f"d_{i}"))
            block = st.enter_context(nc.Block())

            def run(eng_name, eng):
                for o in self.per_eng[eng_name]:
                    for f, d in o.w_eng.items():
                        c = ops[d].cnt
                        eng.wait_ge(esems[f][(c - 1) // EP], (c - 1) % EP + 1)
                    for (b, v) in o.w_dma:
                        eng.wait_ge(b.dsem, v)
                    ins = o.fn(eng)
                    if o.dma_buf is not None:
                        ins.then_inc(o.dma_buf.dsem, 16)
                    elif o.marked:
                        c = o.cnt
                        ins.then_inc(esems[eng_name][(c - 1) // EP], 1)
                if eng_name == final_eng:
                    for b in self.dma_bufs:
                        eng.wait_ge(b.dsem, b.dcnt)
                    for f in ENGS:
                        c = self.final_cnt[f]
                        if c > 0 and f != eng_name:
                            eng.wait_ge(esems[f][(c - 1) // EP], (c - 1) % EP + 1)

            @block.tensor
            def _(e):
                run("pe", e)

            @block.scalar
            def _(e):
                run("act", e)

            @block.vector
            def _(e):
                run("dve", e)

            @block.gpsimd
            def _(e):
                run("pool", e)

            @block.sync
            def _(e):
                run("sp", e)


TOK = 1536
NT = 12
NG = 3
D = 1024
DFF = 2816
EPS = 1e-6
P_COND, P_BADA, P_NRM, P_LBS, P_HNRM, P_QN, P_KN, P_SINK, P_KEEP, P_CBIAS, NPAR = 0, 16, 160, 208, 216, 220, 222, 224, 240, 241, 242
C_ID, C_PERM, C_COS, C_SIN, C_MF, C_MB, NCF = 0, 128, 256, 1280, 2304, 2368, 2432
B_ONES, B_BLK, B_ID, B_DFT, B_MASK, NCB = 0, 128, 256, 384, 640, 640 + 24 * 128
SLOT = 4096
ENABLE_HGRN = True
DEBUG_PAIR = None
ATTN_CORE = True
SKIP = set()
ENABLE_ATTN = True
NSL = 3
ARENA = 23400


def build(stop_after=None):
    nc = bass.Bass("TRN2", target_bir_lowering=False)
    T = Tracker()
    di = lambda n, s: nc.dram_tensor(n, s, F32, kind="ExternalInput").ap()
    do = lambda n, s: nc.dram_tensor(n, s, F32, kind="ExternalOutput").ap()
    xin = di("xin", [TOK, D]); par_d = di("par", [128, NPAR]); cf_d = di("cf", [128, NCF]); cb_d = di("cb", [128, NCB])
    dftl_d = di("dftl", [2, 1024, 1024]); dftp_d = di("dftp", [2, 256, 256])
    ck_d = di("ck", [2, 512, 128]); cv_d = di("cv", [2, 512, 128]); s0_d = di("s0", [2, 2, 4, 64, 64])
    wada_d = di("w_ada", [2, D, 9 * D])
    wgu_d = [di("ffn1_gu", [2, D, 2 * DFF]), di("ffn2_gu", [2, D, 2 * DFF])]
    wdn_d = [di("ffn1_d", [2, DFF, D]), di("ffn2_d", [2, DFF, D])]
    win_d = di("w_in", [2, D, 2304]); wout_d = di("w_out", [2, D, D])
    yout = do("yout", [TOK, D]); kout = do("kout", [2, TOK, 128]); vout = do("vout", [2, TOK, 128])
    sout = do("sout", [2, 6, 2, 4, 64, 64])

    st = ExitStack()
    sbt = lambda n, s, dt: st.enter_context(nc.sbuf_tensor("sb_" + n, s, dt))
    xT = sbt("xT", [128, 8, TOK], F32)
    hT = sbt("hT", [128, 8, TOK], BF16)
    par = sbt("par", [128, NPAR], F32)
    cf = sbt("cf", [128, NCF], F32)
    cb = sbt("cb", [128, NCB], BF16)
    modv = sbt("modv", [128, 2, 72, 2], F32)
    mA = sbt("mA", [128, 2, 3, 8, 2], F32)
    mG = sbt("mG", [128, 2, 3, 8, 2], F32)
    sml = sbt("sml", [128, 64], F32)
    slots = [sbt(f"slot{i}", [128, SLOT], BF16) for i in range(NSL)]
    arena = sbt("arena", [128, ARENA], F32)
    psum = [st.enter_context(nc.psum_tensor(f"ps{i}", [128, 512], F32)) for i in range(8)]
    bps = [Buf(f"ps{i}") for i in range(8)]
    for b_ in bps:
        b_.excl = True
    bslot = [Buf(f"slot{i}") for i in range(NSL)]
    bx = [[Buf(f"x{c}_{g}") for g in range(NG)] for c in range(8)]
    bh = [[Buf(f"h{c}_{g}") for g in range(NG)] for c in range(8)]
    bpar, bcf, bcb, bmod = Buf("par"), Buf("cf"), Buf("cb"), Buf("mod")
    bsml = Buf("sml")

    state = {"ps": 0, "aoff": 0}

    def nps():
        i = state["ps"] % 8
        state["ps"] += 1
        return psum[i], bps[i]

    def aalloc(n32):
        o = state["aoff"]
        state["aoff"] += n32
        assert state["aoff"] <= ARENA, state["aoff"]
        return arena[:, o:o + n32]

    def areset():
        state["aoff"] = 0

    def nb(name):
        return Buf(name, T.fence())

    PE = lambda fn, r, w: T.op("pe", fn, r, w)
    ACT = lambda fn, r, w: T.op("act", fn, r, w)
    DVE = lambda fn, r, w: T.op("dve", fn, r, w)
    POOL = lambda fn, r, w: T.op("pool", fn, r, w)

    def DMA(eng, out, in_, buf, reads=(), writes=()):
        T.op(eng, lambda q: q.dma_start(out=out, in_=in_), reads, writes, dma_buf=buf)

    wq = []
    wstate = {"loaded": 0, "used": 0}

    def witem(parts):
        wq.append(parts)

    def wnext(keep_prev=False):
        i = wstate["used"]
        oldest = i - 1 if keep_prev else i
        while wstate["loaded"] < min(len(wq), oldest + NSL):
            k = wstate["loaded"]
            s = k % NSL
            for (fn, dap) in wq[k]:
                DMA("pool", fn(slots[s]), dap, bslot[s], writes=[bslot[s]])
            wstate["loaded"] += 1
        wstate["used"] += 1
        return slots[i % NSL], bslot[i % NSL]

    def kview(ap2d):
        return ap2d.rearrange("(c p) n -> p c n", p=128)

    def sched_weights():
        for l in range(2):
            for i in range(18):
                witem([(lambda s: s[:, 0:4096].rearrange("p (c n) -> p c n", c=8), kview(wada_d[l][:, i * 512:(i + 1) * 512]))])
        for l in range(2):
            for f in range(2):
                sched_ffn(l, f)
                if f == 0:
                    sched_mixer(l)

    def sched_ffn(l, f):
        for h in range(2):
            for j0 in range(0, 11, 2):
                nf = min(2, 11 - j0)
                c0 = (h * 11 + j0) * 128
                witem([
                    (lambda s, nf=nf: s[:, 0:8 * nf * 128].rearrange("p (c n) -> p c n", c=8), kview(wgu_d[f][l][:, c0:c0 + nf * 128])),
                    (lambda s, nf=nf: s[:, 2048:2048 + 8 * nf * 128].rearrange("p (c n) -> p c n", c=8), kview(wgu_d[f][l][:, DFF + c0:DFF + c0 + nf * 128])),
                ])

    def sched_mixer(l):
        w = win_d[l]
        witem([(lambda s: s[:, 0:2048].rearrange("p (c n) -> p c n", c=8), kview(w[:, 0:256]))])
        for hf in range(2):
            for cs in range(2):
                witem([(lambda s: s[:, 0:4096].rearrange("p (c n) -> p c n", c=8), kview(dftl_d[cs][:, hf * 512:(hf + 1) * 512]))])
        witem([(lambda s, cs=cs: s[:, cs * 512:(cs + 1) * 512].rearrange("p (c n) -> p c n", c=2), kview(dftp_d[cs])) for cs in range(2)])
        for pr in range(2):
            cols = [256 + pr * 128, 512 + pr * 128, 768 + pr * 128, 1024 + pr * 128, 1280 + pr * 128]
            witem([(lambda s, k=k: s[:, k * 1024:(k + 1) * 1024].rearrange("p (c n) -> p c n", c=8), kview(w[:, cols[k]:cols[k] + 128])) for k in range(3)])
            witem([(lambda s, k=k: s[:, k * 1024:(k + 1) * 1024].rearrange("p (c n) -> p c n", c=8), kview(w[:, cols[3 + k]:cols[3 + k] + 128])) for k in range(2)])
        witem([(lambda s: s[:, 0:4096].rearrange("p (c n) -> p c n", c=8), kview(w[:, 1536:2048]))])
        parts = []
        for kv in range(2):
            for dup in range(2):
                parts.append((lambda s, kv=kv, dup=dup: s[:, kv * 1024:(kv + 1) * 1024].rearrange("p (c n) -> p c n", c=8)[:, :, dup * 64:(dup + 1) * 64],
                              kview(w[:, 2048 + kv * 64:2048 + (kv + 1) * 64])))
        parts.append((lambda s: s[:, 2048:3072].rearrange("p (c n) -> p c n", c=8), kview(w[:, 2176:2304])))
        witem(parts)
        for i in range(2):
            witem([(lambda s: s[:, 0:4096].rearrange("p (c n) -> p c n", c=8), kview(wout_d[l][:, i * 512:(i + 1) * 512]))])

    sched_weights()

    DMA("sp", par[:], par_d, bpar, writes=[bpar])
    DMA("sp", cf[:], cf_d, bcf, writes=[bcf])
    DMA("pool", cb[:], cb_d, bcb, writes=[bcb])
    ident = cf[:, C_ID:C_ID + 128]
    ones_b = cb[:, B_ONES:B_ONES + 128]
    blk_b = cb[:, B_BLK:B_BLK + 128]
    ident_b = cb[:, B_ID:B_ID + 128]

    areset()
    xst = [aalloc(1024), aalloc(1024)]
    bxst = [nb("xst0"), nb("xst1")]
    for t in range(NT):
        s = t % 2
        DMA("sp", xst[s], xin[t * 128:(t + 1) * 128, :], bxst[s], writes=[bxst[s]])
        for half in range(2):
            ps, bp = nps()
            for k in range(4):
                c = half * 4 + k
                PE(lambda q, ps=ps, k=k, c=c, s=s: q.transpose(ps[:, k * 128:(k + 1) * 128], xst[s][:, c * 128:(c + 1) * 128], ident),
                   [bxst[s], bcf], [bp])
            g = t // 4
            eng = ACT if half == 0 else DVE
            fn = (lambda q, ps=ps, half=half, t=t: q.activation(out=xT[:, half * 4:half * 4 + 4, t * 128:(t + 1) * 128], in_=ps[:].rearrange("p (k n) -> p k n", k=4), func=AF.Copy)) if half == 0 else \
                 (lambda q, ps=ps, half=half, t=t: q.tensor_copy(out=xT[:, half * 4:half * 4 + 4, t * 128:(t + 1) * 128], in_=ps[:].rearrange("p (k n) -> p k n", k=4)))
            eng(fn, [bp], [bx[c][g] for c in range(half * 4, half * 4 + 4)])

    areset()
    sc = aalloc(16)
    scb = aalloc(8).bitcast(BF16)
    bsc = nb("sc")
    ACT(lambda q: q.activation(out=sc, in_=par[:, P_COND:P_COND + 16], func=AF.Exp, scale=-1.0), [bpar], [bsc])
    DVE(lambda q: q.tensor_scalar_add(out=sc, in0=sc, scalar1=1.0), [bsc], [bsc])
    DVE(lambda q: q.reciprocal(out=sc, in_=sc), [bsc], [bsc])
    DVE(lambda q: q.tensor_tensor(out=scb, in0=sc, in1=par[:, P_COND:P_COND + 16], op=ALU.mult), [bsc, bpar], [bsc])
    scb3 = scb.rearrange("p (c r) -> p c r", c=8)
    for l in range(2):
        ps, bp = nps()
        for i in range(18):
            sl, bs = wnext()
            wv = sl[:, 0:4096].rearrange("p (c n) -> p c n", c=8)
            for k in range(4):
                nn = i * 4 + k
                for c in range(8):
                    PE(lambda q, ps=ps, wv=wv, k=k, c=c, nn=nn: q.matmul(ps[:, nn * 2:nn * 2 + 2], lhsT=wv[:, c, k * 128:(k + 1) * 128], rhs=scb3[:, c, :], start=(c == 0), stop=(c == 7)),
                       [bs, bsc], [bp])
        DVE(lambda q, ps=ps, l=l: q.tensor_tensor(out=modv[:, l, :, :], in0=ps[:, 0:144].rearrange("p (n r) -> p n r", r=2),
                                                 in1=par[:, P_BADA + l * 72:P_BADA + (l + 1) * 72].unsqueeze(2).to_broadcast([128, 72, 2]), op=ALU.add),
            [bp, bpar], [bmod])
        for w3 in range(3):
            nrm = par[:, P_NRM + l * 24 + w3 * 8:P_NRM + l * 24 + w3 * 8 + 8].unsqueeze(2).to_broadcast([128, 8, 2])
            DVE(lambda q, l=l, w3=w3, nrm=nrm: q.scalar_tensor_tensor(out=mA[:, l, w3, :, :], in0=modv[:, l, (3 * w3 + 1) * 8:(3 * w3 + 2) * 8, :], scalar=1.0, in1=nrm, op0=ALU.add, op1=ALU.mult),
                [bmod, bpar], [bmod])
            DVE(lambda q, l=l, w3=w3: q.tensor_scalar_mul(out=mG[:, l, w3, :, :], in0=modv[:, l, (3 * w3 + 2) * 8:(3 * w3 + 3) * 8, :], scalar1=(1.0 if w3 == 1 else 0.5)),
                [bmod], [bmod])

    def mod_A(l, w3, c, g):
        r = 0 if g < 2 else 1
        return mA[:, l, w3, c, r:r + 1]

    def mod_B(l, w3, c, g):
        r = 0 if g < 2 else 1
        return modv[:, l, (3 * w3) * 8 + c, r:r + 1]

    def mod_G(l, w3, c, g):
        r = 0 if g < 2 else 1
        return mG[:, l, w3, c, r:r + 1]

    def norm_mod(l, w3):
        sq = [aalloc(256).bitcast(BF16), aalloc(256).bitcast(BF16)]
        bsq = [nb("sq0"), nb("sq1")]
        tmp = [aalloc(512), aalloc(512)]
        btmp = [nb("tmp0"), nb("tmp1")]
        rs = aalloc(512)
        brs = nb("rs")
        for g in range(NG):
            ps, bp = nps()
            for c in range(8):
                s = c % 2
                ACT(lambda q, s=s, c=c, g=g: q.activation(out=sq[s], in_=xT[:, c, g * 512:(g + 1) * 512], func=AF.Square), [bx[c][g]], [bsq[s]])
                PE(lambda q, ps=ps, s=s, c=c: q.matmul(ps[:], lhsT=ones_b, rhs=sq[s], start=(c == 0), stop=(c == 7)), [bsq[s], bcb], [bp])
            DVE(lambda q, ps=ps: q.tensor_scalar(out=rs, in0=ps[:], scalar1=1.0 / D, scalar2=EPS, op0=ALU.mult, op1=ALU.add), [bp], [brs])
            ACT(lambda q: q.activation(out=rs, in_=rs, func=AF.Ln), [brs], [brs])
            ACT(lambda q: q.activation(out=rs, in_=rs, func=AF.Exp, scale=-0.5), [brs], [brs])
            for c in range(8):
                s = c % 2
                DVE(lambda q, s=s, c=c, g=g: q.tensor_tensor(out=tmp[s], in0=xT[:, c, g * 512:(g + 1) * 512], in1=rs, op=ALU.mult), [bx[c][g], brs], [btmp[s]])
                ACT(lambda q, s=s, c=c, g=g: q.activation(out=hT[:, c, g * 512:(g + 1) * 512], in_=tmp[s], func=AF.Identity,
                                                          scale=mod_A(l, w3, c, g), bias=mod_B(l, w3, c, g)), [btmp[s], bmod], [bh[c][g]])

    def ffn(l, f):
        w3 = 0 if f == 0 else 2
        areset()
        norm_mod(l, w3)
        actT = aalloc(11 * TOK // 2).bitcast(BF16).rearrange("p (j n) -> p j n", j=11)
        bact = [nb(f"act{g}") for g in range(NG)]
        wd = aalloc(11 * 1024 // 2).bitcast(BF16).rearrange("p (j n) -> p j n", j=11)
        bwd = nb("wd")
        ebuf = [aalloc(512), aalloc(512)]
        bebuf = [nb("e0"), nb("e1")]
        tbuf = [aalloc(512), aalloc(512)]
        btbuf = [nb("t0"), nb("t1")]
        k = 0
        for h in range(2):
            DMA("pool", wd, wdn_d[f][l][h * 1408:(h + 1) * 1408, :].rearrange("(j p) n -> p j n", p=128), bwd, writes=[bwd])
            for j0 in range(0, 11, 2):
                nf = min(2, 11 - j0)
                sl, bs = wnext()
                wg = sl[:, 0:8 * nf * 128].rearrange("p (c n) -> p c n", c=8)
                wu = sl[:, 2048:2048 + 8 * nf * 128].rearrange("p (c n) -> p c n", c=8)
                for jj in range(nf):
                    j = j0 + jj
                    for g in range(NG):
                        pg, bpg = nps()
                        pu, bpu = nps()
                        for c in range(8):
                            PE(lambda q, pg=pg, wg=wg, c=c, jj=jj, g=g: q.matmul(pg[:], lhsT=wg[:, c, jj * 128:(jj + 1) * 128], rhs=hT[:, c, g * 512:(g + 1) * 512], start=(c == 0), stop=(c == 7)),
                               [bs, bh[c][g]], [bpg])
                        for c in range(8):
                            PE(lambda q, pu=pu, wu=wu, c=c, jj=jj, g=g: q.matmul(pu[:], lhsT=wu[:, c, jj * 128:(jj + 1) * 128], rhs=hT[:, c, g * 512:(g + 1) * 512], start=(c == 0), stop=(c == 7)),
                               [bs, bh[c][g]], [bpu])
                        s = k % 2
                        k += 1
                        ACT(lambda q, pg=pg, s=s: q.activation(out=tbuf[s], in_=pg[:], func=AF.Silu), [bpg], [btbuf[s]])
                        DVE(lambda q, pu=pu, s=s, j=j, g=g: q.tensor_tensor(out=actT[:, j, g * 512:(g + 1) * 512], in0=pu[:], in1=tbuf[s], op=ALU.mult), [bpu, btbuf[s]], [bact[g]])
            for g in range(NG):
                for dc in range(8):
                    ps, bp = nps()
                    for j in range(11):
                        PE(lambda q, ps=ps, j=j, dc=dc, g=g: q.matmul(ps[:], lhsT=wd[:, j, dc * 128:(dc + 1) * 128], rhs=actT[:, j, g * 512:(g + 1) * 512], start=(j == 0), stop=(j == 10)),
                           [bwd, bact[g]], [bp])
                    DVE(lambda q, ps=ps, dc=dc, g=g: q.scalar_tensor_tensor(out=xT[:, dc, g * 512:(g + 1) * 512], in0=ps[:], scalar=mod_G(l, w3, dc, g), in1=xT[:, dc, g * 512:(g + 1) * 512], op0=ALU.mult, op1=ALU.add),
                        [bp, bmod, bx[dc][g]], [bx[dc][g]])

    def mixer(l):
        areset()
        norm_mod(l, 1)
        areset()
        catT = aalloc(8 * TOK // 2).bitcast(BF16).rearrange("p (c n) -> p c n", c=8)
        bcat = [[nb(f"cat{c}_{g}") for g in range(NG)] for c in range(8)]
        base_off = state["aoff"]

        def inproj(wv, k, g, ps):
            for c in range(8):
                PE(lambda q, c=c: q.matmul(ps[0][:], lhsT=wv[:, c, k * 128:(k + 1) * 128], rhs=hT[:, c, g * 512:(g + 1) * 512], start=(c == 0), stop=(c == 7)),
                   [ps[2], bh[c][g]], [ps[1]])

        uT = aalloc(2 * TOK // 2).bitcast(BF16).rearrange("p (c n) -> p c n", c=2)
        buT = nb("uT")
        ucs = aalloc(NT * 2 * 256 // 2).bitcast(BF16).rearrange("p (t c n) -> p t c n", t=NT, c=2)
        bucs = nb("ucs")
        sl, bs = wnext()
        wv = sl[:, 0:2048].rearrange("p (c n) -> p c n", c=8)
        for k in range(2):
            for g in range(NG):
                ps, bp = nps()
                inproj(wv, k, g, (ps, bp, bs))
                ACT(lambda q, ps=ps, k=k, g=g: q.activation(out=uT[:, k, g * 512:(g + 1) * 512], in_=ps[:], func=AF.Copy), [bp], [buT])
        dft64 = cb[:, B_DFT:B_DFT + 256]
        for t in range(NT):
            ps, bp = nps()
            for k in range(2):
                PE(lambda q, ps=ps, k=k, t=t: q.matmul(ps[:, k * 256:(k + 1) * 256], lhsT=uT[:, k, t * 128:(t + 1) * 128], rhs=dft64, start=True, stop=True), [buT, bcb], [bp])
            DVE(lambda q, ps=ps, t=t: q.tensor_copy(out=ucs[:, t, :, :], in_=ps[:].rearrange("p (c n) -> p c n", c=2)), [bp], [bucs])
        for hf in range(2):
            slc, bsc_ = wnext()
            sls, bss = wnext(keep_prev=True)
            cl = slc[:, 0:4096].rearrange("p (c n) -> p c n", c=8)
            sn = sls[:, 0:4096].rearrange("p (c n) -> p c n", c=8)
            for k in range(2):
                ps, bp = nps()
                for lc in range(8):
                    PE(lambda q, ps=ps, k=k, lc=lc, cl=cl: q.matmul(ps[:], lhsT=ucs[:, lc, k, 0:128], rhs=cl[:, lc, :], start=(lc == 0), stop=False), [bucs, bsc_], [bp])
                    PE(lambda q, ps=ps, k=k, lc=lc, sn=sn: q.matmul(ps[:], lhsT=ucs[:, lc, k, 128:256], rhs=sn[:, lc, :], start=False, stop=(lc == 7)), [bucs, bss], [bp])
                ACT(lambda q, ps=ps, k=k, hf=hf: q.activation(out=catT[:, k, hf * 512:(hf + 1) * 512], in_=ps[:], func=AF.Copy), [bp], [bcat[k][hf]])
        slp, bsp = wnext()
        pc = slp[:, 0:512].rearrange("p (c n) -> p c n", c=2)
        pn = slp[:, 512:1024].rearrange("p (c n) -> p c n", c=2)
        for k in range(2):
            ps, bp = nps()
            for sq_ in range(2):
                for lc in range(2):
                    tt = 8 + sq_ * 2 + lc
                    PE(lambda q, ps=ps, k=k, lc=lc, tt=tt, sq_=sq_: q.matmul(ps[:, sq_ * 256:(sq_ + 1) * 256], lhsT=ucs[:, tt, k, 0:128], rhs=pc[:, lc, :], start=(lc == 0), stop=False), [bucs, bsp], [bp])
                    PE(lambda q, ps=ps, k=k, lc=lc, tt=tt, sq_=sq_: q.matmul(ps[:, sq_ * 256:(sq_ + 1) * 256], lhsT=ucs[:, tt, k, 128:256], rhs=pn[:, lc, :], start=False, stop=(lc == 1)), [bucs, bsp], [bp])
            ACT(lambda q, ps=ps, k=k: q.activation(out=catT[:, k, 1024:1536], in_=ps[:], func=AF.Copy), [bp], [bcat[k][2]])

        if ENABLE_HGRN:
            for pr in range(2):
                if DEBUG_PAIR is not None and pr != DEBUG_PAIR:
                    wnext(); wnext()
                    for g in range(NG):
                        POOL(lambda q, pr=pr, g=g: q.memset(catT[:, 2 + pr, g * 512:(g + 1) * 512], 0.0), [], [bcat[2 + pr][g]])
                    continue
                state["aoff"] = base_off
                qf, sF, sB, tmpb, kb, hoT = [aalloc(TOK) for _ in range(6)]
                bqf, bsF, bsB, btmpb, bkb, bho = [nb(n) for n in ("qf", "sF", "sB", "tmpb", "kb", "hoT")]
                qt = [aalloc(TOK // 2).bitcast(BF16) for _ in range(2)]
                kt_ = [aalloc(TOK // 2).bitcast(BF16) for _ in range(2)]
                vT = aalloc(TOK // 2).bitcast(BF16)
                sgT = aalloc(TOK // 2).bitcast(BF16)
                bqt, bkt = [nb("qt0"), nb("qt1")], [nb("kt0"), nb("kt1")]
                bvT, bsg = nb("vT"), nb("sgT")
                ktp = [aalloc(128).bitcast(BF16).rearrange("p (a n) -> p a n", a=2) for _ in range(3)]
                vpd = [aalloc(128).bitcast(BF16).rearrange("p (a n) -> p a n", a=2) for _ in range(3)]
                bktp, bvpd = [nb("ktp0"), nb("ktp1"), nb("ktp2")], [nb("vpd0"), nb("vpd1"), nb("vpd2")]
                atb = [aalloc(64).bitcast(BF16) for _ in range(2)]
                batb = [nb("at0"), nb("at1")]
                S = aalloc(128); tS = aalloc(128); S0m = aalloc(64).bitcast(BF16)
                bS, btS, bS0m = nb("S"), nb("tS"), nb("S0m")
                sst = [aalloc(128), aalloc(128)]
                bsst = [nb("sst0"), nb("sst1")]
                scl = [[aalloc(48) for _ in range(4)] for _ in range(2)]
                bscl = [nb("scl0"), nb("scl1")]
                lbp = aalloc(8)
                blbp = nb("lbp")
                for i in range(3):
                    POOL(lambda q, i=i: q.memset(ktp[i], 0.0), [], [bktp[i]])
                    POOL(lambda q, i=i: q.memset(vpd[i], 0.0), [], [bvpd[i]])
                for dr in range(2):
                    c0 = dr * 3
                    if l == 0:
                        POOL(lambda q, c0=c0: q.memset(lbp[:, c0:c0 + 1], 1.0), [], [blbp])
                        POOL(lambda q, c0=c0: q.memset(lbp[:, c0 + 1:c0 + 2], 1e-30), [], [blbp])
                        POOL(lambda q, c0=c0: q.memset(lbp[:, c0 + 2:c0 + 3], -1.0), [], [blbp])
                    else:
                        a0 = P_LBS + 0 * 4 + dr * 2 + pr
                        a1 = P_LBS + 1 * 4 + dr * 2 + pr
                        DVE(lambda q, c0=c0, a0=a0, a1=a1: q.tensor_tensor(out=lbp[:, c0 + 1:c0 + 2], in0=par[:, a0:a0 + 1], in1=par[:, a1:a1 + 1], op=ALU.subtract), [bpar], [blbp])
                        ACT(lambda q, c0=c0: q.activation(out=lbp[:, c0 + 1:c0 + 2], in_=lbp[:, c0 + 1:c0 + 2], func=AF.Exp), [blbp], [blbp])
                        DVE(lambda q, c0=c0: q.tensor_scalar_add(out=lbp[:, c0 + 1:c0 + 2], in0=lbp[:, c0 + 1:c0 + 2], scalar1=1.0), [blbp], [blbp])
                        DVE(lambda q, c0=c0: q.reciprocal(out=lbp[:, c0 + 1:c0 + 2], in_=lbp[:, c0 + 1:c0 + 2]), [blbp], [blbp])
                        DVE(lambda q, c0=c0: q.tensor_scalar(out=lbp[:, c0:c0 + 1], in0=lbp[:, c0 + 1:c0 + 2], scalar1=-1.0, scalar2=1.0, op0=ALU.mult, op1=ALU.add), [blbp], [blbp])
                        DVE(lambda q, c0=c0: q.tensor_scalar_mul(out=lbp[:, c0 + 2:c0 + 3], in0=lbp[:, c0:c0 + 1], scalar1=-1.0), [blbp], [blbp])
                        DVE(lambda q, c0=c0: q.tensor_scalar_max(out=lbp[:, c0 + 1:c0 + 2], in0=lbp[:, c0 + 1:c0 + 2], scalar1=1e-30), [blbp], [blbp])
                sl, bs = wnext()
                for k in range(3):
                    wv = sl[:, k * 1024:(k + 1) * 1024].rearrange("p (c n) -> p c n", c=8)
                    for g in range(NG):
                        ps, bp = nps()
                        inproj(wv, 0, g, (ps, bp, bs))
                        gs = slice(g * 512, (g + 1) * 512)
                        if k == 0:
                            ACT(lambda q, ps=ps, gs=gs: q.activation(out=qf[:, gs], in_=ps[:], func=AF.Copy, scale=0.125), [bp], [bqf])
                        elif k == 1:
                            ACT(lambda q, ps=ps, gs=gs: q.activation(out=vT[:, gs], in_=ps[:], func=AF.Copy), [bp], [bvT])
                        else:
                            ACT(lambda q, ps=ps, gs=gs: q.activation(out=sF[:, gs], in_=ps[:], func=AF.Exp, scale=-1.0), [bp], [bsF])
                sl, bs = wnext()
                for k in range(2):
                    wv = sl[:, k * 1024:(k + 1) * 1024].rearrange("p (c n) -> p c n", c=8)
                    for g in range(NG):
                        ps, bp = nps()
                        inproj(wv, 0, g, (ps, bp, bs))
                        gs = slice(g * 512, (g + 1) * 512)
                        if k == 0:
                            ACT(lambda q, ps=ps, gs=gs: q.activation(out=sB[:, gs], in_=ps[:], func=AF.Exp, scale=-1.0), [bp], [bsB])
                        else:
                            ACT(lambda q, ps=ps, gs=gs: q.activation(out=tmpb[:, gs], in_=ps[:], func=AF.Exp, scale=-1.0), [bp], [btmpb])
                            DVE(lambda q, gs=gs: q.tensor_scalar_add(out=tmpb[:, gs], in0=tmpb[:, gs], scalar1=1.0), [btmpb], [btmpb])
                            DVE(lambda q, gs=gs: q.reciprocal(out=tmpb[:, gs], in_=tmpb[:, gs]), [btmpb], [btmpb])
                            DVE(lambda q, ps=ps, gs=gs: q.tensor_tensor(out=sgT[:, gs], in0=ps[:], in1=tmpb[:, gs], op=ALU.mult), [bp, btmpb], [bsg])
                for (sx, bsx) in ((sF, bsF), (sB, bsB)):
                    DVE(lambda q, sx=sx: q.tensor_scalar_add(out=sx, in0=sx, scalar1=1.0), [bsx], [bsx])
                    DVE(lambda q, sx=sx: q.reciprocal(out=sx, in_=sx), [bsx], [bsx])
                CH = 32
                NCH = TOK // CH
                SPC = 256 // CH
                for dr, (sx, bsx) in enumerate(((sF, bsF), (sB, bsB))):
                    c0 = dr * 3
                    DVE(lambda q, sx=sx, c0=c0: q.tensor_scalar(out=kb, in0=sx, scalar1=lbp[:, c0 + 2:c0 + 3], scalar2=lbp[:, c0:c0 + 1], op0=ALU.mult, op1=ALU.add), [bsx, blbp], [bkb])
                    DVE(lambda q, sx=sx, c0=c0: q.tensor_scalar(out=sx, in0=sx, scalar1=lbp[:, c0:c0 + 1], scalar2=lbp[:, c0 + 1:c0 + 2], op0=ALU.mult, op1=ALU.add), [bsx, blbp], [bsx])
                    ACT(lambda q, sx=sx: q.activation(out=sx, in_=sx, func=AF.Ln), [bsx], [bsx])
                    A_, bA_, B_, bB_ = sx, bsx, tmpb, btmpb
                    sh = 1
                    while sh < CH:
                        A3 = A_.rearrange("p (c n) -> p c n", n=CH)
                        B3 = B_.rearrange("p (c n) -> p c n", n=CH)
                        if dr == 0:
                            ACT(lambda q, A3=A3, B3=B3, sh=sh: q.activation(out=B3[:, :, 0:sh], in_=A3[:, :, 0:sh], func=AF.Copy), [bA_], [bB_])
                            DVE(lambda q, A3=A3, B3=B3, sh=sh: q.tensor_tensor(out=B3[:, :, sh:CH], in0=A3[:, :, sh:CH], in1=A3[:, :, 0:CH - sh], op=ALU.add), [bA_], [bB_])
                        else:
                            ACT(lambda q, A3=A3, B3=B3, sh=sh: q.activation(out=B3[:, :, CH - sh:CH], in_=A3[:, :, CH - sh:CH], func=AF.Copy), [bA_], [bB_])
                            DVE(lambda q, A3=A3, B3=B3, sh=sh: q.tensor_tensor(out=B3[:, :, 0:CH - sh], in0=A3[:, :, 0:CH - sh], in1=A3[:, :, sh:CH], op=ALU.add), [bA_], [bB_])
                        A_, bA_, B_, bB_ = B_, bB_, A_, bA_
                        sh *= 2
                    bq_, bbq_, tq_, btq_ = A_, bA_, B_, bB_
                    b3 = bq_.rearrange("p (c n) -> p c n", n=CH)
                    mid, Ti = (CH // 2 - 1, CH - 1) if dr == 0 else (CH // 2, 0)
                    bm, em, eT, eTm = scl[dr]
                    DVE(lambda q, b3=b3, mid=mid, bm=bm: q.tensor_copy(out=bm, in_=b3[:, :, mid]), [bbq_], [bscl[dr]])
                    ACT(lambda q, bm=bm, em=em: q.activation(out=em, in_=bm, func=AF.Exp), [bscl[dr]], [bscl[dr]])
                    ACT(lambda q, b3=b3, Ti=Ti, eT=eT: q.activation(out=eT, in_=b3[:, :, Ti], func=AF.Exp), [bbq_], [bscl[dr]])
                    DVE(lambda q, b3=b3, Ti=Ti, bm=bm, eTm=eTm: q.tensor_tensor(out=eTm, in0=b3[:, :, Ti], in1=bm, op=ALU.subtract), [bbq_, bscl[dr]], [bscl[dr]])
                    ACT(lambda q, eTm=eTm: q.activation(out=eTm, in_=eTm, func=AF.Exp), [bscl[dr]], [bscl[dr]])
                    DVE(lambda q, b3=b3, bm=bm: q.tensor_tensor(out=b3, in0=b3, in1=bm.unsqueeze(2).to_broadcast([128, NCH, CH]), op=ALU.subtract), [bbq_, bscl[dr]], [bbq_])
                    ACT(lambda q, bq_=bq_, tq_=tq_: q.activation(out=tq_, in_=bq_, func=AF.Exp), [bbq_], [btq_])
                    DVE(lambda q, dr=dr, tq_=tq_: q.tensor_tensor(out=qt[dr], in0=qf, in1=tq_, op=ALU.mult), [bqf, btq_], [bqt[dr]])
                    ACT(lambda q, bq_=bq_, tq_=tq_: q.activation(out=tq_, in_=bq_, func=AF.Exp, scale=-1.0), [bbq_], [btq_])
                    DVE(lambda q, dr=dr, tq_=tq_: q.tensor_tensor(out=kt_[dr], in0=kb, in1=tq_, op=ALU.mult), [bkb, btq_], [bkt[dr]])
                ev = [0]

                def emit_state(slot, dr):
                    i = ev[0] % 2
                    ev[0] += 1
                    ACT(lambda q, i=i: q.activation(out=sst[i], in_=S, func=AF.Copy), [bS], [bsst[i]])
                    for hd in range(2):
                        DMA("sp", sout[l, slot, dr, 2 * pr + hd], sst[i][hd * 64:(hd + 1) * 64, hd * 64:(hd + 1) * 64], bsst[i], reads=[bsst[i]])

                def load_state(dr):
                    POOL(lambda q: q.memset(S, 0.0), [bS], [bS])
                    for hd in range(2):
                        DMA("sp", S[hd * 64:(hd + 1) * 64, hd * 64:(hd + 1) * 64], s0_d[l, dr, 2 * pr + hd], bS, reads=[bS], writes=[bS])

                Xb = [aalloc(128), aalloc(128)]
                hoB = aalloc(TOK // 2).bitcast(BF16)
                bhoB = nb("hoB")
                bXb = [nb("X0"), nb("X1")]
                for dr in range(2):
                    bm, em, eT, eTm = scl[dr]
                    order = list(range(NCH)) if dr == 0 else list(range(NCH - 1, -1, -1))
                    mcol = C_MF if dr == 0 else C_MB
                    pst = {}

                    live = {}

                    def A_pe(n_, dr=dr, order=order, live=live):
                        ch = order[n_]
                        cs = slice(ch * CH, (ch + 1) * CH)
                        ptk, bptk = nps()
                        ptkb = ptk[:].bitcast(BF16)
                        live[("ptk", n_)] = (ptkb, bptk)
                        PE(lambda q: q.transpose(ptkb[0:CH, 0:128], kt_[dr][:, cs], ident_b), [bkt[dr], bcb], [bptk])
                        PE(lambda q: q.transpose(ptkb[0:CH, 128:256], vT[:, cs], ident_b), [bvT, bcb], [bptk])

                    def A_post(n_, live=live):
                        i3 = n_ % 3
                        ptkb, bptk = live.pop(("ptk", n_))
                        ACT(lambda q: q.activation(out=ktp[i3][0:CH, 0, 0:64], in_=ptkb[0:CH, 0:64], func=AF.Copy), [bptk], [bktp[i3]])
                        ACT(lambda q: q.activation(out=ktp[i3][0:CH, 1, 64:128], in_=ptkb[0:CH, 64:128], func=AF.Copy), [bptk], [bktp[i3]])
                        ACT(lambda q: q.activation(out=vpd[i3][0:CH, 0, 0:64], in_=ptkb[0:CH, 128:192], func=AF.Copy), [bptk], [bvpd[i3]])
                        ACT(lambda q: q.activation(out=vpd[i3][0:CH, 1, 64:128], in_=ptkb[0:CH, 192:256], func=AF.Copy), [bptk], [bvpd[i3]])

                    def B_pe(n_, dr=dr, order=order, live=live):
                        ch = order[n_]
                        i3 = n_ % 3
                        cs = slice(ch * CH, (ch + 1) * CH)
                        pas = []
                        for hd in range(2):
                            hs = slice(hd * 64, (hd + 1) * 64)
                            pA, bpA = nps()
                            pas.append((pA, bpA))
                            PE(lambda q, pA=pA, hs=hs: q.matmul(pA[0:CH, 0:CH], lhsT=kt_[dr][hs, cs], rhs=qt[dr][hs, cs], start=True, stop=True), [bkt[dr], bqt[dr]], [bpA])
                        pS2, bpS2 = nps()
                        PE(lambda q: q.matmul(pS2[:, 0:128], lhsT=ktp[i3][0:CH, 0, :], rhs=vpd[i3][0:CH, 0, :], start=True, stop=False), [bktp[i3], bvpd[i3]], [bpS2])
                        PE(lambda q: q.matmul(pS2[:, 0:128], lhsT=ktp[i3][0:CH, 1, :], rhs=vpd[i3][0:CH, 1, :], start=False, stop=True), [bktp[i3], bvpd[i3]], [bpS2])
                        live[("B", n_)] = (pas, pS2, bpS2)

                    def B_post(n_, dr=dr, eTm=eTm, mcol=mcol, order=order, live=live):
                        ch = order[n_]
                        i = n_ % 2
                        pas, pS2, bpS2 = live.pop(("B", n_))
                        for hd, (pA, bpA) in enumerate(pas):
                            DVE(lambda q, pA=pA, hd=hd: q.tensor_tensor(out=atb[i][0:CH, hd * CH:(hd + 1) * CH], in0=pA[0:CH, 0:CH], in1=cf[0:CH, mcol:mcol + CH], op=ALU.mult), [bpA, bcf], [batb[i]])
                        DVE(lambda q: q.tensor_scalar_mul(out=Xb[i], in0=pS2[:, 0:128], scalar1=eTm[:, ch:ch + 1]), [bpS2, bscl[dr]], [bXb[i]])

                    def C_dve(n_, dr=dr, em=em, eT=eT, order=order):
                        ch = order[n_]
                        i = n_ % 2
                        first_of_seq = (ch % SPC == 0) if dr == 0 else (ch % SPC == SPC - 1)
                        if first_of_seq:
                            seq = ch // SPC
                            prev = seq - 1 if dr == 0 else seq + 1
                            if n_ > 0:
                                emit_state(prev, dr)
                            if (dr == 0 and seq == 0) or (dr == 1 and seq == 3):
                                load_state(dr)
                            elif seq >= 4:
                                POOL(lambda q: q.memset(S, 0.0), [bS], [bS])
                            else:
                                DVE(lambda q: q.tensor_scalar_mul(out=S, in0=S, scalar1=par[:, P_KEEP:P_KEEP + 1]), [bS, bpar], [bS])
                        DVE(lambda q: q.tensor_scalar_mul(out=S0m, in0=S, scalar1=em[:, ch:ch + 1]), [bS, bscl[dr]], [bS0m])
                        DVE(lambda q: q.scalar_tensor_tensor(out=S, in0=S, scalar=eT[:, ch:ch + 1], in1=Xb[i], op0=ALU.mult, op1=ALU.add), [bS, bscl[dr], bXb[i]], [bS])

                    def C_pe(n_, dr=dr, order=order):
                        ch = order[n_]
                        i = n_ % 2
                        i3 = n_ % 3
                        cs = slice(ch * CH, (ch + 1) * CH)
                        po, bpo = nps()
                        PE(lambda q: q.matmul(po[:, 0:CH], lhsT=vpd[i3][0:CH, 0, :], rhs=atb[i][0:CH, 0:CH], start=True, stop=False), [bvpd[i3], batb[i]], [bpo])
                        PE(lambda q: q.matmul(po[:, 0:CH], lhsT=vpd[i3][0:CH, 1, :], rhs=atb[i][0:CH, CH:2 * CH], start=False, stop=False), [bvpd[i3], batb[i]], [bpo])
                        PE(lambda q: q.matmul(po[:, 0:CH], lhsT=S0m, rhs=qt[dr][:, cs], start=False, stop=True), [bS0m, bqt[dr]], [bpo])
                        if dr == 0:
                            ACT(lambda q: q.activation(out=hoT[:, cs], in_=po[:, 0:CH], func=AF.Copy), [bpo], [bho])
                        else:
                            ACT(lambda q: q.activation(out=hoB[:, cs], in_=po[:, 0:CH], func=AF.Copy), [bpo], [bhoB])

                    A_pe(0); A_post(0); A_pe(1); A_post(1); B_pe(0); B_post(0)
                    for n_ in range(NCH):
                        C_dve(n_)
                        if n_ + 2 < NCH:
                            A_pe(n_ + 2)
                        if n_ + 1 < NCH:
                            B_pe(n_ + 1)
                        C_pe(n_)
                        if n_ + 2 < NCH:
                            A_post(n_ + 2)
                        if n_ + 1 < NCH:
                            B_post(n_ + 1)
                    emit_state(0 if dr == 1 else 5, dr)
                DVE(lambda q: q.tensor_tensor(out=hoT, in0=hoT, in1=hoB, op=ALU.add), [bho, bhoB], [bho])
                for g in range(NG):
                    gs = slice(g * 512, (g + 1) * 512)
                    sqh = tmpb[:, 0:256].bitcast(BF16)
                    ACT(lambda q, gs=gs, sqh=sqh: q.activation(out=sqh, in_=hoT[:, gs], func=AF.Square), [bho], [btmpb])
                    p2, bp2 = nps()
                    PE(lambda q, p2=p2, sqh=sqh: q.matmul(p2[:], lhsT=blk_b, rhs=sqh, start=True, stop=True), [btmpb, bcb], [bp2])
                    rv = kb[:, 0:512]
                    DVE(lambda q, p2=p2, rv=rv: q.tensor_scalar(out=rv, in0=p2[:], scalar1=1.0 / 64, scalar2=EPS, op0=ALU.mult, op1=ALU.add), [bp2], [bkb])
                    ACT(lambda q, rv=rv: q.activation(out=rv, in_=rv, func=AF.Ln), [bkb], [bkb])
                    ACT(lambda q, rv=rv: q.activation(out=rv, in_=rv, func=AF.Exp, scale=-0.5), [bkb], [bkb])
                    DVE(lambda q, rv=rv, gs=gs, pr=pr: q.scalar_tensor_tensor(out=rv, in0=hoT[:, gs], scalar=par[:, P_HNRM + l * 2 + pr:P_HNRM + l * 2 + pr + 1], in1=rv, op0=ALU.mult, op1=ALU.mult), [bho, bkb, bpar], [bkb])
                    DVE(lambda q, rv=rv, gs=gs, g=g, pr=pr: q.tensor_tensor(out=catT[:, 2 + pr, gs], in0=rv, in1=sgT[:, gs], op=ALU.mult), [bkb, bsg], [bcat[2 + pr][g]])
        else:
            for pr in range(2):
                wnext()
                wnext()
                for g in range(NG):
                    POOL(lambda q, pr=pr, g=g: q.memset(catT[:, 2 + pr, g * 512:(g + 1) * 512], 0.0), [], [bcat[2 + pr][g]])
        if ENABLE_ATTN:
            state["aoff"] = base_off
            qT = aalloc(4 * TOK // 2).bitcast(BF16).rearrange("p (c n) -> p c n", c=4)
            kdT = aalloc(4 * TOK // 2).bitcast(BF16).rearrange("p (c v n) -> p c v n", c=2, v=2)
            kcT = aalloc(4 * 512 // 2).bitcast(BF16).rearrange("p (c v n) -> p c v n", c=2, v=2)
            vaug = aalloc(16 * 2 * 96 // 2).bitcast(BF16).rearrange("p (t k n) -> p t k n", t=16, k=2)
            kst = aalloc(NT * 128).rearrange("p (t n) -> p t n", t=NT)
            vst = aalloc(NT * 128).rearrange("p (t n) -> p t n", t=NT)
            kcd = aalloc(4 * 256).rearrange("p (t k n) -> p t k n", t=4, k=2)
            zq = [aalloc(512), aalloc(512)]
            rr = [aalloc(512), aalloc(512)]
            t1 = [aalloc(512)] * 2
            sqa = [aalloc(256).bitcast(BF16), aalloc(256).bitcast(BF16)]
            pT = [aalloc(256).bitcast(BF16) for _ in range(3)]
            otok = [aalloc(256).bitcast(BF16).rearrange("p (h n) -> p h n", h=8), aalloc(256).bitcast(BF16).rearrange("p (h n) -> p h n", h=8)]
            den = [aalloc(4), aalloc(4)]
            esink = aalloc(8)
            bq = [[nb("q") for g in range(NG)] for c in range(4)]
            bkd = [[nb("kd") for g in range(NG)] for c in range(2)]
            bkc, bva, bkst, bvst, bkcd, besk = nb("kc"), [nb("va") for t in range(16)], nb("kst"), nb("vst"), nb("kcd"), nb("esk")
            bzq, brr, bt1, bsqa, bpT, botok, bden = [[nb(n + str(i)) for i in range(3)] for n in ("zq", "rr", "t1", "sqa", "pT", "otok", "den")]
            bt1[1] = bt1[0]
            ACT(lambda q: q.activation(out=esink, in_=par[:, P_SINK + l * 8:P_SINK + l * 8 + 8], func=AF.Exp), [bpar], [besk])
            POOL(lambda q: q.memset(vaug[:, :, :, 64:66], 1.0), [], list(bva))
            allkd = [bkd[c_][g_] for c_ in range(2) for g_ in range(NG)]
            POOL(lambda q: q.memset(kdT[64:128, :, 0, :], 0.0), [], allkd)
            POOL(lambda q: q.memset(kdT[0:64, :, 1, :], 0.0), [], allkd)
            POOL(lambda q: q.memset(kcT[64:128, :, 0, :], 0.0), [], [bkc])
            POOL(lambda q: q.memset(kcT[0:64, :, 1, :], 0.0), [], [bkc])
            for kv in (range(2) if "cachek" not in SKIP else []):
                for dup in range(2):
                    DMA("sp", kcd[:, :, kv, dup * 64:(dup + 1) * 64], ck_d[l][:, kv * 64:(kv + 1) * 64].rearrange("(t p) d -> p t d", p=128), bkcd, writes=[bkcd])
            for t in (range(4) if "cachev" not in SKIP else []):
                DMA("pool", vaug[:, 12 + t, :, 0:64], cv_d[l][t * 128:(t + 1) * 128, :].rearrange("p (k d) -> p k d", k=2), bva[12 + t], writes=[bva[12 + t]])
            for kv in (range(2) if "cachek" not in SKIP else []):
                ps, bp = nps()
                for t in range(4):
                    PE(lambda q, ps=ps, t=t, kv=kv: q.transpose(ps[:, t * 128:(t + 1) * 128], kcd[:, t, kv, :], ident), [bkcd, bcf], [bp])
                ACT(lambda q, ps=ps, kv=kv: q.activation(out=kcT[0:64, kv, 0, :], in_=ps[0:64, :], func=AF.Copy), [bp], [bkc])
                ACT(lambda q, ps=ps, kv=kv: q.activation(out=kcT[64:128, kv, 1, :], in_=ps[64:128, :], func=AF.Copy), [bp], [bkc])
            cnt = [0]

            def qk_post(ps, bp, g, nw_col, dst, bdst, scale_extra, kfin=None):
                s = cnt[0] % 2
                cnt[0] += 1
                ACT(lambda q: q.activation(out=zq[s], in_=ps[:], func=AF.Copy), [bp], [bzq[s]])
                ACT(lambda q: q.activation(out=sqa[s], in_=ps[:], func=AF.Square), [bp], [bsqa[s]])
                p2, bp2 = nps()
                PE(lambda q: q.matmul(p2[:], lhsT=blk_b, rhs=sqa[s], start=True, stop=True), [bsqa[s], bcb], [bp2])
                DVE(lambda q: q.tensor_scalar(out=rr[s], in0=p2[:], scalar1=1.0 / 64, scalar2=EPS, op0=ALU.mult, op1=ALU.add), [bp2], [brr[s]])
                ACT(lambda q: q.activation(out=rr[s], in_=rr[s], func=AF.Ln), [brr[s]], [brr[s]])
                ACT(lambda q: q.activation(out=rr[s], in_=rr[s], func=AF.Exp, scale=-0.5), [brr[s]], [brr[s]])
                DVE(lambda q: q.scalar_tensor_tensor(out=zq[s], in0=zq[s], scalar=par[:, nw_col:nw_col + 1], in1=rr[s], op0=ALU.mult, op1=ALU.mult), [bzq[s], brr[s], bpar], [bzq[s]])
                if g < 2 and "rope" not in SKIP:
                    p3, bp3 = nps()
                    PE(lambda q: q.matmul(p3[:], lhsT=cf[:, C_PERM:C_PERM + 128], rhs=zq[s], start=True, stop=True), [bzq[s], bcf], [bp3])
                    DVE(lambda q: q.tensor_tensor(out=t1[s], in0=p3[:], in1=cf[:, C_SIN + g * 512:C_SIN + (g + 1) * 512], op=ALU.mult), [bp3, bcf], [bt1[s]])
                    DVE(lambda q: q.tensor_tensor(out=zq[s], in0=zq[s], in1=cf[:, C_COS + g * 512:C_COS + (g + 1) * 512], op=ALU.mult), [bzq[s], bcf], [bzq[s]])
                    DVE(lambda q: q.tensor_tensor(out=zq[s], in0=zq[s], in1=t1[s], op=ALU.add), [bzq[s], bt1[s]], [bzq[s]])
                if kfin is None:
                    ACT(lambda q: q.activation(out=dst, in_=zq[s], func=AF.Copy, scale=scale_extra), [bzq[s]], [bdst])
                else:
                    ACT(lambda q: q.activation(out=dst[0][0:64, :], in_=zq[s][0:64, :], func=AF.Copy), [bzq[s]], [bdst])
                    ACT(lambda q: q.activation(out=dst[1][64:128, :], in_=zq[s][64:128, :], func=AF.Copy), [bzq[s]], [bdst])
                if kfin is not None and "ktr" not in SKIP:
                    kv = kfin
                    p4, bp4 = nps()
                    for tt in range(4):
                        PE(lambda q, tt=tt: q.transpose(p4[:, tt * 64:(tt + 1) * 64], zq[s][0:64, tt * 128:(tt + 1) * 128], ident[0:64, 0:64]), [bzq[s], bcf], [bp4])
                    DVE(lambda q: q.tensor_copy(out=kst[:, g * 4:(g + 1) * 4, kv * 64:(kv + 1) * 64], in_=p4[:, 0:256].rearrange("p (t n) -> p t n", t=4)), [bp4], [bkst])

            sl, bs = wnext()
            wv = sl[:, 0:4096].rearrange("p (c n) -> p c n", c=8)
            for c in range(4):
                for g in range(NG):
                    ps, bp = nps()
                    inproj(wv, c, g, (ps, bp, bs))
                    qk_post(ps, bp, g, P_QN + l, qT[:, c, g * 512:(g + 1) * 512], bq[c][g], 0.125)
            sl, bs = wnext()
            for kv in range(2):
                wv = sl[:, kv * 1024:(kv + 1) * 1024].rearrange("p (c n) -> p c n", c=8)
                for g in range(NG):
                    ps, bp = nps()
                    inproj(wv, 0, g, (ps, bp, bs))
                    qk_post(ps, bp, g, P_KN + l, (kdT[:, kv, 0, g * 512:(g + 1) * 512], kdT[:, kv, 1, g * 512:(g + 1) * 512]), bkd[kv][g], 1.0, kfin=kv)
            wv = sl[:, 2048:3072].rearrange("p (c n) -> p c n", c=8)
            for g in range(NG):
                ps, bp = nps()
                inproj(wv, 0, g, (ps, bp, bs))
                s = cnt[0] % 2
                cnt[0] += 1
                ACT(lambda q, ps=ps, s=s: q.activation(out=zq[s], in_=ps[:], func=AF.Copy), [bp], [bzq[s]])
                if "vtr" in SKIP:
                    continue
                p4, bp4 = nps()
                for tt in range(4):
                    PE(lambda q, tt=tt, s=s, p4=p4: q.transpose(p4[:, tt * 128:(tt + 1) * 128], zq[s][:, tt * 128:(tt + 1) * 128], ident), [bzq[s], bcf], [bp4])
                if "vtr_dve" not in SKIP:
                    DVE(lambda q, p4=p4, g=g: q.tensor_copy(out=vst[:, g * 4:(g + 1) * 4, :], in_=p4[:].rearrange("p (t n) -> p t n", t=4)), [bp4], [bvst])
                for tt in (range(4) if "vtr_act" not in SKIP else []):
                    ACT(lambda q, p4=p4, g=g, tt=tt: q.activation(out=vaug[:, g * 4 + tt, :, 0:64], in_=p4[:, tt * 128:(tt + 1) * 128].rearrange("p (k d) -> p k d", k=2), func=AF.Copy), [bp4], [bva[g * 4 + tt]])
            if "kvout" not in SKIP:
                DMA("sp", kout[l].rearrange("(t p) n -> p t n", p=128), kst, bkst, reads=[bkst])
            if "kvout" not in SKIP:
                DMA("sp", vout[l].rearrange("(t p) n -> p t n", p=128), vst, bvst, reads=[bvst])
            it = 0
            if not ATTN_CORE:
                for c in range(4):
                    for g in range(NG):
                        POOL(lambda q, c=c, g=g: q.memset(catT[:, 4 + c, g * 512:(g + 1) * 512], 0.0), [], [bcat[4 + c][g]])
            for j in (range(NT) if ATTN_CORE else []):
                g = j // 4
                if j < 8:
                    kts = [("n", j + d_, (j * 3 + 1 + d_) if d_ != 0 else None) for d_ in (-1, 0, 1) if 0 <= j + d_ < 8] + [("c", t, None) for t in range(4)]
                else:
                    b0 = 8 + 2 * ((j - 8) // 2)
                    kts = [("n", b0, None), ("n", b0 + 1, None)]
                so = j % 2
                for kv in range(2):
                    po, bpo = nps()
                    pov = po[:, 0:264].rearrange("p (h n) -> p h n", h=4)
                    nk = len(kts)

                    def scores(ki, j=j, kv=kv, g=g):
                        kind, kt, mi = kts[ki]
                        pS, bpS = nps()
                        for hh in range(4):
                            cq = kv * 2 + hh // 2
                            hf = hh % 2
                            if kind == "n":
                                lhs = kdT[:, kv, hf, kt * 128:(kt + 1) * 128]
                                rb = bkd[kv][kt // 4]
                            else:
                                lhs = kcT[:, kv, hf, kt * 128:(kt + 1) * 128]
                                rb = bkc
                            PE(lambda q, pS=pS, hh=hh, lhs=lhs, cq=cq, j=j: q.matmul(pS[:, hh * 128:(hh + 1) * 128], lhsT=lhs, rhs=qT[:, cq, j * 128:(j + 1) * 128], start=True, stop=True),
                               [rb, bq[cq][g]], [bpS])
                        return pS, bpS

                    LOOK = 2
                    sc = {}
                    for ki in range(min(LOOK, nk)):
                        sc[ki] = scores(ki)
                    for ki, (kind, kt, mi) in enumerate(kts):
                        pS, bpS = sc.pop(ki)
                        s = it % 3
                        it += 1
                        if kind == "c":
                            ACT(lambda q, pS=pS, s=s: q.activation(out=pT[s], in_=pS[:], func=AF.Exp, bias=par[:, P_CBIAS:P_CBIAS + 1]), [bpS, bpar], [bpT[s]])
                        else:
                            ACT(lambda q, pS=pS, s=s: q.activation(out=pT[s], in_=pS[:], func=AF.Exp), [bpS], [bpT[s]])
                        if mi is not None:
                            DVE(lambda q, s=s, mi=mi: q.tensor_tensor(out=pT[s].rearrange("p (h n) -> p h n", h=4), in0=pT[s].rearrange("p (h n) -> p h n", h=4),
                                                                      in1=cb[:, B_MASK + mi * 128:B_MASK + (mi + 1) * 128].unsqueeze(1).to_broadcast([128, 4, 128]), op=ALU.mult), [bpT[s], bcb], [bpT[s]])
                        if ki + LOOK < nk:
                            sc[ki + LOOK] = scores(ki + LOOK)
                        vt = kt if kind == "n" else 12 + kt
                        for hh in range(4):
                            PE(lambda q, pov=pov, hh=hh, s=s, vt=vt, kv=kv, ki=ki, n=nk: q.matmul(pov[:, hh, 0:66], lhsT=pT[s][:, hh * 128:(hh + 1) * 128], rhs=vaug[:, vt, kv, 0:66], start=(ki == 0 and hh == 0), stop=(ki == n - 1 and hh == 3)),
                               [bpT[s], bva[vt]], [bpo])
                    DVE(lambda q, pov=pov, so=so, kv=kv: q.tensor_tensor(out=den[so], in0=pov[:, :, 64], in1=esink[:, kv * 4:(kv + 1) * 4], op=ALU.add), [bpo, besk], [bden[so]])
                    DVE(lambda q, so=so: q.reciprocal(out=den[so], in_=den[so]), [bden[so]], [bden[so]])
                    DVE(lambda q, pov=pov, so=so, kv=kv: q.tensor_tensor(out=otok[so][:, kv * 4:(kv + 1) * 4, :], in0=pov[:, :, 0:64], in1=den[so].unsqueeze(2).to_broadcast([128, 4, 64]), op=ALU.mult), [bpo, bden[so]], [botok[so]])
                pt_, bpt = nps()
                ptb = pt_[:].bitcast(BF16)
                for c in range(4):
                    PE(lambda q, ptb=ptb, c=c, so=so: q.transpose(ptb[:, c * 128:(c + 1) * 128], otok[so][:, 2 * c:2 * c + 2, :].rearrange("p h n -> p (h n)"), ident_b), [botok[so], bcb], [bpt])
                ACT(lambda q, ptb=ptb, j=j: q.activation(out=catT[:, 4:8, j * 128:(j + 1) * 128], in_=ptb[:, 0:512].rearrange("p (c n) -> p c n", c=4), func=AF.Copy), [bpt], [bcat[4 + c][g] for c in range(4)])

        else:
            wnext()
            wnext()
            for c in range(4):
                for g in range(NG):
                    POOL(lambda q, c=c, g=g: q.memset(catT[:, 4 + c, g * 512:(g + 1) * 512], 0.0), [], [bcat[4 + c][g]])
        for i in range(2):
            sl, bs = wnext()
            wv = sl[:, 0:4096].rearrange("p (c n) -> p c n", c=8)
            for k in range(4):
                dc = i * 4 + k
                for g in range(NG):
                    ps, bp = nps()
                    for c in range(8):
                        PE(lambda q, ps=ps, wv=wv, c=c, k=k, g=g: q.matmul(ps[:], lhsT=wv[:, c, k * 128:(k + 1) * 128], rhs=catT[:, c, g * 512:(g + 1) * 512], start=(c == 0), stop=(c == 7)),
                           [bs, bcat[c][g]], [bp])
                    DVE(lambda q, ps=ps, dc=dc, g=g: q.scalar_tensor_tensor(out=xT[:, dc, g * 512:(g + 1) * 512], in0=ps[:], scalar=mod_G(l, 1, dc, g), in1=xT[:, dc, g * 512:(g + 1) * 512], op0=ALU.mult, op1=ALU.add),
                        [bp, bmod, bx[dc][g]], [bx[dc][g]])

    for l in range(2):
        ffn(l, 0)
        mixer(l)
        ffn(l, 1)

    areset()
    yst = [aalloc(1024), aalloc(1024)]
    byst = [nb("yst0"), nb("yst1")]
    for t in range(NT):
        s = t % 2
        g = t // 4
        for half in range(2):
            ps, bp = nps()
            for k in range(4):
                c = half * 4 + k
                PE(lambda q, ps=ps, k=k, c=c, t=t: q.transpose(ps[:, k * 128:(k + 1) * 128], xT[:, c, t * 128:(t + 1) * 128], ident), [bx[c][g], bcf], [bp])
            if half == 0:
                ACT(lambda q, ps=ps, s=s: q.activation(out=yst[s][:, 0:512], in_=ps[:], func=AF.Copy), [bp], [byst[s]])
            else:
                DVE(lambda q, ps=ps, s=s: q.tensor_copy(out=yst[s][:, 512:1024], in_=ps[:]), [bp], [byst[s]])
        DMA("sp", yout[t * 128:(t + 1) * 128, :], yst[s], byst[s], reads=[byst[s]])

    T.emit(nc)
    st.close()
    return nc


def _consts(sample_mode):
    cf = np.zeros((128, NCF), np.float32)
    cf[:, C_ID:C_ID + 128] = np.eye(128, dtype=np.float32)
    Pm = np.zeros((64, 64), np.float32)
    for d in range(64):
        blk = d // 16
        if blk % 2 == 0:
            Pm[d, d + 16] = -1.0
        else:
            Pm[d, d - 16] = 1.0
    P2 = np.zeros((128, 128), np.float32)
    P2[:64, :64] = Pm
    P2[64:, 64:] = Pm
    cf[:, C_PERM:C_PERM + 128] = P2.T
    t = np.arange(1024)
    inv = (10000.0 ** (-np.arange(0, 32, 2, dtype=np.float32) / 32)).astype(np.float32)
    d = np.arange(128) % 64
    pos = np.where((d < 32)[:, None], (t // 64)[None, :], (t % 64)[None, :]).astype(np.float32)
    ang = pos * inv[d % 16][:, None]
    if sample_mode:
        cf[:, C_COS:C_COS + 1024] = np.cos(ang)
        cf[:, C_SIN:C_SIN + 1024] = np.sin(ang)
    else:
        cf[:, C_COS:C_COS + 1024] = 1.0
    j = np.arange(64)
    cf[:64, C_MF:C_MF + 64] = (j[:, None] <= j[None, :]).astype(np.float32)
    cf[:64, C_MB:C_MB + 64] = (j[:, None] >= j[None, :]).astype(np.float32)
    cb = np.zeros((128, NCB), np.float32)
    cb[:, B_ONES:B_ONES + 128] = 1.0
    cb[:64, B_BLK:B_BLK + 64] = 1.0
    cb[64:, B_BLK + 64:B_BLK + 128] = 1.0
    cb[:, B_ID:B_ID + 128] = np.eye(128, dtype=np.float32)
    c = np.arange(64)
    a = 2 * np.pi * np.outer(c, c) / 64.0
    for gq in range(2):
        cb[gq * 64:(gq + 1) * 64, B_DFT + gq * 64:B_DFT + (gq + 1) * 64] = np.cos(a)
        cb[gq * 64:(gq + 1) * 64, B_DFT + 128 + gq * 64:B_DFT + 128 + (gq + 1) * 64] = np.sin(a)
    ki = np.arange(128)[:, None]
    qi = np.arange(128)[None, :]
    for jt in range(8):
        for s in range(3):
            if sample_mode:
                m = (ki >= qi) if s == 0 else (np.ones((128, 128), bool) if s == 1 else (ki <= qi))
            else:
                if s == 1:
                    m = np.ones((128, 128), bool)
                elif s == 0:
                    m = np.full((128, 128), jt % 2 == 1)
                else:
                    m = np.full((128, 128), jt % 2 == 0)
            cb[:, B_MASK + (jt * 3 + s) * 128:B_MASK + (jt * 3 + s + 1) * 128] = m.astype(np.float32)

    def dft(L):
        ll = np.arange(L)
        aa = 2 * np.pi * np.outer(ll, ll) / L
        sc_ = 1.0 / np.sqrt(L * 64.0)
        return (np.cos(aa) * sc_).astype(np.float32), (-np.sin(aa) * sc_).astype(np.float32)

    dftp = np.stack(dft(256))
    if sample_mode:
        dftl = np.stack(dft(1024))
    else:
        dftl = np.zeros((2, 1024, 1024), np.float32)
        for i in range(4):
            dftl[:, i * 256:(i + 1) * 256, i * 256:(i + 1) * 256] = dftp
    return cf, cb, dftl, dftp


_NC_CACHE = {}


def _prep(inputs):
    I = {k: np.asarray(v) for k, v in inputs.items()}
    xp, xs = I["x_prompt"], I["x_sample"]
    consts = {True: _consts(True), False: _consts(False)}
    fm = lambda v: np.ascontiguousarray(v.reshape(-1, 128).T)
    in_maps = []
    assign = []
    for core in range(8):
        sm = core < 4
        if sm:
            gtok = xs[core]
            pseq = [2 * core, 2 * core + 1]
            gseq = []
            cond = np.stack([I["c"][core], I["c_ctx"]])
        else:
            base = 8 + 6 * (core - 4)
            gseq = [base, base + 1, base + 2, base + 3]
            gtok = xp[gseq].reshape(1024, D)
            pseq = [base + 4, base + 5]
            cond = np.stack([I["c_ctx"], I["c_ctx"]])
        assign.append((sm, gseq, pseq))
        xin = np.concatenate([gtok, xp[pseq].reshape(512, D)], axis=0)
        par = np.zeros((128, NPAR), np.float32)
        par[:, P_COND:P_COND + 16] = cond.reshape(2, 8, 128).transpose(2, 1, 0).reshape(128, 16)
        for l in range(2):
            par[:, P_BADA + l * 72:P_BADA + (l + 1) * 72] = fm(I["b_ada"][l])
            for w3, nm in enumerate(["norm_ffn1", "norm_mix", "norm_ffn2"]):
                par[:, P_NRM + l * 24 + w3 * 8:P_NRM + l * 24 + w3 * 8 + 8] = fm(I[nm][l])
            for dr in range(2):
                par[:, P_LBS + l * 4 + dr * 2:P_LBS + l * 4 + dr * 2 + 2] = fm(I["hgrn_lower_bounds"][l, dr])
            par[:, P_HNRM + l * 2:P_HNRM + l * 2 + 2] = fm(I["hgrn_norm"][l])
            par[:, P_QN + l] = np.tile(I["q_norm"][l], 2)
            par[:, P_KN + l] = np.tile(I["k_norm"][l], 2)
            par[:, P_SINK + l * 8:P_SINK + (l + 1) * 8] = I["attn_sink"][l][None, :]
        par[:, P_KEEP] = 1.0 if sm else 0.0
        par[:, P_CBIAS] = 0.0 if sm else -30000.0
        cf, cb, dftl, dftp = consts[sm]
        if sm:
            ck = I["cache_attn_k"][core].reshape(2, 512, 128)
            cv = I["cache_attn_v"][core].reshape(2, 512, 128)
            s0 = I["state_hgrn"][core]
        else:
            ck = np.zeros((2, 512, 128), np.float32)
            cv = np.zeros((2, 512, 128), np.float32)
            s0 = np.zeros((2, 2, 4, 64, 64), np.float32)
        in_maps.append(dict(
            xin=np.ascontiguousarray(xin), par=par, cf=cf, cb=cb, dftl=dftl, dftp=dftp,
            ck=np.ascontiguousarray(ck), cv=np.ascontiguousarray(cv), s0=np.ascontiguousarray(s0),
            w_ada=I["w_ada"], ffn1_gu=I["ffn1_w_gate_up"], ffn2_gu=I["ffn2_w_gate_up"],
            ffn1_d=I["ffn1_w_down"], ffn2_d=I["ffn2_w_down"], w_in=I["w_in"], w_out=I["w_out"]))
    return in_maps, assign


def _post(results, assign):
    y_p = np.zeros((32, 256, D), np.float32)
    y_s = np.zeros((4, 1024, D), np.float32)
    nk = np.zeros((32, 2, 256, 2, 64), np.float32)
    nv = np.zeros((32, 2, 256, 2, 64), np.float32)
    ns = np.zeros((32, 2, 2, 4, 64, 64), np.float32)
    for core in range(8):
        r = results[core]
        sm, gseq, pseq = assign[core]
        y = r["yout"]
        ko = r["kout"].reshape(2, TOK, 2, 64)
        vo = r["vout"].reshape(2, TOK, 2, 64)
        so = r["sout"]
        if sm:
            y_s[core] = y[:1024]
        seqs = [(s, i) for i, s in enumerate(gseq)] + [(s, 4 + i) for i, s in enumerate(pseq)]
        for s, slot in seqs:
            y_p[s] = y[slot * 256:(slot + 1) * 256]
            nk[s] = ko[:, slot * 256:(slot + 1) * 256]
            nv[s] = vo[:, slot * 256:(slot + 1) * 256]
            ns[s] = so[:, slot]
    return (y_p, y_s, nk, nv, ns)


def kernel(**inputs):
    in_maps, assign = _prep(inputs)
    if "nc" not in _NC_CACHE:
        _NC_CACHE["nc"] = build()
    res = run_bass_kernel_spmd(_NC_CACHE["nc"], in_maps, core_ids=list(range(8)))
    return _post(res.results, assign)
```

```python
import numpy as np
from contextlib import ExitStack
import concourse.bass as bass
import concourse.mybir as mybir
from concourse.bass_utils import run_bass_kernel_spmd

F32 = mybir.dt.float32
BF16 = mybir.dt.bfloat16
AF = mybir.ActivationFunctionType
ALU = mybir.AluOpType

EP = 8192
ENGS = ("pe", "act", "dve", "pool", "sp")
SAME_ENGINE_SYNC = ("act", "dve", "pool")


class Buf:
    def __init__(self, name, fence=None):
        self.name = name
        self.w = None
        self.r = list(fence) if fence else []
        self.dsem = None
        self.dcnt = 0
        self.excl = False


class Op:
    pass


class Tracker:
    def __init__(self):
        self.ops = []
        self.per_eng = {e: [] for e in ENGS}
        self.dma_bufs = []
        self.dma_ids = []

    def fence(self):
        ids = [l[-1].id for l in self.per_eng.values() if l]
        return ids + list(self.dma_ids)

    def op(self, eng, fn, reads=(), writes=(), dma_buf=None):
        o = Op()
        o.eng = eng
        o.fn = fn
        o.dma_buf = dma_buf
        o.marked = False
        o.id = len(self.ops)
        o.pos = len(self.per_eng[eng])
        deps = set()
        for b in reads:
            if b.w is not None:
                deps.add(b.w)
            if b.excl:
                deps.update(b.r)
        for b in writes:
            if b.w is not None:
                deps.add(b.w)
            deps.update(b.r)
        o.deps = deps
        for b in reads:
            b.r.append(o.id)
        for b in writes:
            b.w = o.id
            b.r = []
        if dma_buf is not None:
            if dma_buf.dsem is None:
                dma_buf.dsem = True
                self.dma_bufs.append(dma_buf)
            dma_buf.dcnt += 16
            o.dval = dma_buf.dcnt
            self.dma_ids.append(o.id)
        self.ops.append(o)
        self.per_eng[eng].append(o)
        return o

    def plan(self):
        ops = self.ops
        seen = {e: {f: -1 for f in ENGS} for e in ENGS}
        seen_dma = {e: {} for e in ENGS}
        for o in ops:
            w_eng = {}
            w_dma = {}
            for d in o.deps:
                p = ops[d]
                if p.dma_buf is not None:
                    b = p.dma_buf
                    if seen_dma[o.eng].get(id(b), 0) >= p.dval:
                        continue
                    w_dma[id(b)] = (b, max(p.dval, w_dma.get(id(b), (None, 0))[1]))
                else:
                    if p.eng == o.eng:
                        if p.eng not in SAME_ENGINE_SYNC:
                            continue
                        if p.pos < o.pos - 2:
                            continue
                    if seen[o.eng][p.eng] >= d:
                        continue
                    w_eng[p.eng] = max(w_eng.get(p.eng, -1), d)
            for f, d in w_eng.items():
                seen[o.eng][f] = d
                ops[d].marked = True
            for k, (b, v) in w_dma.items():
                seen_dma[o.eng][k] = v
            o.w_eng = w_eng
            o.w_dma = list(w_dma.values())
        cnt = {e: 0 for e in ENGS}
        for o in ops:
            if o.dma_buf is None and o.marked:
                cnt[o.eng] += 1
            o.cnt = cnt[o.eng]
        self.final_cnt = cnt

    def emit(self, nc, final_eng="sp"):
        self.plan()
        ops = self.ops
        with ExitStack() as st:
            esems = {}
            for e in ENGS:
                n = max(1, (self.final_cnt[e] + EP - 1) // EP)
                esems[e] = [st.enter_context(nc.semaphore(f"s_{e}_{i}")) for i in range(n)]
            for i, b in enumerate(self.dma_bufs):
                b.dsem = st.enter_context(nc.semaphore(f"d_{i}"))
            block = st.enter_context(nc.Block())

            def run(eng_name, eng):
                for o in self.per_eng[eng_name]:
                    for f, d in o.w_eng.items():
                        c = ops[d].cnt
                        eng.wait_ge(esems[f][(c - 1) // EP], (c - 1) % EP + 1)
                    for (b, v) in o.w_dma:
                        eng.wait_ge(b.dsem, v)
                    ins = o.fn(eng)
                    if o.dma_buf is not None:
                        ins.then_inc(o.dma_buf.dsem, 16)
                    elif o.marked:
                        c = o.cnt
                        ins.then_inc(esems[eng_name][(c - 1) // EP], 1)
                if eng_name == final_eng:
                    for b in self.dma_bufs:
                        eng.wait_ge(b.dsem, b.dcnt)
                    for f in ENGS:
                        c = self.final_cnt[f]
                        if c > 0 and f != eng_name:
                            eng.wait_ge(esems[f][(c - 1) // EP], (c - 1) % EP + 1)

            @block.tensor
            def _(e):
                run("pe", e)

            @block.scalar
            def _(e):
                run("act", e)

            @block.vector
            def _(e):
                run("dve", e)

            @block.gpsimd
            def _(e):
                run("pool", e)

            @block.sync
            def _(e):
                run("sp", e)


TOK = 1536
NT = 12
NG = 3
D = 1024
DFF = 2816
EPS = 1e-6
P_COND, P_BADA, P_NRM, P_LBS, P_HNRM, P_QN, P_KN, P_SINK, P_KEEP, P_CBIAS, NPAR = 0, 16, 160, 208, 216, 220, 222, 224, 240, 241, 242
C_ID, C_PERM, C_COS, C_SIN, C_MF, C_MB, NCF = 0, 128, 256, 1280, 2304, 2368, 2432
B_ONES, B_BLK, B_ID, B_DFT, B_MASK, NCB = 0, 128, 256, 384, 640, 640 + 24 * 128
SLOT = 4096
ENABLE_HGRN = True
DEBUG_PAIR = None
ATTN_CORE = True
SKIP = set()
ENABLE_ATTN = True
NSL = 3
ARENA = 23400


def build(stop_after=None):
    nc = bass.Bass("TRN2", target_bir_lowering=False)
    T = Tracker()
    di = lambda n, s: nc.dram_tensor(n, s, F32, kind="ExternalInput").ap()
    do = lambda n, s: nc.dram_tensor(n, s, F32, kind="ExternalOutput").ap()
    xin = di("xin", [TOK, D]); par_d = di("par", [128, NPAR]); cf_d = di("cf", [128, NCF]); cb_d = di("cb", [128, NCB])
    dftl_d = di("dftl", [2, 1024, 1024]); dftp_d = di("dftp", [2, 256, 256])
    ck_d = di("ck", [2, 512, 128]); cv_d = di("cv", [2, 512, 128]); s0_d = di("s0", [2, 2, 4, 64, 64])
    wada_d = di("w_ada", [2, D, 9 * D])
    wgu_d = [di("ffn1_gu", [2, D, 2 * DFF]), di("ffn2_gu", [2, D, 2 * DFF])]
    wdn_d = [di("ffn1_d", [2, DFF, D]), di("ffn2_d", [2, DFF, D])]
    win_d = di("w_in", [2, D, 2304]); wout_d = di("w_out", [2, D, D])
    yout = do("yout", [TOK, D]); kout = do("kout", [2, TOK, 128]); vout = do("vout", [2, TOK, 128])
    sout = do("sout", [2, 6, 2, 4, 64, 64])

    st = ExitStack()
    sbt = lambda n, s, dt: st.enter_context(nc.sbuf_tensor("sb_" + n, s, dt))
    xT = sbt("xT", [128, 8, TOK], F32)
    hT = sbt("hT", [128, 8, TOK], BF16)
    par = sbt("par", [128, NPAR], F32)
    cf = sbt("cf", [128, NCF], F32)
    cb = sbt("cb", [128, NCB], BF16)
    modv = sbt("modv", [128, 2, 72, 2], F32)
    mA = sbt("mA", [128, 2, 3, 8, 2], F32)
    mG = sbt("mG", [128, 2, 3, 8, 2], F32)
    sml = sbt("sml", [128, 64], F32)
    slots = [sbt(f"slot{i}", [128, SLOT], BF16) for i in range(NSL)]
    arena = sbt("arena", [128, ARENA], F32)
    psum = [st.enter_context(nc.psum_tensor(f"ps{i}", [128, 512], F32)) for i in range(8)]
    bps = [Buf(f"ps{i}") for i in range(8)]
    for b_ in bps:
        b_.excl = True
    bslot = [Buf(f"slot{i}") for i in range(NSL)]
    bx = [[Buf(f"x{c}_{g}") for g in range(NG)] for c in range(8)]
    bh = [[Buf(f"h{c}_{g}") for g in range(NG)] for c in range(8)]
    bpar, bcf, bcb, bmod = Buf("par"), Buf("cf"), Buf("cb"), Buf("mod")
    bsml = Buf("sml")

    state = {"ps": 0, "aoff": 0}

    def nps():
        i = state["ps"] % 8
        state["ps"] += 1
        return psum[i], bps[i]

    def aalloc(n32):
        o = state["aoff"]
        state["aoff"] += n32
        assert state["aoff"] <= ARENA, state["aoff"]
        return arena[:, o:o + n32]

    def areset():
        state["aoff"] = 0

    def nb(name):
        return Buf(name, T.fence())

    PE = lambda fn, r, w: T.op("pe", fn, r, w)
    ACT = lambda fn, r, w: T.op("act", fn, r, w)
    DVE = lambda fn, r, w: T.op("dve", fn, r, w)
    POOL = lambda fn, r, w: T.op("pool", fn, r, w)

    def DMA(eng, out, in_, buf, reads=(), writes=()):
        T.op(eng, lambda q: q.dma_start(out=out, in_=in_), reads, writes, dma_buf=buf)

    wq = []
    wstate = {"loaded": 0, "used": 0}

    def witem(parts):
        wq.append(parts)

    def wnext(keep_prev=False):
        i = wstate["used"]
        oldest = i - 1 if keep_prev else i
        while wstate["loaded"] < min(len(wq), oldest + NSL):
            k = wstate["loaded"]
            s = k % NSL
            for (fn, dap) in wq[k]:
                DMA("pool", fn(slots[s]), dap, bslot[s], writes=[bslot[s]])
            wstate["loaded"] += 1
        wstate["used"] += 1
        return slots[i % NSL], bslot[i % NSL]

    def kview(ap2d):
        return ap2d.rearrange("(c p) n -> p c n", p=128)

    def sched_weights():
        for l in range(2):
            for i in range(18):
                witem([(lambda s: s[:, 0:4096].rearrange("p (c n) -> p c n", c=8), kview(wada_d[l][:, i * 512:(i + 1) * 512]))])
        for l in range(2):
            for f in range(2):
                sched_ffn(l, f)
                if f == 0:
                    sched_mixer(l)

    def sched_ffn(l, f):
        for h in range(2):
            for j0 in range(0, 11, 2):
                nf = min(2, 11 - j0)
                c0 = (h * 11 + j0) * 128
                witem([
                    (lambda s, nf=nf: s[:, 0:8 * nf * 128].rearrange("p (c n) -> p c n", c=8), kview(wgu_d[f][l][:, c0:c0 + nf * 128])),
                    (lambda s, nf=nf: s[:, 2048:2048 + 8 * nf * 128].rearrange("p (c n) -> p c n", c=8), kview(wgu_d[f][l][:, DFF + c0:DFF + c0 + nf * 128])),
                ])

    def sched_mixer(l):
        w = win_d[l]
        witem([(lambda s: s[:, 0:2048].rearrange("p (c n) -> p c n", c=8), kview(w[:, 0:256]))])
        for hf in range(2):
            for cs in range(2):
                witem([(lambda s: s[:, 0:4096].rearrange("p (c n) -> p c n", c=8), kview(dftl_d[cs][:, hf * 512:(hf + 1) * 512]))])
        witem([(lambda s, cs=cs: s[:, cs * 512:(cs + 1) * 512].rearrange("p (c n) -> p c n", c=2), kview(dftp_d[cs])) for cs in range(2)])
        for pr in range(2):
            cols = [256 + pr * 128, 512 + pr * 128, 768 + pr * 128, 1024 + pr * 128, 1280 + pr * 128]
            witem([(lambda s, k=k: s[:, k * 1024:(k + 1) * 1024].rearrange("p (c n) -> p c n", c=8), kview(w[:, cols[k]:cols[k] + 128])) for k in range(3)])
            witem([(lambda s, k=k: s[:, k * 1024:(k + 1) * 1024].rearrange("p (c n) -> p c n", c=8), kview(w[:, cols[3 + k]:cols[3 + k] + 128])) for k in range(2)])
        witem([(lambda s: s[:, 0:4096].rearrange("p (c n) -> p c n", c=8), kview(w[:, 1536:2048]))])
        parts = []
        for kv in range(2):
            for dup in range(2):
                parts.append((lambda s, kv=kv, dup=dup: s[:, kv * 1024:(kv + 1) * 1024].rearrange("p (c n) -> p c n", c=8)[:, :, dup * 64:(dup + 1) * 64],
                              kview(w[:, 2048 + kv * 64:2048 + (kv + 1) * 64])))
        parts.append((lambda s: s[:, 2048:3072].rearrange("p (c n) -> p c n", c=8), kview(w[:, 2176:2304])))
        witem(parts)
        for i in range(2):
            witem([(lambda s: s[:, 0:4096].rearrange("p (c n) -> p c n", c=8), kview(wout_d[l][:, i * 512:(i + 1) * 512]))])

    sched_weights()

    DMA("sp", par[:], par_d, bpar, writes=[bpar])
    DMA("sp", cf[:], cf_d, bcf, writes=[bcf])
    DMA("pool", cb[:], cb_d, bcb, writes=[bcb])
    ident = cf[:, C_ID:C_ID + 128]
    ones_b = cb[:, B_ONES:B_ONES + 128]
    blk_b = cb[:, B_BLK:B_BLK + 128]
    ident_b = cb[:, B_ID:B_ID + 128]

    areset()
    xst = [aalloc(1024), aalloc(1024)]
    bxst = [nb("xst0"), nb("xst1")]
    for t in range(NT):
        s = t % 2
        DMA("sp", xst[s], xin[t * 128:(t + 1) * 128, :], bxst[s], writes=[bxst[s]])
        for half in range(2):
            ps, bp = nps()
            for k in range(4):
                c = half * 4 + k
                PE(lambda q, ps=ps, k=k, c=c, s=s: q.transpose(ps[:, k * 128:(k + 1) * 128], xst[s][:, c * 128:(c + 1) * 128], ident),
                   [bxst[s], bcf], [bp])
            g = t // 4
            eng = ACT if half == 0 else DVE
            fn = (lambda q, ps=ps, half=half, t=t: q.activation(out=xT[:, half * 4:half * 4 + 4, t * 128:(t + 1) * 128], in_=ps[:].rearrange("p (k n) -> p k n", k=4), func=AF.Copy)) if half == 0 else \
                 (lambda q, ps=ps, half=half, t=t: q.tensor_copy(out=xT[:, half * 4:half * 4 + 4, t * 128:(t + 1) * 128], in_=ps[:].rearrange("p (k n) -> p k n", k=4)))
            eng(fn, [bp], [bx[c][g] for c in range(half * 4, half * 4 + 4)])

    areset()
    sc = aalloc(16)
    scb = aalloc(8).bitcast(BF16)
    bsc = nb("sc")
    ACT(lambda q: q.activation(out=sc, in_=par[:, P_COND:P_COND + 16], func=AF.Exp, scale=-1.0), [bpar], [bsc])
    DVE(lambda q: q.tensor_scalar_add(out=sc, in0=sc, scalar1=1.0), [bsc], [bsc])
    DVE(lambda q: q.reciprocal(out=sc, in_=sc), [bsc], [bsc])
    DVE(lambda q: q.tensor_tensor(out=scb, in0=sc, in1=par[:, P_COND:P_COND + 16], op=ALU.mult), [bsc, bpar], [bsc])
    scb3 = scb.rearrange("p (c r) -> p c r", c=8)
    for l in range(2):
        ps, bp = nps()
        for i in range(18):
            sl, bs = wnext()
            wv = sl[:, 0:4096].rearrange("p (c n) -> p c n", c=8)
            for k in range(4):
                nn = i * 4 + k
                for c in range(8):
                    PE(lambda q, ps=ps, wv=wv, k=k, c=c, nn=nn: q.matmul(ps[:, nn * 2:nn * 2 + 2], lhsT=wv[:, c, k * 128:(k + 1) * 128], rhs=scb3[:, c, :], start=(c == 0), stop=(c == 7)),
                       [bs, bsc], [bp])
        DVE(lambda q, ps=ps, l=l: q.tensor_tensor(out=modv[:, l, :, :], in0=ps[:, 0:144].rearrange("p (n r) -> p n r", r=2),
                                                 in1=par[:, P_BADA + l * 72:P_BADA + (l + 1) * 72].unsqueeze(2).to_broadcast([128, 72, 2]), op=ALU.add),
            [bp, bpar], [bmod])
        for w3 in range(3):
            nrm = par[:, P_NRM + l * 24 + w3 * 8:P_NRM + l * 24 + w3 * 8 + 8].unsqueeze(2).to_broadcast([128, 8, 2])
            DVE(lambda q, l=l, w3=w3, nrm=nrm: q.scalar_tensor_tensor(out=mA[:, l, w3, :, :], in0=modv[:, l, (3 * w3 + 1) * 8:(3 * w3 + 2) * 8, :], scalar=1.0, in1=nrm, op0=ALU.add, op1=ALU.mult),
                [bmod, bpar], [bmod])
            DVE(lambda q, l=l, w3=w3: q.tensor_scalar_mul(out=mG[:, l, w3, :, :], in0=modv[:, l, (3 * w3 + 2) * 8:(3 * w3 + 3) * 8, :], scalar1=(1.0 if w3 == 1 else 0.5)),
                [bmod], [bmod])

    def mod_A(l, w3, c, g):
        r = 0 if g < 2 else 1
        return mA[:, l, w3, c, r:r + 1]

    def mod_B(l, w3, c, g):
        r = 0 if g < 2 else 1
        return modv[:, l, (3 * w3) * 8 + c, r:r + 1]

    def mod_G(l, w3, c, g):
        r = 0 if g < 2 else 1
        return mG[:, l, w3, c, r:r + 1]

    def norm_mod(l, w3):
        sq = [aalloc(256).bitcast(BF16), aalloc(256).bitcast(BF16)]
        bsq = [nb("sq0"), nb("sq1")]
        tmp = [aalloc(512), aalloc(512)]
        btmp = [nb("tmp0"), nb("tmp1")]
        rs = aalloc(512)
        brs = nb("rs")
        for g in range(NG):
            ps, bp = nps()
            for c in range(8):
                s = c % 2
                ACT(lambda q, s=s, c=c, g=g: q.activation(out=sq[s], in_=xT[:, c, g * 512:(g + 1) * 512], func=AF.Square), [bx[c][g]], [bsq[s]])
                PE(lambda q, ps=ps, s=s, c=c: q.matmul(ps[:], lhsT=ones_b, rhs=sq[s], start=(c == 0), stop=(c == 7)), [bsq[s], bcb], [bp])
            DVE(lambda q, ps=ps: q.tensor_scalar(out=rs, in0=ps[:], scalar1=1.0 / D, scalar2=EPS, op0=ALU.mult, op1=ALU.add), [bp], [brs])
            ACT(lambda q: q.activation(out=rs, in_=rs, func=AF.Ln), [brs], [brs])
            ACT(lambda q: q.activation(out=rs, in_=rs, func=AF.Exp, scale=-0.5), [brs], [brs])
            for c in range(8):
                s = c % 2
                DVE(lambda q, s=s, c=c, g=g: q.tensor_tensor(out=tmp[s], in0=xT[:, c, g * 512:(g + 1) * 512], in1=rs, op=ALU.mult), [bx[c][g], brs], [btmp[s]])
                ACT(lambda q, s=s, c=c, g=g: q.activation(out=hT[:, c, g * 512:(g + 1) * 512], in_=tmp[s], func=AF.Identity,
                                                          scale=mod_A(l, w3, c, g), bias=mod_B(l, w3, c, g)), [btmp[s], bmod], [bh[c][g]])

    def ffn(l, f):
        w3 = 0 if f == 0 else 2
        areset()
        norm_mod(l, w3)
        actT = aalloc(11 * TOK // 2).bitcast(BF16).rearrange("p (j n) -> p j n", j=11)
        bact = [nb(f"act{g}") for g in range(NG)]
        wd = aalloc(11 * 1024 // 2).bitcast(BF16).rearrange("p (j n) -> p j n", j=11)
        bwd = nb("wd")
        ebuf = [aalloc(512), aalloc(512)]
        bebuf = [nb("e0"), nb("e1")]
        tbuf = [aalloc(512), aalloc(512)]
        btbuf = [nb("t0"), nb("t1")]
        k = 0
        for h in range(2):
            DMA("pool", wd, wdn_d[f][l][h * 1408:(h + 1) * 1408, :].rearrange("(j p) n -> p j n", p=128), bwd, writes=[bwd])
            for j0 in range(0, 11, 2):
                nf = min(2, 11 - j0)
                sl, bs = wnext()
                wg = sl[:, 0:8 * nf * 128].rearrange("p (c n) -> p c n", c=8)
                wu = sl[:, 2048:2048 + 8 * nf * 128].rearrange("p (c n) -> p c n", c=8)
                for jj in range(nf):
                    j = j0 + jj
                    for g in range(NG):
                        pg, bpg = nps()
                        pu, bpu = nps()
                        for c in range(8):
                            PE(lambda q, pg=pg, wg=wg, c=c, jj=jj, g=g: q.matmul(pg[:], lhsT=wg[:, c, jj * 128:(jj + 1) * 128], rhs=hT[:, c, g * 512:(g + 1) * 512], start=(c == 0), stop=(c == 7)),
                               [bs, bh[c][g]], [bpg])
                        for c in range(8):
                            PE(lambda q, pu=pu, wu=wu, c=c, jj=jj, g=g: q.matmul(pu[:], lhsT=wu[:, c, jj * 128:(jj + 1) * 128], rhs=hT[:, c, g * 512:(g + 1) * 512], start=(c == 0), stop=(c == 7)),
                               [bs, bh[c][g]], [bpu])
                        s = k % 2
                        k += 1
                        ACT(lambda q, pg=pg, s=s: q.activation(out=tbuf[s], in_=pg[:], func=AF.Silu), [bpg], [btbuf[s]])
                        DVE(lambda q, pu=pu, s=s, j=j, g=g: q.tensor_tensor(out=actT[:, j, g * 512:(g + 1) * 512], in0=pu[:], in1=tbuf[s], op=ALU.mult), [bpu, btbuf[s]], [bact[g]])
            for g in range(NG):
                for dc in range(8):
                    ps, bp = nps()
                    for j in range(11):
                        PE(lambda q, ps=ps, j=j, dc=dc, g=g: q.matmul(ps[:], lhsT=wd[:, j, dc * 128:(dc + 1) * 128], rhs=actT[:, j, g * 512:(g + 1) * 512], start=(j == 0), stop=(j == 10)),
                           [bwd, bact[g]], [bp])
                    DVE(lambda q, ps=ps, dc=dc, g=g: q.scalar_tensor_tensor(out=xT[:, dc, g * 512:(g + 1) * 512], in0=ps[:], scalar=mod_G(l, w3, dc, g), in1=xT[:, dc, g * 512:(g + 1) * 512], op0=ALU.mult, op1=ALU.add),
                        [bp, bmod, bx[dc][g]], [bx[dc][g]])

    def mixer(l):
        areset()
        norm_mod(l, 1)
        areset()
        catT = aalloc(8 * TOK // 2).bitcast(BF16).rearrange("p (c n) -> p c n", c=8)
        bcat = [[nb(f"cat{c}_{g}") for g in range(NG)] for c in range(8)]
        base_off = state["aoff"]

        def inproj(wv, k, g, ps):
            for c in range(8):
                PE(lambda q, c=c: q.matmul(ps[0][:], lhsT=wv[:, c, k * 128:(k + 1) * 128], rhs=hT[:, c, g * 512:(g + 1) * 512], start=(c == 0), stop=(c == 7)),
                   [ps[2], bh[c][g]], [ps[1]])

        uT = aalloc(2 * TOK // 2).bitcast(BF16).rearrange("p (c n) -> p c n", c=2)
        buT = nb("uT")
        ucs = aalloc(NT * 2 * 256 // 2).bitcast(BF16).rearrange("p (t c n) -> p t c n", t=NT, c=2)
        bucs = nb("ucs")
        sl, bs = wnext()
        wv = sl[:, 0:2048].rearrange("p (c n) -> p c n", c=8)
        for k in range(2):
            for g in range(NG):
                ps, bp = nps()
                inproj(wv, k, g, (ps, bp, bs))
                ACT(lambda q, ps=ps, k=k, g=g: q.activation(out=uT[:, k, g * 512:(g + 1) * 512], in_=ps[:], func=AF.Copy), [bp], [buT])
        dft64 = cb[:, B_DFT:B_DFT + 256]
        for t in range(NT):
            ps, bp = nps()
            for k in range(2):
                PE(lambda q, ps=ps, k=k, t=t: q.matmul(ps[:, k * 256:(k + 1) * 256], lhsT=uT[:, k, t * 128:(t + 1) * 128], rhs=dft64, start=True, stop=True), [buT, bcb], [bp])
            DVE(lambda q, ps=ps, t=t: q.tensor_copy(out=ucs[:, t, :, :], in_=ps[:].rearrange("p (c n) -> p c n", c=2)), [bp], [bucs])
        for hf in range(2):
            slc, bsc_ = wnext()
            sls, bss = wnext(keep_prev=True)
            cl = slc[:, 0:4096].rearrange("p (c n) -> p c n", c=8)
            sn = sls[:, 0:4096].rearrange("p (c n) -> p c n", c=8)
            for k in range(2):
                ps, bp = nps()
                for lc in range(8):
                    PE(lambda q, ps=ps, k=k, lc=lc, cl=cl: q.matmul(ps[:], lhsT=ucs[:, lc, k, 0:128], rhs=cl[:, lc, :], start=(lc == 0), stop=False), [bucs, bsc_], [bp])
                    PE(lambda q, ps=ps, k=k, lc=lc, sn=sn: q.matmul(ps[:], lhsT=ucs[:, lc, k, 128:256], rhs=sn[:, lc, :], start=False, stop=(lc == 7)), [bucs, bss], [bp])
                ACT(lambda q, ps=ps, k=k, hf=hf: q.activation(out=catT[:, k, hf * 512:(hf + 1) * 512], in_=ps[:], func=AF.Copy), [bp], [bcat[k][hf]])
        slp, bsp = wnext()
        pc = slp[:, 0:512].rearrange("p (c n) -> p c n", c=2)
        pn = slp[:, 512:1024].rearrange("p (c n) -> p c n", c=2)
        for k in range(2):
            ps, bp = nps()
            for sq_ in range(2):
                for lc in range(2):
                    tt = 8 + sq_ * 2 + lc
                    PE(lambda q, ps=ps, k=k, lc=lc, tt=tt, sq_=sq_: q.matmul(ps[:, sq_ * 256:(sq_ + 1) * 256], lhsT=ucs[:, tt, k, 0:128], rhs=pc[:, lc, :], start=(lc == 0), stop=False), [bucs, bsp], [bp])
                    PE(lambda q, ps=ps, k=k, lc=lc, tt=tt, sq_=sq_: q.matmul(ps[:, sq_ * 256:(sq_ + 1) * 256], lhsT=ucs[:, tt, k, 128:256], rhs=pn[:, lc, :], start=False, stop=(lc == 1)), [bucs, bsp], [bp])
            ACT(lambda q, ps=ps, k=k: q.activation(out=catT[:, k, 1024:1536], in_=ps[:], func=AF.Copy), [bp], [bcat[k][2]])

        if ENABLE_HGRN:
            CH_ = 32
            for pr in range(2):
                if DEBUG_PAIR is not None and pr != DEBUG_PAIR:
                    wnext(); wnext()
                    for g in range(NG):
                        POOL(lambda q, pr=pr, g=g: q.memset(catT[:, 2 + pr, g * 512:(g + 1) * 512], 0.0), [], [bcat[2 + pr][g]])
                    continue
                state["aoff"] = base_off
                qf, sF, sB, tmpb, kb, hoT = [aalloc(TOK) for _ in range(6)]
                bqf, bsF, bsB, btmpb, bkb, bho = [nb(n) for n in ("qf", "sF", "sB", "tmpb", "kb", "hoT")]
                qt = [aalloc(TOK // 2).bitcast(BF16) for _ in range(2)]
                kt_ = [aalloc(TOK // 2).bitcast(BF16) for _ in range(2)]
                vT = aalloc(TOK // 2).bitcast(BF16)
                sgT = aalloc(TOK // 2).bitcast(BF16)
                bqt, bkt = [nb("qt0"), nb("qt1")], [nb("kt0"), nb("kt1")]
                bvT, bsg = nb("vT"), nb("sgT")
                ktp = [aalloc(128).bitcast(BF16).rearrange("p (a n) -> p a n", a=2) for _ in range(4)]
                vpd = [aalloc(128).bitcast(BF16).rearrange("p (a n) -> p a n", a=2) for _ in range(4)]
                bktp, bvpd = [nb("ktp") for _ in range(4)], [nb("vpd") for _ in range(4)]

                def diag(t):
                    x = t[0:CH_, 0, 0:64]
                    return bass.AP(tensor=x.tensor, offset=x.offset, ap=[list(x.ap[0]), [192, 2], [1, 64]])
                atb = [aalloc(64).bitcast(BF16) for _ in range(3)]
                batb = [nb("at") for _ in range(3)]
                S = aalloc(128); tS = aalloc(128); S0m = aalloc(64).bitcast(BF16)
                bS, btS, bS0m = nb("S"), nb("tS"), nb("S0m")
                sst = [aalloc(128), aalloc(128)]
                bsst = [nb("sst0"), nb("sst1")]
                scl = [[aalloc(48) for _ in range(4)] for _ in range(2)]
                bscl = [nb("scl0"), nb("scl1")]
                lbp = aalloc(8)
                blbp = nb("lbp")
                for i in range(4):
                    POOL(lambda q, i=i: q.memset(ktp[i], 0.0), [], [bktp[i]])
                    POOL(lambda q, i=i: q.memset(vpd[i], 0.0), [], [bvpd[i]])
                for dr in range(2):
                    c0 = dr * 3
                    if l == 0:
                        POOL(lambda q, c0=c0: q.memset(lbp[:, c0:c0 + 1], 1.0), [], [blbp])
                        POOL(lambda q, c0=c0: q.memset(lbp[:, c0 + 1:c0 + 2], 1e-30), [], [blbp])
                        POOL(lambda q, c0=c0: q.memset(lbp[:, c0 + 2:c0 + 3], -1.0), [], [blbp])
                    else:
                        a0 = P_LBS + 0 * 4 + dr * 2 + pr
                        a1 = P_LBS + 1 * 4 + dr * 2 + pr
                        DVE(lambda q, c0=c0, a0=a0, a1=a1: q.tensor_tensor(out=lbp[:, c0 + 1:c0 + 2], in0=par[:, a0:a0 + 1], in1=par[:, a1:a1 + 1], op=ALU.subtract), [bpar], [blbp])
                        ACT(lambda q, c0=c0: q.activation(out=lbp[:, c0 + 1:c0 + 2], in_=lbp[:, c0 + 1:c0 + 2], func=AF.Exp), [blbp], [blbp])
                        DVE(lambda q, c0=c0: q.tensor_scalar_add(out=lbp[:, c0 + 1:c0 + 2], in0=lbp[:, c0 + 1:c0 + 2], scalar1=1.0), [blbp], [blbp])
                        DVE(lambda q, c0=c0: q.reciprocal(out=lbp[:, c0 + 1:c0 + 2], in_=lbp[:, c0 + 1:c0 + 2]), [blbp], [blbp])
                        DVE(lambda q, c0=c0: q.tensor_scalar(out=lbp[:, c0:c0 + 1], in0=lbp[:, c0 + 1:c0 + 2], scalar1=-1.0, scalar2=1.0, op0=ALU.mult, op1=ALU.add), [blbp], [blbp])
                        DVE(lambda q, c0=c0: q.tensor_scalar_mul(out=lbp[:, c0 + 2:c0 + 3], in0=lbp[:, c0:c0 + 1], scalar1=-1.0), [blbp], [blbp])
                        DVE(lambda q, c0=c0: q.tensor_scalar_max(out=lbp[:, c0 + 1:c0 + 2], in0=lbp[:, c0 + 1:c0 + 2], scalar1=1e-30), [blbp], [blbp])
                sl, bs = wnext()
                for k in range(3):
                    wv = sl[:, k * 1024:(k + 1) * 1024].rearrange("p (c n) -> p c n", c=8)
                    for g in range(NG):
                        ps, bp = nps()
                        inproj(wv, 0, g, (ps, bp, bs))
                        gs = slice(g * 512, (g + 1) * 512)
                        if k == 0:
                            ACT(lambda q, ps=ps, gs=gs: q.activation(out=qf[:, gs], in_=ps[:], func=AF.Copy, scale=0.125), [bp], [bqf])
                        elif k == 1:
                            ACT(lambda q, ps=ps, gs=gs: q.activation(out=vT[:, gs], in_=ps[:], func=AF.Copy), [bp], [bvT])
                        else:
                            ACT(lambda q, ps=ps, gs=gs: q.activation(out=sF[:, gs], in_=ps[:], func=AF.Exp, scale=-1.0), [bp], [bsF])
                sl, bs = wnext()
                for k in range(2):
                    wv = sl[:, k * 1024:(k + 1) * 1024].rearrange("p (c n) -> p c n", c=8)
                    for g in range(NG):
                        ps, bp = nps()
                        inproj(wv, 0, g, (ps, bp, bs))
                        gs = slice(g * 512, (g + 1) * 512)
                        if k == 0:
                            ACT(lambda q, ps=ps, gs=gs: q.activation(out=sB[:, gs], in_=ps[:], func=AF.Exp, scale=-1.0), [bp], [bsB])
                        else:
                            ACT(lambda q, ps=ps, gs=gs: q.activation(out=tmpb[:, gs], in_=ps[:], func=AF.Exp, scale=-1.0), [bp], [btmpb])
                            DVE(lambda q, gs=gs: q.tensor_scalar_add(out=tmpb[:, gs], in0=tmpb[:, gs], scalar1=1.0), [btmpb], [btmpb])
                            DVE(lambda q, gs=gs: q.reciprocal(out=tmpb[:, gs], in_=tmpb[:, gs]), [btmpb], [btmpb])
                            DVE(lambda q, ps=ps, gs=gs: q.tensor_tensor(out=sgT[:, gs], in0=ps[:], in1=tmpb[:, gs], op=ALU.mult), [bp, btmpb], [bsg])
                for (sx, bsx) in ((sF, bsF), (sB, bsB)):
                    DVE(lambda q, sx=sx: q.tensor_scalar_add(out=sx, in0=sx, scalar1=1.0), [bsx], [bsx])
                    DVE(lambda q, sx=sx: q.reciprocal(out=sx, in_=sx), [bsx], [bsx])
                CH = 32
                NCH = TOK // CH
                SPC = 256 // CH
                for dr, (sx, bsx) in enumerate(((sF, bsF), (sB, bsB))):
                    c0 = dr * 3
                    DVE(lambda q, sx=sx, c0=c0: q.tensor_scalar(out=kb, in0=sx, scalar1=lbp[:, c0 + 2:c0 + 3], scalar2=lbp[:, c0:c0 + 1], op0=ALU.mult, op1=ALU.add), [bsx, blbp], [bkb])
                    DVE(lambda q, sx=sx, c0=c0: q.tensor_scalar(out=sx, in0=sx, scalar1=lbp[:, c0:c0 + 1], scalar2=lbp[:, c0 + 1:c0 + 2], op0=ALU.mult, op1=ALU.add), [bsx, blbp], [bsx])
                    ACT(lambda q, sx=sx: q.activation(out=sx, in_=sx, func=AF.Ln), [bsx], [bsx])
                    A_, bA_, B_, bB_ = sx, bsx, tmpb, btmpb
                    sh = 1
                    while sh < CH:
                        A3 = A_.rearrange("p (c n) -> p c n", n=CH)
                        B3 = B_.rearrange("p (c n) -> p c n", n=CH)
                        if dr == 0:
                            ACT(lambda q, A3=A3, B3=B3, sh=sh: q.activation(out=B3[:, :, 0:sh], in_=A3[:, :, 0:sh], func=AF.Copy), [bA_], [bB_])
                            DVE(lambda q, A3=A3, B3=B3, sh=sh: q.tensor_tensor(out=B3[:, :, sh:CH], in0=A3[:, :, sh:CH], in1=A3[:, :, 0:CH - sh], op=ALU.add), [bA_], [bB_])
                        else:
                            ACT(lambda q, A3=A3, B3=B3, sh=sh: q.activation(out=B3[:, :, CH - sh:CH], in_=A3[:, :, CH - sh:CH], func=AF.Copy), [bA_], [bB_])
                            DVE(lambda q, A3=A3, B3=B3, sh=sh: q.tensor_tensor(out=B3[:, :, 0:CH - sh], in0=A3[:, :, 0:CH - sh], in1=A3[:, :, sh:CH], op=ALU.add), [bA_], [bB_])
                        A_, bA_, B_, bB_ = B_, bB_, A_, bA_
                        sh *= 2
                    bq_, bbq_, tq_, btq_ = A_, bA_, B_, bB_
                    b3 = bq_.rearrange("p (c n) -> p c n", n=CH)
                    mid, Ti = (CH // 2 - 1, CH - 1) if dr == 0 else (CH // 2, 0)
                    bm, em, eT, eTm = scl[dr]
                    DVE(lambda q, b3=b3, mid=mid, bm=bm: q.tensor_copy(out=bm, in_=b3[:, :, mid]), [bbq_], [bscl[dr]])
                    ACT(lambda q, bm=bm, em=em: q.activation(out=em, in_=bm, func=AF.Exp), [bscl[dr]], [bscl[dr]])
                    ACT(lambda q, b3=b3, Ti=Ti, eT=eT: q.activation(out=eT, in_=b3[:, :, Ti], func=AF.Exp), [bbq_], [bscl[dr]])
                    DVE(lambda q, b3=b3, Ti=Ti, bm=bm, eTm=eTm: q.tensor_tensor(out=eTm, in0=b3[:, :, Ti], in1=bm, op=ALU.subtract), [bbq_, bscl[dr]], [bscl[dr]])
                    ACT(lambda q, eTm=eTm: q.activation(out=eTm, in_=eTm, func=AF.Exp), [bscl[dr]], [bscl[dr]])
                    DVE(lambda q, b3=b3, bm=bm: q.tensor_tensor(out=b3, in0=b3, in1=bm.unsqueeze(2).to_broadcast([128, NCH, CH]), op=ALU.subtract), [bbq_, bscl[dr]], [bbq_])
                    ACT(lambda q, bq_=bq_, tq_=tq_: q.activation(out=tq_, in_=bq_, func=AF.Exp), [bbq_], [btq_])
                    DVE(lambda q, dr=dr, tq_=tq_: q.tensor_tensor(out=qt[dr], in0=qf, in1=tq_, op=ALU.mult), [bqf, btq_], [bqt[dr]])
                    ACT(lambda q, bq_=bq_, tq_=tq_: q.activation(out=tq_, in_=bq_, func=AF.Exp, scale=-1.0), [bbq_], [btq_])
                    DVE(lambda q, dr=dr, tq_=tq_: q.tensor_tensor(out=kt_[dr], in0=kb, in1=tq_, op=ALU.mult), [bkb, btq_], [bkt[dr]])
                ev = [0]

                def emit_state(slot, dr):
                    i = ev[0] % 2
                    ev[0] += 1
                    ACT(lambda q, i=i: q.activation(out=sst[i], in_=S, func=AF.Copy), [bS], [bsst[i]])
                    for hd in range(2):
                        DMA("sp", sout[l, slot, dr, 2 * pr + hd], sst[i][hd * 64:(hd + 1) * 64, hd * 64:(hd + 1) * 64], bsst[i], reads=[bsst[i]])

                def load_state(dr):
                    POOL(lambda q: q.memset(S, 0.0), [bS], [bS])
                    for hd in range(2):
                        DMA("sp", S[hd * 64:(hd + 1) * 64, hd * 64:(hd + 1) * 64], s0_d[l, dr, 2 * pr + hd], bS, reads=[bS], writes=[bS])

                Xb = [aalloc(128), aalloc(128), aalloc(128)]
                hoB = aalloc(TOK // 2).bitcast(BF16)
                bhoB = nb("hoB")
                bXb = [nb("X") for _ in range(3)]
                for dr in range(2):
                    bm, em, eT, eTm = scl[dr]
                    order = list(range(NCH)) if dr == 0 else list(range(NCH - 1, -1, -1))
                    mcol = C_MF if dr == 0 else C_MB
                    pst = {}

                    live = {}

                    def A_pe(n_, dr=dr, order=order, live=live):
                        ch = order[n_]
                        cs = slice(ch * CH, (ch + 1) * CH)
                        ptk, bptk = nps()
                        ptkb = ptk[:].bitcast(BF16)
                        live[("ptk", n_)] = (ptkb, bptk)
                        PE(lambda q: q.transpose(ptkb[0:CH, 0:128], kt_[dr][:, cs], ident_b), [bkt[dr], bcb], [bptk])
                        PE(lambda q: q.transpose(ptkb[0:CH, 128:256], vT[:, cs], ident_b), [bvT, bcb], [bptk])

                    def A_post(n_, live=live):
                        i3 = n_ % 4
                        ptkb, bptk = live.pop(("ptk", n_))
                        ACT(lambda q: q.activation(out=diag(ktp[i3]), in_=ptkb[0:CH, 0:128].rearrange("p (a n) -> p a n", a=2), func=AF.Copy), [bptk], [bktp[i3]])
                        ACT(lambda q: q.activation(out=diag(vpd[i3]), in_=ptkb[0:CH, 128:256].rearrange("p (a n) -> p a n", a=2), func=AF.Copy), [bptk], [bvpd[i3]])

                    def B_pe(n_, dr=dr, order=order, live=live):
                        ch = order[n_]
                        i3 = n_ % 4
                        cs = slice(ch * CH, (ch + 1) * CH)
                        pas = []
                        for hd in range(2):
                            hs = slice(hd * 64, (hd + 1) * 64)
                            pA, bpA = nps()
                            pas.append((pA, bpA))
                            PE(lambda q, pA=pA, hs=hs: q.matmul(pA[0:CH, 0:CH], lhsT=kt_[dr][hs, cs], rhs=qt[dr][hs, cs], start=True, stop=True), [bkt[dr], bqt[dr]], [bpA])
                        pS2, bpS2 = nps()
                        PE(lambda q: q.matmul(pS2[:, 0:128], lhsT=ktp[i3][0:CH, 0, :], rhs=vpd[i3][0:CH, 0, :], start=True, stop=False), [bktp[i3], bvpd[i3]], [bpS2])
                        PE(lambda q: q.matmul(pS2[:, 0:128], lhsT=ktp[i3][0:CH, 1, :], rhs=vpd[i3][0:CH, 1, :], start=False, stop=True), [bktp[i3], bvpd[i3]], [bpS2])
                        live[("B", n_)] = (pas, pS2, bpS2)

                    def B_post(n_, dr=dr, eTm=eTm, mcol=mcol, order=order, live=live):
                        ch = order[n_]
                        i = n_ % 3
                        pas, pS2, bpS2 = live.pop(("B", n_))
                        for hd, (pA, bpA) in enumerate(pas):
                            DVE(lambda q, pA=pA, hd=hd: q.tensor_tensor(out=atb[i][0:CH, hd * CH:(hd + 1) * CH], in0=pA[0:CH, 0:CH], in1=cf[0:CH, mcol:mcol + CH], op=ALU.mult), [bpA, bcf], [batb[i]])
                        DVE(lambda q: q.tensor_scalar_mul(out=Xb[i], in0=pS2[:, 0:128], scalar1=eTm[:, ch:ch + 1]), [bpS2, bscl[dr]], [bXb[i]])

                    def C_dve(n_, dr=dr, em=em, eT=eT, order=order):
                        ch = order[n_]
                        i = n_ % 3
                        first_of_seq = (ch % SPC == 0) if dr == 0 else (ch % SPC == SPC - 1)
                        if first_of_seq:
                            seq = ch // SPC
                            prev = seq - 1 if dr == 0 else seq + 1
                            if n_ > 0:
                                emit_state(prev, dr)
                            if (dr == 0 and seq == 0) or (dr == 1 and seq == 3):
                                load_state(dr)
                            elif seq >= 4:
                                POOL(lambda q: q.memset(S, 0.0), [bS], [bS])
                            else:
                                DVE(lambda q: q.tensor_scalar_mul(out=S, in0=S, scalar1=par[:, P_KEEP:P_KEEP + 1]), [bS, bpar], [bS])
                        DVE(lambda q: q.tensor_scalar_mul(out=S0m, in0=S, scalar1=em[:, ch:ch + 1]), [bS, bscl[dr]], [bS0m])
                        DVE(lambda q: q.scalar_tensor_tensor(out=S, in0=S, scalar=eT[:, ch:ch + 1], in1=Xb[i], op0=ALU.mult, op1=ALU.add), [bS, bscl[dr], bXb[i]], [bS])

                    def C_pe(n_, dr=dr, order=order):
                        ch = order[n_]
                        i = n_ % 3
                        i3 = n_ % 4
                        cs = slice(ch * CH, (ch + 1) * CH)
                        po, bpo = nps()
                        PE(lambda q: q.matmul(po[:, 0:CH], lhsT=vpd[i3][0:CH, 0, :], rhs=atb[i][0:CH, 0:CH], start=True, stop=False), [bvpd[i3], batb[i]], [bpo])
                        PE(lambda q: q.matmul(po[:, 0:CH], lhsT=vpd[i3][0:CH, 1, :], rhs=atb[i][0:CH, CH:2 * CH], start=False, stop=False), [bvpd[i3], batb[i]], [bpo])
                        PE(lambda q: q.matmul(po[:, 0:CH], lhsT=S0m, rhs=qt[dr][:, cs], start=False, stop=True), [bS0m, bqt[dr]], [bpo])
                        if dr == 0:
                            ACT(lambda q: q.activation(out=hoT[:, cs], in_=po[:, 0:CH], func=AF.Copy), [bpo], [bho])
                        else:
                            ACT(lambda q: q.activation(out=hoB[:, cs], in_=po[:, 0:CH], func=AF.Copy), [bpo], [bhoB])

                    for k_ in range(3):
                        A_pe(k_); A_post(k_)
                    for k_ in range(2):
                        B_pe(k_); B_post(k_)
                    for n_ in range(NCH):
                        C_dve(n_)
                        if n_ + 3 < NCH:
                            A_pe(n_ + 3)
                        if n_ + 2 < NCH:
                            B_pe(n_ + 2)
                        C_pe(n_)
                        if n_ + 3 < NCH:
                            A_post(n_ + 3)
                        if n_ + 2 < NCH:
                            B_post(n_ + 2)
                    emit_state(0 if dr == 1 else 5, dr)
                DVE(lambda q: q.tensor_tensor(out=hoT, in0=hoT, in1=hoB, op=ALU.add), [bho, bhoB], [bho])
                for g in range(NG):
                    gs = slice(g * 512, (g + 1) * 512)
                    sqh = tmpb[:, 0:256].bitcast(BF16)
                    ACT(lambda q, gs=gs, sqh=sqh: q.activation(out=sqh, in_=hoT[:, gs], func=AF.Square), [bho], [btmpb])
                    p2, bp2 = nps()
                    PE(lambda q, p2=p2, sqh=sqh: q.matmul(p2[:], lhsT=blk_b, rhs=sqh, start=True, stop=True), [btmpb, bcb], [bp2])
                    rv = kb[:, 0:512]
                    DVE(lambda q, p2=p2, rv=rv: q.tensor_scalar(out=rv, in0=p2[:], scalar1=1.0 / 64, scalar2=EPS, op0=ALU.mult, op1=ALU.add), [bp2], [bkb])
                    ACT(lambda q, rv=rv: q.activation(out=rv, in_=rv, func=AF.Ln), [bkb], [bkb])
                    ACT(lambda q, rv=rv: q.activation(out=rv, in_=rv, func=AF.Exp, scale=-0.5), [bkb], [bkb])
                    DVE(lambda q, rv=rv, gs=gs, pr=pr: q.scalar_tensor_tensor(out=rv, in0=hoT[:, gs], scalar=par[:, P_HNRM + l * 2 + pr:P_HNRM + l * 2 + pr + 1], in1=rv, op0=ALU.mult, op1=ALU.mult), [bho, bkb, bpar], [bkb])
                    DVE(lambda q, rv=rv, gs=gs, g=g, pr=pr: q.tensor_tensor(out=catT[:, 2 + pr, gs], in0=rv, in1=sgT[:, gs], op=ALU.mult), [bkb, bsg], [bcat[2 + pr][g]])
        else:
            for pr in range(2):
                wnext()
                wnext()
                for g in range(NG):
                    POOL(lambda q, pr=pr, g=g: q.memset(catT[:, 2 + pr, g * 512:(g + 1) * 512], 0.0), [], [bcat[2 + pr][g]])
        if ENABLE_ATTN:
            state["aoff"] = base_off
            qT = aalloc(4 * TOK // 2).bitcast(BF16).rearrange("p (c n) -> p c n", c=4)
            kdT = aalloc(4 * TOK // 2).bitcast(BF16).rearrange("p (c v n) -> p c v n", c=2, v=2)
            kcT = aalloc(4 * 512 // 2).bitcast(BF16).rearrange("p (c v n) -> p c v n", c=2, v=2)
            vaug = aalloc(16 * 2 * 96 // 2).bitcast(BF16).rearrange("p (t k n) -> p t k n", t=16, k=2)
            kst = aalloc(NT * 128).rearrange("p (t n) -> p t n", t=NT)
            vst = aalloc(NT * 128).rearrange("p (t n) -> p t n", t=NT)
            kcd = aalloc(4 * 256).rearrange("p (t k n) -> p t k n", t=4, k=2)
            zq = [aalloc(512), aalloc(512)]
            rr = [aalloc(512), aalloc(512)]
            t1 = [aalloc(512)] * 2
            sqa = [aalloc(256).bitcast(BF16), aalloc(256).bitcast(BF16)]
            pT = [aalloc(256).bitcast(BF16) for _ in range(3)]
            otok = [aalloc(256).bitcast(BF16).rearrange("p (h n) -> p h n", h=8), aalloc(256).bitcast(BF16).rearrange("p (h n) -> p h n", h=8)]
            den = [aalloc(4), aalloc(4)]
            esink = aalloc(8)
            bq = [[nb("q") for g in range(NG)] for c in range(4)]
            bkd = [[nb("kd") for g in range(NG)] for c in range(2)]
            bkc, bva, bkst, bvst, bkcd, besk = nb("kc"), [nb("va") for t in range(16)], nb("kst"), nb("vst"), nb("kcd"), nb("esk")
            bzq, brr, bt1, bsqa, bpT, botok, bden = [[nb(n + str(i)) for i in range(3)] for n in ("zq", "rr", "t1", "sqa", "pT", "otok", "den")]
            bt1[1] = bt1[0]
            ACT(lambda q: q.activation(out=esink, in_=par[:, P_SINK + l * 8:P_SINK + l * 8 + 8], func=AF.Exp), [bpar], [besk])
            POOL(lambda q: q.memset(vaug[:, :, :, 64:66], 1.0), [], list(bva))
            allkd = [bkd[c_][g_] for c_ in range(2) for g_ in range(NG)]
            POOL(lambda q: q.memset(kdT[64:128, :, 0, :], 0.0), [], allkd)
            POOL(lambda q: q.memset(kdT[0:64, :, 1, :], 0.0), [], allkd)
            POOL(lambda q: q.memset(kcT[64:128, :, 0, :], 0.0), [], [bkc])
            POOL(lambda q: q.memset(kcT[0:64, :, 1, :], 0.0), [], [bkc])
            for kv in (range(2) if "cachek" not in SKIP else []):
                for dup in range(2):
                    DMA("sp", kcd[:, :, kv, dup * 64:(dup + 1) * 64], ck_d[l][:, kv * 64:(kv + 1) * 64].rearrange("(t p) d -> p t d", p=128), bkcd, writes=[bkcd])
            for t in (range(4) if "cachev" not in SKIP else []):
                DMA("pool", vaug[:, 12 + t, :, 0:64], cv_d[l][t * 128:(t + 1) * 128, :].rearrange("p (k d) -> p k d", k=2), bva[12 + t], writes=[bva[12 + t]])
            for kv in (range(2) if "cachek" not in SKIP else []):
                ps, bp = nps()
                for t in range(4):
                    PE(lambda q, ps=ps, t=t, kv=kv: q.transpose(ps[:, t * 128:(t + 1) * 128], kcd[:, t, kv, :], ident), [bkcd, bcf], [bp])
                ACT(lambda q, ps=ps, kv=kv: q.activation(out=kcT[0:64, kv, 0, :], in_=ps[0:64, :], func=AF.Copy), [bp], [bkc])
                ACT(lambda q, ps=ps, kv=kv: q.activation(out=kcT[64:128, kv, 1, :], in_=ps[64:128, :], func=AF.Copy), [bp], [bkc])
            cnt = [0]

            def qk_post(ps, bp, g, nw_col, dst, bdst, scale_extra, kfin=None):
                s = cnt[0] % 2
                cnt[0] += 1
                ACT(lambda q: q.activation(out=zq[s], in_=ps[:], func=AF.Copy), [bp], [bzq[s]])
                ACT(lambda q: q.activation(out=sqa[s], in_=ps[:], func=AF.Square), [bp], [bsqa[s]])
                p2, bp2 = nps()
                PE(lambda q: q.matmul(p2[:], lhsT=blk_b, rhs=sqa[s], start=True, stop=True), [bsqa[s], bcb], [bp2])
                DVE(lambda q: q.tensor_scalar(out=rr[s], in0=p2[:], scalar1=1.0 / 64, scalar2=EPS, op0=ALU.mult, op1=ALU.add), [bp2], [brr[s]])
                ACT(lambda q: q.activation(out=rr[s], in_=rr[s], func=AF.Ln), [brr[s]], [brr[s]])
                ACT(lambda q: q.activation(out=rr[s], in_=rr[s], func=AF.Exp, scale=-0.5), [brr[s]], [brr[s]])
                DVE(lambda q: q.scalar_tensor_tensor(out=zq[s], in0=zq[s], scalar=par[:, nw_col:nw_col + 1], in1=rr[s], op0=ALU.mult, op1=ALU.mult), [bzq[s], brr[s], bpar], [bzq[s]])
                if g < 2 and "rope" not in SKIP:
                    p3, bp3 = nps()
                    PE(lambda q: q.matmul(p3[:], lhsT=cf[:, C_PERM:C_PERM + 128], rhs=zq[s], start=True, stop=True), [bzq[s], bcf], [bp3])
                    DVE(lambda q: q.tensor_tensor(out=t1[s], in0=p3[:], in1=cf[:, C_SIN + g * 512:C_SIN + (g + 1) * 512], op=ALU.mult), [bp3, bcf], [bt1[s]])
                    DVE(lambda q: q.tensor_tensor(out=zq[s], in0=zq[s], in1=cf[:, C_COS + g * 512:C_COS + (g + 1) * 512], op=ALU.mult), [bzq[s], bcf], [bzq[s]])
                    DVE(lambda q: q.tensor_tensor(out=zq[s], in0=zq[s], in1=t1[s], op=ALU.add), [bzq[s], bt1[s]], [bzq[s]])
                if kfin is None:
                    ACT(lambda q: q.activation(out=dst, in_=zq[s], func=AF.Copy, scale=scale_extra), [bzq[s]], [bdst])
                else:
                    ACT(lambda q: q.activation(out=dst[0][0:64, :], in_=zq[s][0:64, :], func=AF.Copy), [bzq[s]], [bdst])
                    ACT(lambda q: q.activation(out=dst[1][64:128, :], in_=zq[s][64:128, :], func=AF.Copy), [bzq[s]], [bdst])
                if kfin is not None and "ktr" not in SKIP:
                    kv = kfin
                    p4, bp4 = nps()
                    for tt in range(4):
                        PE(lambda q, tt=tt: q.transpose(p4[:, tt * 64:(tt + 1) * 64], zq[s][0:64, tt * 128:(tt + 1) * 128], ident[0:64, 0:64]), [bzq[s], bcf], [bp4])
                    DVE(lambda q: q.tensor_copy(out=kst[:, g * 4:(g + 1) * 4, kv * 64:(kv + 1) * 64], in_=p4[:, 0:256].rearrange("p (t n) -> p t n", t=4)), [bp4], [bkst])

            sl, bs = wnext()
            wv = sl[:, 0:4096].rearrange("p (c n) -> p c n", c=8)
            for c in range(4):
                for g in range(NG):
                    ps, bp = nps()
                    inproj(wv, c, g, (ps, bp, bs))
                    qk_post(ps, bp, g, P_QN + l, qT[:, c, g * 512:(g + 1) * 512], bq[c][g], 0.125)
            sl, bs = wnext()
            for kv in range(2):
                wv = sl[:, kv * 1024:(kv + 1) * 1024].rearrange("p (c n) -> p c n", c=8)
                for g in range(NG):
                    ps, bp = nps()
                    inproj(wv, 0, g, (ps, bp, bs))
                    qk_post(ps, bp, g, P_KN + l, (kdT[:, kv, 0, g * 512:(g + 1) * 512], kdT[:, kv, 1, g * 512:(g + 1) * 512]), bkd[kv][g], 1.0, kfin=kv)
            wv = sl[:, 2048:3072].rearrange("p (c n) -> p c n", c=8)
            for g in range(NG):
                ps, bp = nps()
                inproj(wv, 0, g, (ps, bp, bs))
                s = cnt[0] % 2
                cnt[0] += 1
                ACT(lambda q, ps=ps, s=s: q.activation(out=zq[s], in_=ps[:], func=AF.Copy), [bp], [bzq[s]])
                if "vtr" in SKIP:
                    continue
                p4, bp4 = nps()
                for tt in range(4):
                    PE(lambda q, tt=tt, s=s, p4=p4: q.transpose(p4[:, tt * 128:(tt + 1) * 128], zq[s][:, tt * 128:(tt + 1) * 128], ident), [bzq[s], bcf], [bp4])
                if "vtr_dve" not in SKIP:
                    DVE(lambda q, p4=p4, g=g: q.tensor_copy(out=vst[:, g * 4:(g + 1) * 4, :], in_=p4[:].rearrange("p (t n) -> p t n", t=4)), [bp4], [bvst])
                for tt in (range(4) if "vtr_act" not in SKIP else []):
                    ACT(lambda q, p4=p4, g=g, tt=tt: q.activation(out=vaug[:, g * 4 + tt, :, 0:64], in_=p4[:, tt * 128:(tt + 1) * 128].rearrange("p (k d) -> p k d", k=2), func=AF.Copy), [bp4], [bva[g * 4 + tt]])
            if "kvout" not in SKIP:
                DMA("sp", kout[l].rearrange("(t p) n -> p t n", p=128), kst, bkst, reads=[bkst])
            if "kvout" not in SKIP:
                DMA("sp", vout[l].rearrange("(t p) n -> p t n", p=128), vst, bvst, reads=[bvst])
            it = 0
            if not ATTN_CORE:
                for c in range(4):
                    for g in range(NG):
                        POOL(lambda q, c=c, g=g: q.memset(catT[:, 4 + c, g * 512:(g + 1) * 512], 0.0), [], [bcat[4 + c][g]])
            for j in (range(NT) if ATTN_CORE else []):
                g = j // 4
                if j < 8:
                    kts = [("n", j + d_, (j * 3 + 1 + d_) if d_ != 0 else None) for d_ in (-1, 0, 1) if 0 <= j + d_ < 8] + [("c", t, None) for t in range(4)]
                else:
                    b0 = 8 + 2 * ((j - 8) // 2)
                    kts = [("n", b0, None), ("n", b0 + 1, None)]
                so = j % 2
                for kv in range(2):
                    po, bpo = nps()
                    pov = po[:, 0:264].rearrange("p (h n) -> p h n", h=4)
                    nk = len(kts)

                    def scores(ki, j=j, kv=kv, g=g):
                        kind, kt, mi = kts[ki]
                        pS, bpS = nps()
                        for hh in range(4):
                            cq = kv * 2 + hh // 2
                            hf = hh % 2
                            if kind == "n":
                                lhs = kdT[:, kv, hf, kt * 128:(kt + 1) * 128]
                                rb = bkd[kv][kt // 4]
                            else:
                                lhs = kcT[:, kv, hf, kt * 128:(kt + 1) * 128]
                                rb = bkc
                            PE(lambda q, pS=pS, hh=hh, lhs=lhs, cq=cq, j=j: q.matmul(pS[:, hh * 128:(hh + 1) * 128], lhsT=lhs, rhs=qT[:, cq, j * 128:(j + 1) * 128], start=True, stop=True),
                               [rb, bq[cq][g]], [bpS])
                        return pS, bpS

                    LOOK = 2
                    sc = {}
                    for ki in range(min(LOOK, nk)):
                        sc[ki] = scores(ki)
                    for ki, (kind, kt, mi) in enumerate(kts):
                        pS, bpS = sc.pop(ki)
                        s = it % 3
                        it += 1
                        if kind == "c":
                            ACT(lambda q, pS=pS, s=s: q.activation(out=pT[s], in_=pS[:], func=AF.Exp, bias=par[:, P_CBIAS:P_CBIAS + 1]), [bpS, bpar], [bpT[s]])
                        else:
                            ACT(lambda q, pS=pS, s=s: q.activation(out=pT[s], in_=pS[:], func=AF.Exp), [bpS], [bpT[s]])
                        if mi is not None:
                            DVE(lambda q, s=s, mi=mi: q.tensor_tensor(out=pT[s].rearrange("p (h n) -> p h n", h=4), in0=pT[s].rearrange("p (h n) -> p h n", h=4),
                                                                      in1=cb[:, B_MASK + mi * 128:B_MASK + (mi + 1) * 128].unsqueeze(1).to_broadcast([128, 4, 128]), op=ALU.mult), [bpT[s], bcb], [bpT[s]])
                        if ki + LOOK < nk:
                            sc[ki + LOOK] = scores(ki + LOOK)
                        vt = kt if kind == "n" else 12 + kt
                        for hh in range(4):
                            PE(lambda q, pov=pov, hh=hh, s=s, vt=vt, kv=kv, ki=ki, n=nk: q.matmul(pov[:, hh, 0:66], lhsT=pT[s][:, hh * 128:(hh + 1) * 128], rhs=vaug[:, vt, kv, 0:66], start=(ki == 0 and hh == 0), stop=(ki == n - 1 and hh == 3)),
                               [bpT[s], bva[vt]], [bpo])
                    DVE(lambda q, pov=pov, so=so, kv=kv: q.tensor_tensor(out=den[so], in0=pov[:, :, 64], in1=esink[:, kv * 4:(kv + 1) * 4], op=ALU.add), [bpo, besk], [bden[so]])
                    DVE(lambda q, so=so: q.reciprocal(out=den[so], in_=den[so]), [bden[so]], [bden[so]])
                    DVE(lambda q, pov=pov, so=so, kv=kv: q.tensor_tensor(out=otok[so][:, kv * 4:(kv + 1) * 4, :], in0=pov[:, :, 0:64], in1=den[so].unsqueeze(2).to_broadcast([128, 4, 64]), op=ALU.mult), [bpo, bden[so]], [botok[so]])
                pt_, bpt = nps()
                ptb = pt_[:].bitcast(BF16)
                for c in range(4):
                    PE(lambda q, ptb=ptb, c=c, so=so: q.transpose(ptb[:, c * 128:(c + 1) * 128], otok[so][:, 2 * c:2 * c + 2, :].rearrange("p h n -> p (h n)"), ident_b), [botok[so], bcb], [bpt])
                ACT(lambda q, ptb=ptb, j=j: q.activation(out=catT[:, 4:8, j * 128:(j + 1) * 128], in_=ptb[:, 0:512].rearrange("p (c n) -> p c n", c=4), func=AF.Copy), [bpt], [bcat[4 + c][g] for c in range(4)])

        else:
            wnext()
            wnext()
            for c in range(4):
                for g in range(NG):
                    POOL(lambda q, c=c, g=g: q.memset(catT[:, 4 + c, g * 512:(g + 1) * 512], 0.0), [], [bcat[4 + c][g]])
        for i in range(2):
            sl, bs = wnext()
            wv = sl[:, 0:4096].rearrange("p (c n) -> p c n", c=8)
            for k in range(4):
                dc = i * 4 + k
                for g in range(NG):
                    ps, bp = nps()
                    for c in range(8):
                        PE(lambda q, ps=ps, wv=wv, c=c, k=k, g=g: q.matmul(ps[:], lhsT=wv[:, c, k * 128:(k + 1) * 128], rhs=catT[:, c, g * 512:(g + 1) * 512], start=(c == 0), stop=(c == 7)),
                           [bs, bcat[c][g]], [bp])
                    DVE(lambda q, ps=ps, dc=dc, g=g: q.scalar_tensor_tensor(out=xT[:, dc, g * 512:(g + 1) * 512], in0=ps[:], scalar=mod_G(l, 1, dc, g), in1=xT[:, dc, g * 512:(g + 1) * 512], op0=ALU.mult, op1=ALU.add),
                        [bp, bmod, bx[dc][g]], [bx[dc][g]])

    for l in range(2):
        ffn(l, 0)
        mixer(l)
        ffn(l, 1)

    areset()
    yst = [aalloc(1024), aalloc(1024)]
    byst = [nb("yst0"), nb("yst1")]
    for t in range(NT):
        s = t % 2
        g = t // 4
        for half in range(2):
            ps, bp = nps()
            for k in range(4):
                c = half * 4 + k
                PE(lambda q, ps=ps, k=k, c=c, t=t: q.transpose(ps[:, k * 128:(k + 1) * 128], xT[:, c, t * 128:(t + 1) * 128], ident), [bx[c][g], bcf], [bp])
            if half == 0:
                ACT(lambda q, ps=ps, s=s: q.activation(out=yst[s][:, 0:512], in_=ps[:], func=AF.Copy), [bp], [byst[s]])
            else:
                DVE(lambda q, ps=ps, s=s: q.tensor_copy(out=yst[s][:, 512:1024], in_=ps[:]), [bp], [byst[s]])
        DMA("sp", yout[t * 128:(t + 1) * 128, :], yst[s], byst[s], reads=[byst[s]])

    T.emit(nc)
    st.close()
    return nc


def _consts(sample_mode):
    cf = np.zeros((128, NCF), np.float32)
    cf[:, C_ID:C_ID + 128] = np.eye(128, dtype=np.float32)
    Pm = np.zeros((64, 64), np.float32)
    for d in range(64):
        blk = d // 16
        if blk % 2 == 0:
            Pm[d, d + 16] = -1.0
        else:
            Pm[d, d - 16] = 1.0
    P2 = np.zeros((128, 128), np.float32)
    P2[:64, :64] = Pm
    P2[64:, 64:] = Pm
    cf[:, C_PERM:C_PERM + 128] = P2.T
    t = np.arange(1024)
    inv = (10000.0 ** (-np.arange(0, 32, 2, dtype=np.float32) / 32)).astype(np.float32)
    d = np.arange(128) % 64
    pos = np.where((d < 32)[:, None], (t // 64)[None, :], (t % 64)[None, :]).astype(np.float32)
    ang = pos * inv[d % 16][:, None]
    if sample_mode:
        cf[:, C_COS:C_COS + 1024] = np.cos(ang)
        cf[:, C_SIN:C_SIN + 1024] = np.sin(ang)
    else:
        cf[:, C_COS:C_COS + 1024] = 1.0
    j = np.arange(64)
    cf[:64, C_MF:C_MF + 64] = (j[:, None] <= j[None, :]).astype(np.float32)
    cf[:64, C_MB:C_MB + 64] = (j[:, None] >= j[None, :]).astype(np.float32)
    cb = np.zeros((128, NCB), np.float32)
    cb[:, B_ONES:B_ONES + 128] = 1.0
    cb[:64, B_BLK:B_BLK + 64] = 1.0
    cb[64:, B_BLK + 64:B_BLK + 128] = 1.0
    cb[:, B_ID:B_ID + 128] = np.eye(128, dtype=np.float32)
    c = np.arange(64)
    a = 2 * np.pi * np.outer(c, c) / 64.0
    for gq in range(2):
        cb[gq * 64:(gq + 1) * 64, B_DFT + gq * 64:B_DFT + (gq + 1) * 64] = np.cos(a)
        cb[gq * 64:(gq + 1) * 64, B_DFT + 128 + gq * 64:B_DFT + 128 + (gq + 1) * 64] = np.sin(a)
    ki = np.arange(128)[:, None]
    qi = np.arange(128)[None, :]
    for jt in range(8):
        for s in range(3):
            if sample_mode:
                m = (ki >= qi) if s == 0 else (np.ones((128, 128), bool) if s == 1 else (ki <= qi))
            else:
                if s == 1:
                    m = np.ones((128, 128), bool)
                elif s == 0:
                    m = np.full((128, 128), jt % 2 == 1)
                else:
                    m = np.full((128, 128), jt % 2 == 0)
            cb[:, B_MASK + (jt * 3 + s) * 128:B_MASK + (jt * 3 + s + 1) * 128] = m.astype(np.float32)

    def dft(L):
        ll = np.arange(L)
        aa = 2 * np.pi * np.outer(ll, ll) / L
        sc_ = 1.0 / np.sqrt(L * 64.0)
        return (np.cos(aa) * sc_).astype(np.float32), (-np.sin(aa) * sc_).astype(np.float32)

    dftp = np.stack(dft(256))
    if sample_mode:
        dftl = np.stack(dft(1024))
    else:
        dftl = np.zeros((2, 1024, 1024), np.float32)
        for i in range(4):
            dftl[:, i * 256:(i + 1) * 256, i * 256:(i + 1) * 256] = dftp
    return cf, cb, dftl, dftp


_NC_CACHE = {}


def _prep(inputs):
    I = {k: np.asarray(v) for k, v in inputs.items()}
    xp, xs = I["x_prompt"], I["x_sample"]
    consts = {True: _consts(True), False: _consts(False)}
    fm = lambda v: np.ascontiguousarray(v.reshape(-1, 128).T)
    in_maps = []
    assign = []
    for core in range(8):
        sm = core < 4
        if sm:
            gtok = xs[core]
            pseq = [2 * core, 2 * core + 1]
            gseq = []
            cond = np.stack([I["c"][core], I["c_ctx"]])
        else:
            base = 8 + 6 * (core - 4)
            gseq = [base, base + 1, base + 2, base + 3]
            gtok = xp[gseq].reshape(1024, D)
            pseq = [base + 4, base + 5]
            cond = np.stack([I["c_ctx"], I["c_ctx"]])
        assign.append((sm, gseq, pseq))
        xin = np.concatenate([gtok, xp[pseq].reshape(512, D)], axis=0)
        par = np.zeros((128, NPAR), np.float32)
        par[:, P_COND:P_COND + 16] = cond.reshape(2, 8, 128).transpose(2, 1, 0).reshape(128, 16)
        for l in range(2):
            par[:, P_BADA + l * 72:P_BADA + (l + 1) * 72] = fm(I["b_ada"][l])
            for w3, nm in enumerate(["norm_ffn1", "norm_mix", "norm_ffn2"]):
                par[:, P_NRM + l * 24 + w3 * 8:P_NRM + l * 24 + w3 * 8 + 8] = fm(I[nm][l])
            for dr in range(2):
                par[:, P_LBS + l * 4 + dr * 2:P_LBS + l * 4 + dr * 2 + 2] = fm(I["hgrn_lower_bounds"][l, dr])
            par[:, P_HNRM + l * 2:P_HNRM + l * 2 + 2] = fm(I["hgrn_norm"][l])
            par[:, P_QN + l] = np.tile(I["q_norm"][l], 2)
            par[:, P_KN + l] = np.tile(I["k_norm"][l], 2)
            par[:, P_SINK + l * 8:P_SINK + (l + 1) * 8] = I["attn_sink"][l][None, :]
        par[:, P_KEEP] = 1.0 if sm else 0.0
        par[:, P_CBIAS] = 0.0 if sm else -30000.0
        cf, cb, dftl, dftp = consts[sm]
        if sm:
            ck = I["cache_attn_k"][core].reshape(2, 512, 128)
            cv = I["cache_attn_v"][core].reshape(2, 512, 128)
            s0 = I["state_hgrn"][core]
        else:
            ck = np.zeros((2, 512, 128), np.float32)
            cv = np.zeros((2, 512, 128), np.float32)
            s0 = np.zeros((2, 2, 4, 64, 64), np.float32)
        in_maps.append(dict(
            xin=np.ascontiguousarray(xin), par=par, cf=cf, cb=cb, dftl=dftl, dftp=dftp,
            ck=np.ascontiguousarray(ck), cv=np.ascontiguousarray(cv), s0=np.ascontiguousarray(s0),
            w_ada=I["w_ada"], ffn1_gu=I["ffn1_w_gate_up"], ffn2_gu=I["ffn2_w_gate_up"],
            ffn1_d=I["ffn1_w_down"], ffn2_d=I["ffn2_w_down"], w_in=I["w_in"], w_out=I["w_out"]))
    return in_maps, assign


def _post(results, assign):
    y_p = np.zeros((32, 256, D), np.float32)
    y_s = np.zeros((4, 1024, D), np.float32)
    nk = np.zeros((32, 2, 256, 2, 64), np.float32)
    nv = np.zeros((32, 2, 256, 2, 64), np.float32)
    ns = np.zeros((32, 2, 2, 4, 64, 64), np.float32)
    for core in range(8):
        r = results[core]
        sm, gseq, pseq = assign[core]
        y = r["yout"]
        ko = r["kout"].reshape(2, TOK, 2, 64)
        vo = r["vout"].reshape(2, TOK, 2, 64)
        so = r["sout"]
        if sm:
            y_s[core] = y[:1024]
        seqs = [(s, i) for i, s in enumerate(gseq)] + [(s, 4 + i) for i, s in enumerate(pseq)]
        for s, slot in seqs:
            y_p[s] = y[slot * 256:(slot + 1) * 256]
            nk[s] = ko[:, slot * 256:(slot + 1) * 256]
            nv[s] = vo[:, slot * 256:(slot + 1) * 256]
            ns[s] = so[:, slot]
    return (y_p, y_s, nk, nv, ns)


def kernel(**inputs):
    in_maps, assign = _prep(inputs)
    if "nc" not in _NC_CACHE:
        _NC_CACHE["nc"] = build()
    res = run_bass_kernel_spmd(_NC_CACHE["nc"], in_maps, core_ids=list(range(8)))
    return _post(res.results, assign)
```

```python
import numpy as np
from contextlib import ExitStack
import concourse.bass as bass
import concourse.mybir as mybir
from concourse.bass_utils import run_bass_kernel_spmd

F32 = mybir.dt.float32
BF16 = mybir.dt.bfloat16
AF = mybir.ActivationFunctionType
ALU = mybir.AluOpType

EP = 8192
ENGS = ("pe", "act", "dve", "pool", "sp")
SAME_ENGINE_SYNC = ("act", "dve", "pool")


class Buf:
    def __init__(self, name, fence=None):
        self.name = name
        self.w = None
        self.r = list(fence) if fence else []
        self.dsem = None
        self.dcnt = 0
        self.excl = False


class Op:
    pass


class Tracker:
    def __init__(self):
        self.ops = []
        self.per_eng = {e: [] for e in ENGS}
        self.dma_bufs = []
        self.dma_ids = []

    def fence(self):
        ids = [l[-1].id for l in self.per_eng.values() if l]
        return ids + list(self.dma_ids)

    def op(self, eng, fn, reads=(), writes=(), dma_buf=None):
        o = Op()
        o.eng = eng
        o.fn = fn
        o.dma_buf = dma_buf
        o.marked = False
        o.id = len(self.ops)
        o.pos = len(self.per_eng[eng])
        deps = set()
        for b in reads:
            if b.w is not None:
                deps.add(b.w)
            if b.excl:
                deps.update(b.r)
        for b in writes:
            if b.w is not None:
                deps.add(b.w)
            deps.update(b.r)
        o.deps = deps
        for b in reads:
            b.r.append(o.id)
        for b in writes:
            b.w = o.id
            b.r = []
        if dma_buf is not None:
            if dma_buf.dsem is None:
                dma_buf.dsem = True
                self.dma_bufs.append(dma_buf)
            dma_buf.dcnt += 16
            o.dval = dma_buf.dcnt
            self.dma_ids.append(o.id)
        self.ops.append(o)
        self.per_eng[eng].append(o)
        return o

    def plan(self):
        ops = self.ops
        seen = {e: {f: -1 for f in ENGS} for e in ENGS}
        seen_dma = {e: {} for e in ENGS}
        for o in ops:
            w_eng = {}
            w_dma = {}
            for d in o.deps:
                p = ops[d]
                if p.dma_buf is not None:
                    b = p.dma_buf
                    if seen_dma[o.eng].get(id(b), 0) >= p.dval:
                        continue
                    w_dma[id(b)] = (b, max(p.dval, w_dma.get(id(b), (None, 0))[1]))
                else:
                    if p.eng == o.eng:
                        if p.eng not in SAME_ENGINE_SYNC:
                            continue
                        if p.pos < o.pos - 2:
                            continue
                    if seen[o.eng][p.eng] >= d:
                        continue
                    w_eng[p.eng] = max(w_eng.get(p.eng, -1), d)
            for f, d in w_eng.items():
                seen[o.eng][f] = d
                ops[d].marked = True
            for k, (b, v) in w_dma.items():
                seen_dma[o.eng][k] = v
            o.w_eng = w_eng
            o.w_dma = list(w_dma.values())
        cnt = {e: 0 for e in ENGS}
        for o in ops:
            if o.dma_buf is None and o.marked:
                cnt[o.eng] += 1
            o.cnt = cnt[o.eng]
        self.final_cnt = cnt

    def emit(self, nc, final_eng="sp"):
        self.plan()
        ops = self.ops
        with ExitStack() as st:
            esems = {}
            for e in ENGS:
                n = max(1, (self.final_cnt[e] + EP - 1) // EP)
                esems[e] = [st.enter_context(nc.semaphore(f"s_{e}_{i}")) for i in range(n)]
            for i, b in enumerate(self.dma_bufs):
                b.dsem = st.enter_context(nc.semaphore(f"d_{i}"))
            block = st.enter_context(nc.Block())

            def run(eng_name, eng):
                for o in self.per_eng[eng_name]:
                    for f, d in o.w_eng.items():
                        c = ops[d].cnt
                        eng.wait_ge(esems[f][(c - 1) // EP], (c - 1) % EP + 1)
                    for (b, v) in o.w_dma:
                        eng.wait_ge(b.dsem, v)
                    ins = o.fn(eng)
                    if o.dma_buf is not None:
                        ins.then_inc(o.dma_buf.dsem, 16)
                    elif o.marked:
                        c = o.cnt
                        ins.then_inc(esems[eng_name][(c - 1) // EP], 1)
                if eng_name == final_eng:
                    for b in self.dma_bufs:
                        eng.wait_ge(b.dsem, b.dcnt)
                    for f in ENGS:
                        c = self.final_cnt[f]
                        if c > 0 and f != eng_name:
                            eng.wait_ge(esems[f][(c - 1) // EP], (c - 1) % EP + 1)

            @block.tensor
            def _(e):
                run("pe", e)

            @block.scalar
            def _(e):
                run("act", e)

            @block.vector
            def _(e):
                run("dve", e)

            @block.gpsimd
            def _(e):
                run("pool", e)

            @block.sync
            def _(e):
                run("sp", e)


TOK = 1536
NT = 12
NG = 3
D = 1024
DFF = 2816
EPS = 1e-6
P_COND, P_BADA, P_NRM, P_LBS, P_HNRM, P_QN, P_KN, P_SINK, P_KEEP, P_CBIAS, NPAR = 0, 16, 160, 208, 216, 220, 222, 224, 240, 241, 242
C_ID, C_PERM, C_COS, C_SIN, C_MF, C_MB, NCF = 0, 128, 256, 1280, 2304, 2368, 2432
B_ONES, B_BLK, B_ID, B_DFT, B_MASK, NCB = 0, 128, 256, 384, 640, 640 + 24 * 128
SLOT = 4096
ENABLE_HGRN = True
DEBUG_PAIR = None
ATTN_CORE = True
SKIP = set()
ENABLE_ATTN = True
NSL = 3
ARENA = 23400


def build(stop_after=None):
    nc = bass.Bass("TRN2", target_bir_lowering=False)
    T = Tracker()
    di = lambda n, s: nc.dram_tensor(n, s, F32, kind="ExternalInput").ap()
    do = lambda n, s: nc.dram_tensor(n, s, F32, kind="ExternalOutput").ap()
    xin = di("xin", [TOK, D]); par_d = di("par", [128, NPAR]); cf_d = di("cf", [128, NCF]); cb_d = di("cb", [128, NCB])
    dftl_d = di("dftl", [2, 1024, 1024]); dftp_d = di("dftp", [2, 256, 256])
    ck_d = di("ck", [2, 512, 128]); cv_d = di("cv", [2, 512, 128]); s0_d = di("s0", [2, 2, 4, 64, 64])
    wada_d = di("w_ada", [2, D, 9 * D])
    wgu_d = [di("ffn1_gu", [2, D, 2 * DFF]), di("ffn2_gu", [2, D, 2 * DFF])]
    wdn_d = [di("ffn1_d", [2, DFF, D]), di("ffn2_d", [2, DFF, D])]
    win_d = di("w_in", [2, D, 2304]); wout_d = di("w_out", [2, D, D])
    yout = do("yout", [TOK, D]); kout = do("kout", [2, TOK, 128]); vout = do("vout", [2, TOK, 128])
    sout = do("sout", [2, 6, 2, 4, 64, 64])

    st = ExitStack()
    sbt = lambda n, s, dt: st.enter_context(nc.sbuf_tensor("sb_" + n, s, dt))
    xT = sbt("xT", [128, 8, TOK], F32)
    hT = sbt("hT", [128, 8, TOK], BF16)
    par = sbt("par", [128, NPAR], F32)
    cf = sbt("cf", [128, NCF], F32)
    cb = sbt("cb", [128, NCB], BF16)
    modv = sbt("modv", [128, 2, 72, 2], F32)
    mA = sbt("mA", [128, 2, 3, 8, 2], F32)
    mG = sbt("mG", [128, 2, 3, 8, 2], F32)
    sml = sbt("sml", [128, 64], F32)
    slots = [sbt(f"slot{i}", [128, SLOT], BF16) for i in range(NSL)]
    arena = sbt("arena", [128, ARENA], F32)
    psum = [st.enter_context(nc.psum_tensor(f"ps{i}", [128, 512], F32)) for i in range(8)]
    bps = [Buf(f"ps{i}") for i in range(8)]
    for b_ in bps:
        b_.excl = True
    bslot = [Buf(f"slot{i}") for i in range(NSL)]
    bx = [[Buf(f"x{c}_{g}") for g in range(NG)] for c in range(8)]
    bh = [[Buf(f"h{c}_{g}") for g in range(NG)] for c in range(8)]
    bpar, bcf, bcb, bmod = Buf("par"), Buf("cf"), Buf("cb"), Buf("mod")
    bsml = Buf("sml")

    state = {"ps": 0, "aoff": 0}

    def nps():
        i = state["ps"] % 8
        state["ps"] += 1
        return psum[i], bps[i]

    def aalloc(n32):
        o = state["aoff"]
        state["aoff"] += n32
        assert state["aoff"] <= ARENA, state["aoff"]
        return arena[:, o:o + n32]

    def areset():
        state["aoff"] = 0

    def nb(name):
        return Buf(name, T.fence())

    PE = lambda fn, r, w: T.op("pe", fn, r, w)
    ACT = lambda fn, r, w: T.op("act", fn, r, w)
    DVE = lambda fn, r, w: T.op("dve", fn, r, w)
    POOL = lambda fn, r, w: T.op("pool", fn, r, w)

    def DMA(eng, out, in_, buf, reads=(), writes=()):
        T.op(eng, lambda q: q.dma_start(out=out, in_=in_), reads, writes, dma_buf=buf)

    wq = []
    wstate = {"loaded": 0, "used": 0}

    def witem(parts):
        wq.append(parts)

    def wnext(keep_prev=False):
        i = wstate["used"]
        oldest = i - 1 if keep_prev else i
        while wstate["loaded"] < min(len(wq), oldest + NSL):
            k = wstate["loaded"]
            s = k % NSL
            for (fn, dap) in wq[k]:
                DMA("pool", fn(slots[s]), dap, bslot[s], writes=[bslot[s]])
            wstate["loaded"] += 1
        wstate["used"] += 1
        return slots[i % NSL], bslot[i % NSL]

    def kview(ap2d):
        return ap2d.rearrange("(c p) n -> p c n", p=128)

    def sched_weights():
        for l in range(2):
            for i in range(18):
                witem([(lambda s: s[:, 0:4096].rearrange("p (c n) -> p c n", c=8), kview(wada_d[l][:, i * 512:(i + 1) * 512]))])
        for l in range(2):
            for f in range(2):
                sched_ffn(l, f)
                if f == 0:
                    sched_mixer(l)

    def sched_ffn(l, f):
        for h in range(2):
            for j0 in range(0, 11, 2):
                nf = min(2, 11 - j0)
                c0 = (h * 11 + j0) * 128
                witem([
                    (lambda s, nf=nf: s[:, 0:8 * nf * 128].rearrange("p (c n) -> p c n", c=8), kview(wgu_d[f][l][:, c0:c0 + nf * 128])),
                    (lambda s, nf=nf: s[:, 2048:2048 + 8 * nf * 128].rearrange("p (c n) -> p c n", c=8), kview(wgu_d[f][l][:, DFF + c0:DFF + c0 + nf * 128])),
                ])

    def sched_mixer(l):
        w = win_d[l]
        witem([(lambda s: s[:, 0:2048].rearrange("p (c n) -> p c n", c=8), kview(w[:, 0:256]))])
        for hf in range(2):
            for cs in range(2):
                witem([(lambda s: s[:, 0:4096].rearrange("p (c n) -> p c n", c=8), kview(dftl_d[cs][:, hf * 512:(hf + 1) * 512]))])
        witem([(lambda s, cs=cs: s[:, cs * 512:(cs + 1) * 512].rearrange("p (c n) -> p c n", c=2), kview(dftp_d[cs])) for cs in range(2)])
        for pr in range(2):
            cols = [256 + pr * 128, 512 + pr * 128, 768 + pr * 128, 1024 + pr * 128, 1280 + pr * 128]
            witem([(lambda s, k=k: s[:, k * 1024:(k + 1) * 1024].rearrange("p (c n) -> p c n", c=8), kview(w[:, cols[k]:cols[k] + 128])) for k in range(3)])
            witem([(lambda s, k=k: s[:, k * 1024:(k + 1) * 1024].rearrange("p (c n) -> p c n", c=8), kview(w[:, cols[3 + k]:cols[3 + k] + 128])) for k in range(2)])
        witem([(lambda s: s[:, 0:4096].rearrange("p (c n) -> p c n", c=8), kview(w[:, 1536:2048]))])
        parts = []
        for kv in range(2):
            for dup in range(2):
                parts.append((lambda s, kv=kv, dup=dup: s[:, kv * 1024:(kv + 1) * 1024].rearrange("p (c n) -> p c n", c=8)[:, :, dup * 64:(dup + 1) * 64],
                              kview(w[:, 2048 + kv * 64:2048 + (kv + 1) * 64])))
        parts.append((lambda s: s[:, 2048:3072].rearrange("p (c n) -> p c n", c=8), kview(w[:, 2176:2304])))
        witem(parts)
        for i in range(2):
            witem([(lambda s: s[:, 0:4096].rearrange("p (c n) -> p c n", c=8), kview(wout_d[l][:, i * 512:(i + 1) * 512]))])

    sched_weights()

    DMA("sp", par[:], par_d, bpar, writes=[bpar])
    DMA("sp", cf[:], cf_d, bcf, writes=[bcf])
    DMA("pool", cb[:], cb_d, bcb, writes=[bcb])
    ident = cf[:, C_ID:C_ID + 128]
    ones_b = cb[:, B_ONES:B_ONES + 128]
    blk_b = cb[:, B_BLK:B_BLK + 128]
    ident_b = cb[:, B_ID:B_ID + 128]

    areset()
    xst = [aalloc(1024), aalloc(1024)]
    bxst = [nb("xst0"), nb("xst1")]
    for t in range(NT):
        s = t % 2
        DMA("sp", xst[s], xin[t * 128:(t + 1) * 128, :], bxst[s], writes=[bxst[s]])
        for half in range(2):
            ps, bp = nps()
            for k in range(4):
                c = half * 4 + k
                PE(lambda q, ps=ps, k=k, c=c, s=s: q.transpose(ps[:, k * 128:(k + 1) * 128], xst[s][:, c * 128:(c + 1) * 128], ident),
                   [bxst[s], bcf], [bp])
            g = t // 4
            eng = ACT if half == 0 else DVE
            fn = (lambda q, ps=ps, half=half, t=t: q.activation(out=xT[:, half * 4:half * 4 + 4, t * 128:(t + 1) * 128], in_=ps[:].rearrange("p (k n) -> p k n", k=4), func=AF.Copy)) if half == 0 else \
                 (lambda q, ps=ps, half=half, t=t: q.tensor_copy(out=xT[:, half * 4:half * 4 + 4, t * 128:(t + 1) * 128], in_=ps[:].rearrange("p (k n) -> p k n", k=4)))
            eng(fn, [bp], [bx[c][g] for c in range(half * 4, half * 4 + 4)])

    areset()
    sc = aalloc(16)
    scb = aalloc(8).bitcast(BF16)
    bsc = nb("sc")
    ACT(lambda q: q.activation(out=sc, in_=par[:, P_COND:P_COND + 16], func=AF.Exp, scale=-1.0), [bpar], [bsc])
    DVE(lambda q: q.tensor_scalar_add(out=sc, in0=sc, scalar1=1.0), [bsc], [bsc])
    DVE(lambda q: q.reciprocal(out=sc, in_=sc), [bsc], [bsc])
    DVE(lambda q: q.tensor_tensor(out=scb, in0=sc, in1=par[:, P_COND:P_COND + 16], op=ALU.mult), [bsc, bpar], [bsc])
    scb3 = scb.rearrange("p (c r) -> p c r", c=8)
    for l in range(2):
        ps, bp = nps()
        for i in range(18):
            sl, bs = wnext()
            wv = sl[:, 0:4096].rearrange("p (c n) -> p c n", c=8)
            for k in range(4):
                nn = i * 4 + k
                for c in range(8):
                    PE(lambda q, ps=ps, wv=wv, k=k, c=c, nn=nn: q.matmul(ps[:, nn * 2:nn * 2 + 2], lhsT=wv[:, c, k * 128:(k + 1) * 128], rhs=scb3[:, c, :], start=(c == 0), stop=(c == 7)),
                       [bs, bsc], [bp])
        DVE(lambda q, ps=ps, l=l: q.tensor_tensor(out=modv[:, l, :, :], in0=ps[:, 0:144].rearrange("p (n r) -> p n r", r=2),
                                                 in1=par[:, P_BADA + l * 72:P_BADA + (l + 1) * 72].unsqueeze(2).to_broadcast([128, 72, 2]), op=ALU.add),
            [bp, bpar], [bmod])
        for w3 in range(3):
            nrm = par[:, P_NRM + l * 24 + w3 * 8:P_NRM + l * 24 + w3 * 8 + 8].unsqueeze(2).to_broadcast([128, 8, 2])
            DVE(lambda q, l=l, w3=w3, nrm=nrm: q.scalar_tensor_tensor(out=mA[:, l, w3, :, :], in0=modv[:, l, (3 * w3 + 1) * 8:(3 * w3 + 2) * 8, :], scalar=1.0, in1=nrm, op0=ALU.add, op1=ALU.mult),
                [bmod, bpar], [bmod])
            DVE(lambda q, l=l, w3=w3: q.tensor_scalar_mul(out=mG[:, l, w3, :, :], in0=modv[:, l, (3 * w3 + 2) * 8:(3 * w3 + 3) * 8, :], scalar1=(1.0 if w3 == 1 else 0.5)),
                [bmod], [bmod])

    def mod_A(l, w3, c, g):
        r = 0 if g < 2 else 1
        return mA[:, l, w3, c, r:r + 1]

    def mod_B(l, w3, c, g):
        r = 0 if g < 2 else 1
        return modv[:, l, (3 * w3) * 8 + c, r:r + 1]

    def mod_G(l, w3, c, g):
        r = 0 if g < 2 else 1
        return mG[:, l, w3, c, r:r + 1]

    def norm_mod(l, w3):
        sq = [aalloc(256).bitcast(BF16), aalloc(256).bitcast(BF16)]
        bsq = [nb("sq0"), nb("sq1")]
        tmp = [aalloc(512), aalloc(512)]
        btmp = [nb("tmp0"), nb("tmp1")]
        rs = aalloc(512)
        brs = nb("rs")
        for g in range(NG):
            ps, bp = nps()
            for c in range(8):
                s = c % 2
                ACT(lambda q, s=s, c=c, g=g: q.activation(out=sq[s], in_=xT[:, c, g * 512:(g + 1) * 512], func=AF.Square), [bx[c][g]], [bsq[s]])
                PE(lambda q, ps=ps, s=s, c=c: q.matmul(ps[:], lhsT=ones_b, rhs=sq[s], start=(c == 0), stop=(c == 7)), [bsq[s], bcb], [bp])
            DVE(lambda q, ps=ps: q.tensor_scalar(out=rs, in0=ps[:], scalar1=1.0 / D, scalar2=EPS, op0=ALU.mult, op1=ALU.add), [bp], [brs])
            ACT(lambda q: q.activation(out=rs, in_=rs, func=AF.Ln), [brs], [brs])
            ACT(lambda q: q.activation(out=rs, in_=rs, func=AF.Exp, scale=-0.5), [brs], [brs])
            for c in range(8):
                s = c % 2
                DVE(lambda q, s=s, c=c, g=g: q.tensor_tensor(out=tmp[s], in0=xT[:, c, g * 512:(g + 1) * 512], in1=rs, op=ALU.mult), [bx[c][g], brs], [btmp[s]])
                ACT(lambda q, s=s, c=c, g=g: q.activation(out=hT[:, c, g * 512:(g + 1) * 512], in_=tmp[s], func=AF.Identity,
                                                          scale=mod_A(l, w3, c, g), bias=mod_B(l, w3, c, g)), [btmp[s], bmod], [bh[c][g]])

    def ffn(l, f):
        w3 = 0 if f == 0 else 2
        areset()
        norm_mod(l, w3)
        actT = aalloc(11 * TOK // 2).bitcast(BF16).rearrange("p (j n) -> p j n", j=11)
        bact = [nb(f"act{g}") for g in range(NG)]
        wd = aalloc(11 * 1024 // 2).bitcast(BF16).rearrange("p (j n) -> p j n", j=11)
        bwd = nb("wd")
        ebuf = [aalloc(512), aalloc(512)]
        bebuf = [nb("e0"), nb("e1")]
        tbuf = [aalloc(512), aalloc(512)]
        btbuf = [nb("t0"), nb("t1")]
        k = 0
        for h in range(2):
            DMA("pool", wd, wdn_d[f][l][h * 1408:(h + 1) * 1408, :].rearrange("(j p) n -> p j n", p=128), bwd, writes=[bwd])
            for j0 in range(0, 11, 2):
                nf = min(2, 11 - j0)
                sl, bs = wnext()
                wg = sl[:, 0:8 * nf * 128].rearrange("p (c n) -> p c n", c=8)
                wu = sl[:, 2048:2048 + 8 * nf * 128].rearrange("p (c n) -> p c n", c=8)
                for jj in range(nf):
                    j = j0 + jj
                    for g in range(NG):
                        pg, bpg = nps()
                        pu, bpu = nps()
                        for c in range(8):
                            PE(lambda q, pg=pg, wg=wg, c=c, jj=jj, g=g: q.matmul(pg[:], lhsT=wg[:, c, jj * 128:(jj + 1) * 128], rhs=hT[:, c, g * 512:(g + 1) * 512], start=(c == 0), stop=(c == 7)),
                               [bs, bh[c][g]], [bpg])
                        for c in range(8):
                            PE(lambda q, pu=pu, wu=wu, c=c, jj=jj, g=g: q.matmul(pu[:], lhsT=wu[:, c, jj * 128:(jj + 1) * 128], rhs=hT[:, c, g * 512:(g + 1) * 512], start=(c == 0), stop=(c == 7)),
                               [bs, bh[c][g]], [bpu])
                        s = k % 2
                        k += 1
                        ACT(lambda q, pg=pg, s=s: q.activation(out=tbuf[s], in_=pg[:], func=AF.Silu), [bpg], [btbuf[s]])
                        DVE(lambda q, pu=pu, s=s, j=j, g=g: q.tensor_tensor(out=actT[:, j, g * 512:(g + 1) * 512], in0=pu[:], in1=tbuf[s], op=ALU.mult), [bpu, btbuf[s]], [bact[g]])
            for g in range(NG):
                for dc in range(8):
                    ps, bp = nps()
                    for j in range(11):
                        PE(lambda q, ps=ps, j=j, dc=dc, g=g: q.matmul(ps[:], lhsT=wd[:, j, dc * 128:(dc + 1) * 128], rhs=actT[:, j, g * 512:(g + 1) * 512], start=(j == 0), stop=(j == 10)),
                           [bwd, bact[g]], [bp])
                    DVE(lambda q, ps=ps, dc=dc, g=g: q.scalar_tensor_tensor(out=xT[:, dc, g * 512:(g + 1) * 512], in0=ps[:], scalar=mod_G(l, w3, dc, g), in1=xT[:, dc, g * 512:(g + 1) * 512], op0=ALU.mult, op1=ALU.add),
                        [bp, bmod, bx[dc][g]], [bx[dc][g]])

    def mixer(l):
        areset()
        norm_mod(l, 1)
        areset()
        catT = aalloc(8 * TOK // 2).bitcast(BF16).rearrange("p (c n) -> p c n", c=8)
        bcat = [[nb(f"cat{c}_{g}") for g in range(NG)] for c in range(8)]
        base_off = state["aoff"]

        def inproj(wv, k, g, ps):
            for c in range(8):
                PE(lambda q, c=c: q.matmul(ps[0][:], lhsT=wv[:, c, k * 128:(k + 1) * 128], rhs=hT[:, c, g * 512:(g + 1) * 512], start=(c == 0), stop=(c == 7)),
                   [ps[2], bh[c][g]], [ps[1]])

        uT = aalloc(2 * TOK // 2).bitcast(BF16).rearrange("p (c n) -> p c n", c=2)
        buT = nb("uT")
        ucs = aalloc(NT * 2 * 256 // 2).bitcast(BF16).rearrange("p (t c n) -> p t c n", t=NT, c=2)
        bucs = nb("ucs")
        sl, bs = wnext()
        wv = sl[:, 0:2048].rearrange("p (c n) -> p c n", c=8)
        for k in range(2):
            for g in range(NG):
                ps, bp = nps()
                inproj(wv, k, g, (ps, bp, bs))
                ACT(lambda q, ps=ps, k=k, g=g: q.activation(out=uT[:, k, g * 512:(g + 1) * 512], in_=ps[:], func=AF.Copy), [bp], [buT])
        dft64 = cb[:, B_DFT:B_DFT + 256]
        for t in range(NT):
            ps, bp = nps()
            for k in range(2):
                PE(lambda q, ps=ps, k=k, t=t: q.matmul(ps[:, k * 256:(k + 1) * 256], lhsT=uT[:, k, t * 128:(t + 1) * 128], rhs=dft64, start=True, stop=True), [buT, bcb], [bp])
            DVE(lambda q, ps=ps, t=t: q.tensor_copy(out=ucs[:, t, :, :], in_=ps[:].rearrange("p (c n) -> p c n", c=2)), [bp], [bucs])
        for hf in range(2):
            slc, bsc_ = wnext()
            sls, bss = wnext(keep_prev=True)
            cl = slc[:, 0:4096].rearrange("p (c n) -> p c n", c=8)
            sn = sls[:, 0:4096].rearrange("p (c n) -> p c n", c=8)
            for k in range(2):
                ps, bp = nps()
                for lc in range(8):
                    PE(lambda q, ps=ps, k=k, lc=lc, cl=cl: q.matmul(ps[:], lhsT=ucs[:, lc, k, 0:128], rhs=cl[:, lc, :], start=(lc == 0), stop=False), [bucs, bsc_], [bp])
                    PE(lambda q, ps=ps, k=k, lc=lc, sn=sn: q.matmul(ps[:], lhsT=ucs[:, lc, k, 128:256], rhs=sn[:, lc, :], start=False, stop=(lc == 7)), [bucs, bss], [bp])
                ACT(lambda q, ps=ps, k=k, hf=hf: q.activation(out=catT[:, k, hf * 512:(hf + 1) * 512], in_=ps[:], func=AF.Copy), [bp], [bcat[k][hf]])
        slp, bsp = wnext()
        pc = slp[:, 0:512].rearrange("p (c n) -> p c n", c=2)
        pn = slp[:, 512:1024].rearrange("p (c n) -> p c n", c=2)
        for k in range(2):
            ps, bp = nps()
            for sq_ in range(2):
                for lc in range(2):
                    tt = 8 + sq_ * 2 + lc
                    PE(lambda q, ps=ps, k=k, lc=lc, tt=tt, sq_=sq_: q.matmul(ps[:, sq_ * 256:(sq_ + 1) * 256], lhsT=ucs[:, tt, k, 0:128], rhs=pc[:, lc, :], start=(lc == 0), stop=False), [bucs, bsp], [bp])
                    PE(lambda q, ps=ps, k=k, lc=lc, tt=tt, sq_=sq_: q.matmul(ps[:, sq_ * 256:(sq_ + 1) * 256], lhsT=ucs[:, tt, k, 128:256], rhs=pn[:, lc, :], start=False, stop=(lc == 1)), [bucs, bsp], [bp])
            ACT(lambda q, ps=ps, k=k: q.activation(out=catT[:, k, 1024:1536], in_=ps[:], func=AF.Copy), [bp], [bcat[k][2]])

        if ENABLE_HGRN:
            CH_ = 32
            for pr in range(2):
                if DEBUG_PAIR is not None and pr != DEBUG_PAIR:
                    wnext(); wnext()
                    for g in range(NG):
                        POOL(lambda q, pr=pr, g=g: q.memset(catT[:, 2 + pr, g * 512:(g + 1) * 512], 0.0), [], [bcat[2 + pr][g]])
                    continue
                state["aoff"] = base_off
                qf, sF, sB, tmpb, kb, hoT = [aalloc(TOK) for _ in range(6)]
                bqf, bsF, bsB, btmpb, bkb, bho = [nb(n) for n in ("qf", "sF", "sB", "tmpb", "kb", "hoT")]
                qt = [aalloc(TOK // 2).bitcast(BF16) for _ in range(2)]
                kt_ = [aalloc(TOK // 2).bitcast(BF16) for _ in range(2)]
                vT = aalloc(TOK // 2).bitcast(BF16)
                sgT = aalloc(TOK // 2).bitcast(BF16)
                bqt, bkt = [nb("qt0"), nb("qt1")], [nb("kt0"), nb("kt1")]
                bvT, bsg = nb("vT"), nb("sgT")
                ktp = [aalloc(128).bitcast(BF16).rearrange("p (a n) -> p a n", a=2) for _ in range(4)]
                vpd = [aalloc(128).bitcast(BF16).rearrange("p (a n) -> p a n", a=2) for _ in range(4)]
                bktp, bvpd = [nb("ktp") for _ in range(4)], [nb("vpd") for _ in range(4)]

                def diag(t):
                    x = t[0:CH_, 0, 0:64]
                    return bass.AP(tensor=x.tensor, offset=x.offset, ap=[list(x.ap[0]), [192, 2], [1, 64]])
                atb = [aalloc(64).bitcast(BF16) for _ in range(3)]
                batb = [nb("at") for _ in range(3)]
                S = aalloc(128); tS = aalloc(128); S0m = aalloc(64).bitcast(BF16)
                bS, btS, bS0m = nb("S"), nb("tS"), nb("S0m")
                sst = [aalloc(128), aalloc(128)]
                bsst = [nb("sst0"), nb("sst1")]
                scl = [[aalloc(48) for _ in range(4)] for _ in range(2)]
                bscl = [nb("scl0"), nb("scl1")]
                lbp = aalloc(8)
                blbp = nb("lbp")
                for i in range(4):
                    POOL(lambda q, i=i: q.memset(ktp[i], 0.0), [], [bktp[i]])
                    POOL(lambda q, i=i: q.memset(vpd[i], 0.0), [], [bvpd[i]])
                for dr in range(2):
                    c0 = dr * 3
                    if l == 0:
                        POOL(lambda q, c0=c0: q.memset(lbp[:, c0:c0 + 1], 1.0), [], [blbp])
                        POOL(lambda q, c0=c0: q.memset(lbp[:, c0 + 1:c0 + 2], 1e-30), [], [blbp])
                        POOL(lambda q, c0=c0: q.memset(lbp[:, c0 + 2:c0 + 3], -1.0), [], [blbp])
                    else:
                        a0 = P_LBS + 0 * 4 + dr * 2 + pr
                        a1 = P_LBS + 1 * 4 + dr * 2 + pr
                        DVE(lambda q, c0=c0, a0=a0, a1=a1: q.tensor_tensor(out=lbp[:, c0 + 1:c0 + 2], in0=par[:, a0:a0 + 1], in1=par[:, a1:a1 + 1], op=ALU.subtract), [bpar], [blbp])
                        ACT(lambda q, c0=c0: q.activation(out=lbp[:, c0 + 1:c0 + 2], in_=lbp[:, c0 + 1:c0 + 2], func=AF.Exp), [blbp], [blbp])
                        DVE(lambda q, c0=c0: q.tensor_scalar_add(out=lbp[:, c0 + 1:c0 + 2], in0=lbp[:, c0 + 1:c0 + 2], scalar1=1.0), [blbp], [blbp])
                        DVE(lambda q, c0=c0: q.reciprocal(out=lbp[:, c0 + 1:c0 + 2], in_=lbp[:, c0 + 1:c0 + 2]), [blbp], [blbp])
                        DVE(lambda q, c0=c0: q.tensor_scalar(out=lbp[:, c0:c0 + 1], in0=lbp[:, c0 + 1:c0 + 2], scalar1=-1.0, scalar2=1.0, op0=ALU.mult, op1=ALU.add), [blbp], [blbp])
                        DVE(lambda q, c0=c0: q.tensor_scalar_mul(out=lbp[:, c0 + 2:c0 + 3], in0=lbp[:, c0:c0 + 1], scalar1=-1.0), [blbp], [blbp])
                        DVE(lambda q, c0=c0: q.tensor_scalar_max(out=lbp[:, c0 + 1:c0 + 2], in0=lbp[:, c0 + 1:c0 + 2], scalar1=1e-30), [blbp], [blbp])
                sl, bs = wnext()
                for k in range(3):
                    wv = sl[:, k * 1024:(k + 1) * 1024].rearrange("p (c n) -> p c n", c=8)
                    for g in range(NG):
                        ps, bp = nps()
                        inproj(wv, 0, g, (ps, bp, bs))
                        gs = slice(g * 512, (g + 1) * 512)
                        if k == 0:
                            ACT(lambda q, ps=ps, gs=gs: q.activation(out=qf[:, gs], in_=ps[:], func=AF.Copy, scale=0.125), [bp], [bqf])
                        elif k == 1:
                            ACT(lambda q, ps=ps, gs=gs: q.activation(out=vT[:, gs], in_=ps[:], func=AF.Copy), [bp], [bvT])
                        else:
                            ACT(lambda q, ps=ps, gs=gs: q.activation(out=sF[:, gs], in_=ps[:], func=AF.Exp, scale=-1.0), [bp], [bsF])
                sl, bs = wnext()
                for k in range(2):
                    wv = sl[:, k * 1024:(k + 1) * 1024].rearrange("p (c n) -> p c n", c=8)
                    for g in range(NG):
                        ps, bp = nps()
                        inproj(wv, 0, g, (ps, bp, bs))
                        gs = slice(g * 512, (g + 1) * 512)
                        if k == 0:
                            ACT(lambda q, ps=ps, gs=gs: q.activation(out=sB[:, gs], in_=ps[:], func=AF.Exp, scale=-1.0), [bp], [bsB])
                        else:
                            ACT(lambda q, ps=ps, gs=gs: q.activation(out=tmpb[:, gs], in_=ps[:], func=AF.Exp, scale=-1.0), [bp], [btmpb])
                            DVE(lambda q, gs=gs: q.tensor_scalar_add(out=tmpb[:, gs], in0=tmpb[:, gs], scalar1=1.0), [btmpb], [btmpb])
                            DVE(lambda q, gs=gs: q.reciprocal(out=tmpb[:, gs], in_=tmpb[:, gs]), [btmpb], [btmpb])
                            DVE(lambda q, ps=ps, gs=gs: q.tensor_tensor(out=sgT[:, gs], in0=ps[:], in1=tmpb[:, gs], op=ALU.mult), [bp, btmpb], [bsg])
                for (sx, bsx) in ((sF, bsF), (sB, bsB)):
                    DVE(lambda q, sx=sx: q.tensor_scalar_add(out=sx, in0=sx, scalar1=1.0), [bsx], [bsx])
                    DVE(lambda q, sx=sx: q.reciprocal(out=sx, in_=sx), [bsx], [bsx])
                CH = 32
                NCH = TOK // CH
                SPC = 256 // CH
                for dr, (sx, bsx) in enumerate(((sF, bsF), (sB, bsB))):
                    c0 = dr * 3
                    DVE(lambda q, sx=sx, c0=c0: q.tensor_scalar(out=kb, in0=sx, scalar1=lbp[:, c0 + 2:c0 + 3], scalar2=lbp[:, c0:c0 + 1], op0=ALU.mult, op1=ALU.add), [bsx, blbp], [bkb])
                    DVE(lambda q, sx=sx, c0=c0: q.tensor_scalar(out=sx, in0=sx, scalar1=lbp[:, c0:c0 + 1], scalar2=lbp[:, c0 + 1:c0 + 2], op0=ALU.mult, op1=ALU.add), [bsx, blbp], [bsx])
                    ACT(lambda q, sx=sx: q.activation(out=sx, in_=sx, func=AF.Ln), [bsx], [bsx])
                    A_, bA_, B_, bB_ = sx, bsx, tmpb, btmpb
                    sh = 1
                    while sh < CH:
                        A3 = A_.rearrange("p (c n) -> p c n", n=CH)
                        B3 = B_.rearrange("p (c n) -> p c n", n=CH)
                        if dr == 0:
                            ACT(lambda q, A3=A3, B3=B3, sh=sh: q.activation(out=B3[:, :, 0:sh], in_=A3[:, :, 0:sh], func=AF.Copy), [bA_], [bB_])
                            DVE(lambda q, A3=A3, B3=B3, sh=sh: q.tensor_tensor(out=B3[:, :, sh:CH], in0=A3[:, :, sh:CH], in1=A3[:, :, 0:CH - sh], op=ALU.add), [bA_], [bB_])
                        else:
                            ACT(lambda q, A3=A3, B3=B3, sh=sh: q.activation(out=B3[:, :, CH - sh:CH], in_=A3[:, :, CH - sh:CH], func=AF.Copy), [bA_], [bB_])
                            DVE(lambda q, A3=A3, B3=B3, sh=sh: q.tensor_tensor(out=B3[:, :, 0:CH - sh], in0=A3[:, :, 0:CH - sh], in1=A3[:, :, sh:CH], op=ALU.add), [bA_], [bB_])
                        A_, bA_, B_, bB_ = B_, bB_, A_, bA_
                        sh *= 2
                    bq_, bbq_, tq_, btq_ = A_, bA_, B_, bB_
                    b3 = bq_.rearrange("p (c n) -> p c n", n=CH)
                    mid, Ti = (CH // 2 - 1, CH - 1) if dr == 0 else (CH // 2, 0)
                    bm, em, eT, eTm = scl[dr]
                    DVE(lambda q, b3=b3, mid=mid, bm=bm: q.tensor_copy(out=bm, in_=b3[:, :, mid]), [bbq_], [bscl[dr]])
                    ACT(lambda q, bm=bm, em=em: q.activation(out=em, in_=bm, func=AF.Exp), [bscl[dr]], [bscl[dr]])
                    ACT(lambda q, b3=b3, Ti=Ti, eT=eT: q.activation(out=eT, in_=b3[:, :, Ti], func=AF.Exp), [bbq_], [bscl[dr]])
                    DVE(lambda q, b3=b3, Ti=Ti, bm=bm, eTm=eTm: q.tensor_tensor(out=eTm, in0=b3[:, :, Ti], in1=bm, op=ALU.subtract), [bbq_, bscl[dr]], [bscl[dr]])
                    ACT(lambda q, eTm=eTm: q.activation(out=eTm, in_=eTm, func=AF.Exp), [bscl[dr]], [bscl[dr]])
                    DVE(lambda q, b3=b3, bm=bm: q.tensor_tensor(out=b3, in0=b3, in1=bm.unsqueeze(2).to_broadcast([128, NCH, CH]), op=ALU.subtract), [bbq_, bscl[dr]], [bbq_])
                    ACT(lambda q, bq_=bq_, tq_=tq_: q.activation(out=tq_, in_=bq_, func=AF.Exp), [bbq_], [btq_])
                    DVE(lambda q, dr=dr, tq_=tq_: q.tensor_tensor(out=qt[dr], in0=qf, in1=tq_, op=ALU.mult), [bqf, btq_], [bqt[dr]])
                    ACT(lambda q, bq_=bq_, tq_=tq_: q.activation(out=tq_, in_=bq_, func=AF.Exp, scale=-1.0), [bbq_], [btq_])
                    DVE(lambda q, dr=dr, tq_=tq_: q.tensor_tensor(out=kt_[dr], in0=kb, in1=tq_, op=ALU.mult), [bkb, btq_], [bkt[dr]])
                ev = [0]

                def emit_state(slot, dr):
                    i = ev[0] % 2
                    ev[0] += 1
                    ACT(lambda q, i=i: q.activation(out=sst[i], in_=S, func=AF.Copy), [bS], [bsst[i]])
                    for hd in range(2):
                        DMA("sp", sout[l, slot, dr, 2 * pr + hd], sst[i][hd * 64:(hd + 1) * 64, hd * 64:(hd + 1) * 64], bsst[i], reads=[bsst[i]])

                def load_state(dr):
                    POOL(lambda q: q.memset(S, 0.0), [bS], [bS])
                    for hd in range(2):
                        DMA("sp", S[hd * 64:(hd + 1) * 64, hd * 64:(hd + 1) * 64], s0_d[l, dr, 2 * pr + hd], bS, reads=[bS], writes=[bS])

                Xb = [aalloc(128), aalloc(128), aalloc(128)]
                hoB = aalloc(TOK // 2).bitcast(BF16)
                bhoB = nb("hoB")
                bXb = [nb("X") for _ in range(3)]
                for dr in range(2):
                    bm, em, eT, eTm = scl[dr]
                    order = list(range(NCH)) if dr == 0 else list(range(NCH - 1, -1, -1))
                    mcol = C_MF if dr == 0 else C_MB
                    pst = {}

                    live = {}

                    def A_pe(n_, dr=dr, order=order, live=live):
                        ch = order[n_]
                        cs = slice(ch * CH, (ch + 1) * CH)
                        ptk, bptk = nps()
                        ptkb = ptk[:].bitcast(BF16)
                        live[("ptk", n_)] = (ptkb, bptk)
                        PE(lambda q: q.transpose(ptkb[0:CH, 0:128], kt_[dr][:, cs], ident_b), [bkt[dr], bcb], [bptk])
                        PE(lambda q: q.transpose(ptkb[0:CH, 128:256], vT[:, cs], ident_b), [bvT, bcb], [bptk])

                    def A_post(n_, live=live):
                        i3 = n_ % 4
                        ptkb, bptk = live.pop(("ptk", n_))
                        ACT(lambda q: q.activation(out=diag(ktp[i3]), in_=ptkb[0:CH, 0:128].rearrange("p (a n) -> p a n", a=2), func=AF.Copy), [bptk], [bktp[i3]])
                        ACT(lambda q: q.activation(out=diag(vpd[i3]), in_=ptkb[0:CH, 128:256].rearrange("p (a n) -> p a n", a=2), func=AF.Copy), [bptk], [bvpd[i3]])

                    def B_pe(n_, dr=dr, order=order, live=live):
                        ch = order[n_]
                        i3 = n_ % 4
                        cs = slice(ch * CH, (ch + 1) * CH)
                        pas = []
                        for hd in range(2):
                            hs = slice(hd * 64, (hd + 1) * 64)
                            pA, bpA = nps()
                            pas.append((pA, bpA))
                            PE(lambda q, pA=pA, hs=hs: q.matmul(pA[0:CH, 0:CH], lhsT=kt_[dr][hs, cs], rhs=qt[dr][hs, cs], start=True, stop=True), [bkt[dr], bqt[dr]], [bpA])
                        pS2, bpS2 = nps()
                        PE(lambda q: q.matmul(pS2[:, 0:128], lhsT=ktp[i3][0:CH, 0, :], rhs=vpd[i3][0:CH, 0, :], start=True, stop=False), [bktp[i3], bvpd[i3]], [bpS2])
                        PE(lambda q: q.matmul(pS2[:, 0:128], lhsT=ktp[i3][0:CH, 1, :], rhs=vpd[i3][0:CH, 1, :], start=False, stop=True), [bktp[i3], bvpd[i3]], [bpS2])
                        live[("B", n_)] = (pas, pS2, bpS2)

                    def B_post(n_, dr=dr, eTm=eTm, mcol=mcol, order=order, live=live):
                        ch = order[n_]
                        i = n_ % 3
                        pas, pS2, bpS2 = live.pop(("B", n_))
                        for hd, (pA, bpA) in enumerate(pas):
                            DVE(lambda q, pA=pA, hd=hd: q.tensor_tensor(out=atb[i][0:CH, hd * CH:(hd + 1) * CH], in0=pA[0:CH, 0:CH], in1=cf[0:CH, mcol:mcol + CH], op=ALU.mult), [bpA, bcf], [batb[i]])
                        DVE(lambda q: q.tensor_scalar_mul(out=Xb[i], in0=pS2[:, 0:128], scalar1=eTm[:, ch:ch + 1]), [bpS2, bscl[dr]], [bXb[i]])

                    def C_dve(n_, dr=dr, em=em, eT=eT, order=order):
                        ch = order[n_]
                        i = n_ % 3
                        first_of_seq = (ch % SPC == 0) if dr == 0 else (ch % SPC == SPC - 1)
                        if first_of_seq:
                            seq = ch // SPC
                            prev = seq - 1 if dr == 0 else seq + 1
                            if n_ > 0:
                                emit_state(prev, dr)
                            if (dr == 0 and seq == 0) or (dr == 1 and seq == 3):
                                load_state(dr)
                            elif seq >= 4:
                                POOL(lambda q: q.memset(S, 0.0), [bS], [bS])
                            else:
                                DVE(lambda q: q.tensor_scalar_mul(out=S, in0=S, scalar1=par[:, P_KEEP:P_KEEP + 1]), [bS, bpar], [bS])
                        DVE(lambda q: q.tensor_scalar_mul(out=S0m, in0=S, scalar1=em[:, ch:ch + 1]), [bS, bscl[dr]], [bS0m])
                        DVE(lambda q: q.scalar_tensor_tensor(out=S, in0=S, scalar=eT[:, ch:ch + 1], in1=Xb[i], op0=ALU.mult, op1=ALU.add), [bS, bscl[dr], bXb[i]], [bS])

                    def C_pe(n_, dr=dr, order=order):
                        ch = order[n_]
                        i = n_ % 3
                        i3 = n_ % 4
                        cs = slice(ch * CH, (ch + 1) * CH)
                        po, bpo = nps()
                        PE(lambda q: q.matmul(po[:, 0:CH], lhsT=vpd[i3][0:CH, 0, :], rhs=atb[i][0:CH, 0:CH], start=True, stop=False), [bvpd[i3], batb[i]], [bpo])
                        PE(lambda q: q.matmul(po[:, 0:CH], lhsT=vpd[i3][0:CH, 1, :], rhs=atb[i][0:CH, CH:2 * CH], start=False, stop=False), [bvpd[i3], batb[i]], [bpo])
                        PE(lambda q: q.matmul(po[:, 0:CH], lhsT=S0m, rhs=qt[dr][:, cs], start=False, stop=True), [bS0m, bqt[dr]], [bpo])
                        if dr == 0:
                            ACT(lambda q: q.activation(out=hoT[:, cs], in_=po[:, 0:CH], func=AF.Copy), [bpo], [bho])
                        else:
                            ACT(lambda q: q.activation(out=hoB[:, cs], in_=po[:, 0:CH], func=AF.Copy), [bpo], [bhoB])

                    for k_ in range(3):
                        A_pe(k_); A_post(k_)
                    for k_ in range(2):
                        B_pe(k_); B_post(k_)
                    for n_ in range(NCH):
                        C_dve(n_)
                        if n_ + 3 < NCH:
                            A_pe(n_ + 3)
                        if n_ + 2 < NCH:
                            B_pe(n_ + 2)
                        C_pe(n_)
                        if n_ + 3 < NCH:
                            A_post(n_ + 3)
                        if n_ + 2 < NCH:
                            B_post(n_ + 2)
                    emit_state(0 if dr == 1 else 5, dr)
                DVE(lambda q: q.tensor_tensor(out=hoT, in0=hoT, in1=hoB, op=ALU.add), [bho, bhoB], [bho])
                for g in range(NG):
                    gs = slice(g * 512, (g + 1) * 512)
                    sqh = tmpb[:, 0:256].bitcast(BF16)
                    ACT(lambda q, gs=gs, sqh=sqh: q.activation(out=sqh, in_=hoT[:, gs], func=AF.Square), [bho], [btmpb])
                    p2, bp2 = nps()
                    PE(lambda q, p2=p2, sqh=sqh: q.matmul(p2[:], lhsT=blk_b, rhs=sqh, start=True, stop=True), [btmpb, bcb], [bp2])
                    rv = kb[:, 0:512]
                    DVE(lambda q, p2=p2, rv=rv: q.tensor_scalar(out=rv, in0=p2[:], scalar1=1.0 / 64, scalar2=EPS, op0=ALU.mult, op1=ALU.add), [bp2], [bkb])
                    ACT(lambda q, rv=rv: q.activation(out=rv, in_=rv, func=AF.Ln), [bkb], [bkb])
                    ACT(lambda q, rv=rv: q.activation(out=rv, in_=rv, func=AF.Exp, scale=-0.5), [bkb], [bkb])
                    DVE(lambda q, rv=rv, gs=gs, pr=pr: q.scalar_tensor_tensor(out=rv, in0=hoT[:, gs], scalar=par[:, P_HNRM + l * 2 + pr:P_HNRM + l * 2 + pr + 1], in1=rv, op0=ALU.mult, op1=ALU.mult), [bho, bkb, bpar], [bkb])
                    DVE(lambda q, rv=rv, gs=gs, g=g, pr=pr: q.tensor_tensor(out=catT[:, 2 + pr, gs], in0=rv, in1=sgT[:, gs], op=ALU.mult), [bkb, bsg], [bcat[2 + pr][g]])
        else:
            for pr in range(2):
                wnext()
                wnext()
                for g in range(NG):
                    POOL(lambda q, pr=pr, g=g: q.memset(catT[:, 2 + pr, g * 512:(g + 1) * 512], 0.0), [], [bcat[2 + pr][g]])
        if ENABLE_ATTN:
            state["aoff"] = base_off
            qT = aalloc(4 * TOK // 2).bitcast(BF16).rearrange("p (c n) -> p c n", c=4)
            kdT = aalloc(4 * TOK // 2).bitcast(BF16).rearrange("p (c v n) -> p c v n", c=2, v=2)
            kcT = aalloc(4 * 512 // 2).bitcast(BF16).rearrange("p (c v n) -> p c v n", c=2, v=2)
            vaug = aalloc(16 * 2 * 96 // 2).bitcast(BF16).rearrange("p (t k n) -> p t k n", t=16, k=2)
            kst = aalloc(NT * 128).rearrange("p (t n) -> p t n", t=NT)
            vst = aalloc(NT * 128).rearrange("p (t n) -> p t n", t=NT)
            kcd = aalloc(4 * 256).rearrange("p (t k n) -> p t k n", t=4, k=2)
            zq = [aalloc(512), aalloc(512)]
            rr = [aalloc(512), aalloc(512)]
            t1 = [aalloc(512)] * 2
            sqa = [aalloc(256).bitcast(BF16), aalloc(256).bitcast(BF16)]
            pT = [aalloc(256).bitcast(BF16) for _ in range(3)]
            otok = [aalloc(256).bitcast(BF16).rearrange("p (h n) -> p h n", h=8), aalloc(256).bitcast(BF16).rearrange("p (h n) -> p h n", h=8)]
            den = [aalloc(4), aalloc(4)]
            esink = aalloc(8)
            bq = [[nb("q") for g in range(NG)] for c in range(4)]
            bkd = [[nb("kd") for g in range(NG)] for c in range(2)]
            bkc, bva, bkst, bvst, bkcd, besk = nb("kc"), [nb("va") for t in range(16)], nb("kst"), nb("vst"), nb("kcd"), nb("esk")
            bzq, brr, bt1, bsqa, bpT, botok, bden = [[nb(n + str(i)) for i in range(3)] for n in ("zq", "rr", "t1", "sqa", "pT", "otok", "den")]
            bt1[1] = bt1[0]
            ACT(lambda q: q.activation(out=esink, in_=par[:, P_SINK + l * 8:P_SINK + l * 8 + 8], func=AF.Exp), [bpar], [besk])
            POOL(lambda q: q.memset(vaug[:, :, :, 64:66], 1.0), [], list(bva))
            allkd = [bkd[c_][g_] for c_ in range(2) for g_ in range(NG)]
            POOL(lambda q: q.memset(kdT[64:128, :, 0, :], 0.0), [], allkd)
            POOL(lambda q: q.memset(kdT[0:64, :, 1, :], 0.0), [], allkd)
            POOL(lambda q: q.memset(kcT[64:128, :, 0, :], 0.0), [], [bkc])
            POOL(lambda q: q.memset(kcT[0:64, :, 1, :], 0.0), [], [bkc])
            for kv in (range(2) if "cachek" not in SKIP else []):
                for dup in range(2):
                    DMA("sp", kcd[:, :, kv, dup * 64:(dup + 1) * 64], ck_d[l][:, kv * 64:(kv + 1) * 64].rearrange("(t p) d -> p t d", p=128), bkcd, writes=[bkcd])
            for t in (range(4) if "cachev" not in SKIP else []):
                DMA("pool", vaug[:, 12 + t, :, 0:64], cv_d[l][t * 128:(t + 1) * 128, :].rearrange("p (k d) -> p k d", k=2), bva[12 + t], writes=[bva[12 + t]])
            for kv in (range(2) if "cachek" not in SKIP else []):
                ps, bp = nps()
                for t in range(4):
                    PE(lambda q, ps=ps, t=t, kv=kv: q.transpose(ps[:, t * 128:(t + 1) * 128], kcd[:, t, kv, :], ident), [bkcd, bcf], [bp])
                ACT(lambda q, ps=ps, kv=kv: q.activation(out=kcT[0:64, kv, 0, :], in_=ps[0:64, :], func=AF.Copy), [bp], [bkc])
                ACT(lambda q, ps=ps, kv=kv: q.activation(out=kcT[64:128, kv, 1, :], in_=ps[64:128, :], func=AF.Copy), [bp], [bkc])
            cnt = [0]

            def qk_unit(s, wv, kcol, bs, g, nw_col, dst, bdst, scale_extra, kfin=None):
                ps, bp = nps()
                inproj(wv, kcol, g, (ps, bp, bs))
                yield
                ACT(lambda q: q.activation(out=zq[s], in_=ps[:], func=AF.Copy), [bp], [bzq[s]])
                ACT(lambda q: q.activation(out=sqa[s], in_=ps[:], func=AF.Square), [bp], [bsqa[s]])
                yield
                p2, bp2 = nps()
                PE(lambda q: q.matmul(p2[:], lhsT=blk_b, rhs=sqa[s], start=True, stop=True), [bsqa[s], bcb], [bp2])
                yield
                DVE(lambda q: q.tensor_scalar(out=rr[s], in0=p2[:], scalar1=1.0 / 64, scalar2=EPS, op0=ALU.mult, op1=ALU.add), [bp2], [brr[s]])
                ACT(lambda q: q.activation(out=rr[s], in_=rr[s], func=AF.Ln), [brr[s]], [brr[s]])
                ACT(lambda q: q.activation(out=rr[s], in_=rr[s], func=AF.Exp, scale=-0.5), [brr[s]], [brr[s]])
                yield
                DVE(lambda q: q.scalar_tensor_tensor(out=zq[s], in0=zq[s], scalar=par[:, nw_col:nw_col + 1], in1=rr[s], op0=ALU.mult, op1=ALU.mult), [bzq[s], brr[s], bpar], [bzq[s]])
                yield
                if g < 2:
                    p3, bp3 = nps()
                    PE(lambda q: q.matmul(p3[:], lhsT=cf[:, C_PERM:C_PERM + 128], rhs=zq[s], start=True, stop=True), [bzq[s], bcf], [bp3])
                    yield
                    DVE(lambda q: q.tensor_tensor(out=t1[s], in0=p3[:], in1=cf[:, C_SIN + g * 512:C_SIN + (g + 1) * 512], op=ALU.mult), [bp3, bcf], [bt1[s]])
                    DVE(lambda q: q.tensor_tensor(out=zq[s], in0=zq[s], in1=cf[:, C_COS + g * 512:C_COS + (g + 1) * 512], op=ALU.mult), [bzq[s], bcf], [bzq[s]])
                    DVE(lambda q: q.tensor_tensor(out=zq[s], in0=zq[s], in1=t1[s], op=ALU.add), [bzq[s], bt1[s]], [bzq[s]])
                    yield
                if kfin is None:
                    ACT(lambda q: q.activation(out=dst, in_=zq[s], func=AF.Copy, scale=scale_extra), [bzq[s]], [bdst])
                else:
                    ACT(lambda q: q.activation(out=dst[0][0:64, :], in_=zq[s][0:64, :], func=AF.Copy), [bzq[s]], [bdst])
                    ACT(lambda q: q.activation(out=dst[1][64:128, :], in_=zq[s][64:128, :], func=AF.Copy), [bzq[s]], [bdst])
                    kv = kfin
                    p4, bp4 = nps()
                    for tt in range(4):
                        PE(lambda q, tt=tt: q.transpose(p4[:, tt * 64:(tt + 1) * 64], zq[s][0:64, tt * 128:(tt + 1) * 128], ident[0:64, 0:64]), [bzq[s], bcf], [bp4])
                    yield
                    DVE(lambda q: q.tensor_copy(out=kst[:, g * 4:(g + 1) * 4, kv * 64:(kv + 1) * 64], in_=p4[:, 0:256].rearrange("p (t n) -> p t n", t=4)), [bp4], [bkst])

            def run_interleaved(specs, width=2):
                specs = list(specs)
                free = list(range(width))
                active = []
                while specs or active:
                    while specs and free:
                        sidx = free.pop(0)
                        active.append((sidx, specs.pop(0)(sidx)))
                    for item in list(active):
                        sidx, gen = item
                        try:
                            next(gen)
                        except StopIteration:
                            active.remove(item)
                            free.append(sidx)

            sl, bs = wnext()
            wv = sl[:, 0:4096].rearrange("p (c n) -> p c n", c=8)
            run_interleaved([(lambda s_, c=c, g=g, wv=wv, bs=bs: qk_unit(s_, wv, c, bs, g, P_QN + l, qT[:, c, g * 512:(g + 1) * 512], bq[c][g], 0.125))
                             for c in range(4) for g in range(NG)])
            sl, bs = wnext()
            run_interleaved([(lambda s_, kv=kv, g=g, sl=sl, bs=bs: qk_unit(s_, sl[:, kv * 1024:(kv + 1) * 1024].rearrange("p (c n) -> p c n", c=8), 0, bs, g, P_KN + l,
                                                                          (kdT[:, kv, 0, g * 512:(g + 1) * 512], kdT[:, kv, 1, g * 512:(g + 1) * 512]), bkd[kv][g], 1.0, kfin=kv))
                             for kv in range(2) for g in range(NG)])
            wv = sl[:, 2048:3072].rearrange("p (c n) -> p c n", c=8)
            for g in range(NG):
                ps, bp = nps()
                inproj(wv, 0, g, (ps, bp, bs))
                s = cnt[0] % 2
                cnt[0] += 1
                ACT(lambda q, ps=ps, s=s: q.activation(out=zq[s], in_=ps[:], func=AF.Copy), [bp], [bzq[s]])
                if "vtr" in SKIP:
                    continue
                p4, bp4 = nps()
                for tt in range(4):
                    PE(lambda q, tt=tt, s=s, p4=p4: q.transpose(p4[:, tt * 128:(tt + 1) * 128], zq[s][:, tt * 128:(tt + 1) * 128], ident), [bzq[s], bcf], [bp4])
                if "vtr_dve" not in SKIP:
                    DVE(lambda q, p4=p4, g=g: q.tensor_copy(out=vst[:, g * 4:(g + 1) * 4, :], in_=p4[:].rearrange("p (t n) -> p t n", t=4)), [bp4], [bvst])
                for tt in (range(4) if "vtr_act" not in SKIP else []):
                    ACT(lambda q, p4=p4, g=g, tt=tt: q.activation(out=vaug[:, g * 4 + tt, :, 0:64], in_=p4[:, tt * 128:(tt + 1) * 128].rearrange("p (k d) -> p k d", k=2), func=AF.Copy), [bp4], [bva[g * 4 + tt]])
            if "kvout" not in SKIP:
                DMA("sp", kout[l].rearrange("(t p) n -> p t n", p=128), kst, bkst, reads=[bkst])
            if "kvout" not in SKIP:
                DMA("sp", vout[l].rearrange("(t p) n -> p t n", p=128), vst, bvst, reads=[bvst])
            it = 0
            if not ATTN_CORE:
                for c in range(4):
                    for g in range(NG):
                        POOL(lambda q, c=c, g=g: q.memset(catT[:, 4 + c, g * 512:(g + 1) * 512], 0.0), [], [bcat[4 + c][g]])
            for j in (range(NT) if ATTN_CORE else []):
                g = j // 4
                if j < 8:
                    kts = [("n", j + d_, (j * 3 + 1 + d_) if d_ != 0 else None) for d_ in (-1, 0, 1) if 0 <= j + d_ < 8] + [("c", t, None) for t in range(4)]
                else:
                    b0 = 8 + 2 * ((j - 8) // 2)
                    kts = [("n", b0, None), ("n", b0 + 1, None)]
                so = j % 2
                for kv in range(2):
                    po, bpo = nps()
                    pov = po[:, 0:264].rearrange("p (h n) -> p h n", h=4)
                    nk = len(kts)

                    def scores(ki, j=j, kv=kv, g=g):
                        kind, kt, mi = kts[ki]
                        pS, bpS = nps()
                        for hh in range(4):
                            cq = kv * 2 + hh // 2
                            hf = hh % 2
                            if kind == "n":
                                lhs = kdT[:, kv, hf, kt * 128:(kt + 1) * 128]
                                rb = bkd[kv][kt // 4]
                            else:
                                lhs = kcT[:, kv, hf, kt * 128:(kt + 1) * 128]
                                rb = bkc
                            PE(lambda q, pS=pS, hh=hh, lhs=lhs, cq=cq, j=j: q.matmul(pS[:, hh * 128:(hh + 1) * 128], lhsT=lhs, rhs=qT[:, cq, j * 128:(j + 1) * 128], start=True, stop=True),
                               [rb, bq[cq][g]], [bpS])
                        return pS, bpS

                    LOOK = 2
                    sc = {}
                    for ki in range(min(LOOK, nk)):
                        sc[ki] = scores(ki)
                    for ki, (kind, kt, mi) in enumerate(kts):
                        pS, bpS = sc.pop(ki)
                        s = it % 3
                        it += 1
                        if kind == "c":
                            ACT(lambda q, pS=pS, s=s: q.activation(out=pT[s], in_=pS[:], func=AF.Exp, bias=par[:, P_CBIAS:P_CBIAS + 1]), [bpS, bpar], [bpT[s]])
                        else:
                            ACT(lambda q, pS=pS, s=s: q.activation(out=pT[s], in_=pS[:], func=AF.Exp), [bpS], [bpT[s]])
                        if mi is not None:
                            DVE(lambda q, s=s, mi=mi: q.tensor_tensor(out=pT[s].rearrange("p (h n) -> p h n", h=4), in0=pT[s].rearrange("p (h n) -> p h n", h=4),
                                                                      in1=cb[:, B_MASK + mi * 128:B_MASK + (mi + 1) * 128].unsqueeze(1).to_broadcast([128, 4, 128]), op=ALU.mult), [bpT[s], bcb], [bpT[s]])
                        if ki + LOOK < nk:
                            sc[ki + LOOK] = scores(ki + LOOK)
                        vt = kt if kind == "n" else 12 + kt
                        for hh in range(4):
                            PE(lambda q, pov=pov, hh=hh, s=s, vt=vt, kv=kv, ki=ki, n=nk: q.matmul(pov[:, hh, 0:66], lhsT=pT[s][:, hh * 128:(hh + 1) * 128], rhs=vaug[:, vt, kv, 0:66], start=(ki == 0 and hh == 0), stop=(ki == n - 1 and hh == 3)),
                               [bpT[s], bva[vt]], [bpo])
                    DVE(lambda q, pov=pov, so=so, kv=kv: q.tensor_tensor(out=den[so], in0=pov[:, :, 64], in1=esink[:, kv * 4:(kv + 1) * 4], op=ALU.add), [bpo, besk], [bden[so]])
                    DVE(lambda q, so=so: q.reciprocal(out=den[so], in_=den[so]), [bden[so]], [bden[so]])
                    DVE(lambda q, pov=pov, so=so, kv=kv: q.tensor_tensor(out=otok[so][:, kv * 4:(kv + 1) * 4, :], in0=pov[:, :, 0:64], in1=den[so].unsqueeze(2).to_broadcast([128, 4, 64]), op=ALU.mult), [bpo, bden[so]], [botok[so]])
                pt_, bpt = nps()
                ptb = pt_[:].bitcast(BF16)
                for c in range(4):
                    PE(lambda q, ptb=ptb, c=c, so=so: q.transpose(ptb[:, c * 128:(c + 1) * 128], otok[so][:, 2 * c:2 * c + 2, :].rearrange("p h n -> p (h n)"), ident_b), [botok[so], bcb], [bpt])
                ACT(lambda q, ptb=ptb, j=j: q.activation(out=catT[:, 4:8, j * 128:(j + 1) * 128], in_=ptb[:, 0:512].rearrange("p (c n) -> p c n", c=4), func=AF.Copy), [bpt], [bcat[4 + c][g] for c in range(4)])

        else:
            wnext()
            wnext()
            for c in range(4):
                for g in range(NG):
                    POOL(lambda q, c=c, g=g: q.memset(catT[:, 4 + c, g * 512:(g + 1) * 512], 0.0), [], [bcat[4 + c][g]])
        for i in range(2):
            sl, bs = wnext()
            wv = sl[:, 0:4096].rearrange("p (c n) -> p c n", c=8)
            for k in range(4):
                dc = i * 4 + k
                for g in range(NG):
                    ps, bp = nps()
                    for c in range(8):
                        PE(lambda q, ps=ps, wv=wv, c=c, k=k, g=g: q.matmul(ps[:], lhsT=wv[:, c, k * 128:(k + 1) * 128], rhs=catT[:, c, g * 512:(g + 1) * 512], start=(c == 0), stop=(c == 7)),
                           [bs, bcat[c][g]], [bp])
                    DVE(lambda q, ps=ps, dc=dc, g=g: q.scalar_tensor_tensor(out=xT[:, dc, g * 512:(g + 1) * 512], in0=ps[:], scalar=mod_G(l, 1, dc, g), in1=xT[:, dc, g * 512:(g + 1) * 512], op0=ALU.mult, op1=ALU.add),
                        [bp, bmod, bx[dc][g]], [bx[dc][g]])

    for l in range(2):
        ffn(l, 0)
        mixer(l)
        ffn(l, 1)

    areset()
    yst = [aalloc(1024), aalloc(1024)]
    byst = [nb("yst0"), nb("yst1")]
    for t in range(NT):
        s = t % 2
        g = t // 4
        for half in range(2):
            ps, bp = nps()
            for k in range(4):
                c = half * 4 + k
                PE(lambda q, ps=ps, k=k, c=c, t=t: q.transpose(ps[:, k * 128:(k + 1) * 128], xT[:, c, t * 128:(t + 1) * 128], ident), [bx[c][g], bcf], [bp])
            if half == 0:
                ACT(lambda q, ps=ps, s=s: q.activation(out=yst[s][:, 0:512], in_=ps[:], func=AF.Copy), [bp], [byst[s]])
            else:
                DVE(lambda q, ps=ps, s=s: q.tensor_copy(out=yst[s][:, 512:1024], in_=ps[:]), [bp], [byst[s]])
        DMA("sp", yout[t * 128:(t + 1) * 128, :], yst[s], byst[s], reads=[byst[s]])

    T.emit(nc)
    st.close()
    return nc


def _consts(sample_mode):
    cf = np.zeros((128, NCF), np.float32)
    cf[:, C_ID:C_ID + 128] = np.eye(128, dtype=np.float32)
    Pm = np.zeros((64, 64), np.float32)
    for d in range(64):
        blk = d // 16
        if blk % 2 == 0:
            Pm[d, d + 16] = -1.0
        else:
            Pm[d, d - 16] = 1.0
    P2 = np.zeros((128, 128), np.float32)
    P2[:64, :64] = Pm
    P2[64:, 64:] = Pm
    cf[:, C_PERM:C_PERM + 128] = P2.T
    t = np.arange(1024)
    inv = (10000.0 ** (-np.arange(0, 32, 2, dtype=np.float32) / 32)).astype(np.float32)
    d = np.arange(128) % 64
    pos = np.where((d < 32)[:, None], (t // 64)[None, :], (t % 64)[None, :]).astype(np.float32)
    ang = pos * inv[d % 16][:, None]
    if sample_mode:
        cf[:, C_COS:C_COS + 1024] = np.cos(ang)
        cf[:, C_SIN:C_SIN + 1024] = np.sin(ang)
    else:
        cf[:, C_COS:C_COS + 1024] = 1.0
    j = np.arange(64)
    cf[:64, C_MF:C_MF + 64] = (j[:, None] <= j[None, :]).astype(np.float32)
    cf[:64, C_MB:C_MB + 64] = (j[:, None] >= j[None, :]).astype(np.float32)
    cb = np.zeros((128, NCB), np.float32)
    cb[:, B_ONES:B_ONES + 128] = 1.0
    cb[:64, B_BLK:B_BLK + 64] = 1.0
    cb[64:, B_BLK + 64:B_BLK + 128] = 1.0
    cb[:, B_ID:B_ID + 128] = np.eye(128, dtype=np.float32)
    c = np.arange(64)
    a = 2 * np.pi * np.outer(c, c) / 64.0
    for gq in range(2):
        cb[gq * 64:(gq + 1) * 64, B_DFT + gq * 64:B_DFT + (gq + 1) * 64] = np.cos(a)
        cb[gq * 64:(gq + 1) * 64, B_DFT + 128 + gq * 64:B_DFT + 128 + (gq + 1) * 64] = np.sin(a)
    ki = np.arange(128)[:, None]
    qi = np.arange(128)[None, :]
    for jt in range(8):
        for s in range(3):
            if sample_mode:
                m = (ki >= qi) if s == 0 else (np.ones((128, 128), bool) if s == 1 else (ki <= qi))
            else:
                if s == 1:
                    m = np.ones((128, 128), bool)
                elif s == 0:
                    m = np.full((128, 128), jt % 2 == 1)
                else:
                    m = np.full((128, 128), jt % 2 == 0)
            cb[:, B_MASK + (jt * 3 + s) * 128:B_MASK + (jt * 3 + s + 1) * 128] = m.astype(np.float32)

    def dft(L):
        ll = np.arange(L)
        aa = 2 * np.pi * np.outer(ll, ll) / L
        sc_ = 1.0 / np.sqrt(L * 64.0)
        return (np.cos(aa) * sc_).astype(np.float32), (-np.sin(aa) * sc_).astype(np.float32)

    dftp = np.stack(dft(256))
    if sample_mode:
        dftl = np.stack(dft(1024))
    else:
        dftl = np.zeros((2, 1024, 1024), np.float32)
        for i in range(4):
            dftl[:, i * 256:(i + 1) * 256, i * 256:(i + 1) * 256] = dftp
    return cf, cb, dftl, dftp


_NC_CACHE = {}


def _prep(inputs):
    I = {k: np.asarray(v) for k, v in inputs.items()}
    xp, xs = I["x_prompt"], I["x_sample"]
    consts = {True: _consts(True), False: _consts(False)}
    fm = lambda v: np.ascontiguousarray(v.reshape(-1, 128).T)
    in_maps = []
    assign = []
    for core in range(8):
        sm = core < 4
        if sm:
            gtok = xs[core]
            pseq = [2 * core, 2 * core + 1]
            gseq = []
            cond = np.stack([I["c"][core], I["c_ctx"]])
        else:
            base = 8 + 6 * (core - 4)
            gseq = [base, base + 1, base + 2, base + 3]
            gtok = xp[gseq].reshape(1024, D)
            pseq = [base + 4, base + 5]
            cond = np.stack([I["c_ctx"], I["c_ctx"]])
        assign.append((sm, gseq, pseq))
        xin = np.concatenate([gtok, xp[pseq].reshape(512, D)], axis=0)
        par = np.zeros((128, NPAR), np.float32)
        par[:, P_COND:P_COND + 16] = cond.reshape(2, 8, 128).transpose(2, 1, 0).reshape(128, 16)
        for l in range(2):
            par[:, P_BADA + l * 72:P_BADA + (l + 1) * 72] = fm(I["b_ada"][l])
            for w3, nm in enumerate(["norm_ffn1", "norm_mix", "norm_ffn2"]):
                par[:, P_NRM + l * 24 + w3 * 8:P_NRM + l * 24 + w3 * 8 + 8] = fm(I[nm][l])
            for dr in range(2):
                par[:, P_LBS + l * 4 + dr * 2:P_LBS + l * 4 + dr * 2 + 2] = fm(I["hgrn_lower_bounds"][l, dr])
            par[:, P_HNRM + l * 2:P_HNRM + l * 2 + 2] = fm(I["hgrn_norm"][l])
            par[:, P_QN + l] = np.tile(I["q_norm"][l], 2)
            par[:, P_KN + l] = np.tile(I["k_norm"][l], 2)
            par[:, P_SINK + l * 8:P_SINK + (l + 1) * 8] = I["attn_sink"][l][None, :]
        par[:, P_KEEP] = 1.0 if sm else 0.0
        par[:, P_CBIAS] = 0.0 if sm else -30000.0
        cf, cb, dftl, dftp = consts[sm]
        if sm:
            ck = I["cache_attn_k"][core].reshape(2, 512, 128)
            cv = I["cache_attn_v"][core].reshape(2, 512, 128)
            s0 = I["state_hgrn"][core]
        else:
            ck = np.zeros((2, 512, 128), np.float32)
            cv = np.zeros((2, 512, 128), np.float32)
            s0 = np.zeros((2, 2, 4, 64, 64), np.float32)
        in_maps.append(dict(
            xin=np.ascontiguousarray(xin), par=par, cf=cf, cb=cb, dftl=dftl, dftp=dftp,
            ck=np.ascontiguousarray(ck), cv=np.ascontiguousarray(cv), s0=np.ascontiguousarray(s0),
            w_ada=I["w_ada"], ffn1_gu=I["ffn1_w_gate_up"], ffn2_gu=I["ffn2_w_gate_up"],
            ffn1_d=I["ffn1_w_down"], ffn2_d=I["ffn2_w_down"], w_in=I["w_in"], w_out=I["w_out"]))
    return in_maps, assign


def _post(results, assign):
    y_p = np.zeros((32, 256, D), np.float32)
    y_s = np.zeros((4, 1024, D), np.float32)
    nk = np.zeros((32, 2, 256, 2, 64), np.float32)
    nv = np.zeros((32, 2, 256, 2, 64), np.float32)
    ns = np.zeros((32, 2, 2, 4, 64, 64), np.float32)
    for core in range(8):
        r = results[core]
        sm, gseq, pseq = assign[core]
        y = r["yout"]
        ko = r["kout"].reshape(2, TOK, 2, 64)
        vo = r["vout"].reshape(2, TOK, 2, 64)
        so = r["sout"]
        if sm:
            y_s[core] = y[:1024]
        seqs = [(s, i) for i, s in enumerate(gseq)] + [(s, 4 + i) for i, s in enumerate(pseq)]
        for s, slot in seqs:
            y_p[s] = y[slot * 256:(slot + 1) * 256]
            nk[s] = ko[:, slot * 256:(slot + 1) * 256]
            nv[s] = vo[:, slot * 256:(slot + 1) * 256]
            ns[s] = so[:, slot]
    return (y_p, y_s, nk, nv, ns)


def kernel(**inputs):
    in_maps, assign = _prep(inputs)
    if "nc" not in _NC_CACHE:
        _NC_CACHE["nc"] = build()
    res = run_bass_kernel_spmd(_NC_CACHE["nc"], in_maps, core_ids=list(range(8)))
    return _post(res.results, assign)
```

```python
import numpy as np
from contextlib import ExitStack
import concourse.bass as bass
import concourse.mybir as mybir
from concourse.bass_utils import run_bass_kernel_spmd

F32 = mybir.dt.float32
BF16 = mybir.dt.bfloat16
AF = mybir.ActivationFunctionType
ALU = mybir.AluOpType

EP = 8192
ENGS = ("pe", "act", "dve", "pool", "sp")
SAME_ENGINE_SYNC = ("act", "dve", "pool")


class Buf:
    def __init__(self, name, fence=None):
        self.name = name
        self.w = None
        self.r = list(fence) if fence else []
        self.dsem = None
        self.dcnt = 0
        self.excl = False


class Op:
    pass


class Tracker:
    def __init__(self):
        self.ops = []
        self.per_eng = {e: [] for e in ENGS}
        self.dma_bufs = []
        self.dma_ids = []

    def fence(self):
        ids = [l[-1].id for l in self.per_eng.values() if l]
        return ids + list(self.dma_ids)

    def op(self, eng, fn, reads=(), writes=(), dma_buf=None):
        o = Op()
        o.eng = eng
        o.fn = fn
        o.dma_buf = dma_buf
        o.marked = False
        o.id = len(self.ops)
        o.pos = len(self.per_eng[eng])
        deps = set()
        for b in reads:
            if b.w is not None:
                deps.add(b.w)
            if b.excl:
                deps.update(b.r)
        for b in writes:
            if b.w is not None:
                deps.add(b.w)
            deps.update(b.r)
        o.deps = deps
        for b in reads:
            b.r.append(o.id)
        for b in writes:
            b.w = o.id
            b.r = []
        if dma_buf is not None:
            if dma_buf.dsem is None:
                dma_buf.dsem = True
                self.dma_bufs.append(dma_buf)
            dma_buf.dcnt += 16
            o.dval = dma_buf.dcnt
            self.dma_ids.append(o.id)
        self.ops.append(o)
        self.per_eng[eng].append(o)
        return o

    def plan(self):
        ops = self.ops
        seen = {e: {f: -1 for f in ENGS} for e in ENGS}
        seen_dma = {e: {} for e in ENGS}
        for o in ops:
            w_eng = {}
            w_dma = {}
            for d in o.deps:
                p = ops[d]
                if p.dma_buf is not None:
                    b = p.dma_buf
                    if seen_dma[o.eng].get(id(b), 0) >= p.dval:
                        continue
                    w_dma[id(b)] = (b, max(p.dval, w_dma.get(id(b), (None, 0))[1]))
                else:
                    if p.eng == o.eng:
                        if p.eng not in SAME_ENGINE_SYNC:
                            continue
                        if p.pos < o.pos - 2:
                            continue
                    if seen[o.eng][p.eng] >= d:
                        continue
                    w_eng[p.eng] = max(w_eng.get(p.eng, -1), d)
            for f, d in w_eng.items():
                seen[o.eng][f] = d
                ops[d].marked = True
            for k, (b, v) in w_dma.items():
                seen_dma[o.eng][k] = v
            o.w_eng = w_eng
            o.w_dma = list(w_dma.values())
        cnt = {e: 0 for e in ENGS}
        for o in ops:
            if o.dma_buf is None and o.marked:
                cnt[o.eng] += 1
            o.cnt = cnt[o.eng]
        self.final_cnt = cnt

    def emit(self, nc, final_eng="sp"):
        self.plan()
        ops = self.ops
        with ExitStack() as st:
            esems = {}
            for e in ENGS:
                n = max(1, (self.final_cnt[e] + EP - 1) // EP)
                esems[e] = [st.enter_context(nc.semaphore(f"s_{e}_{i}")) for i in range(n)]
            for i, b in enumerate(self.dma_bufs):
                b.dsem = st.enter_context(nc.semaphore(f"d_{i}"))
            block = st.enter_context(nc.Block())

            def run(eng_name, eng):
                for o in self.per_eng[eng_name]:
                    for f, d in o.w_eng.items():
                        c = ops[d].cnt
                        eng.wait_ge(esems[f][(c - 1) // EP], (c - 1) % EP + 1)
                    for (b, v) in o.w_dma:
                        eng.wait_ge(b.dsem, v)
                    ins = o.fn(eng)
                    if o.dma_buf is not None:
                        ins.then_inc(o.dma_buf.dsem, 16)
                    elif o.marked:
                        c = o.cnt
                        ins.then_inc(esems[eng_name][(c - 1) // EP], 1)
                if eng_name == final_eng:
                    for b in self.dma_bufs:
                        eng.wait_ge(b.dsem, b.dcnt)
                    for f in ENGS:
                        c = self.final_cnt[f]
                        if c > 0 and f != eng_name:
                            eng.wait_ge(esems[f][(c - 1) // EP], (c - 1) % EP + 1)

            @block.tensor
            def _(e):
                run("pe", e)

            @block.scalar
            def _(e):
                run("act", e)

            @block.vector
            def _(e):
                run("dve", e)

            @block.gpsimd
            def _(e):
                run("pool", e)

            @block.sync
            def _(e):
                run("sp", e)


TOK = 1536
NT = 12
NG = 3
D = 1024
DFF = 2816
EPS = 1e-6
P_COND, P_BADA, P_NRM, P_LBS, P_HNRM, P_QN, P_KN, P_SINK, P_KEEP, P_CBIAS, NPAR = 0, 16, 160, 208, 216, 220, 222, 224, 240, 241, 242
C_ID, C_PERM, C_COS, C_SIN, C_MF, C_MB, NCF = 0, 128, 256, 1280, 2304, 2368, 2432
B_ONES, B_BLK, B_ID, B_DFT, B_MASK, NCB = 0, 128, 256, 384, 640, 640 + 24 * 128
SLOT = 4096
ENABLE_HGRN = True
DEBUG_PAIR = None
ATTN_CORE = True
SKIP = set()
ENABLE_ATTN = True
NSL = 3
ARENA = 23400


def build(stop_after=None):
    nc = bass.Bass("TRN2", target_bir_lowering=False)
    T = Tracker()
    di = lambda n, s: nc.dram_tensor(n, s, F32, kind="ExternalInput").ap()
    do = lambda n, s: nc.dram_tensor(n, s, F32, kind="ExternalOutput").ap()
    xin = di("xin", [TOK, D]); par_d = di("par", [128, NPAR]); cf_d = di("cf", [128, NCF]); cb_d = di("cb", [128, NCB])
    dftl_d = di("dftl", [2, 1024, 1024]); dftp_d = di("dftp", [2, 256, 256])
    ck_d = di("ck", [2, 512, 128]); cv_d = di("cv", [2, 512, 128]); s0_d = di("s0", [2, 2, 4, 64, 64])
    wada_d = di("w_ada", [2, D, 9 * D])
    wgu_d = [di("ffn1_gu", [2, D, 2 * DFF]), di("ffn2_gu", [2, D, 2 * DFF])]
    wdn_d = [di("ffn1_d", [2, DFF, D]), di("ffn2_d", [2, DFF, D])]
    win_d = di("w_in", [2, D, 2304]); wout_d = di("w_out", [2, D, D])
    yout = do("yout", [TOK, D]); kout = do("kout", [2, TOK, 128]); vout = do("vout", [2, TOK, 128])
    sout = do("sout", [2, 6, 2, 4, 64, 64])

    st = ExitStack()
    sbt = lambda n, s, dt: st.enter_context(nc.sbuf_tensor("sb_" + n, s, dt))
    xT = sbt("xT", [128, 8, TOK], F32)
    hT = sbt("hT", [128, 8, TOK], BF16)
    par = sbt("par", [128, NPAR], F32)
    cf = sbt("cf", [128, NCF], F32)
    cb = sbt("cb", [128, NCB], BF16)
    modv = sbt("modv", [128, 2, 72, 2], F32)
    mA = sbt("mA", [128, 2, 3, 8, 2], F32)
    mG = sbt("mG", [128, 2, 3, 8, 2], F32)
    sml = sbt("sml", [128, 64], F32)
    slots = [sbt(f"slot{i}", [128, SLOT], BF16) for i in range(NSL)]
    arena = sbt("arena", [128, ARENA], F32)
    psum = [st.enter_context(nc.psum_tensor(f"ps{i}", [128, 512], F32)) for i in range(8)]
    bps = [Buf(f"ps{i}") for i in range(8)]
    for b_ in bps:
        b_.excl = True
    bslot = [Buf(f"slot{i}") for i in range(NSL)]
    bx = [[Buf(f"x{c}_{g}") for g in range(NG)] for c in range(8)]
    bh = [[Buf(f"h{c}_{g}") for g in range(NG)] for c in range(8)]
    bpar, bcf, bcb, bmod = Buf("par"), Buf("cf"), Buf("cb"), Buf("mod")
    bsml = Buf("sml")

    state = {"ps": 0, "aoff": 0, "nbanks": 8}

    def nps():
        i = state["ps"] % state["nbanks"]
        state["ps"] += 1
        return psum[i], bps[i]

    def aalloc(n32):
        o = state["aoff"]
        state["aoff"] += n32
        assert state["aoff"] <= ARENA, state["aoff"]
        return arena[:, o:o + n32]

    def areset():
        state["aoff"] = 0

    def nb(name):
        return Buf(name, T.fence())

    PE = lambda fn, r, w: T.op("pe", fn, r, w)
    ACT = lambda fn, r, w: T.op("act", fn, r, w)
    DVE = lambda fn, r, w: T.op("dve", fn, r, w)
    POOL = lambda fn, r, w: T.op("pool", fn, r, w)

    def DMA(eng, out, in_, buf, reads=(), writes=()):
        T.op(eng, lambda q: q.dma_start(out=out, in_=in_), reads, writes, dma_buf=buf)

    wq = []
    wstate = {"loaded": 0, "used": 0}

    def witem(parts):
        wq.append(parts)

    def wnext(keep_prev=False):
        i = wstate["used"]
        oldest = i - 1 if keep_prev else i
        while wstate["loaded"] < min(len(wq), oldest + NSL):
            k = wstate["loaded"]
            s = k % NSL
            for (fn, dap) in wq[k]:
                DMA("pool", fn(slots[s]), dap, bslot[s], writes=[bslot[s]])
            wstate["loaded"] += 1
        wstate["used"] += 1
        return slots[i % NSL], bslot[i % NSL]

    def kview(ap2d):
        return ap2d.rearrange("(c p) n -> p c n", p=128)

    def sched_weights():
        sched_ada(0)
        for l in range(2):
            for f in range(2):
                sched_ffn(l, f)
                if f == 0:
                    sched_mixer(l)

    def sched_ada(l):
        for i in range(18):
            witem([(lambda s: s[:, 0:4096].rearrange("p (c n) -> p c n", c=8), kview(wada_d[l][:, i * 512:(i + 1) * 512]))])

    def sched_ffn(l, f):
        for h in range(2):
            for j0 in range(0, 11, 2):
                nf = min(2, 11 - j0)
                c0 = (h * 11 + j0) * 128
                witem([
                    (lambda s, nf=nf: s[:, 0:8 * nf * 128].rearrange("p (c n) -> p c n", c=8), kview(wgu_d[f][l][:, c0:c0 + nf * 128])),
                    (lambda s, nf=nf: s[:, 2048:2048 + 8 * nf * 128].rearrange("p (c n) -> p c n", c=8), kview(wgu_d[f][l][:, DFF + c0:DFF + c0 + nf * 128])),
                ])

    def sched_mixer(l):
        w = win_d[l]
        witem([(lambda s: s[:, 0:2048].rearrange("p (c n) -> p c n", c=8), kview(w[:, 0:256]))])
        for hf in range(2):
            for cs in range(2):
                witem([(lambda s: s[:, 0:4096].rearrange("p (c n) -> p c n", c=8), kview(dftl_d[cs][:, hf * 512:(hf + 1) * 512]))])
        witem([(lambda s, cs=cs: s[:, cs * 512:(cs + 1) * 512].rearrange("p (c n) -> p c n", c=2), kview(dftp_d[cs])) for cs in range(2)])
        for pr in range(2):
            cols = [256 + pr * 128, 512 + pr * 128, 768 + pr * 128, 1024 + pr * 128, 1280 + pr * 128]
            witem([(lambda s, k=k: s[:, k * 1024:(k + 1) * 1024].rearrange("p (c n) -> p c n", c=8), kview(w[:, cols[k]:cols[k] + 128])) for k in range(3)])
            witem([(lambda s, k=k: s[:, k * 1024:(k + 1) * 1024].rearrange("p (c n) -> p c n", c=8), kview(w[:, cols[3 + k]:cols[3 + k] + 128])) for k in range(2)])
            if l == 0 and pr == 0:
                sched_ada(1)
        witem([(lambda s: s[:, 0:4096].rearrange("p (c n) -> p c n", c=8), kview(w[:, 1536:2048]))])
        parts = []
        for kv in range(2):
            for dup in range(2):
                parts.append((lambda s, kv=kv, dup=dup: s[:, kv * 1024:(kv + 1) * 1024].rearrange("p (c n) -> p c n", c=8)[:, :, dup * 64:(dup + 1) * 64],
                              kview(w[:, 2048 + kv * 64:2048 + (kv + 1) * 64])))
        parts.append((lambda s: s[:, 2048:3072].rearrange("p (c n) -> p c n", c=8), kview(w[:, 2176:2304])))
        witem(parts)
        for i in range(2):
            witem([(lambda s: s[:, 0:4096].rearrange("p (c n) -> p c n", c=8), kview(wout_d[l][:, i * 512:(i + 1) * 512]))])

    sched_weights()

    DMA("sp", par[:], par_d, bpar, writes=[bpar])
    DMA("sp", cf[:], cf_d, bcf, writes=[bcf])
    DMA("pool", cb[:], cb_d, bcb, writes=[bcb])
    ident = cf[:, C_ID:C_ID + 128]
    ones_b = cb[:, B_ONES:B_ONES + 128]
    blk_b = cb[:, B_BLK:B_BLK + 128]
    ident_b = cb[:, B_ID:B_ID + 128]

    areset()
    xst = [aalloc(1024), aalloc(1024)]
    bxst = [nb("xst0"), nb("xst1")]
    for t in range(NT):
        s = t % 2
        DMA("sp", xst[s], xin[t * 128:(t + 1) * 128, :], bxst[s], writes=[bxst[s]])
        for half in range(2):
            ps, bp = nps()
            for k in range(4):
                c = half * 4 + k
                PE(lambda q, ps=ps, k=k, c=c, s=s: q.transpose(ps[:, k * 128:(k + 1) * 128], xst[s][:, c * 128:(c + 1) * 128], ident),
                   [bxst[s], bcf], [bp])
            g = t // 4
            eng = ACT if half == 0 else DVE
            fn = (lambda q, ps=ps, half=half, t=t: q.activation(out=xT[:, half * 4:half * 4 + 4, t * 128:(t + 1) * 128], in_=ps[:].rearrange("p (k n) -> p k n", k=4), func=AF.Copy)) if half == 0 else \
                 (lambda q, ps=ps, half=half, t=t: q.tensor_copy(out=xT[:, half * 4:half * 4 + 4, t * 128:(t + 1) * 128], in_=ps[:].rearrange("p (k n) -> p k n", k=4)))
            eng(fn, [bp], [bx[c][g] for c in range(half * 4, half * 4 + 4)])

    areset()
    sc = sml[:, 0:16]
    scb = sml[:, 16:24].bitcast(BF16)
    bsc = Buf("sc")
    ACT(lambda q: q.activation(out=sc, in_=par[:, P_COND:P_COND + 16], func=AF.Exp, scale=-1.0), [bpar], [bsc])
    DVE(lambda q: q.tensor_scalar_add(out=sc, in0=sc, scalar1=1.0), [bsc], [bsc])
    DVE(lambda q: q.reciprocal(out=sc, in_=sc), [bsc], [bsc])
    DVE(lambda q: q.tensor_tensor(out=scb, in0=sc, in1=par[:, P_COND:P_COND + 16], op=ALU.mult), [bsc, bpar], [bsc])
    scb3 = scb.rearrange("p (c r) -> p c r", c=8)

    def adaln_gen(l, ps, bp):
        for i in range(18):
            sl, bs = wnext()
            wv = sl[:, 0:4096].rearrange("p (c n) -> p c n", c=8)
            for k in range(4):
                nn = i * 4 + k
                for c in range(8):
                    PE(lambda q, wv=wv, k=k, c=c, nn=nn: q.matmul(ps[:, nn * 2:nn * 2 + 2], lhsT=wv[:, c, k * 128:(k + 1) * 128], rhs=scb3[:, c, :], start=(c == 0), stop=(c == 7)),
                       [bs, bsc], [bp])
                yield
        DVE(lambda q: q.tensor_tensor(out=modv[:, l, :, :], in0=ps[:, 0:144].rearrange("p (n r) -> p n r", r=2),
                                      in1=par[:, P_BADA + l * 72:P_BADA + (l + 1) * 72].unsqueeze(2).to_broadcast([128, 72, 2]), op=ALU.add),
            [bp, bpar], [bmodl[l]])
        for w3 in range(3):
            nrm = par[:, P_NRM + l * 24 + w3 * 8:P_NRM + l * 24 + w3 * 8 + 8].unsqueeze(2).to_broadcast([128, 8, 2])
            DVE(lambda q, w3=w3, nrm=nrm: q.scalar_tensor_tensor(out=mA[:, l, w3, :, :], in0=modv[:, l, (3 * w3 + 1) * 8:(3 * w3 + 2) * 8, :], scalar=1.0, in1=nrm, op0=ALU.add, op1=ALU.mult),
                [bmodl[l], bpar], [bmodl[l]])
            DVE(lambda q, w3=w3: q.tensor_scalar_mul(out=mG[:, l, w3, :, :], in0=modv[:, l, (3 * w3 + 2) * 8:(3 * w3 + 3) * 8, :], scalar1=(1.0 if w3 == 1 else 0.5)),
                [bmodl[l]], [bmodl[l]])

    bmodl = [Buf("mod0"), Buf("mod1")]
    ps0_, bp0_ = nps()
    for _ in adaln_gen(0, ps0_, bp0_):
        pass

    def mod_A(l, w3, c, g):
        r = 0 if g < 2 else 1
        return mA[:, l, w3, c, r:r + 1]

    def mod_B(l, w3, c, g):
        r = 0 if g < 2 else 1
        return modv[:, l, (3 * w3) * 8 + c, r:r + 1]

    def mod_G(l, w3, c, g):
        r = 0 if g < 2 else 1
        return mG[:, l, w3, c, r:r + 1]

    def norm_mod(l, w3):
        sq = [aalloc(256).bitcast(BF16), aalloc(256).bitcast(BF16)]
        bsq = [nb("sq0"), nb("sq1")]
        tmp = [aalloc(512), aalloc(512)]
        btmp = [nb("tmp0"), nb("tmp1")]
        rs = aalloc(512)
        brs = nb("rs")
        for g in range(NG):
            ps, bp = nps()
            for c in range(8):
                s = c % 2
                ACT(lambda q, s=s, c=c, g=g: q.activation(out=sq[s], in_=xT[:, c, g * 512:(g + 1) * 512], func=AF.Square), [bx[c][g]], [bsq[s]])
                PE(lambda q, ps=ps, s=s, c=c: q.matmul(ps[:], lhsT=ones_b, rhs=sq[s], start=(c == 0), stop=(c == 7)), [bsq[s], bcb], [bp])
            DVE(lambda q, ps=ps: q.tensor_scalar(out=rs, in0=ps[:], scalar1=1.0 / D, scalar2=EPS, op0=ALU.mult, op1=ALU.add), [bp], [brs])
            ACT(lambda q: q.activation(out=rs, in_=rs, func=AF.Ln), [brs], [brs])
            ACT(lambda q: q.activation(out=rs, in_=rs, func=AF.Exp, scale=-0.5), [brs], [brs])
            for c in range(8):
                s = c % 2
                DVE(lambda q, s=s, c=c, g=g: q.tensor_tensor(out=tmp[s], in0=xT[:, c, g * 512:(g + 1) * 512], in1=rs, op=ALU.mult), [bx[c][g], brs], [btmp[s]])
                ACT(lambda q, s=s, c=c, g=g: q.activation(out=hT[:, c, g * 512:(g + 1) * 512], in_=tmp[s], func=AF.Identity,
                                                          scale=mod_A(l, w3, c, g), bias=mod_B(l, w3, c, g)), [btmp[s], bmodl[l]], [bh[c][g]])

    def ffn(l, f):
        w3 = 0 if f == 0 else 2
        areset()
        norm_mod(l, w3)
        actT = aalloc(11 * TOK // 2).bitcast(BF16).rearrange("p (j n) -> p j n", j=11)
        bact = [nb(f"act{g}") for g in range(NG)]
        wd = aalloc(11 * 1024 // 2).bitcast(BF16).rearrange("p (j n) -> p j n", j=11)
        bwd = nb("wd")
        ebuf = [aalloc(512), aalloc(512)]
        bebuf = [nb("e0"), nb("e1")]
        tbuf = [aalloc(512), aalloc(512)]
        btbuf = [nb("t0"), nb("t1")]
        k = 0
        for h in range(2):
            DMA("pool", wd, wdn_d[f][l][h * 1408:(h + 1) * 1408, :].rearrange("(j p) n -> p j n", p=128), bwd, writes=[bwd])
            for j0 in range(0, 11, 2):
                nf = min(2, 11 - j0)
                sl, bs = wnext()
                wg = sl[:, 0:8 * nf * 128].rearrange("p (c n) -> p c n", c=8)
                wu = sl[:, 2048:2048 + 8 * nf * 128].rearrange("p (c n) -> p c n", c=8)
                for jj in range(nf):
                    j = j0 + jj
                    for g in range(NG):
                        pg, bpg = nps()
                        pu, bpu = nps()
                        for c in range(8):
                            PE(lambda q, pg=pg, wg=wg, c=c, jj=jj, g=g: q.matmul(pg[:], lhsT=wg[:, c, jj * 128:(jj + 1) * 128], rhs=hT[:, c, g * 512:(g + 1) * 512], start=(c == 0), stop=(c == 7)),
                               [bs, bh[c][g]], [bpg])
                        for c in range(8):
                            PE(lambda q, pu=pu, wu=wu, c=c, jj=jj, g=g: q.matmul(pu[:], lhsT=wu[:, c, jj * 128:(jj + 1) * 128], rhs=hT[:, c, g * 512:(g + 1) * 512], start=(c == 0), stop=(c == 7)),
                               [bs, bh[c][g]], [bpu])
                        s = k % 2
                        k += 1
                        ACT(lambda q, pg=pg, s=s: q.activation(out=tbuf[s], in_=pg[:], func=AF.Silu), [bpg], [btbuf[s]])
                        DVE(lambda q, pu=pu, s=s, j=j, g=g: q.tensor_tensor(out=actT[:, j, g * 512:(g + 1) * 512], in0=pu[:], in1=tbuf[s], op=ALU.mult), [bpu, btbuf[s]], [bact[g]])
            for g in range(NG):
                for dc in range(8):
                    ps, bp = nps()
                    for j in range(11):
                        PE(lambda q, ps=ps, j=j, dc=dc, g=g: q.matmul(ps[:], lhsT=wd[:, j, dc * 128:(dc + 1) * 128], rhs=actT[:, j, g * 512:(g + 1) * 512], start=(j == 0), stop=(j == 10)),
                           [bwd, bact[g]], [bp])
                    DVE(lambda q, ps=ps, dc=dc, g=g: q.scalar_tensor_tensor(out=xT[:, dc, g * 512:(g + 1) * 512], in0=ps[:], scalar=mod_G(l, w3, dc, g), in1=xT[:, dc, g * 512:(g + 1) * 512], op0=ALU.mult, op1=ALU.add),
                        [bp, bmodl[l], bx[dc][g]], [bx[dc][g]])

    def mixer(l):
        areset()
        norm_mod(l, 1)
        areset()
        catT = aalloc(8 * TOK // 2).bitcast(BF16).rearrange("p (c n) -> p c n", c=8)
        bcat = [[nb(f"cat{c}_{g}") for g in range(NG)] for c in range(8)]
        base_off = state["aoff"]

        def inproj(wv, k, g, ps):
            for c in range(8):
                PE(lambda q, c=c: q.matmul(ps[0][:], lhsT=wv[:, c, k * 128:(k + 1) * 128], rhs=hT[:, c, g * 512:(g + 1) * 512], start=(c == 0), stop=(c == 7)),
                   [ps[2], bh[c][g]], [ps[1]])

        uT = aalloc(2 * TOK // 2).bitcast(BF16).rearrange("p (c n) -> p c n", c=2)
        buT = nb("uT")
        ucs = aalloc(NT * 2 * 256 // 2).bitcast(BF16).rearrange("p (t c n) -> p t c n", t=NT, c=2)
        bucs = nb("ucs")
        sl, bs = wnext()
        wv = sl[:, 0:2048].rearrange("p (c n) -> p c n", c=8)
        for k in range(2):
            for g in range(NG):
                ps, bp = nps()
                inproj(wv, k, g, (ps, bp, bs))
                ACT(lambda q, ps=ps, k=k, g=g: q.activation(out=uT[:, k, g * 512:(g + 1) * 512], in_=ps[:], func=AF.Copy), [bp], [buT])
        dft64 = cb[:, B_DFT:B_DFT + 256]
        for t in range(NT):
            ps, bp = nps()
            for k in range(2):
                PE(lambda q, ps=ps, k=k, t=t: q.matmul(ps[:, k * 256:(k + 1) * 256], lhsT=uT[:, k, t * 128:(t + 1) * 128], rhs=dft64, start=True, stop=True), [buT, bcb], [bp])
            DVE(lambda q, ps=ps, t=t: q.tensor_copy(out=ucs[:, t, :, :], in_=ps[:].rearrange("p (c n) -> p c n", c=2)), [bp], [bucs])
        for hf in range(2):
            slc, bsc_ = wnext()
            sls, bss = wnext(keep_prev=True)
            cl = slc[:, 0:4096].rearrange("p (c n) -> p c n", c=8)
            sn = sls[:, 0:4096].rearrange("p (c n) -> p c n", c=8)
            for k in range(2):
                ps, bp = nps()
                for lc in range(8):
                    PE(lambda q, ps=ps, k=k, lc=lc, cl=cl: q.matmul(ps[:], lhsT=ucs[:, lc, k, 0:128], rhs=cl[:, lc, :], start=(lc == 0), stop=False), [bucs, bsc_], [bp])
                    PE(lambda q, ps=ps, k=k, lc=lc, sn=sn: q.matmul(ps[:], lhsT=ucs[:, lc, k, 128:256], rhs=sn[:, lc, :], start=False, stop=(lc == 7)), [bucs, bss], [bp])
                ACT(lambda q, ps=ps, k=k, hf=hf: q.activation(out=catT[:, k, hf * 512:(hf + 1) * 512], in_=ps[:], func=AF.Copy), [bp], [bcat[k][hf]])
        slp, bsp = wnext()
        pc = slp[:, 0:512].rearrange("p (c n) -> p c n", c=2)
        pn = slp[:, 512:1024].rearrange("p (c n) -> p c n", c=2)
        for k in range(2):
            ps, bp = nps()
            for sq_ in range(2):
                for lc in range(2):
                    tt = 8 + sq_ * 2 + lc
                    PE(lambda q, ps=ps, k=k, lc=lc, tt=tt, sq_=sq_: q.matmul(ps[:, sq_ * 256:(sq_ + 1) * 256], lhsT=ucs[:, tt, k, 0:128], rhs=pc[:, lc, :], start=(lc == 0), stop=False), [bucs, bsp], [bp])
                    PE(lambda q, ps=ps, k=k, lc=lc, tt=tt, sq_=sq_: q.matmul(ps[:, sq_ * 256:(sq_ + 1) * 256], lhsT=ucs[:, tt, k, 128:256], rhs=pn[:, lc, :], start=False, stop=(lc == 1)), [bucs, bsp], [bp])
            ACT(lambda q, ps=ps, k=k: q.activation(out=catT[:, k, 1024:1536], in_=ps[:], func=AF.Copy), [bp], [bcat[k][2]])

        if ENABLE_HGRN:
            CH_ = 32
            for pr in range(2):
                if DEBUG_PAIR is not None and pr != DEBUG_PAIR:
                    wnext(); wnext()
                    for g in range(NG):
                        POOL(lambda q, pr=pr, g=g: q.memset(catT[:, 2 + pr, g * 512:(g + 1) * 512], 0.0), [], [bcat[2 + pr][g]])
                    continue
                state["aoff"] = base_off
                qf, sF, sB, tmpb, kb, hoT = [aalloc(TOK) for _ in range(6)]
                bqf, bsF, bsB, btmpb, bkb, bho = [nb(n) for n in ("qf", "sF", "sB", "tmpb", "kb", "hoT")]
                qt = [aalloc(TOK // 2).bitcast(BF16) for _ in range(2)]
                kt_ = [aalloc(TOK // 2).bitcast(BF16) for _ in range(2)]
                vT = aalloc(TOK // 2).bitcast(BF16)
                sgT = aalloc(TOK // 2).bitcast(BF16)
                bqt, bkt = [nb("qt0"), nb("qt1")], [nb("kt0"), nb("kt1")]
                bvT, bsg = nb("vT"), nb("sgT")
                ktp = [aalloc(128).bitcast(BF16).rearrange("p (a n) -> p a n", a=2) for _ in range(4)]
                vpd = [aalloc(128).bitcast(BF16).rearrange("p (a n) -> p a n", a=2) for _ in range(4)]
                bktp, bvpd = [nb("ktp") for _ in range(4)], [nb("vpd") for _ in range(4)]

                def diag(t):
                    x = t[0:CH_, 0, 0:64]
                    return bass.AP(tensor=x.tensor, offset=x.offset, ap=[list(x.ap[0]), [192, 2], [1, 64]])
                atb = [aalloc(64).bitcast(BF16) for _ in range(3)]
                batb = [nb("at") for _ in range(3)]
                S = aalloc(128); tS = aalloc(128); S0m = aalloc(64).bitcast(BF16)
                bS, btS, bS0m = nb("S"), nb("tS"), nb("S0m")
                sst = [aalloc(128), aalloc(128)]
                bsst = [nb("sst0"), nb("sst1")]
                scl = [[aalloc(48) for _ in range(4)] for _ in range(2)]
                bscl = [nb("scl0"), nb("scl1")]
                lbp = aalloc(8)
                blbp = nb("lbp")
                for i in range(4):
                    POOL(lambda q, i=i: q.memset(ktp[i], 0.0), [], [bktp[i]])
                    POOL(lambda q, i=i: q.memset(vpd[i], 0.0), [], [bvpd[i]])
                for dr in range(2):
                    c0 = dr * 3
                    if l == 0:
                        POOL(lambda q, c0=c0: q.memset(lbp[:, c0:c0 + 1], 1.0), [], [blbp])
                        POOL(lambda q, c0=c0: q.memset(lbp[:, c0 + 1:c0 + 2], 1e-30), [], [blbp])
                        POOL(lambda q, c0=c0: q.memset(lbp[:, c0 + 2:c0 + 3], -1.0), [], [blbp])
                    else:
                        a0 = P_LBS + 0 * 4 + dr * 2 + pr
                        a1 = P_LBS + 1 * 4 + dr * 2 + pr
                        DVE(lambda q, c0=c0, a0=a0, a1=a1: q.tensor_tensor(out=lbp[:, c0 + 1:c0 + 2], in0=par[:, a0:a0 + 1], in1=par[:, a1:a1 + 1], op=ALU.subtract), [bpar], [blbp])
                        ACT(lambda q, c0=c0: q.activation(out=lbp[:, c0 + 1:c0 + 2], in_=lbp[:, c0 + 1:c0 + 2], func=AF.Exp), [blbp], [blbp])
                        DVE(lambda q, c0=c0: q.tensor_scalar_add(out=lbp[:, c0 + 1:c0 + 2], in0=lbp[:, c0 + 1:c0 + 2], scalar1=1.0), [blbp], [blbp])
                        DVE(lambda q, c0=c0: q.reciprocal(out=lbp[:, c0 + 1:c0 + 2], in_=lbp[:, c0 + 1:c0 + 2]), [blbp], [blbp])
                        DVE(lambda q, c0=c0: q.tensor_scalar(out=lbp[:, c0:c0 + 1], in0=lbp[:, c0 + 1:c0 + 2], scalar1=-1.0, scalar2=1.0, op0=ALU.mult, op1=ALU.add), [blbp], [blbp])
                        DVE(lambda q, c0=c0: q.tensor_scalar_mul(out=lbp[:, c0 + 2:c0 + 3], in0=lbp[:, c0:c0 + 1], scalar1=-1.0), [blbp], [blbp])
                        DVE(lambda q, c0=c0: q.tensor_scalar_max(out=lbp[:, c0 + 1:c0 + 2], in0=lbp[:, c0 + 1:c0 + 2], scalar1=1e-30), [blbp], [blbp])
                sl, bs = wnext()
                for k in range(3):
                    wv = sl[:, k * 1024:(k + 1) * 1024].rearrange("p (c n) -> p c n", c=8)
                    for g in range(NG):
                        ps, bp = nps()
                        inproj(wv, 0, g, (ps, bp, bs))
                        gs = slice(g * 512, (g + 1) * 512)
                        if k == 0:
                            ACT(lambda q, ps=ps, gs=gs: q.activation(out=qf[:, gs], in_=ps[:], func=AF.Copy, scale=0.125), [bp], [bqf])
                        elif k == 1:
                            ACT(lambda q, ps=ps, gs=gs: q.activation(out=vT[:, gs], in_=ps[:], func=AF.Copy), [bp], [bvT])
                        else:
                            ACT(lambda q, ps=ps, gs=gs: q.activation(out=sF[:, gs], in_=ps[:], func=AF.Exp, scale=-1.0), [bp], [bsF])
                sl, bs = wnext()
                for k in range(2):
                    wv = sl[:, k * 1024:(k + 1) * 1024].rearrange("p (c n) -> p c n", c=8)
                    for g in range(NG):
                        ps, bp = nps()
                        inproj(wv, 0, g, (ps, bp, bs))
                        gs = slice(g * 512, (g + 1) * 512)
                        if k == 0:
                            ACT(lambda q, ps=ps, gs=gs: q.activation(out=sB[:, gs], in_=ps[:], func=AF.Exp, scale=-1.0), [bp], [bsB])
                        else:
                            ACT(lambda q, ps=ps, gs=gs: q.activation(out=tmpb[:, gs], in_=ps[:], func=AF.Exp, scale=-1.0), [bp], [btmpb])
                            DVE(lambda q, gs=gs: q.tensor_scalar_add(out=tmpb[:, gs], in0=tmpb[:, gs], scalar1=1.0), [btmpb], [btmpb])
                            DVE(lambda q, gs=gs: q.reciprocal(out=tmpb[:, gs], in_=tmpb[:, gs]), [btmpb], [btmpb])
                            DVE(lambda q, ps=ps, gs=gs: q.tensor_tensor(out=sgT[:, gs], in0=ps[:], in1=tmpb[:, gs], op=ALU.mult), [bp, btmpb], [bsg])
                for (sx, bsx) in ((sF, bsF), (sB, bsB)):
                    DVE(lambda q, sx=sx: q.tensor_scalar_add(out=sx, in0=sx, scalar1=1.0), [bsx], [bsx])
                    DVE(lambda q, sx=sx: q.reciprocal(out=sx, in_=sx), [bsx], [bsx])
                CH = 32
                NCH = TOK // CH
                SPC = 256 // CH
                for dr, (sx, bsx) in enumerate(((sF, bsF), (sB, bsB))):
                    c0 = dr * 3
                    DVE(lambda q, sx=sx, c0=c0: q.tensor_scalar(out=kb, in0=sx, scalar1=lbp[:, c0 + 2:c0 + 3], scalar2=lbp[:, c0:c0 + 1], op0=ALU.mult, op1=ALU.add), [bsx, blbp], [bkb])
                    DVE(lambda q, sx=sx, c0=c0: q.tensor_scalar(out=sx, in0=sx, scalar1=lbp[:, c0:c0 + 1], scalar2=lbp[:, c0 + 1:c0 + 2], op0=ALU.mult, op1=ALU.add), [bsx, blbp], [bsx])
                    ACT(lambda q, sx=sx: q.activation(out=sx, in_=sx, func=AF.Ln), [bsx], [bsx])
                    A_, bA_, B_, bB_ = sx, bsx, tmpb, btmpb
                    sh = 1
                    while sh < CH:
                        A3 = A_.rearrange("p (c n) -> p c n", n=CH)
                        B3 = B_.rearrange("p (c n) -> p c n", n=CH)
                        if dr == 0:
                            ACT(lambda q, A3=A3, B3=B3, sh=sh: q.activation(out=B3[:, :, 0:sh], in_=A3[:, :, 0:sh], func=AF.Copy), [bA_], [bB_])
                            DVE(lambda q, A3=A3, B3=B3, sh=sh: q.tensor_tensor(out=B3[:, :, sh:CH], in0=A3[:, :, sh:CH], in1=A3[:, :, 0:CH - sh], op=ALU.add), [bA_], [bB_])
                        else:
                            ACT(lambda q, A3=A3, B3=B3, sh=sh: q.activation(out=B3[:, :, CH - sh:CH], in_=A3[:, :, CH - sh:CH], func=AF.Copy), [bA_], [bB_])
                            DVE(lambda q, A3=A3, B3=B3, sh=sh: q.tensor_tensor(out=B3[:, :, 0:CH - sh], in0=A3[:, :, 0:CH - sh], in1=A3[:, :, sh:CH], op=ALU.add), [bA_], [bB_])
                        A_, bA_, B_, bB_ = B_, bB_, A_, bA_
                        sh *= 2
                    bq_, bbq_, tq_, btq_ = A_, bA_, B_, bB_
                    b3 = bq_.rearrange("p (c n) -> p c n", n=CH)
                    mid, Ti = (CH // 2 - 1, CH - 1) if dr == 0 else (CH // 2, 0)
                    bm, em, eT, eTm = scl[dr]
                    DVE(lambda q, b3=b3, mid=mid, bm=bm: q.tensor_copy(out=bm, in_=b3[:, :, mid]), [bbq_], [bscl[dr]])
                    ACT(lambda q, bm=bm, em=em: q.activation(out=em, in_=bm, func=AF.Exp), [bscl[dr]], [bscl[dr]])
                    ACT(lambda q, b3=b3, Ti=Ti, eT=eT: q.activation(out=eT, in_=b3[:, :, Ti], func=AF.Exp), [bbq_], [bscl[dr]])
                    DVE(lambda q, b3=b3, Ti=Ti, bm=bm, eTm=eTm: q.tensor_tensor(out=eTm, in0=b3[:, :, Ti], in1=bm, op=ALU.subtract), [bbq_, bscl[dr]], [bscl[dr]])
                    ACT(lambda q, eTm=eTm: q.activation(out=eTm, in_=eTm, func=AF.Exp), [bscl[dr]], [bscl[dr]])
                    DVE(lambda q, b3=b3, bm=bm: q.tensor_tensor(out=b3, in0=b3, in1=bm.unsqueeze(2).to_broadcast([128, NCH, CH]), op=ALU.subtract), [bbq_, bscl[dr]], [bbq_])
                    ACT(lambda q, bq_=bq_, tq_=tq_: q.activation(out=tq_, in_=bq_, func=AF.Exp), [bbq_], [btq_])
                    DVE(lambda q, dr=dr, tq_=tq_: q.tensor_tensor(out=qt[dr], in0=qf, in1=tq_, op=ALU.mult), [bqf, btq_], [bqt[dr]])
                    ACT(lambda q, bq_=bq_, tq_=tq_: q.activation(out=tq_, in_=bq_, func=AF.Exp, scale=-1.0), [bbq_], [btq_])
                    DVE(lambda q, dr=dr, tq_=tq_: q.tensor_tensor(out=kt_[dr], in0=kb, in1=tq_, op=ALU.mult), [bkb, btq_], [bkt[dr]])
                ev = [0]

                def emit_state(slot, dr):
                    i = ev[0] % 2
                    ev[0] += 1
                    ACT(lambda q, i=i: q.activation(out=sst[i], in_=S, func=AF.Copy), [bS], [bsst[i]])
                    for hd in range(2):
                        DMA("sp", sout[l, slot, dr, 2 * pr + hd], sst[i][hd * 64:(hd + 1) * 64, hd * 64:(hd + 1) * 64], bsst[i], reads=[bsst[i]])

                def load_state(dr):
                    POOL(lambda q: q.memset(S, 0.0), [bS], [bS])
                    for hd in range(2):
                        DMA("sp", S[hd * 64:(hd + 1) * 64, hd * 64:(hd + 1) * 64], s0_d[l, dr, 2 * pr + hd], bS, reads=[bS], writes=[bS])

                Xb = [aalloc(128), aalloc(128), aalloc(128)]
                ada = None
                if l == 0 and pr == 0:
                    state["nbanks"] = 7
                    ada = adaln_gen(1, psum[7], bps[7])
                hoB = aalloc(TOK // 2).bitcast(BF16)
                bhoB = nb("hoB")
                bXb = [nb("X") for _ in range(3)]
                for dr in range(2):
                    bm, em, eT, eTm = scl[dr]
                    order = list(range(NCH)) if dr == 0 else list(range(NCH - 1, -1, -1))
                    mcol = C_MF if dr == 0 else C_MB
                    pst = {}

                    live = {}

                    def A_pe(n_, dr=dr, order=order, live=live):
                        ch = order[n_]
                        cs = slice(ch * CH, (ch + 1) * CH)
                        ptk, bptk = nps()
                        ptkb = ptk[:].bitcast(BF16)
                        live[("ptk", n_)] = (ptkb, bptk)
                        PE(lambda q: q.transpose(ptkb[0:CH, 0:128], kt_[dr][:, cs], ident_b), [bkt[dr], bcb], [bptk])
                        PE(lambda q: q.transpose(ptkb[0:CH, 128:256], vT[:, cs], ident_b), [bvT, bcb], [bptk])

                    def A_post(n_, live=live):
                        i3 = n_ % 4
                        ptkb, bptk = live.pop(("ptk", n_))
                        ACT(lambda q: q.activation(out=diag(ktp[i3]), in_=ptkb[0:CH, 0:128].rearrange("p (a n) -> p a n", a=2), func=AF.Copy), [bptk], [bktp[i3]])
                        ACT(lambda q: q.activation(out=diag(vpd[i3]), in_=ptkb[0:CH, 128:256].rearrange("p (a n) -> p a n", a=2), func=AF.Copy), [bptk], [bvpd[i3]])

                    def B_pe(n_, dr=dr, order=order, live=live):
                        ch = order[n_]
                        i3 = n_ % 4
                        cs = slice(ch * CH, (ch + 1) * CH)
                        pas = []
                        for hd in range(2):
                            hs = slice(hd * 64, (hd + 1) * 64)
                            pA, bpA = nps()
                            pas.append((pA, bpA))
                            PE(lambda q, pA=pA, hs=hs: q.matmul(pA[0:CH, 0:CH], lhsT=kt_[dr][hs, cs], rhs=qt[dr][hs, cs], start=True, stop=True), [bkt[dr], bqt[dr]], [bpA])
                        pS2, bpS2 = nps()
                        PE(lambda q: q.matmul(pS2[:, 0:128], lhsT=ktp[i3][0:CH, 0, :], rhs=vpd[i3][0:CH, 0, :], start=True, stop=False), [bktp[i3], bvpd[i3]], [bpS2])
                        PE(lambda q: q.matmul(pS2[:, 0:128], lhsT=ktp[i3][0:CH, 1, :], rhs=vpd[i3][0:CH, 1, :], start=False, stop=True), [bktp[i3], bvpd[i3]], [bpS2])
                        live[("B", n_)] = (pas, pS2, bpS2)

                    def B_post(n_, dr=dr, eTm=eTm, mcol=mcol, order=order, live=live):
                        ch = order[n_]
                        i = n_ % 3
                        pas, pS2, bpS2 = live.pop(("B", n_))
                        for hd, (pA, bpA) in enumerate(pas):
                            DVE(lambda q, pA=pA, hd=hd: q.tensor_tensor(out=atb[i][0:CH, hd * CH:(hd + 1) * CH], in0=pA[0:CH, 0:CH], in1=cf[0:CH, mcol:mcol + CH], op=ALU.mult), [bpA, bcf], [batb[i]])
                        DVE(lambda q: q.tensor_scalar_mul(out=Xb[i], in0=pS2[:, 0:128], scalar1=eTm[:, ch:ch + 1]), [bpS2, bscl[dr]], [bXb[i]])

                    def C_dve(n_, dr=dr, em=em, eT=eT, order=order):
                        ch = order[n_]
                        i = n_ % 3
                        first_of_seq = (ch % SPC == 0) if dr == 0 else (ch % SPC == SPC - 1)
                        if first_of_seq:
                            seq = ch // SPC
                            prev = seq - 1 if dr == 0 else seq + 1
                            if n_ > 0:
                                emit_state(prev, dr)
                            if (dr == 0 and seq == 0) or (dr == 1 and seq == 3):
                                load_state(dr)
                            elif seq >= 4:
                                POOL(lambda q: q.memset(S, 0.0), [bS], [bS])
                            else:
                                DVE(lambda q: q.tensor_scalar_mul(out=S, in0=S, scalar1=par[:, P_KEEP:P_KEEP + 1]), [bS, bpar], [bS])
                        DVE(lambda q: q.tensor_scalar_mul(out=S0m, in0=S, scalar1=em[:, ch:ch + 1]), [bS, bscl[dr]], [bS0m])
                        DVE(lambda q: q.scalar_tensor_tensor(out=S, in0=S, scalar=eT[:, ch:ch + 1], in1=Xb[i], op0=ALU.mult, op1=ALU.add), [bS, bscl[dr], bXb[i]], [bS])

                    def C_pe(n_, dr=dr, order=order):
                        ch = order[n_]
                        i = n_ % 3
                        i3 = n_ % 4
                        cs = slice(ch * CH, (ch + 1) * CH)
                        po, bpo = nps()
                        PE(lambda q: q.matmul(po[:, 0:CH], lhsT=vpd[i3][0:CH, 0, :], rhs=atb[i][0:CH, 0:CH], start=True, stop=False), [bvpd[i3], batb[i]], [bpo])
                        PE(lambda q: q.matmul(po[:, 0:CH], lhsT=vpd[i3][0:CH, 1, :], rhs=atb[i][0:CH, CH:2 * CH], start=False, stop=False), [bvpd[i3], batb[i]], [bpo])
                        PE(lambda q: q.matmul(po[:, 0:CH], lhsT=S0m, rhs=qt[dr][:, cs], start=False, stop=True), [bS0m, bqt[dr]], [bpo])
                        if dr == 0:
                            ACT(lambda q: q.activation(out=hoT[:, cs], in_=po[:, 0:CH], func=AF.Copy), [bpo], [bho])
                        else:
                            ACT(lambda q: q.activation(out=hoB[:, cs], in_=po[:, 0:CH], func=AF.Copy), [bpo], [bhoB])

                    for k_ in range(3):
                        A_pe(k_); A_post(k_)
                    for k_ in range(2):
                        B_pe(k_); B_post(k_)
                    for n_ in range(NCH):
                        C_dve(n_)
                        if n_ + 3 < NCH:
                            A_pe(n_ + 3)
                        if n_ + 2 < NCH:
                            B_pe(n_ + 2)
                        C_pe(n_)
                        if ada is not None:
                            next(ada, None)
                        if n_ + 3 < NCH:
                            A_post(n_ + 3)
                        if n_ + 2 < NCH:
                            B_post(n_ + 2)
                    emit_state(0 if dr == 1 else 5, dr)
                if ada is not None:
                    for _ in ada:
                        pass
                    state["nbanks"] = 8
                DVE(lambda q: q.tensor_tensor(out=hoT, in0=hoT, in1=hoB, op=ALU.add), [bho, bhoB], [bho])
                for g in range(NG):
                    gs = slice(g * 512, (g + 1) * 512)
                    sqh = tmpb[:, 0:256].bitcast(BF16)
                    ACT(lambda q, gs=gs, sqh=sqh: q.activation(out=sqh, in_=hoT[:, gs], func=AF.Square), [bho], [btmpb])
                    p2, bp2 = nps()
                    PE(lambda q, p2=p2, sqh=sqh: q.matmul(p2[:], lhsT=blk_b, rhs=sqh, start=True, stop=True), [btmpb, bcb], [bp2])
                    rv = kb[:, 0:512]
                    DVE(lambda q, p2=p2, rv=rv: q.tensor_scalar(out=rv, in0=p2[:], scalar1=1.0 / 64, scalar2=EPS, op0=ALU.mult, op1=ALU.add), [bp2], [bkb])
                    ACT(lambda q, rv=rv: q.activation(out=rv, in_=rv, func=AF.Ln), [bkb], [bkb])
                    ACT(lambda q, rv=rv: q.activation(out=rv, in_=rv, func=AF.Exp, scale=-0.5), [bkb], [bkb])
                    DVE(lambda q, rv=rv, gs=gs, pr=pr: q.scalar_tensor_tensor(out=rv, in0=hoT[:, gs], scalar=par[:, P_HNRM + l * 2 + pr:P_HNRM + l * 2 + pr + 1], in1=rv, op0=ALU.mult, op1=ALU.mult), [bho, bkb, bpar], [bkb])
                    DVE(lambda q, rv=rv, gs=gs, g=g, pr=pr: q.tensor_tensor(out=catT[:, 2 + pr, gs], in0=rv, in1=sgT[:, gs], op=ALU.mult), [bkb, bsg], [bcat[2 + pr][g]])
        else:
            for pr in range(2):
                wnext()
                wnext()
                for g in range(NG):
                    POOL(lambda q, pr=pr, g=g: q.memset(catT[:, 2 + pr, g * 512:(g + 1) * 512], 0.0), [], [bcat[2 + pr][g]])
        if ENABLE_ATTN:
            state["aoff"] = base_off
            qT = aalloc(4 * TOK // 2).bitcast(BF16).rearrange("p (c n) -> p c n", c=4)
            kdT = aalloc(4 * TOK // 2).bitcast(BF16).rearrange("p (c v n) -> p c v n", c=2, v=2)
            kcT = aalloc(4 * 512 // 2).bitcast(BF16).rearrange("p (c v n) -> p c v n", c=2, v=2)
            vaug = aalloc(16 * 2 * 96 // 2).bitcast(BF16).rearrange("p (t k n) -> p t k n", t=16, k=2)
            kst = aalloc(NT * 128).rearrange("p (t n) -> p t n", t=NT)
            vst = aalloc(NT * 128).rearrange("p (t n) -> p t n", t=NT)
            kcd = aalloc(4 * 256).rearrange("p (t k n) -> p t k n", t=4, k=2)
            zq = [aalloc(512), aalloc(512)]
            rr = [aalloc(512), aalloc(512)]
            t1 = [aalloc(512)] * 2
            sqa = [aalloc(256).bitcast(BF16), aalloc(256).bitcast(BF16)]
            pT = [aalloc(256).bitcast(BF16) for _ in range(3)]
            otok = [aalloc(256).bitcast(BF16).rearrange("p (h n) -> p h n", h=8), aalloc(256).bitcast(BF16).rearrange("p (h n) -> p h n", h=8)]
            den = [aalloc(4), aalloc(4)]
            esink = aalloc(8)
            bq = [[nb("q") for g in range(NG)] for c in range(4)]
            bkd = [[nb("kd") for g in range(NG)] for c in range(2)]
            bkc, bva, bkst, bvst, bkcd, besk = nb("kc"), [nb("va") for t in range(16)], nb("kst"), nb("vst"), nb("kcd"), nb("esk")
            bzq, brr, bt1, bsqa, bpT, botok, bden = [[nb(n + str(i)) for i in range(3)] for n in ("zq", "rr", "t1", "sqa", "pT", "otok", "den")]
            bt1[1] = bt1[0]
            ACT(lambda q: q.activation(out=esink, in_=par[:, P_SINK + l * 8:P_SINK + l * 8 + 8], func=AF.Exp), [bpar], [besk])
            POOL(lambda q: q.memset(vaug[:, :, :, 64:66], 1.0), [], list(bva))
            allkd = [bkd[c_][g_] for c_ in range(2) for g_ in range(NG)]
            POOL(lambda q: q.memset(kdT[64:128, :, 0, :], 0.0), [], allkd)
            POOL(lambda q: q.memset(kdT[0:64, :, 1, :], 0.0), [], allkd)
            POOL(lambda q: q.memset(kcT[64:128, :, 0, :], 0.0), [], [bkc])
            POOL(lambda q: q.memset(kcT[0:64, :, 1, :], 0.0), [], [bkc])
            for kv in (range(2) if "cachek" not in SKIP else []):
                for dup in range(2):
                    DMA("sp", kcd[:, :, kv, dup * 64:(dup + 1) * 64], ck_d[l][:, kv * 64:(kv + 1) * 64].rearrange("(t p) d -> p t d", p=128), bkcd, writes=[bkcd])
            for t in (range(4) if "cachev" not in SKIP else []):
                DMA("pool", vaug[:, 12 + t, :, 0:64], cv_d[l][t * 128:(t + 1) * 128, :].rearrange("p (k d) -> p k d", k=2), bva[12 + t], writes=[bva[12 + t]])
            for kv in (range(2) if "cachek" not in SKIP else []):
                ps, bp = nps()
                for t in range(4):
                    PE(lambda q, ps=ps, t=t, kv=kv: q.transpose(ps[:, t * 128:(t + 1) * 128], kcd[:, t, kv, :], ident), [bkcd, bcf], [bp])
                ACT(lambda q, ps=ps, kv=kv: q.activation(out=kcT[0:64, kv, 0, :], in_=ps[0:64, :], func=AF.Copy), [bp], [bkc])
                ACT(lambda q, ps=ps, kv=kv: q.activation(out=kcT[64:128, kv, 1, :], in_=ps[64:128, :], func=AF.Copy), [bp], [bkc])
            cnt = [0]

            def qk_unit(s, wv, kcol, bs, g, nw_col, dst, bdst, scale_extra, kfin=None):
                ps, bp = nps()
                inproj(wv, kcol, g, (ps, bp, bs))
                yield
                ACT(lambda q: q.activation(out=zq[s], in_=ps[:], func=AF.Copy), [bp], [bzq[s]])
                ACT(lambda q: q.activation(out=sqa[s], in_=ps[:], func=AF.Square), [bp], [bsqa[s]])
                yield
                p2, bp2 = nps()
                PE(lambda q: q.matmul(p2[:], lhsT=blk_b, rhs=sqa[s], start=True, stop=True), [bsqa[s], bcb], [bp2])
                yield
                DVE(lambda q: q.tensor_scalar(out=rr[s], in0=p2[:], scalar1=1.0 / 64, scalar2=EPS, op0=ALU.mult, op1=ALU.add), [bp2], [brr[s]])
                ACT(lambda q: q.activation(out=rr[s], in_=rr[s], func=AF.Ln), [brr[s]], [brr[s]])
                ACT(lambda q: q.activation(out=rr[s], in_=rr[s], func=AF.Exp, scale=-0.5), [brr[s]], [brr[s]])
                yield
                DVE(lambda q: q.scalar_tensor_tensor(out=zq[s], in0=zq[s], scalar=par[:, nw_col:nw_col + 1], in1=rr[s], op0=ALU.mult, op1=ALU.mult), [bzq[s], brr[s], bpar], [bzq[s]])
                yield
                if g < 2:
                    p3, bp3 = nps()
                    PE(lambda q: q.matmul(p3[:], lhsT=cf[:, C_PERM:C_PERM + 128], rhs=zq[s], start=True, stop=True), [bzq[s], bcf], [bp3])
                    yield
                    DVE(lambda q: q.tensor_tensor(out=t1[s], in0=p3[:], in1=cf[:, C_SIN + g * 512:C_SIN + (g + 1) * 512], op=ALU.mult), [bp3, bcf], [bt1[s]])
                    DVE(lambda q: q.tensor_tensor(out=zq[s], in0=zq[s], in1=cf[:, C_COS + g * 512:C_COS + (g + 1) * 512], op=ALU.mult), [bzq[s], bcf], [bzq[s]])
                    DVE(lambda q: q.tensor_tensor(out=zq[s], in0=zq[s], in1=t1[s], op=ALU.add), [bzq[s], bt1[s]], [bzq[s]])
                    yield
                if kfin is None:
                    ACT(lambda q: q.activation(out=dst, in_=zq[s], func=AF.Copy, scale=scale_extra), [bzq[s]], [bdst])
                else:
                    ACT(lambda q: q.activation(out=dst[0][0:64, :], in_=zq[s][0:64, :], func=AF.Copy), [bzq[s]], [bdst])
                    ACT(lambda q: q.activation(out=dst[1][64:128, :], in_=zq[s][64:128, :], func=AF.Copy), [bzq[s]], [bdst])
                    kv = kfin
                    p4, bp4 = nps()
                    for tt in range(4):
                        PE(lambda q, tt=tt: q.transpose(p4[:, tt * 64:(tt + 1) * 64], zq[s][0:64, tt * 128:(tt + 1) * 128], ident[0:64, 0:64]), [bzq[s], bcf], [bp4])
                    yield
                    DVE(lambda q: q.tensor_copy(out=kst[:, g * 4:(g + 1) * 4, kv * 64:(kv + 1) * 64], in_=p4[:, 0:256].rearrange("p (t n) -> p t n", t=4)), [bp4], [bkst])

            def run_interleaved(specs, width=2):
                specs = list(specs)
                free = list(range(width))
                active = []
                while specs or active:
                    while specs and free:
                        sidx = free.pop(0)
                        active.append((sidx, specs.pop(0)(sidx)))
                    for item in list(active):
                        sidx, gen = item
                        try:
                            next(gen)
                        except StopIteration:
                            active.remove(item)
                            free.append(sidx)

            sl, bs = wnext()
            wv = sl[:, 0:4096].rearrange("p (c n) -> p c n", c=8)
            run_interleaved([(lambda s_, c=c, g=g, wv=wv, bs=bs: qk_unit(s_, wv, c, bs, g, P_QN + l, qT[:, c, g * 512:(g + 1) * 512], bq[c][g], 0.125))
                             for c in range(4) for g in range(NG)])
            sl, bs = wnext()
            run_interleaved([(lambda s_, kv=kv, g=g, sl=sl, bs=bs: qk_unit(s_, sl[:, kv * 1024:(kv + 1) * 1024].rearrange("p (c n) -> p c n", c=8), 0, bs, g, P_KN + l,
                                                                          (kdT[:, kv, 0, g * 512:(g + 1) * 512], kdT[:, kv, 1, g * 512:(g + 1) * 512]), bkd[kv][g], 1.0, kfin=kv))
                             for kv in range(2) for g in range(NG)])
            wv = sl[:, 2048:3072].rearrange("p (c n) -> p c n", c=8)
            for g in range(NG):
                ps, bp = nps()
                inproj(wv, 0, g, (ps, bp, bs))
                s = cnt[0] % 2
                cnt[0] += 1
                ACT(lambda q, ps=ps, s=s: q.activation(out=zq[s], in_=ps[:], func=AF.Copy), [bp], [bzq[s]])
                if "vtr" in SKIP:
                    continue
                p4, bp4 = nps()
                for tt in range(4):
                    PE(lambda q, tt=tt, s=s, p4=p4: q.transpose(p4[:, tt * 128:(tt + 1) * 128], zq[s][:, tt * 128:(tt + 1) * 128], ident), [bzq[s], bcf], [bp4])
                if "vtr_dve" not in SKIP:
                    DVE(lambda q, p4=p4, g=g: q.tensor_copy(out=vst[:, g * 4:(g + 1) * 4, :], in_=p4[:].rearrange("p (t n) -> p t n", t=4)), [bp4], [bvst])
                for tt in (range(4) if "vtr_act" not in SKIP else []):
                    ACT(lambda q, p4=p4, g=g, tt=tt: q.activation(out=vaug[:, g * 4 + tt, :, 0:64], in_=p4[:, tt * 128:(tt + 1) * 128].rearrange("p (k d) -> p k d", k=2), func=AF.Copy), [bp4], [bva[g * 4 + tt]])
            if "kvout" not in SKIP:
                DMA("sp", kout[l].rearrange("(t p) n -> p t n", p=128), kst, bkst, reads=[bkst])
            if "kvout" not in SKIP:
                DMA("sp", vout[l].rearrange("(t p) n -> p t n", p=128), vst, bvst, reads=[bvst])
            it = 0
            if not ATTN_CORE:
                for c in range(4):
                    for g in range(NG):
                        POOL(lambda q, c=c, g=g: q.memset(catT[:, 4 + c, g * 512:(g + 1) * 512], 0.0), [], [bcat[4 + c][g]])
            for j in (range(NT) if ATTN_CORE else []):
                g = j // 4
                if j < 8:
                    kts = [("n", j + d_, (j * 3 + 1 + d_) if d_ != 0 else None) for d_ in (-1, 0, 1) if 0 <= j + d_ < 8] + [("c", t, None) for t in range(4)]
                else:
                    b0 = 8 + 2 * ((j - 8) // 2)
                    kts = [("n", b0, None), ("n", b0 + 1, None)]
                so = j % 2
                for kv in range(2):
                    po, bpo = nps()
                    pov = po[:, 0:264].rearrange("p (h n) -> p h n", h=4)
                    nk = len(kts)

                    def scores(ki, j=j, kv=kv, g=g):
                        kind, kt, mi = kts[ki]
                        pS, bpS = nps()
                        for hh in range(4):
                            cq = kv * 2 + hh // 2
                            hf = hh % 2
                            if kind == "n":
                                lhs = kdT[:, kv, hf, kt * 128:(kt + 1) * 128]
                                rb = bkd[kv][kt // 4]
                            else:
                                lhs = kcT[:, kv, hf, kt * 128:(kt + 1) * 128]
                                rb = bkc
                            PE(lambda q, pS=pS, hh=hh, lhs=lhs, cq=cq, j=j: q.matmul(pS[:, hh * 128:(hh + 1) * 128], lhsT=lhs, rhs=qT[:, cq, j * 128:(j + 1) * 128], start=True, stop=True),
                               [rb, bq[cq][g]], [bpS])
                        return pS, bpS

                    LOOK = 2
                    sc = {}
                    for ki in range(min(LOOK, nk)):
                        sc[ki] = scores(ki)
                    for ki, (kind, kt, mi) in enumerate(kts):
                        pS, bpS = sc.pop(ki)
                        s = it % 3
                        it += 1
                        if kind == "c":
                            ACT(lambda q, pS=pS, s=s: q.activation(out=pT[s], in_=pS[:], func=AF.Exp, bias=par[:, P_CBIAS:P_CBIAS + 1]), [bpS, bpar], [bpT[s]])
                        else:
                            ACT(lambda q, pS=pS, s=s: q.activation(out=pT[s], in_=pS[:], func=AF.Exp), [bpS], [bpT[s]])
                        if mi is not None:
                            DVE(lambda q, s=s, mi=mi: q.tensor_tensor(out=pT[s].rearrange("p (h n) -> p h n", h=4), in0=pT[s].rearrange("p (h n) -> p h n", h=4),
                                                                      in1=cb[:, B_MASK + mi * 128:B_MASK + (mi + 1) * 128].unsqueeze(1).to_broadcast([128, 4, 128]), op=ALU.mult), [bpT[s], bcb], [bpT[s]])
                        if ki + LOOK < nk:
                            sc[ki + LOOK] = scores(ki + LOOK)
                        vt = kt if kind == "n" else 12 + kt
                        for hh in range(4):
                            PE(lambda q, pov=pov, hh=hh, s=s, vt=vt, kv=kv, ki=ki, n=nk: q.matmul(pov[:, hh, 0:66], lhsT=pT[s][:, hh * 128:(hh + 1) * 128], rhs=vaug[:, vt, kv, 0:66], start=(ki == 0 and hh == 0), stop=(ki == n - 1 and hh == 3)),
                               [bpT[s], bva[vt]], [bpo])
                    DVE(lambda q, pov=pov, so=so, kv=kv: q.tensor_tensor(out=den[so], in0=pov[:, :, 64], in1=esink[:, kv * 4:(kv + 1) * 4], op=ALU.add), [bpo, besk], [bden[so]])
                    DVE(lambda q, so=so: q.reciprocal(out=den[so], in_=den[so]), [bden[so]], [bden[so]])
                    DVE(lambda q, pov=pov, so=so, kv=kv: q.tensor_tensor(out=otok[so][:, kv * 4:(kv + 1) * 4, :], in0=pov[:, :, 0:64], in1=den[so].unsqueeze(2).to_broadcast([128, 4, 64]), op=ALU.mult), [bpo, bden[so]], [botok[so]])
                pt_, bpt = nps()
                ptb = pt_[:].bitcast(BF16)
                for c in range(4):
                    PE(lambda q, ptb=ptb, c=c, so=so: q.transpose(ptb[:, c * 128:(c + 1) * 128], otok[so][:, 2 * c:2 * c + 2, :].rearrange("p h n -> p (h n)"), ident_b), [botok[so], bcb], [bpt])
                ACT(lambda q, ptb=ptb, j=j: q.activation(out=catT[:, 4:8, j * 128:(j + 1) * 128], in_=ptb[:, 0:512].rearrange("p (c n) -> p c n", c=4), func=AF.Copy), [bpt], [bcat[4 + c][g] for c in range(4)])

        else:
            wnext()
            wnext()
            for c in range(4):
                for g in range(NG):
                    POOL(lambda q, c=c, g=g: q.memset(catT[:, 4 + c, g * 512:(g + 1) * 512], 0.0), [], [bcat[4 + c][g]])
        for i in range(2):
            sl, bs = wnext()
            wv = sl[:, 0:4096].rearrange("p (c n) -> p c n", c=8)
            for k in range(4):
                dc = i * 4 + k
                for g in range(NG):
                    ps, bp = nps()
                    for c in range(8):
                        PE(lambda q, ps=ps, wv=wv, c=c, k=k, g=g: q.matmul(ps[:], lhsT=wv[:, c, k * 128:(k + 1) * 128], rhs=catT[:, c, g * 512:(g + 1) * 512], start=(c == 0), stop=(c == 7)),
                           [bs, bcat[c][g]], [bp])
                    DVE(lambda q, ps=ps, dc=dc, g=g: q.scalar_tensor_tensor(out=xT[:, dc, g * 512:(g + 1) * 512], in0=ps[:], scalar=mod_G(l, 1, dc, g), in1=xT[:, dc, g * 512:(g + 1) * 512], op0=ALU.mult, op1=ALU.add),
                        [bp, bmodl[l], bx[dc][g]], [bx[dc][g]])

    for l in range(2):
        ffn(l, 0)
        mixer(l)
        ffn(l, 1)

    areset()
    yst = [aalloc(1024), aalloc(1024)]
    byst = [nb("yst0"), nb("yst1")]
    for t in range(NT):
        s = t % 2
        g = t // 4
        for half in range(2):
            ps, bp = nps()
            for k in range(4):
                c = half * 4 + k
                PE(lambda q, ps=ps, k=k, c=c, t=t: q.transpose(ps[:, k * 128:(k + 1) * 128], xT[:, c, t * 128:(t + 1) * 128], ident), [bx[c][g], bcf], [bp])
            if half == 0:
                ACT(lambda q, ps=ps, s=s: q.activation(out=yst[s][:, 0:512], in_=ps[:], func=AF.Copy), [bp], [byst[s]])
            else:
                DVE(lambda q, ps=ps, s=s: q.tensor_copy(out=yst[s][:, 512:1024], in_=ps[:]), [bp], [byst[s]])
        DMA("sp", yout[t * 128:(t + 1) * 128, :], yst[s], byst[s], reads=[byst[s]])

    T.emit(nc)
    st.close()
    return nc


def _consts(sample_mode):
    cf = np.zeros((128, NCF), np.float32)
    cf[:, C_ID:C_ID + 128] = np.eye(128, dtype=np.float32)
    Pm = np.zeros((64, 64), np.float32)
    for d in range(64):
        blk = d // 16
        if blk % 2 == 0:
            Pm[d, d + 16] = -1.0
        else:
            Pm[d, d - 16] = 1.0
    P2 = np.zeros((128, 128), np.float32)
    P2[:64, :64] = Pm
    P2[64:, 64:] = Pm
    cf[:, C_PERM:C_PERM + 128] = P2.T
    t = np.arange(1024)
    inv = (10000.0 ** (-np.arange(0, 32, 2, dtype=np.float32) / 32)).astype(np.float32)
    d = np.arange(128) % 64
    pos = np.where((d < 32)[:, None], (t // 64)[None, :], (t % 64)[None, :]).astype(np.float32)
    ang = pos * inv[d % 16][:, None]
    if sample_mode:
        cf[:, C_COS:C_COS + 1024] = np.cos(ang)
        cf[:, C_SIN:C_SIN + 1024] = np.sin(ang)
    else:
        cf[:, C_COS:C_COS + 1024] = 1.0
    j = np.arange(64)
    cf[:64, C_MF:C_MF + 64] = (j[:, None] <= j[None, :]).astype(np.float32)
    cf[:64, C_MB:C_MB + 64] = (j[:, None] >= j[None, :]).astype(np.float32)
    cb = np.zeros((128, NCB), np.float32)
    cb[:, B_ONES:B_ONES + 128] = 1.0
    cb[:64, B_BLK:B_BLK + 64] = 1.0
    cb[64:, B_BLK + 64:B_BLK + 128] = 1.0
    cb[:, B_ID:B_ID + 128] = np.eye(128, dtype=np.float32)
    c = np.arange(64)
    a = 2 * np.pi * np.outer(c, c) / 64.0
    for gq in range(2):
        cb[gq * 64:(gq + 1) * 64, B_DFT + gq * 64:B_DFT + (gq + 1) * 64] = np.cos(a)
        cb[gq * 64:(gq + 1) * 64, B_DFT + 128 + gq * 64:B_DFT + 128 + (gq + 1) * 64] = np.sin(a)
    ki = np.arange(128)[:, None]
    qi = np.arange(128)[None, :]
    for jt in range(8):
        for s in range(3):
            if sample_mode:
                m = (ki >= qi) if s == 0 else (np.ones((128, 128), bool) if s == 1 else (ki <= qi))
            else:
                if s == 1:
                    m = np.ones((128, 128), bool)
                elif s == 0:
                    m = np.full((128, 128), jt % 2 == 1)
                else:
                    m = np.full((128, 128), jt % 2 == 0)
            cb[:, B_MASK + (jt * 3 + s) * 128:B_MASK + (jt * 3 + s + 1) * 128] = m.astype(np.float32)

    def dft(L):
        ll = np.arange(L)
        aa = 2 * np.pi * np.outer(ll, ll) / L
        sc_ = 1.0 / np.sqrt(L * 64.0)
        return (np.cos(aa) * sc_).astype(np.float32), (-np.sin(aa) * sc_).astype(np.float32)

    dftp = np.stack(dft(256))
    if sample_mode:
        dftl = np.stack(dft(1024))
    else:
        dftl = np.zeros((2, 1024, 1024), np.float32)
        for i in range(4):
            dftl[:, i * 256:(i + 1) * 256, i * 256:(i + 1) * 256] = dftp
    return cf, cb, dftl, dftp


_NC_CACHE = {}


def _prep(inputs):
    I = {k: np.asarray(v) for k, v in inputs.items()}
    xp, xs = I["x_prompt"], I["x_sample"]
    consts = {True: _consts(True), False: _consts(False)}
    fm = lambda v: np.ascontiguousarray(v.reshape(-1, 128).T)
    in_maps = []
    assign = []
    for core in range(8):
        sm = core < 4
        if sm:
            gtok = xs[core]
            pseq = [2 * core, 2 * core + 1]
            gseq = []
            cond = np.stack([I["c"][core], I["c_ctx"]])
        else:
            base = 8 + 6 * (core - 4)
            gseq = [base, base + 1, base + 2, base + 3]
            gtok = xp[gseq].reshape(1024, D)
            pseq = [base + 4, base + 5]
            cond = np.stack([I["c_ctx"], I["c_ctx"]])
        assign.append((sm, gseq, pseq))
        xin = np.concatenate([gtok, xp[pseq].reshape(512, D)], axis=0)
        par = np.zeros((128, NPAR), np.float32)
        par[:, P_COND:P_COND + 16] = cond.reshape(2, 8, 128).transpose(2, 1, 0).reshape(128, 16)
        for l in range(2):
            par[:, P_BADA + l * 72:P_BADA + (l + 1) * 72] = fm(I["b_ada"][l])
            for w3, nm in enumerate(["norm_ffn1", "norm_mix", "norm_ffn2"]):
                par[:, P_NRM + l * 24 + w3 * 8:P_NRM + l * 24 + w3 * 8 + 8] = fm(I[nm][l])
            for dr in range(2):
                par[:, P_LBS + l * 4 + dr * 2:P_LBS + l * 4 + dr * 2 + 2] = fm(I["hgrn_lower_bounds"][l, dr])
            par[:, P_HNRM + l * 2:P_HNRM + l * 2 + 2] = fm(I["hgrn_norm"][l])
            par[:, P_QN + l] = np.tile(I["q_norm"][l], 2)
            par[:, P_KN + l] = np.tile(I["k_norm"][l], 2)
            par[:, P_SINK + l * 8:P_SINK + (l + 1) * 8] = I["attn_sink"][l][None, :]
        par[:, P_KEEP] = 1.0 if sm else 0.0
        par[:, P_CBIAS] = 0.0 if sm else -30000.0
        cf, cb, dftl, dftp = consts[sm]
        if sm:
            ck = I["cache_attn_k"][core].reshape(2, 512, 128)
            cv = I["cache_attn_v"][core].reshape(2, 512, 128)
            s0 = I["state_hgrn"][core]
        else:
            ck = np.zeros((2, 512, 128), np.float32)
            cv = np.zeros((2, 512, 128), np.float32)
            s0 = np.zeros((2, 2, 4, 64, 64), np.float32)
        in_maps.append(dict(
            xin=np.ascontiguousarray(xin), par=par, cf=cf, cb=cb, dftl=dftl, dftp=dftp,
            ck=np.ascontiguousarray(ck), cv=np.ascontiguousarray(cv), s0=np.ascontiguousarray(s0),
            w_ada=I["w_ada"], ffn1_gu=I["ffn1_w_gate_up"], ffn2_gu=I["ffn2_w_gate_up"],
            ffn1_d=I["ffn1_w_down"], ffn2_d=I["ffn2_w_down"], w_in=I["w_in"], w_out=I["w_out"]))
    return in_maps, assign


def _post(results, assign):
    y_p = np.zeros((32, 256, D), np.float32)
    y_s = np.zeros((4, 1024, D), np.float32)
    nk = np.zeros((32, 2, 256, 2, 64), np.float32)
    nv = np.zeros((32, 2, 256, 2, 64), np.float32)
    ns = np.zeros((32, 2, 2, 4, 64, 64), np.float32)
    for core in range(8):
        r = results[core]
        sm, gseq, pseq = assign[core]
        y = r["yout"]
        ko = r["kout"].reshape(2, TOK, 2, 64)
        vo = r["vout"].reshape(2, TOK, 2, 64)
        so = r["sout"]
        if sm:
            y_s[core] = y[:1024]
        seqs = [(s, i) for i, s in enumerate(gseq)] + [(s, 4 + i) for i, s in enumerate(pseq)]
        for s, slot in seqs:
            y_p[s] = y[slot * 256:(slot + 1) * 256]
            nk[s] = ko[:, slot * 256:(slot + 1) * 256]
            nv[s] = vo[:, slot * 256:(slot + 1) * 256]
            ns[s] = so[:, slot]
    return (y_p, y_s, nk, nv, ns)


def kernel(**inputs):
    in_maps, assign = _prep(inputs)
    if "nc" not in _NC_CACHE:
        _NC_CACHE["nc"] = build()
    res = run_bass_kernel_spmd(_NC_CACHE["nc"], in_maps, core_ids=list(range(8)))
    return _post(res.results, assign)
```
